# Optimizing a Trainium2 kernel written in Bass

```python
import math
import jax, jax.numpy as jnp
from jax import lax
import numpy as np

D_MODEL = 1024
BATCH = 4
SEQ = 4096
DEPTH = 4

GRID_W = 64
N_HEADS = 8
N_KV_HEADS = 2
HEAD_DIM = 128
Q_BLOCK = 128
ROPE_THETA = 10000.0
C_CONV = D_MODEL
CONV_K = 31
N_EXPERTS = 16
D_FF_EXPERT = 2 * D_MODEL
EC_FACTOR = 2
LN_EPS = 1e-5
RMS_EPS = 1e-6
ALPHA = (2.0 * DEPTH) ** 0.25
BETA = (8.0 * DEPTH) ** -0.25

Q_W = N_HEADS * HEAD_DIM
KV_W = N_KV_HEADS * HEAD_DIM
OFF_CONV = 0
OFF_Q = OFF_CONV + 2 * C_CONV
OFF_K = OFF_Q + Q_W
OFF_V = OFF_K + KV_W
OFF_GC = OFF_V + KV_W
OFF_GA = OFF_GC + D_MODEL
N_IN = OFF_GA + D_MODEL

kernel_name = "hybrid_conv_axialgqa_ecmoe_deepnorm"


def layer_norm(x, g, b):
    xf = x.astype(jnp.float32)
    mu = jnp.mean(xf, axis=-1, keepdims=True)
    xc = xf - mu
    var = jnp.mean(xc * xc, axis=-1, keepdims=True)
    return (xc * lax.rsqrt(var + LN_EPS) * g + b).astype(x.dtype)


def rms_norm(x, g):
    xf = x.astype(jnp.float32)
    ms = jnp.mean(xf * xf, axis=-1, keepdims=True)
    return (xf * lax.rsqrt(ms + RMS_EPS) * g).astype(x.dtype)


def axial_rope_tables(seq):
    rows = seq // GRID_W
    row = jnp.repeat(jnp.arange(rows, dtype=jnp.int32), GRID_W).astype(jnp.float32)
    col = jnp.tile(jnp.arange(GRID_W, dtype=jnp.int32), rows).astype(jnp.float32)
    axis_dim = HEAD_DIM // 2
    freqs = 1.0 / (ROPE_THETA ** (jnp.arange(0, axis_dim, 2, dtype=jnp.float32) / axis_dim))
    ang = jnp.concatenate([row[:, None] * freqs[None], col[:, None] * freqs[None]], axis=-1)
    return jnp.cos(ang), jnp.sin(ang)


def apply_rope(x, cos, sin):
    xf = x.astype(jnp.float32).reshape(x.shape[:-1] + (HEAD_DIM // 2, 2))
    x0, x1 = xf[..., 0], xf[..., 1]
    c = cos[None, :, None, :]
    s = sin[None, :, None, :]
    out = jnp.stack([x0 * c - x1 * s, x0 * s + x1 * c], axis=-1)
    return out.reshape(x.shape).astype(x.dtype)


def gqa_attention(q, k, v):
    b, s = q.shape[0], q.shape[1]
    n_blk = s // Q_BLOCK
    grp = N_HEADS // N_KV_HEADS
    scale = 1.0 / math.sqrt(HEAD_DIM)
    qb = q.reshape(b, n_blk, Q_BLOCK, N_KV_HEADS, grp, HEAD_DIM).transpose(1, 0, 2, 3, 4, 5)

    def block(qi):
        sc = jnp.einsum('bqkgd,bskd->bkgqs', qi, k, preferred_element_type=jnp.float32) * scale
        p = jax.nn.softmax(sc, axis=-1)
        return jnp.einsum('bkgqs,bskd->bqkgd', p.astype(v.dtype), v)

    o = lax.map(block, qb)
    return o.transpose(1, 0, 2, 3, 4, 5).reshape(b, s, N_HEADS * HEAD_DIM)


def conformer_conv(val, gate, dw, dw_b, ln_g, ln_b, pw_w, pw_b):
    u = val * jax.nn.sigmoid(gate)
    u = lax.conv_general_dilated(
        u, dw[:, None, :], window_strides=(1,),
        padding=[(CONV_K // 2, CONV_K // 2)],
        dimension_numbers=('NWC', 'WIO', 'NWC'),
        feature_group_count=C_CONV) + dw_b
    u = jax.nn.silu(layer_norm(u, ln_g, ln_b))
    return u @ pw_w + pw_b


def expert_choice_ffn(h, w_router, w_gate, w_up, w_down):
    b, s, d = h.shape
    cap = EC_FACTOR * s // N_EXPERTS
    logits = jnp.einsum('bsd,de->bse', h, w_router, preferred_element_type=jnp.float32)
    aff = jax.nn.softmax(logits, axis=-1)
    g, idx = lax.top_k(aff.transpose(0, 2, 1), cap)
    xe = jax.vmap(lambda hb, ib: hb[ib])(h, idx)
    hid = jax.nn.silu(jnp.einsum('becd,edf->becf', xe, w_gate)) * jnp.einsum('becd,edf->becf', xe, w_up)
    ye = jnp.einsum('becf,efd->becd', hid, w_down) * g[..., None].astype(h.dtype)
    out = jax.vmap(lambda ib, yb: jnp.zeros((s, d), yb.dtype).at[ib.reshape(-1)].add(yb.reshape(-1, d)))(idx, ye)
    return out


def setup_inputs(seed: int = 0) -> dict:
    key = jax.random.key(seed)
    ks = jax.random.split(key, 24)
    L, D = DEPTH, D_MODEL
    f32 = jnp.float32

    def nrm(k, shape, scale):
        return jax.random.normal(k, shape, f32) * scale

    col_scale = jnp.concatenate([
        jnp.ones((OFF_V,), f32), jnp.full((KV_W,), BETA, f32), jnp.ones((N_IN - OFF_GC,), f32)])
    w_in = nrm(ks[1], (L, D, N_IN), D ** -0.5) * col_scale
    return {
        "x": nrm(ks[0], (BATCH, SEQ, D), 1.0),
        "ln0_g": 1.0 + nrm(ks[2], (D,), 0.02),
        "ln0_b": nrm(ks[3], (D,), 0.02),
        "w_in": w_in,
        "b_in": nrm(ks[4], (L, N_IN), 0.02),
        "conv_dw": nrm(ks[5], (L, CONV_K, C_CONV), CONV_K ** -0.5),
        "conv_dw_b": nrm(ks[6], (L, C_CONV), 0.02),
        "conv_ln_g": 1.0 + nrm(ks[7], (L, C_CONV), 0.02),
        "conv_ln_b": nrm(ks[8], (L, C_CONV), 0.02),
        "conv_pw_w": nrm(ks[9], (L, C_CONV, D), BETA * C_CONV ** -0.5),
        "conv_pw_b": nrm(ks[10], (L, D), 0.02),
        "q_norm_g": 1.0 + nrm(ks[11], (L, HEAD_DIM), 0.02),
        "k_norm_g": 1.0 + nrm(ks[12], (L, HEAD_DIM), 0.02),
        "w_o": nrm(ks[13], (L, Q_W, D), BETA * Q_W ** -0.5),
        "w_out": nrm(ks[14], (L, D, D), BETA * D ** -0.5),
        "b_out": nrm(ks[15], (L, D), 0.02),
        "ln1_g": 1.0 + nrm(ks[16], (L, D), 0.02),
        "ln1_b": nrm(ks[17], (L, D), 0.02),
        "w_router": nrm(ks[18], (L, D, N_EXPERTS), D ** -0.5),
        "w_gate": nrm(ks[19], (L, N_EXPERTS, D, D_FF_EXPERT), D ** -0.5),
        "w_up": nrm(ks[20], (L, N_EXPERTS, D, D_FF_EXPERT), D ** -0.5),
        "w_down": nrm(ks[21], (L, N_EXPERTS, D_FF_EXPERT, D), BETA * D_FF_EXPERT ** -0.5),
        "ln2_g": 1.0 + nrm(ks[22], (L, D), 0.02),
        "ln2_b": nrm(ks[23], (L, D), 0.02),
    }


def reference(x, ln0_g, ln0_b, w_in, b_in, conv_dw, conv_dw_b, conv_ln_g, conv_ln_b,
              conv_pw_w, conv_pw_b, q_norm_g, k_norm_g, w_o, w_out, b_out, ln1_g, ln1_b,
              w_router, w_gate, w_up, w_down, ln2_g, ln2_b):
    b, s, _ = x.shape
    cos, sin = axial_rope_tables(s)
    x = layer_norm(x, ln0_g, ln0_b)
    for l in range(DEPTH):
        h = x
        z = h @ w_in[l] + b_in[l]
        y_c = conformer_conv(z[..., OFF_CONV:OFF_CONV + C_CONV], z[..., OFF_CONV + C_CONV:OFF_Q],
                             conv_dw[l], conv_dw_b[l], conv_ln_g[l], conv_ln_b[l],
                             conv_pw_w[l], conv_pw_b[l])
        q = z[..., OFF_Q:OFF_K].reshape(b, s, N_HEADS, HEAD_DIM)
        k = z[..., OFF_K:OFF_V].reshape(b, s, N_KV_HEADS, HEAD_DIM)
        v = z[..., OFF_V:OFF_GC].reshape(b, s, N_KV_HEADS, HEAD_DIM)
        q = apply_rope(rms_norm(q, q_norm_g[l]), cos, sin)
        k = apply_rope(rms_norm(k, k_norm_g[l]), cos, sin)
        y_a = gqa_attention(q, k, v) @ w_o[l]
        g_c = jax.nn.sigmoid(z[..., OFF_GC:OFF_GA])
        g_a = jax.nn.sigmoid(z[..., OFF_GA:N_IN])
        mix = (g_c * y_c + g_a * y_a) @ w_out[l] + b_out[l]
        x = layer_norm(ALPHA * x + mix, ln1_g[l], ln1_b[l])
        moe = expert_choice_ffn(x, w_router[l], w_gate[l], w_up[l], w_down[l])
        x = layer_norm(ALPHA * x + moe, ln2_g[l], ln2_b[l])
    return x
```

```python
import math
import numpy as np
from contextlib import ExitStack
import concourse.bass as bass
import concourse.mybir as mybir
from concourse.bass_utils import run_bass_kernel_spmd

F32 = mybir.dt.float32
BF16 = mybir.dt.bfloat16
U32 = mybir.dt.uint32
AF = mybir.ActivationFunctionType
ALU = mybir.AluOpType

S_LEN = 4096
D = 1024
NL = 4
TT = 512
NT = S_LEN // TT
XW = 544
PADC = 16
NH = 8
NKV = 2
DH = 128
NE = 16
CAP = 512
DFF = 2048
KC = 31
N_IN = 5632
OFF_Q = 2048
OFF_K = 3072
OFF_V = 3328
OFF_GC = 3584
OFF_GA = 4608
LN_EPS = 1e-5
RMS_EPS = 1e-6
ALPHA = (2.0 * NL) ** 0.25
ATT_SCALE = 1.0 / math.sqrt(DH)
NRING = 4

ENGS = ("tensor", "vector", "scalar", "gpsimd", "sync")


class Sched:
    def __init__(self, nc, stack):
        self.nc = nc
        self.stack = stack
        self.prog = {e: [] for e in ENGS}
        self.cnt = {e: 0 for e in ENGS}
        self.esem = {e: stack.enter_context(nc.semaphore("es_" + e)) for e in ENGS}
        self.known = {e: {} for e in ENGS}
        self.last_w = {}
        self.readers = {}
        self.dsem = {}
        self.dcnt = {}
        self.nops = 0
        self.nwaits = 0

    def _sem_for_key(self, key):
        if key not in self.dsem:
            self.dsem[key] = self.stack.enter_context(self.nc.semaphore("ds%d" % len(self.dsem)))
            self.dcnt[key] = 0
        return self.dsem[key]

    def _waits(self, eng, reads, writes):
        waits = {}
        own = self.esem[eng].num

        def need(sv, raw):
            s, v = sv
            if s.num == own and not raw and eng == "tensor":
                return
            cur = waits.get(s.num)
            if cur is None or cur[1] < v:
                waits[s.num] = (s, v)

        for k in reads:
            for sv in self.last_w.get(k, {}).values():
                need(sv, True)
        for k in writes:
            for sv in self.last_w.get(k, {}).values():
                need(sv, False)
            for sv in self.readers.get(k, {}).values():
                need(sv, False)
        out = []
        kn = self.known[eng]
        for num, (s, v) in waits.items():
            if kn.get(num, 0) >= v:
                continue
            kn[num] = v
            out.append((s, v))
        return out

    def _commit(self, s, v, reads, writes):
        for k in writes:
            self.last_w.setdefault(k, {})[s.num] = (s, v)
            self.readers[k] = {}
        for k in reads:
            self.readers.setdefault(k, {})[s.num] = (s, v)

    def op(self, eng, fn, reads=(), writes=()):
        waits = self._waits(eng, reads, writes)
        self.cnt[eng] += 1
        s = self.esem[eng]
        self.prog[eng].append((waits, fn, s, 1))
        self._commit(s, self.cnt[eng], reads, writes)
        self.nops += 1
        self.nwaits += len(waits)

    def dma(self, eng, fn, reads=(), writes=(), semkey=None):
        waits = self._waits(eng, reads, writes)
        key = semkey if semkey is not None else writes[0]
        s = self._sem_for_key(key)
        self.dcnt[key] += 16
        self.prog[eng].append((waits, fn, s, 16))
        self._commit(s, self.dcnt[key], reads, writes)
        self.nops += 1
        self.nwaits += len(waits)

    def finish(self, eng, keys):
        waits = self._waits(eng, keys, ())
        self.prog[eng].append((waits, None, None, 0))

    def emit(self):
        with self.nc.Block() as block:
            for e in ENGS:
                prog = self.prog[e]
                if not prog:
                    continue

                def body(engine, prog=prog):
                    for waits, fn, s, inc in prog:
                        for ws, wv in waits:
                            engine.wait_ge(ws, wv)
                        if fn is not None:
                            fn(engine).then_inc(s, inc)

                getattr(block, e)(body)


class MK:
    def __init__(self, layers=(0, 1, 2, 3), first=True, last=True, dbg=()):
        self.layers = list(layers)
        self.first = first
        self.last = last
        self.dbg = set(dbg)
        self.nc = bass.Bass("TRN2", target_bir_lowering=False)
        self.bank_rr = 0
        self.pinned = set()
        self.tmp_rr = 0

    def din(self, name, shape, dt=F32):
        return self.nc.dram_tensor(name, list(shape), dt, kind="ExternalInput").ap()

    def sb(self, name, shape, dt):
        return self.st.enter_context(self.nc.sbuf_tensor(name, list(shape), dt))

    def declare(self):
        nc = self.nc
        L = len(self.layers)
        self.x_in = self.din("x", [S_LEN, D])
        self.p = {}
        for nm, shp in [("ln0_g", [1, D]), ("ln0_b", [1, D]), ("w_in", [L, D, N_IN]), ("b_in", [L, N_IN // 128, 128]),
                        ("conv_dw", [L, KC, D]), ("conv_dw_b", [L, 8, 128]), ("conv_ln_g", [L, 8, 128]),
                        ("conv_ln_b", [L, 8, 128]), ("conv_pw_w", [L, D, D]), ("conv_pw_b", [L, 8, 128]),
                        ("q_norm_g", [L, 1, DH]), ("k_norm_g", [L, 1, DH]), ("w_o", [L, D, D]), ("w_out", [L, D, D]),
                        ("b_out", [L, D]), ("ln1_g", [L, D]), ("ln1_b", [L, D]), ("w_router", [L, D, NE]),
                        ("w_gate", [L, NE, D, DFF]), ("w_up", [L, NE, D, DFF]), ("w_down", [L, NE, DFF, D]),
                        ("ln2_g", [L, D]), ("ln2_b", [L, D])]:
            self.p[nm] = self.din(nm, shp)
        self.c_ident = self.din("c_ident", [128, 128])
        self.c_swap = self.din("c_swap", [128, 128])
        self.c_ropeC = self.din("c_ropeC", [128, S_LEN])
        self.c_ropeS = self.din("c_ropeS", [128, S_LEN])
        self.out = nc.dram_tensor("out", [S_LEN, D], F32, kind="ExternalOutput").ap()
        self.Xd = nc.dram_tensor("Xd", [S_LEN, D], F32, kind="Internal").ap()
        self.XTd = nc.dram_tensor("XTd", [D, S_LEN + 2 * PADC], BF16, kind="Internal").ap()
        self.X1d = nc.dram_tensor("X1d", [S_LEN, D], BF16, kind="Internal").ap()
        self.ACCd = nc.dram_tensor("ACCd", [S_LEN, D], F32, kind="Internal").ap()
        self.dbg_t = {}

    def dbg_out(self, name, shape, dt=F32):
        t = self.nc.dram_tensor("dbg_" + name, list(shape), dt, kind="ExternalOutput").ap()
        self.dbg_t[name] = t
        return t

    def alloc(self):
        sb = self.sb
        self.KT = sb("KT", [128, NKV, S_LEN], BF16)
        self.Vt = sb("Vt", [128, S_LEN // 128, NKV * DH], BF16)
        self.aff = sb("aff", [128, S_LEN // 128, NE], F32)
        self.ident_f = sb("ident_f", [128, 128], F32)
        self.ident_b = sb("ident_b", [128, 128], BF16)
        self.swapm = sb("swapm", [128, 128], F32)
        self.ones_d = sb("ones_d", [128, 128], F32)
        self.ones_h = sb("ones_h", [128, 128], F32)
        self.ones_b = sb("ones_b", [128, 128], BF16)
        self.prm_rows = sb("prm_rows", [80, 128], F32)
        self.prm = sb("prm", [128, 80], F32)
        self.dw_rows = sb("dw_rows", [KC, D], F32)
        self.dw_pp = sb("dw_pp", [128, 8, KC], F32)
        self.bc = [sb("bc%d" % i, [128, D], F32) for i in range(3)]
        self.bv_bc = sb("bv_bc", [128, NKV * DH], F32)
        self.wr = sb("wr", [128, 8, NE], F32)
        self.ring = [sb("ring%d" % i, [128, 8, 512], BF16) for i in range(NRING)]
        self.xT = sb("xT", [128, 8, XW], BF16)
        self.uT = sb("uT", [128, 8, XW], BF16)
        self.Bf = sb("Bf", [128, 8, TT], F32)
        self.Cb = sb("Cb", [128, 8, TT], BF16)
        self.Db = sb("Db", [128, 8, TT], BF16)
        self.Eb = sb("Eb", [128, 8, TT], BF16)
        self.tmp = [sb("tmp%d" % i, [128, TT], F32) for i in range(8)]
        self.PT = [sb("PT%d" % i, [128, 2 * TT], BF16) for i in range(3)]
        self.ropeC = [sb("ropeC%d" % i, [128, TT], F32) for i in range(2)]
        self.ropeS = [sb("ropeS%d" % i, [128, TT], F32) for i in range(2)]
        self.row = [sb("row%d" % i, [128, D], F32) for i in range(3)]
        self.rowb = sb("rowb", [128, D], BF16)
        self.x1T = sb("x1T", [128, 8, 128], F32)
        self.dg = [sb("dg%d" % i, [128, 128], BF16) for i in range(8)]
        self.small = sb("small", [128, 64], F32)
        self.bst = sb("bst", [128, 2, 6], F32)
        self.zpad = sb("zpad", [128, 8, PADC], BF16)
        self.topv = sb("topv", [NE, CAP], F32)
        self.topi = sb("topi", [NE, CAP], U32)
        self.topif = sb("topif", [NE, CAP], F32)
        self.slot_g = sb("slot_g", [128, NE, 4], F32)
        self.slot_i = sb("slot_i", [128, NE, 4], U32)
        self.ps = self.st.enter_context(self.nc.psum_tensor("ps", [128, 8, 512], F32))

    def bank(self, pin=False):
        while True:
            b = self.bank_rr
            self.bank_rr = (self.bank_rr + 1) % 8
            if b not in self.pinned:
                break
        if pin:
            self.pinned.add(b)
        return b

    def unpin(self, b):
        self.pinned.discard(b)

    def T(self, fn, r=(), w=()):
        self.S.op("tensor", fn, r, w)

    def V(self, fn, r=(), w=()):
        self.S.op("vector", fn, r, w)

    def A(self, fn, r=(), w=()):
        self.S.op("scalar", fn, r, w)

    def G(self, fn, r=(), w=()):
        self.S.op("gpsimd", fn, r, w)

    def dma(self, fn, r=(), w=(), eng="sync", semkey=None):
        self.S.dma(eng, fn, r, w, semkey)

    def tmpk(self):
        i = self.tmp_rr
        self.tmp_rr = (self.tmp_rr + 1) % 8
        return i

    def ws_reset(self, specs):
        self.ws_specs = specs
        self.ws_issued = 0
        self.ws_used = 0

    def ws_issue_upto(self, n):
        n = min(n, len(self.ws_specs))
        while self.ws_issued < n:
            i = self.ws_issued
            slot = i % NRING
            for (c0, ncol, src) in self.ws_specs[i]:
                self.dma(lambda e, slot=slot, c0=c0, ncol=ncol, src=src:
                         e.dma_start(out=self.ring[slot][:, :, c0:c0 + ncol],
                                     in_=src.rearrange("(k p) n -> p k n", p=128)),
                         r=(), w=[("ring", slot)], eng="gpsimd")
            self.ws_issued += 1

    def ws_next(self):
        i = self.ws_used
        self.ws_issue_upto(i + NRING - 1)
        self.ws_used += 1
        slot = i % NRING
        return self.ring[slot], ("ring", slot)

    def setup_consts(self):
        self.dma(lambda e: e.dma_start(out=self.ident_f[:], in_=self.c_ident), w=["ident_f"])
        self.dma(lambda e: e.dma_start(out=self.swapm[:], in_=self.c_swap), w=["swapm"])
        self.V(lambda e: e.tensor_copy(out=self.ident_b[:], in_=self.ident_f[:]), r=["ident_f"], w=["ident_b"])
        self.V(lambda e: e.memset(self.ones_d[:], 1.0 / D), w=["ones_d"])
        self.V(lambda e: e.memset(self.ones_h[:], 1.0 / DH), w=["ones_h"])
        self.V(lambda e: e.memset(self.ones_b[:], 1.0), w=["ones_b"])
        self.V(lambda e: e.memset(self.zpad[:], 0.0), w=["zpad"])
        for c0 in (0, PADC + S_LEN):
            self.dma(lambda e, c0=c0: e.dma_start(out=self.XTd[:, c0:c0 + PADC].rearrange("(k p) n -> p k n", p=128),
                                                  in_=self.zpad[:]), r=["zpad"], w=["XTd"], semkey=("st", "zpad"))

    def load_bc(self, i, src_row):
        self.dma(lambda e: e.dma_start(out=self.bc[i][:], in_=src_row.to_broadcast([128, D])), w=[("bc", i)])

    P_BIN = 0
    P_DWB = 44
    P_CLG = 52
    P_CLB = 60
    P_PWB = 68
    P_QG = 76
    P_KG = 77

    def load_layer_params(self, l):
        p = self.p
        rows = self.prm_rows
        segs = [(self.P_BIN, 44, p["b_in"][l]), (self.P_DWB, 8, p["conv_dw_b"][l]), (self.P_CLG, 8, p["conv_ln_g"][l]),
                (self.P_CLB, 8, p["conv_ln_b"][l]), (self.P_PWB, 8, p["conv_pw_b"][l]), (self.P_QG, 1, p["q_norm_g"][l]),
                (self.P_KG, 1, p["k_norm_g"][l])]
        for (r0, n, src) in segs:
            self.dma(lambda e, r0=r0, n=n, src=src: e.dma_start(out=rows[r0:r0 + n, :], in_=src), w=["prm_rows"])
        b = self.bank()
        pk = ("ps", b)
        self.T(lambda e, b=b: e.transpose(out=self.ps[:, b, 0:78], in_=rows[0:78, :], identity=self.ident_f[0:78, 0:78]),
               r=["prm_rows", "ident_f"], w=[pk])
        self.V(lambda e, b=b: e.tensor_copy(out=self.prm[:, 0:78], in_=self.ps[:, b, 0:78]), r=[pk], w=["prm"])
        self.dma(lambda e: e.dma_start(out=self.dw_rows[:], in_=p["conv_dw"][l]), w=["dw_rows"])
        for c in range(8):
            b = self.bank()
            pk = ("ps", b)
            self.T(lambda e, b=b, c=c: e.transpose(out=self.ps[:, b, 0:KC], in_=self.dw_rows[:, c * 128:(c + 1) * 128],
                                                   identity=self.ident_f[0:KC, 0:KC]),
                   r=["dw_rows", "ident_f"], w=[pk])
            self.V(lambda e, b=b, c=c: e.tensor_copy(out=self.dw_pp[:, c, :], in_=self.ps[:, b, 0:KC]), r=[pk], w=["dw_pp"])
        self.dma(lambda e: e.dma_start(out=self.bv_bc[:], in_=p["b_in"][l].rearrange("c p -> (c p)")[OFF_V:OFF_V + 256]
                                       .rearrange("(o n) -> o n", o=1).to_broadcast([128, 256])), w=["bv_bc"])
        self.dma(lambda e: e.dma_start(out=self.wr[:], in_=p["w_router"][l].rearrange("(k p) n -> p k n", p=128)), w=["wr"])

    def ln_rows(self, src, dst, gi, bi, eps=LN_EPS, srck=None, dstk=None):
        sm = self.small
        self.V(lambda e: e.bn_stats(out=self.bst[:, 0, :], in_=src[:, 0:512]), r=[srck], w=["bst0"])
        self.V(lambda e: e.bn_stats(out=self.bst[:, 1, :], in_=src[:, 512:1024]), r=[srck], w=["bst1"])
        self.V(lambda e: e.bn_aggr(out=sm[:, 0:2], in_=self.bst[:].rearrange("p a b -> p (a b)")), r=["bst0", "bst1"], w=["mv"])
        self.A(lambda e: e.activation(out=sm[:, 2:3], in_=sm[:, 1:2], func=AF.Sqrt, bias=eps, scale=1.0), r=["mv"], w=["sd"])
        self.V(lambda e: e.reciprocal(out=sm[:, 3:4], in_=sm[:, 2:3]), r=["sd"], w=["rstd"])
        self.V(lambda e: e.tensor_scalar(out=dst, in0=src, scalar1=sm[:, 0:1], scalar2=sm[:, 3:4],
                                         op0=ALU.subtract, op1=ALU.mult), r=[srck, "mv", "rstd"], w=[dstk])
        self.V(lambda e: e.tensor_tensor(out=dst, in0=dst, in1=self.bc[gi][:], op=ALU.mult), r=[dstk, ("bc", gi)], w=[dstk])
        self.V(lambda e: e.tensor_tensor(out=dst, in0=dst, in1=self.bc[bi][:], op=ALU.add), r=[dstk, ("bc", bi)], w=[dstk])

    def norm_rope(self, pk, psap, bias_ap, g_ap, rp, out_ap, outk):
        i_raw, i_sq, i_rs, i_g, i_t = self.tmpk(), self.tmpk(), self.tmpk(), self.tmpk(), self.tmpk()
        raw, sq, rs, qg, t1 = (self.tmp[i] for i in (i_raw, i_sq, i_rs, i_g, i_t))
        kr, ks, krs, kg, kt = (("tmp", i) for i in (i_raw, i_sq, i_rs, i_g, i_t))
        self.A(lambda e: e.activation(out=raw[:], in_=psap, func=AF.Identity, bias=bias_ap, scale=1.0), r=[pk, "prm"], w=[kr])
        self.A(lambda e: e.activation(out=sq[:], in_=psap, func=AF.Square, bias=bias_ap, scale=1.0), r=[pk, "prm"], w=[ks])
        b2 = self.bank()
        pk2 = ("ps", b2)
        self.T(lambda e: e.matmul(self.ps[:, b2, :], lhsT=self.ones_h[:], rhs=sq[:], start=True, stop=True),
               r=[ks, "ones_h"], w=[pk2])
        self.A(lambda e: e.activation(out=rs[:], in_=self.ps[:, b2, :], func=AF.Sqrt, bias=RMS_EPS, scale=1.0), r=[pk2], w=[krs])
        self.V(lambda e: e.reciprocal(out=rs[:], in_=rs[:]), r=[krs], w=[krs])
        self.V(lambda e: e.scalar_tensor_tensor(out=qg[:], in0=raw[:], scalar=g_ap, in1=rs[:], op0=ALU.mult, op1=ALU.mult),
               r=[kr, krs, "prm"], w=[kg])
        b3 = self.bank()
        pk3 = ("ps", b3)
        self.T(lambda e: e.matmul(self.ps[:, b3, :], lhsT=self.swapm[:], rhs=qg[:], start=True, stop=True),
               r=[kg, "swapm"], w=[pk3])
        self.V(lambda e: e.tensor_tensor(out=t1[:], in0=qg[:], in1=self.ropeC[rp][:], op=ALU.mult), r=[kg, ("ropeC", rp)], w=[kt])
        self.V(lambda e: e.tensor_tensor(out=raw[:], in0=self.ps[:, b3, :], in1=self.ropeS[rp][:], op=ALU.mult),
               r=[pk3, ("ropeS", rp)], w=[kr])
        self.V(lambda e: e.tensor_tensor(out=out_ap, in0=t1[:], in1=raw[:], op=ALU.add), r=[kt, kr], w=[outk])

    def load_rope(self, t, rp):
        t0 = t * TT
        self.dma(lambda e: e.dma_start(out=self.ropeC[rp][:], in_=self.c_ropeC[:, t0:t0 + TT]), w=[("ropeC", rp)])
        self.dma(lambda e: e.dma_start(out=self.ropeS[rp][:], in_=self.c_ropeS[:, t0:t0 + TT]), w=[("ropeS", rp)])

    def pass1(self, l, src_d, gi_src):
        p = self.p
        g_row, b_row = gi_src
        self.load_bc(0, g_row)
        self.load_bc(1, b_row)
        self.ws_reset([[(0, 512, p["w_in"][l][:, OFF_K:OFF_K + 512])]])
        wkv, wk_key = self.ws_next()
        for t in range(NT):
            t0 = t * TT
            rp = t % 2
            self.load_rope(t, rp)
            for st in range(4):
                r0 = t0 + st * 128
                rw = self.row[st % 2]
                rk = ("row", st % 2)
                self.dma(lambda e, rw=rw, r0=r0: e.dma_start(out=rw[:], in_=src_d[r0:r0 + 128, :]), r=["ACCd"], w=[rk])
                self.ln_rows(rw[:], rw[:], 0, 1, srck=rk, dstk=rk)
                self.dma(lambda e, rw=rw, r0=r0: e.dma_start(out=self.Xd[r0:r0 + 128, :], in_=rw[:]), r=[rk], w=["Xd"], semkey=("st", rk))
                for half in range(2):
                    b = self.bank()
                    pk = ("ps", b)
                    for j in range(4):
                        k = half * 4 + j
                        self.T(lambda e, b=b, j=j, k=k, rw=rw: e.transpose(out=self.ps[:, b, j * 128:(j + 1) * 128],
                                                                          in_=rw[:, k * 128:(k + 1) * 128], identity=self.ident_f[:]),
                               r=[rk, "ident_f"], w=[pk])
                    self.A(lambda e, b=b, half=half, st=st: e.copy(
                        out=self.xT[:, half * 4:half * 4 + 4, st * 128:(st + 1) * 128],
                        in_=self.ps[:, b, :].rearrange("p (j n) -> p j n", j=4)), r=[pk], w=["xT"])
            self.dma(lambda e, t0=t0: e.dma_start(out=self.XTd[:, PADC + t0:PADC + t0 + TT].rearrange("(k p) n -> p k n", p=128),
                                                  in_=self.xT[:, :, 0:TT]), r=["xT"], w=["XTd"], semkey=("st", "xT"))
            for h in range(NKV):
                b = self.bank()
                pk = ("ps", b)
                for k in range(8):
                    self.T(lambda e, b=b, k=k, h=h: e.matmul(self.ps[:, b, :], lhsT=wkv[:, k, h * 128:(h + 1) * 128],
                                                            rhs=self.xT[:, k, 0:TT], start=(k == 0), stop=(k == 7)),
                           r=[wk_key, "xT"], w=[pk])
                cb = self.P_BIN + (OFF_K // 128) + h
                self.norm_rope(pk, self.ps[:, b, :], self.prm[:, cb:cb + 1], self.prm[:, self.P_KG:self.P_KG + 1], rp,
                               self.KT[:, h, t0:t0 + TT], "KT")
            for st in range(4):
                b = self.bank()
                pk = ("ps", b)
                for k in range(8):
                    self.T(lambda e, b=b, k=k, st=st: e.matmul(self.ps[:, b, 0:256], lhsT=self.xT[:, k, st * 128:(st + 1) * 128],
                                                              rhs=wkv[:, k, 256:512], start=(k == 0), stop=(k == 7)),
                           r=[wk_key, "xT"], w=[pk])
                sg = t * 4 + st
                self.V(lambda e, b=b, sg=sg: e.tensor_tensor(out=self.Vt[:, sg, :], in0=self.ps[:, b, 0:256], in1=self.bv_bc[:],
                                                            op=ALU.add), r=[pk, "bv_bc"], w=["Vt"])

    def pass2_specs(self, l):
        p = self.p
        w_in = p["w_in"][l]
        one = []
        for i in range(4):
            one.append([(0, 256, w_in[:, i * 256:(i + 1) * 256]), (256, 256, w_in[:, D + i * 256:D + (i + 1) * 256])])
        for i in range(2):
            one.append([(0, 512, p["conv_pw_w"][l][:, i * 512:(i + 1) * 512])])
            one.append([(0, 512, w_in[:, OFF_GC + i * 512:OFF_GC + (i + 1) * 512])])
        for i in range(2):
            one.append([(0, 512, w_in[:, OFF_Q + i * 512:OFF_Q + (i + 1) * 512])])
        for i in range(2):
            one.append([(0, 512, w_in[:, OFF_GA + i * 512:OFF_GA + (i + 1) * 512])])
            one.append([(0, 512, p["w_o"][l][:, i * 512:(i + 1) * 512])])
        for i in range(2):
            one.append([(0, 512, p["w_out"][l][:, i * 512:(i + 1) * 512])])
        return one

    def pass2(self, l):
        p = self.p
        prm = self.prm
        ps = self.ps
        self.load_bc(0, p["ln1_g"][l:l + 1, :])
        self.load_bc(1, p["ln1_b"][l:l + 1, :])
        self.load_bc(2, p["b_out"][l:l + 1, :])
        specs = []
        for t in range(NT):
            specs += self.pass2_specs(l)
        self.ws_reset(specs)
        for t in range(NT):
            t0 = t * TT
            rp = t % 2
            self.load_rope(t, rp)
            self.dma(lambda e, t0=t0: e.dma_start(out=self.xT[:], in_=self.XTd[:, t0:t0 + XW].rearrange("(k p) n -> p k n", p=128)),
                     r=["XTd"], w=["xT"])
            for i in range(4):
                wp, wk = self.ws_next()
                for cc in range(2):
                    c = i * 2 + cc
                    bA, bB, bC = self.bank(), self.bank(), self.bank()
                    kA, kB, kC = ("ps", bA), ("ps", bB), ("ps", bC)
                    for (col0, bm, bh, hc) in ((cc * 128, bA, bB, 0), (256 + cc * 128, bC, bB, 32)):
                        for k in range(8):
                            self.T(lambda e, k=k, col0=col0, bm=bm, wp=wp: e.matmul(ps[:, bm, :], lhsT=wp[:, k, col0:col0 + 128],
                                                                            rhs=self.xT[:, k, 0:512], start=(k == 0), stop=(k == 7)),
                                   r=[wk, "xT"], w=[("ps", bm)])
                        for k in range(8):
                            self.T(lambda e, k=k, col0=col0, bh=bh, hc=hc, wp=wp: e.matmul(ps[:, bh, hc:hc + 32], lhsT=wp[:, k, col0:col0 + 128],
                                                                                    rhs=self.xT[:, k, 512:544], start=(k == 0), stop=(k == 7)),
                                   r=[wk, "xT"], w=[("ps", bh)])
                    cv = self.P_BIN + c
                    cg = self.P_BIN + 8 + c
                    ti = self.tmpk()
                    sg = self.tmp[ti]
                    kt = ("tmp", ti)
                    ti2 = self.tmpk()
                    sg2 = self.tmp[ti2]
                    kt2 = ("tmp", ti2)
                    self.A(lambda e, bC=bC, cg=cg, sg=sg: e.activation(out=sg[:], in_=ps[:, bC, :], func=AF.Sigmoid,
                                                                      bias=prm[:, cg:cg + 1], scale=1.0), r=[kC, "prm"], w=[kt])
                    self.A(lambda e, bB=bB, cg=cg, sg2=sg2: e.activation(out=sg2[:, 0:32], in_=ps[:, bB, 32:64], func=AF.Sigmoid,
                                                                        bias=prm[:, cg:cg + 1], scale=1.0), r=[kB, "prm"], w=[kt2])
                    uk = ("uT", c)
                    self.V(lambda e, bA=bA, cv=cv, sg=sg, c=c: e.scalar_tensor_tensor(out=self.uT[:, c, 0:512], in0=ps[:, bA, :],
                                                                                     scalar=prm[:, cv:cv + 1], in1=sg[:],
                                                                                     op0=ALU.add, op1=ALU.mult), r=[kA, kt, "prm"], w=[uk])
                    self.V(lambda e, bB=bB, cv=cv, sg2=sg2, c=c: e.scalar_tensor_tensor(out=self.uT[:, c, 512:544], in0=ps[:, bB, 0:32],
                                                                                       scalar=prm[:, cv:cv + 1], in1=sg2[:, 0:32],
                                                                                       op0=ALU.add, op1=ALU.mult), r=[kB, kt2, "prm"], w=[uk])
                    if t == 0:
                        self.V(lambda e, c=c: e.memset(self.uT[:, c, 0:16], 0.0), w=[uk])
                    if t == NT - 1:
                        self.V(lambda e, c=c: e.memset(self.uT[:, c, 528:544], 0.0), w=[uk])
            bM = self.bank(pin=True)
            bQ = self.bank(pin=True)
            kM, kQ = ("ps", bM), ("ps", bQ)
            for c in range(8):
                b = self.bank()
                pk = ("ps", b)
                for k in range(KC):
                    di = (c * KC + k) % 8
                    dk = ("dg", di)
                    self.V(lambda e, di=di, c=c, k=k: e.tensor_scalar(out=self.dg[di][:], in0=self.ident_b[:],
                                                                     scalar1=self.dw_pp[:, c, k:k + 1], scalar2=None, op0=ALU.mult),
                           r=["ident_b", "dw_pp"], w=[dk])
                    self.T(lambda e, b=b, di=di, c=c, k=k: e.matmul(ps[:, b, :], lhsT=self.dg[di][:], rhs=self.uT[:, c, k + 1:k + 1 + 512],
                                                                   start=(k == 0), stop=(k == KC - 1)), r=[dk, ("uT", c)], w=[pk])
                cdb = self.P_DWB + c
                self.A(lambda e, b=b, c=c, cdb=cdb: e.activation(out=self.Bf[:, c, :], in_=ps[:, b, :], func=AF.Identity,
                                                                bias=prm[:, cdb:cdb + 1], scale=1.0), r=[pk, "prm"], w=[("Bf", c)])
                ti = self.tmpk()
                sq = self.tmp[ti]
                kt = ("tmp", ti)
                self.A(lambda e, b=b, cdb=cdb, sq=sq: e.activation(out=sq[:], in_=ps[:, b, :], func=AF.Square,
                                                                  bias=prm[:, cdb:cdb + 1], scale=1.0), r=[pk, "prm"], w=[kt])
                self.T(lambda e, c=c, bM=bM: e.matmul(ps[:, bM, :], lhsT=self.ones_d[:], rhs=self.Bf[:, c, :], start=(c == 0), stop=(c == 7)),
                       r=[("Bf", c), "ones_d"], w=[kM])
                self.T(lambda e, c=c, sq=sq, bQ=bQ: e.matmul(ps[:, bQ, :], lhsT=self.ones_d[:], rhs=sq[:], start=(c == 0), stop=(c == 7)),
                       r=[kt, "ones_d"], w=[kQ])
            im, iv = self.tmpk(), self.tmpk()
            mean, rstd = self.tmp[im], self.tmp[iv]
            kmn, krs = ("tmp", im), ("tmp", iv)
            self.A(lambda e, mean=mean, bM=bM: e.copy(out=mean[:], in_=ps[:, bM, :]), r=[kM], w=[kmn])
            self.V(lambda e, mean=mean, rstd=rstd, bM=bM: e.tensor_tensor(out=rstd[:], in0=mean[:], in1=ps[:, bM, :], op=ALU.mult), r=[kmn, kM], w=[krs])
            self.V(lambda e, rstd=rstd, bQ=bQ: e.tensor_tensor(out=rstd[:], in0=ps[:, bQ, :], in1=rstd[:], op=ALU.subtract), r=[kQ, krs], w=[krs])
            self.A(lambda e, rstd=rstd: e.activation(out=rstd[:], in_=rstd[:], func=AF.Sqrt, bias=LN_EPS, scale=1.0), r=[krs], w=[krs])
            self.V(lambda e, rstd=rstd: e.reciprocal(out=rstd[:], in_=rstd[:]), r=[krs], w=[krs])
            self.unpin(bM)
            self.unpin(bQ)
            for c in range(8):
                kb = ("Bf", c)
                self.V(lambda e, c=c, mean=mean: e.tensor_tensor(out=self.Bf[:, c, :], in0=self.Bf[:, c, :], in1=mean[:], op=ALU.subtract),
                       r=[kb, kmn], w=[kb])
                self.V(lambda e, c=c, rstd=rstd: e.tensor_tensor(out=self.Bf[:, c, :], in0=self.Bf[:, c, :], in1=rstd[:], op=ALU.mult),
                       r=[kb, krs], w=[kb])
                cg, cb2 = self.P_CLG + c, self.P_CLB + c
                self.A(lambda e, c=c, cg=cg, cb2=cb2: e.activation(out=self.Cb[:, c, :], in_=self.Bf[:, c, :], func=AF.Silu,
                                                                  bias=prm[:, cb2:cb2 + 1], scale=prm[:, cg:cg + 1]),
                       r=[kb, "prm"], w=[("Cb", c)])
            if getattr(self, 'dbg_stop', None) == 'conv':
                return
            for i in range(2):
                wpw, wpwk = self.ws_next()
                wgc, wgck = self.ws_next()
                for jj in range(4):
                    j = i * 4 + jj
                    bg = self.bank()
                    kg_ = ("ps", bg)
                    for k in range(8):
                        self.T(lambda e, bg=bg, k=k, jj=jj, wgc=wgc: e.matmul(ps[:, bg, :], lhsT=wgc[:, k, jj * 128:(jj + 1) * 128],
                                                                             rhs=self.xT[:, k, PADC:PADC + TT], start=(k == 0), stop=(k == 7)),
                               r=[wgck, "xT"], w=[kg_])
                    cgc = self.P_BIN + OFF_GC // 128 + j
                    ti = self.tmpk()
                    gs = self.tmp[ti]
                    kt = ("tmp", ti)
                    self.A(lambda e, bg=bg, cgc=cgc, gs=gs: e.activation(out=gs[:], in_=ps[:, bg, :], func=AF.Sigmoid,
                                                                        bias=prm[:, cgc:cgc + 1], scale=1.0), r=[kg_, "prm"], w=[kt])
                    by = self.bank()
                    ky = ("ps", by)
                    for c in range(8):
                        self.T(lambda e, by=by, c=c, jj=jj, wpw=wpw: e.matmul(ps[:, by, :], lhsT=wpw[:, c, jj * 128:(jj + 1) * 128],
                                                                             rhs=self.Cb[:, c, :], start=(c == 0), stop=(c == 7)),
                               r=[wpwk, ("Cb", c)], w=[ky])
                    cpb = self.P_PWB + j
                    self.V(lambda e, by=by, j=j, cpb=cpb, gs=gs: e.scalar_tensor_tensor(out=self.Bf[:, j, :], in0=ps[:, by, :],
                                                                                      scalar=prm[:, cpb:cpb + 1], in1=gs[:],
                                                                                      op0=ALU.add, op1=ALU.mult),
                           r=[ky, kt, "prm"], w=[("Bf", j)])
            if getattr(self, 'dbg_stop', None) == 'pw':
                return
            for i in range(2):
                wp, wk = self.ws_next()
                for jj in range(4):
                    h = i * 4 + jj
                    b = self.bank()
                    pk = ("ps", b)
                    for k in range(8):
                        self.T(lambda e, b=b, k=k, jj=jj, wp=wp: e.matmul(ps[:, b, :], lhsT=wp[:, k, jj * 128:(jj + 1) * 128],
                                                                         rhs=self.xT[:, k, PADC:PADC + TT], start=(k == 0), stop=(k == 7)),
                               r=[wk, "xT"], w=[pk])
                    cq = self.P_BIN + OFF_Q // 128 + h
                    self.norm_rope(pk, ps[:, b, :], prm[:, cq:cq + 1], prm[:, self.P_QG:self.P_QG + 1], rp,
                                   self.Db[:, h, :], ("Db", h))
            NPAIR = S_LEN // 256
            units = [(g, qs) for g in range(NKV) for qs in range(4)]

            def emit_S(f):
                n, j = divmod(f, NPAIR)
                g, qs = units[n]
                sp = (0, 1) if f % 2 == 0 else (2, 3)
                qk = [("Db", g * 4 + hh) for hh in range(4)]
                for u in range(2):
                    kc = 2 * j + u
                    self.T(lambda e, b=sp[u], g=g, kc=kc, qs=qs: e.matmul(
                        ps[:, b, :].rearrange("p (h q) -> p h q", h=4), lhsT=self.KT[:, g, kc * 128:(kc + 1) * 128],
                        rhs=self.Db[:, g * 4:g * 4 + 4, qs * 128:(qs + 1) * 128], start=True, stop=True),
                        r=["KT"] + qk, w=[("ps", sp[u])])

            NF = len(units) * NPAIR
            emit_S(0)
            for f in range(NF):
                n, j = divmod(f, NPAIR)
                g, qs = units[n]
                if f + 1 < NF:
                    emit_S(f + 1)
                sp = (0, 1) if f % 2 == 0 else (2, 3)
                bO, bS = (4, 5) if n % 2 == 0 else (6, 7)
                kO, kS = ("ps", bO), ("ps", bS)
                pi = f % 3
                pt = self.PT[pi]
                kp = ("PT", pi)
                self.A(lambda e, s0=sp[0], pt=pt: e.activation(out=pt[:].rearrange("p (u n) -> p u n", u=2), in_=ps[:, s0:s0 + 2, :],
                                                               func=AF.Exp, scale=ATT_SCALE),
                       r=[("ps", sp[0]), ("ps", sp[1])], w=[kp])
                for u in range(2):
                    kc = 2 * j + u
                    first = (j == 0 and u == 0)
                    last = (j == NPAIR - 1 and u == 1)
                    self.T(lambda e, g=g, kc=kc, pt=pt, bO=bO, u=u, first=first, last=last: e.matmul(
                        ps[:, bO, :], lhsT=self.Vt[:, kc, g * 128:(g + 1) * 128], rhs=pt[:, u * 512:(u + 1) * 512],
                        start=first, stop=last), r=["Vt", kp], w=[kO])
                    self.T(lambda e, pt=pt, bS=bS, u=u, first=first, last=last: e.matmul(
                        ps[:, bS, :], lhsT=self.ones_b[:], rhs=pt[:, u * 512:(u + 1) * 512],
                        start=first, stop=last), r=["ones_b", kp], w=[kS])
                if j == NPAIR - 1:
                    ti = self.tmpk()
                    rec = self.tmp[ti]
                    kt = ("tmp", ti)
                    self.V(lambda e, rec=rec, bS=bS: e.reciprocal(out=rec[:], in_=ps[:, bS, :]), r=[kS], w=[kt])
                    ek = [("Eb", g * 4 + hh) for hh in range(4)]
                    self.V(lambda e, g=g, qs=qs, rec=rec, bO=bO: e.tensor_tensor(
                        out=self.Eb[:, g * 4:g * 4 + 4, qs * 128:(qs + 1) * 128],
                        in0=ps[:, bO, :].rearrange("p (h q) -> p h q", h=4), in1=rec[:].rearrange("p (h q) -> p h q", h=4),
                        op=ALU.mult), r=[kO, kt] + ek, w=ek)
            for i in range(2):
                wga, wgak = self.ws_next()
                wwo, wwok = self.ws_next()
                for jj in range(4):
                    j = i * 4 + jj
                    bg = self.bank()
                    kg_ = ("ps", bg)
                    for k in range(8):
                        self.T(lambda e, bg=bg, k=k, jj=jj, wga=wga: e.matmul(ps[:, bg, :], lhsT=wga[:, k, jj * 128:(jj + 1) * 128],
                                                                             rhs=self.xT[:, k, PADC:PADC + TT], start=(k == 0), stop=(k == 7)),
                               r=[wgak, "xT"], w=[kg_])
                    cga = self.P_BIN + OFF_GA // 128 + j
                    ti = self.tmpk()
                    gs = self.tmp[ti]
                    kt = ("tmp", ti)
                    self.A(lambda e, bg=bg, cga=cga, gs=gs: e.activation(out=gs[:], in_=ps[:, bg, :], func=AF.Sigmoid,
                                                                        bias=prm[:, cga:cga + 1], scale=1.0), r=[kg_, "prm"], w=[kt])
                    by = self.bank()
                    ky = ("ps", by)
                    for h in range(8):
                        self.T(lambda e, by=by, h=h, jj=jj, wwo=wwo: e.matmul(ps[:, by, :], lhsT=wwo[:, h, jj * 128:(jj + 1) * 128],
                                                                             rhs=self.Eb[:, h, :], start=(h == 0), stop=(h == 7)),
                               r=[wwok, ("Eb", h)], w=[ky])
                    self.V(lambda e, by=by, gs=gs: e.tensor_tensor(out=gs[:], in0=gs[:], in1=ps[:, by, :], op=ALU.mult),
                           r=[kt, ky], w=[kt])
                    self.V(lambda e, j=j, gs=gs: e.tensor_tensor(out=self.Cb[:, j, :], in0=self.Bf[:, j, :], in1=gs[:], op=ALU.add),
                           r=[("Bf", j), kt], w=[("Cb", j)])
            wpa, wka = self.ws_next()
            wpb, wkb = self.ws_next()
            for st in range(4):
                r0 = t0 + st * 128
                sg = t * 4 + st
                xr, tr, ar = self.row[0], self.row[1], self.row[2]
                self.dma(lambda e, r0=r0: e.dma_start(out=xr[:], in_=self.Xd[r0:r0 + 128, :]), r=["Xd"], w=[("row", 0)])
                bb = []
                for (wp, wk) in ((wpa, wka), (wpb, wkb)):
                    b = self.bank()
                    bb.append(b)
                    for k in range(8):
                        self.T(lambda e, b=b, k=k, st=st, wp=wp: e.matmul(ps[:, b, :], lhsT=self.Cb[:, k, st * 128:(st + 1) * 128],
                                                                         rhs=wp[:, k, :], start=(k == 0), stop=(k == 7)),
                               r=[wk, ("Cb", k)], w=[("ps", b)])
                for hf in range(2):
                    b = bb[hf]
                    self.V(lambda e, b=b, hf=hf: e.scalar_tensor_tensor(out=tr[:, hf * 512:(hf + 1) * 512], in0=xr[:, hf * 512:(hf + 1) * 512],
                                                                       scalar=ALPHA, in1=ps[:, b, :], op0=ALU.mult, op1=ALU.add),
                           r=[("row", 0), ("ps", b)], w=[("row", 1)])
                self.V(lambda e: e.tensor_tensor(out=tr[:], in0=tr[:], in1=self.bc[2][:], op=ALU.add), r=[("row", 1), ("bc", 2)], w=[("row", 1)])
                self.ln_rows(tr[:], tr[:], 0, 1, srck=("row", 1), dstk=("row", 1))
                self.A(lambda e: e.copy(out=self.rowb[:], in_=tr[:]), r=[("row", 1)], w=["rowb"])
                self.dma(lambda e, r0=r0: e.dma_start(out=self.X1d[r0:r0 + 128, :], in_=self.rowb[:]), r=["rowb"], w=["X1d"], semkey=("st", "rowb"))
                self.A(lambda e: e.mul(ar[:], tr[:], ALPHA), r=[("row", 1)], w=[("row", 2)])
                self.dma(lambda e, r0=r0: e.dma_start(out=self.ACCd[r0:r0 + 128, :], in_=ar[:]), r=[("row", 2)], w=["ACCd"], semkey=("st", "row2"))
                if "x1" in self.dbg and l == 0:
                    self.dma(lambda e, r0=r0: e.dma_start(out=self.dbg_t["x1"][r0:r0 + 128, :], in_=tr[:]), r=[("row", 1)], w=["dbg_x1"], semkey=("st", "row1"))
                for half in range(2):
                    b = self.bank()
                    pk = ("ps", b)
                    for jx in range(4):
                        k = half * 4 + jx
                        self.T(lambda e, b=b, jx=jx, k=k: e.transpose(out=ps[:, b, jx * 128:(jx + 1) * 128],
                                                                      in_=tr[:, k * 128:(k + 1) * 128], identity=self.ident_f[:]),
                               r=[("row", 1), "ident_f"], w=[pk])
                    self.A(lambda e, b=b, half=half: e.copy(out=self.x1T[:, half * 4:half * 4 + 4, :],
                                                           in_=ps[:, b, :].rearrange("p (j n) -> p j n", j=4)), r=[pk], w=["x1T"])
                b = self.bank()
                pk = ("ps", b)
                for k in range(8):
                    self.T(lambda e, b=b, k=k: e.matmul(ps[:, b, 0:NE], lhsT=self.x1T[:, k, :], rhs=self.wr[:, k, :],
                                                        start=(k == 0), stop=(k == 7)), r=["x1T", "wr"], w=[pk])
                sm = self.small
                self.V(lambda e, b=b: e.reduce_max(out=sm[:, 8:9], in_=ps[:, b, 0:NE], axis=mybir.AxisListType.X), r=[pk], w=["rmax"])
                self.V(lambda e: e.tensor_scalar(out=sm[:, 9:10], in0=sm[:, 8:9], scalar1=-1.0, scalar2=None, op0=ALU.mult),
                       r=["rmax"], w=["nmax"])
                self.A(lambda e, b=b: e.activation(out=sm[:, 16:32], in_=ps[:, b, 0:NE], func=AF.Exp, bias=sm[:, 9:10], scale=1.0,
                                                   accum_out=sm[:, 10:11]), r=[pk, "nmax"], w=["rexp", "rsum"])
                self.V(lambda e: e.reciprocal(out=sm[:, 11:12], in_=sm[:, 10:11]), r=["rsum"], w=["rrec"])
                self.V(lambda e, sg=sg: e.tensor_scalar(out=self.aff[:, sg, :], in0=sm[:, 16:32], scalar1=sm[:, 11:12], scalar2=None,
                                                        op0=ALU.mult), r=["rexp", "rrec"], w=["aff"])

    def topk(self, l):
        ps = self.ps
        work = self.Bf[0:NE, :, :].rearrange("p a b -> p (a b)")
        wk = [("Bf", c) for c in range(8)]
        for bi in range(8):
            b = self.bank()
            pk = ("ps", b)
            for j in range(4):
                sg = bi * 4 + j
                self.T(lambda e, b=b, j=j, sg=sg: e.transpose(out=ps[0:NE, b, j * 128:(j + 1) * 128], in_=self.aff[:, sg, :],
                                                              identity=self.ident_f[:]), r=["aff", "ident_f"], w=[pk])
            self.V(lambda e, b=b, bi=bi: e.tensor_copy(out=work[:, bi * 512:(bi + 1) * 512], in_=ps[0:NE, b, :]), r=[pk], w=[wk[bi]])
        for r in range(CAP // 8):
            self.V(lambda e, r=r: e.max(out=self.topv[:, r * 8:(r + 1) * 8], in_=work), r=wk, w=["topv"])
            self.V(lambda e, r=r: e.max_index(out=self.topi[:, r * 8:(r + 1) * 8], in_max=self.topv[:, r * 8:(r + 1) * 8], in_values=work),
                   r=wk + ["topv"], w=["topi"])
            self.V(lambda e, r=r: e.match_replace(out=work, in_to_replace=self.topv[:, r * 8:(r + 1) * 8], in_values=work, imm_value=-1.0),
                   r=wk + ["topv"], w=wk)
        self.V(lambda e: e.tensor_copy(out=self.topif[:], in_=self.topi[:]), r=["topi"], w=["topif"])
        for (src, srck, dst, dstk) in ((self.topv, "topv", self.slot_g, "slot_g"), (self.topif, "topif", self.slot_i, "slot_i")):
            b = self.bank()
            pk = ("ps", b)
            for col in range(4):
                self.T(lambda e, b=b, col=col, src=src: e.transpose(out=ps[:, b, col * NE:(col + 1) * NE], in_=src[:, col * 128:(col + 1) * 128],
                                                                    identity=self.ident_f[0:NE, 0:NE]), r=[srck, "ident_f"], w=[pk])
            self.V(lambda e, b=b, dst=dst: e.tensor_copy(out=dst[:].rearrange("p e c -> p c e"),
                                                         in_=ps[:, b, 0:4 * NE].rearrange("p (c e) -> p c e", c=4)), r=[pk], w=[dstk])

    def moe_specs(self, l):
        p = self.p
        specs = []
        for ex in range(NE):
            wg, wu, wd = p["w_gate"][l, ex], p["w_up"][l, ex], p["w_down"][l, ex]
            for i in range(4):
                specs.append([(0, 512, wg[:, i * 512:(i + 1) * 512])])
                specs.append([(0, 512, wu[:, i * 512:(i + 1) * 512])])
            for ch in range(2):
                for rh in range(2):
                    specs.append([(0, 512, wd[rh * 1024:(rh + 1) * 1024, ch * 512:(ch + 1) * 512])])
        return specs

    def moe(self, l):
        ps = self.ps
        self.ws_reset(self.moe_specs(l))
        xe = [self.row[0], self.row[1]]
        for ex in range(NE):
            xeb = self.Db
            xebv = xeb[:].rearrange("p (c a) n -> p c (a n)", c=4)
            xk = [("Db", h) for h in range(8)]
            for col in range(4):
                self.dma(lambda e, ex=ex, col=col: e.indirect_dma_start(
                    out=xebv[:, col, :], out_offset=None, in_=self.X1d[:, :],
                    in_offset=bass.IndirectOffsetOnAxis(ap=self.slot_i[:, ex, col:col + 1], axis=0)),
                    r=["X1d", "slot_i"], w=[("Db", 2 * col), ("Db", 2 * col + 1)], eng="gpsimd", semkey=("gather", col))
            for k in range(8):
                b = self.bank()
                pk = ("ps", b)
                pv = ps[:, b, :].bitcast(BF16)
                for col in range(4):
                    self.T(lambda e, pv=pv, col=col, k=k: e.transpose(out=pv[:, col * 128:(col + 1) * 128],
                                                                      in_=xebv[:, col, k * 128:(k + 1) * 128], identity=self.ident_b[:]),
                           r=[("Db", 2 * col), ("Db", 2 * col + 1), "ident_b"], w=[pk])
                self.A(lambda e, pv=pv, k=k: e.copy(out=self.Eb[:, k, :], in_=pv[:, 0:512]), r=[pk], w=[("Eb", k)])
            ek = [("Eb", k) for k in range(8)]
            hid = [self.uT[:, f, 0:512] for f in range(8)] + [self.Cb[:, f, :] for f in range(8)]
            hk = [("uT", f) for f in range(8)] + [("Cb", f) for f in range(8)]
            for i in range(4):
                wg, wgk = self.ws_next()
                wu, wuk = self.ws_next()
                for jj in range(4):
                    f = i * 4 + jj
                    bg, bu = self.bank(), self.bank()
                    for (w_, wk_, b_) in ((wg, wgk, bg), (wu, wuk, bu)):
                        for k in range(8):
                            self.T(lambda e, w_=w_, b_=b_, k=k, jj=jj: e.matmul(ps[:, b_, :], lhsT=w_[:, k, jj * 128:(jj + 1) * 128],
                                                                               rhs=self.Eb[:, k, :], start=(k == 0), stop=(k == 7)),
                                   r=[wk_, ("Eb", k)], w=[("ps", b_)])
                    ti = self.tmpk()
                    sg = self.tmp[ti]
                    kt = ("tmp", ti)
                    self.A(lambda e, bg=bg, sg=sg: e.activation(out=sg[:], in_=ps[:, bg, :], func=AF.Silu), r=[("ps", bg)], w=[kt])
                    self.V(lambda e, bu=bu, sg=sg, f=f: e.tensor_tensor(out=hid[f], in0=sg[:], in1=ps[:, bu, :], op=ALU.mult),
                           r=[kt, ("ps", bu)], w=[hk[f]])
            ye = self.Bf
            yev = ye[:].rearrange("p (c a) n -> p c (a n)", c=4)
            yk = [("Bf", c) for c in range(8)]
            for ch in range(2):
                wd0, wdk0 = self.ws_next()
                wd1, wdk1 = self.ws_next()
                for col in range(4):
                    b = self.bank()
                    pk = ("ps", b)
                    for f in range(16):
                        w_, wk_ = (wd0, wdk0) if f < 8 else (wd1, wdk1)
                        self.T(lambda e, b=b, f=f, col=col, w_=w_: e.matmul(ps[:, b, :], lhsT=hid[f][:, col * 128:(col + 1) * 128],
                                                                           rhs=w_[:, f % 8, :], start=(f == 0), stop=(f == 15)),
                               r=[wk_, hk[f]], w=[pk])
                    self.V(lambda e, b=b, col=col, ch=ch, ex=ex: e.tensor_scalar(
                        out=yev[:, col, ch * 512:(ch + 1) * 512], in0=ps[:, b, :], scalar1=self.slot_g[:, ex, col:col + 1], scalar2=None,
                        op0=ALU.mult), r=[pk, "slot_g"], w=[("Bf", col * 2), ("Bf", col * 2 + 1)])
            for col in range(4):
                self.dma(lambda e, ex=ex, col=col: e.indirect_dma_start(
                    out=self.ACCd[:, :], out_offset=bass.IndirectOffsetOnAxis(ap=self.slot_i[:, ex, col:col + 1], axis=0),
                    in_=yev[:, col, :], in_offset=None, compute_op=ALU.add),
                    r=yk + ["slot_i"], w=["ACCd"], eng="gpsimd", semkey=("scat", col))

    def final(self, l):
        p = self.p
        self.load_bc(0, p["ln2_g"][l:l + 1, :])
        self.load_bc(1, p["ln2_b"][l:l + 1, :])
        for sg in range(S_LEN // 128):
            r0 = sg * 128
            rw = self.row[sg % 2]
            rk = ("row", sg % 2)
            self.dma(lambda e, rw=rw, r0=r0: e.dma_start(out=rw[:], in_=self.ACCd[r0:r0 + 128, :]), r=["ACCd"], w=[rk])
            self.ln_rows(rw[:], rw[:], 0, 1, srck=rk, dstk=rk)
            self.dma(lambda e, rw=rw, r0=r0: e.dma_start(out=self.out[r0:r0 + 128, :], in_=rw[:]), r=[rk], w=["out"], semkey=("st", rk))

    def build(self, stop_after=None):
        with ExitStack() as st:
            self.st = st
            self.declare()
            for nm, shp, dt in self.dbg_decl:
                self.dbg_out(nm, shp, dt)
            self.alloc()
            self.S = Sched(self.nc, st)
            self.setup_consts()
            p = self.p
            for li, l in enumerate(self.layers):
                self.load_layer_params(li)
                if l == 0:
                    src, gb = self.x_in, (p["ln0_g"], p["ln0_b"])
                else:
                    src, gb = self.ACCd, (p["ln2_g"][li - 1:li, :], p["ln2_b"][li - 1:li, :])
                self.pass1(li, src, gb)
                if stop_after == "pass1":
                    break
                self.pass2(li)
                if stop_after == "pass2":
                    break
                self.topk(li)
                if stop_after == "topk":
                    break
                self.moe(li)
            if stop_after is None:
                self.final(len(self.layers) - 1)
                fin = ["out"]
            else:
                fin = []
            self.debug_dumps(stop_after)
            fin += ["dbg_" + n for n in self.dbg_written]
            self.S.finish("sync", fin)
            self.S.emit()
        return self.nc

    dbg_decl = ()
    dbg_written = ()

    def debug_dumps(self, stop_after):
        pass


def rope_tables():
    t = np.arange(S_LEN)
    row = (t // 64).astype(np.float32)
    col = (t % 64).astype(np.float32)
    axis_dim = DH // 2
    freqs = (1.0 / (np.float32(10000.0) ** (np.arange(0, axis_dim, 2, dtype=np.float32) / np.float32(axis_dim)))).astype(np.float32)
    ang = np.concatenate([row[:, None] * freqs[None], col[:, None] * freqs[None]], axis=-1).astype(np.float32)
    cos = np.cos(ang).astype(np.float32)
    sin = np.sin(ang).astype(np.float32)
    C = np.repeat(cos.T, 2, axis=0)
    Sg = np.repeat(sin.T, 2, axis=0)
    sign = np.where(np.arange(DH) % 2 == 0, -1.0, 1.0).astype(np.float32)[:, None]
    return np.ascontiguousarray(C), np.ascontiguousarray(Sg * sign)


def const_inputs():
    ident = np.eye(128, dtype=np.float32)
    swap = np.zeros((128, 128), np.float32)
    idx = np.arange(128)
    swap[idx, idx ^ 1] = 1.0
    C, Sg = rope_tables()
    return {"c_ident": ident, "c_swap": swap, "c_ropeC": C, "c_ropeS": Sg}


def core_inputs(inputs, b, layers=None):
    m = {"x": np.ascontiguousarray(inputs["x"][b])}
    if layers is not None:
        inputs = dict(inputs)
        for nm in inputs:
            if nm not in ("x", "ln0_g", "ln0_b"):
                inputs[nm] = np.ascontiguousarray(inputs[nm][list(layers)])
    L = NL if layers is None else len(layers)
    m["ln0_g"] = inputs["ln0_g"].reshape(1, D)
    m["ln0_b"] = inputs["ln0_b"].reshape(1, D)
    m["b_in"] = inputs["b_in"].reshape(L, N_IN // 128, 128)
    for nm in ("conv_dw_b", "conv_ln_g", "conv_ln_b", "conv_pw_b"):
        m[nm] = inputs[nm].reshape(L, 8, 128)
    m["q_norm_g"] = inputs["q_norm_g"].reshape(L, 1, DH)
    m["k_norm_g"] = inputs["k_norm_g"].reshape(L, 1, DH)
    for nm in ("w_in", "conv_dw", "conv_pw_w", "w_o", "w_out", "b_out", "ln1_g", "ln1_b", "w_router", "w_gate", "w_up",
               "w_down", "ln2_g", "ln2_b"):
        m[nm] = inputs[nm]
    m.update(const_inputs())
    return m


def kernel(**inputs):
    inputs = {k: np.asarray(v) for k, v in inputs.items()}
    mk = MK()
    nc = mk.build()
    n = 4
    in_maps = [core_inputs(inputs, c) for c in range(n)]
    res = run_bass_kernel_spmd(nc, in_maps, core_ids=list(range(n)))
    out = np.stack([np.asarray(res.results[c]["out"]) for c in range(n)], axis=0)
    return out.astype(np.float32)
```

```python
import math
import numpy as np
from contextlib import ExitStack
import concourse.bass as bass
import concourse.mybir as mybir
from concourse.bass_utils import run_bass_kernel_spmd

F32 = mybir.dt.float32
BF16 = mybir.dt.bfloat16
U32 = mybir.dt.uint32
AF = mybir.ActivationFunctionType
ALU = mybir.AluOpType

S_LEN = 4096
D = 1024
NL = 4
TT = 512
NT = S_LEN // TT
XW = 544
PADC = 16
NH = 8
NKV = 2
DH = 128
NE = 16
CAP = 512
DFF = 2048
KC = 31
N_IN = 5632
OFF_Q = 2048
OFF_K = 3072
OFF_V = 3328
OFF_GC = 3584
OFF_GA = 4608
LN_EPS = 1e-5
RMS_EPS = 1e-6
ALPHA = (2.0 * NL) ** 0.25
ATT_SCALE = 1.0 / math.sqrt(DH)
NRING = 4
NTMP = 10

ENGS = ("tensor", "vector", "scalar", "gpsimd", "sync")


class Sched:
    def __init__(self, nc, stack):
        self.nc = nc
        self.stack = stack
        self.prog = {e: [] for e in ENGS}
        self.cnt = {e: 0 for e in ENGS}
        self.esem = {e: stack.enter_context(nc.semaphore("es_" + e)) for e in ENGS}
        self.known = {e: {} for e in ENGS}
        self.last_w = {}
        self.readers = {}
        self.dsem = {}
        self.dcnt = {}
        self.nops = 0
        self.nwaits = 0

    def _sem_for_key(self, key):
        if key not in self.dsem:
            self.dsem[key] = self.stack.enter_context(self.nc.semaphore("ds%d" % len(self.dsem)))
            self.dcnt[key] = 0
        return self.dsem[key]

    def _waits(self, eng, reads, writes):
        waits = {}
        own = self.esem[eng].num

        def need(sv, raw):
            s, v = sv
            if s.num == own and not raw and eng == "tensor":
                return
            cur = waits.get(s.num)
            if cur is None or cur[1] < v:
                waits[s.num] = (s, v)

        for k in reads:
            for sv in self.last_w.get(k, {}).values():
                need(sv, True)
        for k in writes:
            for sv in self.last_w.get(k, {}).values():
                need(sv, False)
            for sv in self.readers.get(k, {}).values():
                need(sv, False)
        out = []
        kn = self.known[eng]
        for num, (s, v) in waits.items():
            if kn.get(num, 0) >= v:
                continue
            kn[num] = v
            out.append((s, v))
        return out

    def _commit(self, s, v, reads, writes):
        for k in writes:
            self.last_w.setdefault(k, {})[s.num] = (s, v)
            self.readers[k] = {}
        for k in reads:
            self.readers.setdefault(k, {})[s.num] = (s, v)

    def op(self, eng, fn, reads=(), writes=()):
        waits = self._waits(eng, reads, writes)
        self.cnt[eng] += 1
        s = self.esem[eng]
        self.prog[eng].append((waits, fn, s, 1))
        self._commit(s, self.cnt[eng], reads, writes)
        self.nops += 1
        self.nwaits += len(waits)

    def dma(self, eng, fn, reads=(), writes=(), semkey=None):
        waits = self._waits(eng, reads, writes)
        key = semkey if semkey is not None else writes[0]
        s = self._sem_for_key(key)
        self.dcnt[key] += 16
        self.prog[eng].append((waits, fn, s, 16))
        self._commit(s, self.dcnt[key], reads, writes)
        self.nops += 1
        self.nwaits += len(waits)

    def finish(self, eng, keys):
        waits = self._waits(eng, keys, ())
        self.prog[eng].append((waits, None, None, 0))

    def emit(self):
        with self.nc.Block() as block:
            for e in ENGS:
                prog = self.prog[e]
                if not prog:
                    continue

                def body(engine, prog=prog):
                    for waits, fn, s, inc in prog:
                        for ws, wv in waits:
                            engine.wait_ge(ws, wv)
                        if fn is not None:
                            fn(engine).then_inc(s, inc)

                getattr(block, e)(body)


class MK:
    def __init__(self, layers=(0, 1, 2, 3), first=True, last=True, dbg=()):
        self.layers = list(layers)
        self.first = first
        self.last = last
        self.dbg = set(dbg)
        self.nc = bass.Bass("TRN2", target_bir_lowering=False)
        self.bank_rr = 0
        self.pinned = set()
        self.tmp_rr = 0

    def din(self, name, shape, dt=F32):
        return self.nc.dram_tensor(name, list(shape), dt, kind="ExternalInput").ap()

    def sb(self, name, shape, dt):
        return self.st.enter_context(self.nc.sbuf_tensor(name, list(shape), dt))

    def declare(self):
        nc = self.nc
        L = len(self.layers)
        self.x_in = self.din("x", [S_LEN, D])
        self.p = {}
        for nm, shp in [("ln0_g", [1, D]), ("ln0_b", [1, D]), ("w_in", [L, D, N_IN]), ("b_in", [L, N_IN // 128, 128]),
                        ("conv_dw", [L, KC, D]), ("conv_dw_b", [L, 8, 128]), ("conv_ln_g", [L, 8, 128]),
                        ("conv_ln_b", [L, 8, 128]), ("conv_pw_w", [L, D, D]), ("conv_pw_b", [L, 8, 128]),
                        ("q_norm_g", [L, 1, DH]), ("k_norm_g", [L, 1, DH]), ("w_o", [L, D, D]), ("w_out", [L, D, D]),
                        ("b_out", [L, D]), ("ln1_g", [L, D]), ("ln1_b", [L, D]), ("w_router", [L, D, NE]),
                        ("w_gate", [L, NE, D, DFF]), ("w_up", [L, NE, D, DFF]), ("w_down", [L, NE, DFF, D]),
                        ("ln2_g", [L, D]), ("ln2_b", [L, D])]:
            self.p[nm] = self.din(nm, shp)
        self.c_ident = self.din("c_ident", [128, 128])
        self.c_swap = self.din("c_swap", [128, 128])
        self.c_ropeC = self.din("c_ropeC", [128, S_LEN])
        self.c_ropeS = self.din("c_ropeS", [128, S_LEN])
        self.out = nc.dram_tensor("out", [S_LEN, D], F32, kind="ExternalOutput").ap()
        self.Xd = nc.dram_tensor("Xd", [S_LEN, D], F32, kind="Internal").ap()
        self.XTd = nc.dram_tensor("XTd", [D, S_LEN + 2 * PADC], BF16, kind="Internal").ap()
        self.X1d = nc.dram_tensor("X1d", [S_LEN, D], BF16, kind="Internal").ap()
        self.ACCd = nc.dram_tensor("ACCd", [S_LEN, D], F32, kind="Internal").ap()
        self.dbg_t = {}

    def dbg_out(self, name, shape, dt=F32):
        t = self.nc.dram_tensor("dbg_" + name, list(shape), dt, kind="ExternalOutput").ap()
        self.dbg_t[name] = t
        return t

    def alloc(self):
        sb = self.sb
        self.KT = sb("KT", [128, NKV, S_LEN], BF16)
        self.Vt = sb("Vt", [128, S_LEN // 128, NKV * DH], BF16)
        self.aff = sb("aff", [128, S_LEN // 128, NE], F32)
        self.ident_f = sb("ident_f", [128, 128], F32)
        self.ident_b = sb("ident_b", [128, 128], BF16)
        self.swapm = sb("swapm", [128, 128], F32)
        self.ones_d = sb("ones_d", [128, 128], F32)
        self.ones_h = sb("ones_h", [128, 128], F32)
        self.ones_b = sb("ones_b", [128, 128], BF16)
        self.prm_rows = sb("prm_rows", [80, 128], F32)
        self.prm = sb("prm", [128, 80], F32)
        self.dw_rows = sb("dw_rows", [KC, D], F32)
        self.dw_pp = sb("dw_pp", [128, 8, KC], F32)
        self.bc = [sb("bc%d" % i, [128, D], F32) for i in range(3)]
        self.bv_bc = sb("bv_bc", [128, NKV * DH], F32)
        self.wr = sb("wr", [128, 8, NE], F32)
        self.ring = [sb("ring%d" % i, [128, 8, 512], BF16) for i in range(NRING)]
        self.xT = sb("xT", [128, 8, XW], BF16)
        self.uT = sb("uT", [128, 8, XW], BF16)
        self.Bf = sb("Bf", [128, 8, TT], F32)
        self.Cb = sb("Cb", [128, 8, TT], BF16)
        self.Db = sb("Db", [128, 8, TT], BF16)
        self.Eb = sb("Eb", [128, 8, TT], BF16)
        self.tmp = [sb("tmp%d" % i, [128, TT], F32) for i in range(NTMP)]
        self.PT = [sb("PT%d" % i, [128, 2 * TT], BF16) for i in range(3)]
        self.ropeC = [sb("ropeC%d" % i, [128, TT], F32) for i in range(2)]
        self.ropeS = [sb("ropeS%d" % i, [128, TT], F32) for i in range(2)]
        self.row = [sb("row%d" % i, [128, D], F32) for i in range(3)]
        self.rowb = sb("rowb", [128, D], BF16)
        self.x1T = sb("x1T", [128, 8, 128], F32)
        self.dg = [sb("dg%d" % i, [128, 128], BF16) for i in range(8)]
        self.small = sb("small", [128, 64], F32)
        self.bst = [sb("bst%d" % i, [128, 2, 6], F32) for i in range(2)]
        self.zpad = sb("zpad", [128, 8, PADC], BF16)
        self.topv = sb("topv", [NE, CAP], F32)
        self.topi = sb("topi", [NE, CAP], U32)
        self.topif = sb("topif", [NE, CAP], F32)
        self.slot_g = sb("slot_g", [128, NE, 4], F32)
        self.slot_i = sb("slot_i", [128, NE, 4], U32)
        self.ps = self.st.enter_context(self.nc.psum_tensor("ps", [128, 8, 512], F32))

    def bank(self, pin=False):
        while True:
            b = self.bank_rr
            self.bank_rr = (self.bank_rr + 1) % 8
            if b not in self.pinned:
                break
        if pin:
            self.pinned.add(b)
        return b

    def unpin(self, b):
        self.pinned.discard(b)

    def T(self, fn, r=(), w=()):
        self.S.op("tensor", fn, r, w)

    def V(self, fn, r=(), w=()):
        self.S.op("vector", fn, r, w)

    def A(self, fn, r=(), w=()):
        self.S.op("scalar", fn, r, w)

    def G(self, fn, r=(), w=()):
        self.S.op("gpsimd", fn, r, w)

    def dma(self, fn, r=(), w=(), eng="sync", semkey=None):
        self.S.dma(eng, fn, r, w, semkey)

    def tmpk(self):
        i = self.tmp_rr
        self.tmp_rr = (self.tmp_rr + 1) % NTMP
        return i

    def ws_reset(self, specs):
        self.ws_specs = specs
        self.ws_issued = 0
        self.ws_used = 0

    def ws_issue_upto(self, n):
        n = min(n, len(self.ws_specs))
        while self.ws_issued < n:
            i = self.ws_issued
            slot = i % NRING
            for (c0, ncol, src) in self.ws_specs[i]:
                self.dma(lambda e, slot=slot, c0=c0, ncol=ncol, src=src:
                         e.dma_start(out=self.ring[slot][:, :, c0:c0 + ncol],
                                     in_=src.rearrange("(k p) n -> p k n", p=128)),
                         r=(), w=[("ring", slot)], eng="gpsimd")
            self.ws_issued += 1

    def ws_next(self):
        i = self.ws_used
        self.ws_issue_upto(i + NRING - 1)
        self.ws_used += 1
        slot = i % NRING
        return self.ring[slot], ("ring", slot)

    def setup_consts(self):
        self.dma(lambda e: e.dma_start(out=self.ident_f[:], in_=self.c_ident), w=["ident_f"])
        self.dma(lambda e: e.dma_start(out=self.swapm[:], in_=self.c_swap), w=["swapm"])
        self.V(lambda e: e.tensor_copy(out=self.ident_b[:], in_=self.ident_f[:]), r=["ident_f"], w=["ident_b"])
        self.V(lambda e: e.memset(self.ones_d[:], 1.0 / D), w=["ones_d"])
        self.V(lambda e: e.memset(self.ones_h[:], 1.0 / DH), w=["ones_h"])
        self.V(lambda e: e.memset(self.ones_b[:], 1.0), w=["ones_b"])
        self.V(lambda e: e.memset(self.zpad[:], 0.0), w=["zpad"])
        for c0 in (0, PADC + S_LEN):
            self.dma(lambda e, c0=c0: e.dma_start(out=self.XTd[:, c0:c0 + PADC].rearrange("(k p) n -> p k n", p=128),
                                                  in_=self.zpad[:]), r=["zpad"], w=["XTd"], semkey=("st", "zpad"))

    def load_bc(self, i, src_row):
        self.dma(lambda e: e.dma_start(out=self.bc[i][:], in_=src_row.to_broadcast([128, D])), w=[("bc", i)])

    P_BIN = 0
    P_DWB = 44
    P_CLG = 52
    P_CLB = 60
    P_PWB = 68
    P_QG = 76
    P_KG = 77

    def load_layer_params(self, l):
        p = self.p
        rows = self.prm_rows
        segs = [(self.P_BIN, 44, p["b_in"][l]), (self.P_DWB, 8, p["conv_dw_b"][l]), (self.P_CLG, 8, p["conv_ln_g"][l]),
                (self.P_CLB, 8, p["conv_ln_b"][l]), (self.P_PWB, 8, p["conv_pw_b"][l]), (self.P_QG, 1, p["q_norm_g"][l]),
                (self.P_KG, 1, p["k_norm_g"][l])]
        for (r0, n, src) in segs:
            self.dma(lambda e, r0=r0, n=n, src=src: e.dma_start(out=rows[r0:r0 + n, :], in_=src), w=["prm_rows"])
        b = self.bank()
        pk = ("ps", b)
        self.T(lambda e, b=b: e.transpose(out=self.ps[:, b, 0:78], in_=rows[0:78, :], identity=self.ident_f[0:78, 0:78]),
               r=["prm_rows", "ident_f"], w=[pk])
        self.V(lambda e, b=b: e.tensor_copy(out=self.prm[:, 0:78], in_=self.ps[:, b, 0:78]), r=[pk], w=["prm"])
        self.dma(lambda e: e.dma_start(out=self.dw_rows[:], in_=p["conv_dw"][l]), w=["dw_rows"])
        for c in range(8):
            b = self.bank()
            pk = ("ps", b)
            self.T(lambda e, b=b, c=c: e.transpose(out=self.ps[:, b, 0:KC], in_=self.dw_rows[:, c * 128:(c + 1) * 128],
                                                   identity=self.ident_f[0:KC, 0:KC]),
                   r=["dw_rows", "ident_f"], w=[pk])
            self.V(lambda e, b=b, c=c: e.tensor_copy(out=self.dw_pp[:, c, :], in_=self.ps[:, b, 0:KC]), r=[pk], w=["dw_pp"])
        self.dma(lambda e: e.dma_start(out=self.bv_bc[:], in_=p["b_in"][l].rearrange("c p -> (c p)")[OFF_V:OFF_V + 256]
                                       .rearrange("(o n) -> o n", o=1).to_broadcast([128, 256])), w=["bv_bc"])
        self.dma(lambda e: e.dma_start(out=self.wr[:], in_=p["w_router"][l].rearrange("(k p) n -> p k n", p=128)), w=["wr"])

    def ln_rows(self, src, dst, gi, bi, eps=LN_EPS, srck=None, dstk=None, par=0):
        sm = self.small
        c0 = par * 4
        bst = self.bst[par]
        kb0, kb1, kmv, ksd, krs = (("lnst", par, i) for i in range(5))
        self.V(lambda e: e.bn_stats(out=bst[:, 0, :], in_=src[:, 0:512]), r=[srck], w=[kb0])
        self.V(lambda e: e.bn_stats(out=bst[:, 1, :], in_=src[:, 512:1024]), r=[srck], w=[kb1])
        self.V(lambda e: e.bn_aggr(out=sm[:, c0:c0 + 2], in_=bst[:].rearrange("p a b -> p (a b)")), r=[kb0, kb1], w=[kmv])
        self.A(lambda e: e.activation(out=sm[:, c0 + 2:c0 + 3], in_=sm[:, c0 + 1:c0 + 2], func=AF.Sqrt, bias=eps, scale=1.0), r=[kmv], w=[ksd])
        self.V(lambda e: e.reciprocal(out=sm[:, c0 + 3:c0 + 4], in_=sm[:, c0 + 2:c0 + 3]), r=[ksd], w=[krs])
        self.V(lambda e: e.tensor_scalar(out=dst, in0=src, scalar1=sm[:, c0:c0 + 1], scalar2=sm[:, c0 + 3:c0 + 4],
                                         op0=ALU.subtract, op1=ALU.mult), r=[srck, kmv, krs], w=[dstk])
        self.V(lambda e: e.tensor_tensor(out=dst, in0=dst, in1=self.bc[gi][:], op=ALU.mult), r=[dstk, ("bc", gi)], w=[dstk])
        self.V(lambda e: e.tensor_tensor(out=dst, in0=dst, in1=self.bc[bi][:], op=ALU.add), r=[dstk, ("bc", bi)], w=[dstk])

    def nr_stageA(self, pk, psap, bias_ap):
        i_raw, i_sq, i_rs = self.tmpk(), self.tmpk(), self.tmpk()
        stt = {"raw": self.tmp[i_raw], "sq": self.tmp[i_sq], "rs": self.tmp[i_rs],
               "kr": ("tmp", i_raw), "ks": ("tmp", i_sq), "krs": ("tmp", i_rs)}
        raw, sq = stt["raw"], stt["sq"]
        self.A(lambda e: e.activation(out=raw[:], in_=psap, func=AF.Identity, bias=bias_ap, scale=1.0), r=[pk, "prm"], w=[stt["kr"]])
        self.A(lambda e: e.activation(out=sq[:], in_=psap, func=AF.Square, bias=bias_ap, scale=1.0), r=[pk, "prm"], w=[stt["ks"]])
        return stt

    def nr_stageB(self, stt, g_ap):
        raw, sq, rs = stt["raw"], stt["sq"], stt["rs"]
        kr, ks, krs = stt["kr"], stt["ks"], stt["krs"]
        b2 = self.bank()
        pk2 = ("ps", b2)
        self.T(lambda e: e.matmul(self.ps[:, b2, :], lhsT=self.ones_h[:], rhs=sq[:], start=True, stop=True),
               r=[ks, "ones_h"], w=[pk2])
        self.A(lambda e: e.activation(out=rs[:], in_=self.ps[:, b2, :], func=AF.Sqrt, bias=RMS_EPS, scale=1.0), r=[pk2], w=[krs])
        self.V(lambda e: e.reciprocal(out=rs[:], in_=rs[:]), r=[krs], w=[krs])
        self.V(lambda e: e.scalar_tensor_tensor(out=raw[:], in0=raw[:], scalar=g_ap, in1=rs[:], op0=ALU.mult, op1=ALU.mult),
               r=[kr, krs, "prm"], w=[kr])

    def nr_stageC(self, stt, rp, out_ap, outk):
        raw, sq = stt["raw"], stt["sq"]
        kr, ks = stt["kr"], stt["ks"]
        b3 = self.bank()
        pk3 = ("ps", b3)
        self.T(lambda e: e.matmul(self.ps[:, b3, :], lhsT=self.swapm[:], rhs=raw[:], start=True, stop=True),
               r=[kr, "swapm"], w=[pk3])
        self.V(lambda e: e.tensor_tensor(out=sq[:], in0=self.ps[:, b3, :], in1=self.ropeS[rp][:], op=ALU.mult),
               r=[pk3, ("ropeS", rp)], w=[ks])
        self.G(lambda e: e.tensor_tensor(out=raw[:], in0=raw[:], in1=self.ropeC[rp][:], op=ALU.mult), r=[kr, ("ropeC", rp)], w=[kr])
        self.V(lambda e: e.tensor_tensor(out=out_ap, in0=raw[:], in1=sq[:], op=ALU.add), r=[kr, ks], w=[outk])

    def norm_rope_pipe(self, n, projA, g_ap, rp, outs):
        stts = [None] * n
        for step in range(n + 2):
            if step < n:
                pk, psap, bias_ap = projA(step)
                stts[step] = self.nr_stageA(pk, psap, bias_ap)
            if 0 <= step - 1 < n:
                self.nr_stageB(stts[step - 1], g_ap)
            if 0 <= step - 2 < n:
                self.nr_stageC(stts[step - 2], rp, *outs[step - 2])

    def load_rope(self, t, rp):
        t0 = t * TT
        self.dma(lambda e: e.dma_start(out=self.ropeC[rp][:], in_=self.c_ropeC[:, t0:t0 + TT]), w=[("ropeC", rp)])
        self.dma(lambda e: e.dma_start(out=self.ropeS[rp][:], in_=self.c_ropeS[:, t0:t0 + TT]), w=[("ropeS", rp)])

    def pass1(self, l, src_d, gi_src):
        p = self.p
        g_row, b_row = gi_src
        self.load_bc(0, g_row)
        self.load_bc(1, b_row)
        self.ws_reset([[(0, 512, p["w_in"][l][:, OFF_K:OFF_K + 512])]])
        wkv, wk_key = self.ws_next()
        for t in range(NT):
            t0 = t * TT
            rp = t % 2
            self.load_rope(t, rp)
            for st in range(4):
                r0 = t0 + st * 128
                rw = self.row[st % 2]
                rk = ("row", st % 2)
                self.dma(lambda e, rw=rw, r0=r0: e.dma_start(out=rw[:], in_=src_d[r0:r0 + 128, :]), r=["ACCd"], w=[rk])
                self.ln_rows(rw[:], rw[:], 0, 1, srck=rk, dstk=rk, par=st % 2)
                self.dma(lambda e, rw=rw, r0=r0: e.dma_start(out=self.Xd[r0:r0 + 128, :], in_=rw[:]), r=[rk], w=["Xd"], semkey=("st", rk))
                for half in range(2):
                    b = self.bank()
                    pk = ("ps", b)
                    for j in range(4):
                        k = half * 4 + j
                        self.T(lambda e, b=b, j=j, k=k, rw=rw: e.transpose(out=self.ps[:, b, j * 128:(j + 1) * 128],
                                                                          in_=rw[:, k * 128:(k + 1) * 128], identity=self.ident_f[:]),
                               r=[rk, "ident_f"], w=[pk])
                    self.A(lambda e, b=b, half=half, st=st: e.copy(
                        out=self.xT[:, half * 4:half * 4 + 4, st * 128:(st + 1) * 128],
                        in_=self.ps[:, b, :].rearrange("p (j n) -> p j n", j=4)), r=[pk], w=["xT"])
            self.dma(lambda e, t0=t0: e.dma_start(out=self.XTd[:, PADC + t0:PADC + t0 + TT].rearrange("(k p) n -> p k n", p=128),
                                                  in_=self.xT[:, :, 0:TT]), r=["xT"], w=["XTd"], semkey=("st", "xT"))
            def projK(h, t0=t0):
                b = self.bank()
                pk = ("ps", b)
                for k in range(8):
                    self.T(lambda e, b=b, k=k, h=h: e.matmul(self.ps[:, b, :], lhsT=wkv[:, k, h * 128:(h + 1) * 128],
                                                            rhs=self.xT[:, k, 0:TT], start=(k == 0), stop=(k == 7)),
                           r=[wk_key, "xT"], w=[pk])
                cb = self.P_BIN + (OFF_K // 128) + h
                return pk, self.ps[:, b, :], self.prm[:, cb:cb + 1]

            self.norm_rope_pipe(NKV, projK, self.prm[:, self.P_KG:self.P_KG + 1], rp,
                                [(self.KT[:, h, t0:t0 + TT], "KT") for h in range(NKV)])
            for st in range(4):
                b = self.bank()
                pk = ("ps", b)
                for k in range(8):
                    self.T(lambda e, b=b, k=k, st=st: e.matmul(self.ps[:, b, 0:256], lhsT=self.xT[:, k, st * 128:(st + 1) * 128],
                                                              rhs=wkv[:, k, 256:512], start=(k == 0), stop=(k == 7)),
                           r=[wk_key, "xT"], w=[pk])
                sg = t * 4 + st
                self.V(lambda e, b=b, sg=sg: e.tensor_tensor(out=self.Vt[:, sg, :], in0=self.ps[:, b, 0:256], in1=self.bv_bc[:],
                                                            op=ALU.add), r=[pk, "bv_bc"], w=["Vt"])

    def pass2_specs(self, l):
        p = self.p
        w_in = p["w_in"][l]
        one = []
        for i in range(4):
            one.append([(0, 256, w_in[:, i * 256:(i + 1) * 256]), (256, 256, w_in[:, D + i * 256:D + (i + 1) * 256])])
        for i in range(2):
            one.append([(0, 512, p["conv_pw_w"][l][:, i * 512:(i + 1) * 512])])
            one.append([(0, 512, w_in[:, OFF_GC + i * 512:OFF_GC + (i + 1) * 512])])
        for i in range(2):
            one.append([(0, 512, w_in[:, OFF_Q + i * 512:OFF_Q + (i + 1) * 512])])
        for i in range(2):
            one.append([(0, 512, w_in[:, OFF_GA + i * 512:OFF_GA + (i + 1) * 512])])
            one.append([(0, 512, p["w_o"][l][:, i * 512:(i + 1) * 512])])
        for i in range(2):
            one.append([(0, 512, p["w_out"][l][:, i * 512:(i + 1) * 512])])
        return one

    def pass2(self, l):
        p = self.p
        prm = self.prm
        ps = self.ps
        self.load_bc(0, p["ln1_g"][l:l + 1, :])
        self.load_bc(1, p["ln1_b"][l:l + 1, :])
        self.load_bc(2, p["b_out"][l:l + 1, :])
        specs = []
        for t in range(NT):
            specs += self.pass2_specs(l)
        self.ws_reset(specs)
        for t in range(NT):
            t0 = t * TT
            rp = t % 2
            self.load_rope(t, rp)
            self.dma(lambda e, t0=t0: e.dma_start(out=self.xT[:], in_=self.XTd[:, t0:t0 + XW].rearrange("(k p) n -> p k n", p=128)),
                     r=["XTd"], w=["xT"])
            for i in range(4):
                wp, wk = self.ws_next()
                for cc in range(2):
                    c = i * 2 + cc
                    bA, bB, bC = self.bank(), self.bank(), self.bank()
                    kA, kB, kC = ("ps", bA), ("ps", bB), ("ps", bC)
                    for (col0, bm, bh, hc) in ((cc * 128, bA, bB, 0), (256 + cc * 128, bC, bB, 32)):
                        for k in range(8):
                            self.T(lambda e, k=k, col0=col0, bm=bm, wp=wp: e.matmul(ps[:, bm, :], lhsT=wp[:, k, col0:col0 + 128],
                                                                            rhs=self.xT[:, k, 0:512], start=(k == 0), stop=(k == 7)),
                                   r=[wk, "xT"], w=[("ps", bm)])
                        for k in range(8):
                            self.T(lambda e, k=k, col0=col0, bh=bh, hc=hc, wp=wp: e.matmul(ps[:, bh, hc:hc + 32], lhsT=wp[:, k, col0:col0 + 128],
                                                                                    rhs=self.xT[:, k, 512:544], start=(k == 0), stop=(k == 7)),
                                   r=[wk, "xT"], w=[("ps", bh)])
                    cv = self.P_BIN + c
                    cg = self.P_BIN + 8 + c
                    ti = self.tmpk()
                    sg = self.tmp[ti]
                    kt = ("tmp", ti)
                    ti2 = self.tmpk()
                    sg2 = self.tmp[ti2]
                    kt2 = ("tmp", ti2)
                    self.A(lambda e, bC=bC, cg=cg, sg=sg: e.activation(out=sg[:], in_=ps[:, bC, :], func=AF.Sigmoid,
                                                                      bias=prm[:, cg:cg + 1], scale=1.0), r=[kC, "prm"], w=[kt])
                    self.A(lambda e, bB=bB, cg=cg, sg2=sg2: e.activation(out=sg2[:, 0:32], in_=ps[:, bB, 32:64], func=AF.Sigmoid,
                                                                        bias=prm[:, cg:cg + 1], scale=1.0), r=[kB, "prm"], w=[kt2])
                    uk = ("uT", c)
                    self.V(lambda e, bA=bA, cv=cv, sg=sg, c=c: e.scalar_tensor_tensor(out=self.uT[:, c, 0:512], in0=ps[:, bA, :],
                                                                                     scalar=prm[:, cv:cv + 1], in1=sg[:],
                                                                                     op0=ALU.add, op1=ALU.mult), r=[kA, kt, "prm"], w=[uk])
                    self.V(lambda e, bB=bB, cv=cv, sg2=sg2, c=c: e.scalar_tensor_tensor(out=self.uT[:, c, 512:544], in0=ps[:, bB, 0:32],
                                                                                       scalar=prm[:, cv:cv + 1], in1=sg2[:, 0:32],
                                                                                       op0=ALU.add, op1=ALU.mult), r=[kB, kt2, "prm"], w=[uk])
                    if t == 0:
                        self.V(lambda e, c=c: e.memset(self.uT[:, c, 0:16], 0.0), w=[uk])
                    if t == NT - 1:
                        self.V(lambda e, c=c: e.memset(self.uT[:, c, 528:544], 0.0), w=[uk])
            bM = self.bank(pin=True)
            bQ = self.bank(pin=True)
            kM, kQ = ("ps", bM), ("ps", bQ)
            for c in range(8):
                b = self.bank()
                pk = ("ps", b)
                for k in range(KC):
                    di = (c * KC + k) % 8
                    dk = ("dg", di)
                    self.V(lambda e, di=di, c=c, k=k: e.tensor_scalar(out=self.dg[di][:], in0=self.ident_b[:],
                                                                     scalar1=self.dw_pp[:, c, k:k + 1], scalar2=None, op0=ALU.mult),
                           r=["ident_b", "dw_pp"], w=[dk])
                    self.T(lambda e, b=b, di=di, c=c, k=k: e.matmul(ps[:, b, :], lhsT=self.dg[di][:], rhs=self.uT[:, c, k + 1:k + 1 + 512],
                                                                   start=(k == 0), stop=(k == KC - 1)), r=[dk, ("uT", c)], w=[pk])
                cdb = self.P_DWB + c
                self.A(lambda e, b=b, c=c, cdb=cdb: e.activation(out=self.Bf[:, c, :], in_=ps[:, b, :], func=AF.Identity,
                                                                bias=prm[:, cdb:cdb + 1], scale=1.0), r=[pk, "prm"], w=[("Bf", c)])
                ti = self.tmpk()
                sq = self.tmp[ti]
                kt = ("tmp", ti)
                self.A(lambda e, b=b, cdb=cdb, sq=sq: e.activation(out=sq[:], in_=ps[:, b, :], func=AF.Square,
                                                                  bias=prm[:, cdb:cdb + 1], scale=1.0), r=[pk, "prm"], w=[kt])
                self.T(lambda e, c=c, bM=bM: e.matmul(ps[:, bM, :], lhsT=self.ones_d[:], rhs=self.Bf[:, c, :], start=(c == 0), stop=(c == 7)),
                       r=[("Bf", c), "ones_d"], w=[kM])
                self.T(lambda e, c=c, sq=sq, bQ=bQ: e.matmul(ps[:, bQ, :], lhsT=self.ones_d[:], rhs=sq[:], start=(c == 0), stop=(c == 7)),
                       r=[kt, "ones_d"], w=[kQ])
            im, iv = self.tmpk(), self.tmpk()
            mean, rstd = self.tmp[im], self.tmp[iv]
            kmn, krs = ("tmp", im), ("tmp", iv)
            self.A(lambda e, mean=mean, bM=bM: e.copy(out=mean[:], in_=ps[:, bM, :]), r=[kM], w=[kmn])
            self.V(lambda e, mean=mean, rstd=rstd, bM=bM: e.tensor_tensor(out=rstd[:], in0=mean[:], in1=ps[:, bM, :], op=ALU.mult), r=[kmn, kM], w=[krs])
            self.V(lambda e, rstd=rstd, bQ=bQ: e.tensor_tensor(out=rstd[:], in0=ps[:, bQ, :], in1=rstd[:], op=ALU.subtract), r=[kQ, krs], w=[krs])
            self.A(lambda e, rstd=rstd: e.activation(out=rstd[:], in_=rstd[:], func=AF.Sqrt, bias=LN_EPS, scale=1.0), r=[krs], w=[krs])
            self.V(lambda e, rstd=rstd: e.reciprocal(out=rstd[:], in_=rstd[:]), r=[krs], w=[krs])
            self.unpin(bM)
            self.unpin(bQ)
            for c in range(8):
                kb = ("Bf", c)
                self.V(lambda e, c=c, mean=mean: e.tensor_tensor(out=self.Bf[:, c, :], in0=self.Bf[:, c, :], in1=mean[:], op=ALU.subtract),
                       r=[kb, kmn], w=[kb])
                self.V(lambda e, c=c, rstd=rstd: e.tensor_tensor(out=self.Bf[:, c, :], in0=self.Bf[:, c, :], in1=rstd[:], op=ALU.mult),
                       r=[kb, krs], w=[kb])
                cg, cb2 = self.P_CLG + c, self.P_CLB + c
                self.A(lambda e, c=c, cg=cg, cb2=cb2: e.activation(out=self.Cb[:, c, :], in_=self.Bf[:, c, :], func=AF.Silu,
                                                                  bias=prm[:, cb2:cb2 + 1], scale=prm[:, cg:cg + 1]),
                       r=[kb, "prm"], w=[("Cb", c)])
            if getattr(self, 'dbg_stop', None) == 'conv':
                return
            for i in range(2):
                wpw, wpwk = self.ws_next()
                wgc, wgck = self.ws_next()
                for jj in range(4):
                    j = i * 4 + jj
                    bg = self.bank()
                    kg_ = ("ps", bg)
                    for k in range(8):
                        self.T(lambda e, bg=bg, k=k, jj=jj, wgc=wgc: e.matmul(ps[:, bg, :], lhsT=wgc[:, k, jj * 128:(jj + 1) * 128],
                                                                             rhs=self.xT[:, k, PADC:PADC + TT], start=(k == 0), stop=(k == 7)),
                               r=[wgck, "xT"], w=[kg_])
                    cgc = self.P_BIN + OFF_GC // 128 + j
                    ti = self.tmpk()
                    gs = self.tmp[ti]
                    kt = ("tmp", ti)
                    self.A(lambda e, bg=bg, cgc=cgc, gs=gs: e.activation(out=gs[:], in_=ps[:, bg, :], func=AF.Sigmoid,
                                                                        bias=prm[:, cgc:cgc + 1], scale=1.0), r=[kg_, "prm"], w=[kt])
                    by = self.bank()
                    ky = ("ps", by)
                    for c in range(8):
                        self.T(lambda e, by=by, c=c, jj=jj, wpw=wpw: e.matmul(ps[:, by, :], lhsT=wpw[:, c, jj * 128:(jj + 1) * 128],
                                                                             rhs=self.Cb[:, c, :], start=(c == 0), stop=(c == 7)),
                               r=[wpwk, ("Cb", c)], w=[ky])
                    cpb = self.P_PWB + j
                    self.V(lambda e, by=by, j=j, cpb=cpb, gs=gs: e.scalar_tensor_tensor(out=self.Bf[:, j, :], in0=ps[:, by, :],
                                                                                      scalar=prm[:, cpb:cpb + 1], in1=gs[:],
                                                                                      op0=ALU.add, op1=ALU.mult),
                           r=[ky, kt, "prm"], w=[("Bf", j)])
            if getattr(self, 'dbg_stop', None) == 'pw':
                return
            qw = {}

            def projQ(h):
                i, jj = divmod(h, 4)
                if jj == 0:
                    qw["wp"], qw["wk"] = self.ws_next()
                wp, wk = qw["wp"], qw["wk"]
                b = self.bank()
                pk = ("ps", b)
                for k in range(8):
                    self.T(lambda e, b=b, k=k, jj=jj, wp=wp: e.matmul(ps[:, b, :], lhsT=wp[:, k, jj * 128:(jj + 1) * 128],
                                                                     rhs=self.xT[:, k, PADC:PADC + TT], start=(k == 0), stop=(k == 7)),
                           r=[wk, "xT"], w=[pk])
                cq = self.P_BIN + OFF_Q // 128 + h
                return pk, ps[:, b, :], prm[:, cq:cq + 1]

            self.norm_rope_pipe(NH, projQ, prm[:, self.P_QG:self.P_QG + 1], rp,
                                [(self.Db[:, h, :], ("Db", h)) for h in range(NH)])
            NPAIR = S_LEN // 256
            units = [(g, qs) for g in range(NKV) for qs in range(4)]

            def emit_S(f):
                n, j = divmod(f, NPAIR)
                g, qs = units[n]
                sp = (0, 1) if f % 2 == 0 else (2, 3)
                qk = [("Db", g * 4 + hh) for hh in range(4)]
                for u in range(2):
                    kc = 2 * j + u
                    self.T(lambda e, b=sp[u], g=g, kc=kc, qs=qs: e.matmul(
                        ps[:, b, :].rearrange("p (h q) -> p h q", h=4), lhsT=self.KT[:, g, kc * 128:(kc + 1) * 128],
                        rhs=self.Db[:, g * 4:g * 4 + 4, qs * 128:(qs + 1) * 128], start=True, stop=True),
                        r=["KT"] + qk, w=[("ps", sp[u])])

            NF = len(units) * NPAIR
            emit_S(0)
            for f in range(NF):
                n, j = divmod(f, NPAIR)
                g, qs = units[n]
                if f + 1 < NF:
                    emit_S(f + 1)
                sp = (0, 1) if f % 2 == 0 else (2, 3)
                bO, bS = (4, 5) if n % 2 == 0 else (6, 7)
                kO, kS = ("ps", bO), ("ps", bS)
                pi = f % 3
                pt = self.PT[pi]
                kp = ("PT", pi)
                self.A(lambda e, s0=sp[0], pt=pt: e.activation(out=pt[:].rearrange("p (u n) -> p u n", u=2), in_=ps[:, s0:s0 + 2, :],
                                                               func=AF.Exp, scale=ATT_SCALE),
                       r=[("ps", sp[0]), ("ps", sp[1])], w=[kp])
                for u in range(2):
                    kc = 2 * j + u
                    first = (j == 0 and u == 0)
                    last = (j == NPAIR - 1 and u == 1)
                    self.T(lambda e, g=g, kc=kc, pt=pt, bO=bO, u=u, first=first, last=last: e.matmul(
                        ps[:, bO, :], lhsT=self.Vt[:, kc, g * 128:(g + 1) * 128], rhs=pt[:, u * 512:(u + 1) * 512],
                        start=first, stop=last), r=["Vt", kp], w=[kO])
                    self.T(lambda e, pt=pt, bS=bS, u=u, first=first, last=last: e.matmul(
                        ps[:, bS, :], lhsT=self.ones_b[:], rhs=pt[:, u * 512:(u + 1) * 512],
                        start=first, stop=last), r=["ones_b", kp], w=[kS])
                if j == NPAIR - 1:
                    ti = self.tmpk()
                    rec = self.tmp[ti]
                    kt = ("tmp", ti)
                    self.V(lambda e, rec=rec, bS=bS: e.reciprocal(out=rec[:], in_=ps[:, bS, :]), r=[kS], w=[kt])
                    ek = [("Eb", g * 4 + hh) for hh in range(4)]
                    self.V(lambda e, g=g, qs=qs, rec=rec, bO=bO: e.tensor_tensor(
                        out=self.Eb[:, g * 4:g * 4 + 4, qs * 128:(qs + 1) * 128],
                        in0=ps[:, bO, :].rearrange("p (h q) -> p h q", h=4), in1=rec[:].rearrange("p (h q) -> p h q", h=4),
                        op=ALU.mult), r=[kO, kt] + ek, w=ek)
            for i in range(2):
                wga, wgak = self.ws_next()
                wwo, wwok = self.ws_next()
                for jj in range(4):
                    j = i * 4 + jj
                    bg = self.bank()
                    kg_ = ("ps", bg)
                    for k in range(8):
                        self.T(lambda e, bg=bg, k=k, jj=jj, wga=wga: e.matmul(ps[:, bg, :], lhsT=wga[:, k, jj * 128:(jj + 1) * 128],
                                                                             rhs=self.xT[:, k, PADC:PADC + TT], start=(k == 0), stop=(k == 7)),
                               r=[wgak, "xT"], w=[kg_])
                    cga = self.P_BIN + OFF_GA // 128 + j
                    ti = self.tmpk()
                    gs = self.tmp[ti]
                    kt = ("tmp", ti)
                    self.A(lambda e, bg=bg, cga=cga, gs=gs: e.activation(out=gs[:], in_=ps[:, bg, :], func=AF.Sigmoid,
                                                                        bias=prm[:, cga:cga + 1], scale=1.0), r=[kg_, "prm"], w=[kt])
                    by = self.bank()
                    ky = ("ps", by)
                    for h in range(8):
                        self.T(lambda e, by=by, h=h, jj=jj, wwo=wwo: e.matmul(ps[:, by, :], lhsT=wwo[:, h, jj * 128:(jj + 1) * 128],
                                                                             rhs=self.Eb[:, h, :], start=(h == 0), stop=(h == 7)),
                               r=[wwok, ("Eb", h)], w=[ky])
                    self.V(lambda e, by=by, gs=gs: e.tensor_tensor(out=gs[:], in0=gs[:], in1=ps[:, by, :], op=ALU.mult),
                           r=[kt, ky], w=[kt])
                    self.V(lambda e, j=j, gs=gs: e.tensor_tensor(out=self.Cb[:, j, :], in0=self.Bf[:, j, :], in1=gs[:], op=ALU.add),
                           r=[("Bf", j), kt], w=[("Cb", j)])
            wpa, wka = self.ws_next()
            wpb, wkb = self.ws_next()
            def wo_p1(st):
                r0 = t0 + st * 128
                sg = t * 4 + st
                par = st % 2
                tr, ar = self.row[par], self.row[2]
                ktr = ("row", par)
                bb = []
                for (wp, wk) in ((wpa, wka), (wpb, wkb)):
                    b = self.bank()
                    bb.append(b)
                    for k in range(8):
                        self.T(lambda e, b=b, k=k, st=st, wp=wp: e.matmul(ps[:, b, :], lhsT=self.Cb[:, k, st * 128:(st + 1) * 128],
                                                                         rhs=wp[:, k, :], start=(k == 0), stop=(k == 7)),
                               r=[wk, ("Cb", k)], w=[("ps", b)])
                for hf in range(2):
                    b = bb[hf]
                    self.V(lambda e, b=b, hf=hf, tr=tr: e.scalar_tensor_tensor(out=tr[:, hf * 512:(hf + 1) * 512], in0=tr[:, hf * 512:(hf + 1) * 512],
                                                                              scalar=ALPHA, in1=ps[:, b, :], op0=ALU.mult, op1=ALU.add),
                           r=[ktr, ("ps", b)], w=[ktr])
                self.V(lambda e, tr=tr: e.tensor_tensor(out=tr[:], in0=tr[:], in1=self.bc[2][:], op=ALU.add), r=[ktr, ("bc", 2)], w=[ktr])
                self.ln_rows(tr[:], tr[:], 0, 1, srck=ktr, dstk=ktr, par=par)

            def wo_p2(st):
                r0 = t0 + st * 128
                sg = t * 4 + st
                par = st % 2
                tr, ar = self.row[par], self.row[2]
                ktr = ("row", par)
                self.A(lambda e, tr=tr: e.copy(out=self.rowb[:], in_=tr[:]), r=[ktr], w=["rowb"])
                self.dma(lambda e, r0=r0: e.dma_start(out=self.X1d[r0:r0 + 128, :], in_=self.rowb[:]), r=["rowb"], w=["X1d"], semkey=("st", "rowb"))
                self.A(lambda e, tr=tr: e.mul(ar[:], tr[:], ALPHA), r=[ktr], w=[("row", 2)])
                self.dma(lambda e, r0=r0: e.dma_start(out=self.ACCd[r0:r0 + 128, :], in_=ar[:]), r=[("row", 2)], w=["ACCd"], semkey=("st", "row2"))
                if "x1" in self.dbg and l == 0:
                    self.dma(lambda e, r0=r0, tr=tr: e.dma_start(out=self.dbg_t["x1"][r0:r0 + 128, :], in_=tr[:]), r=[ktr], w=["dbg_x1"],
                             semkey=("st", "dbgx1"))
                for half in range(2):
                    b = self.bank()
                    pk = ("ps", b)
                    for jx in range(4):
                        k = half * 4 + jx
                        self.T(lambda e, b=b, jx=jx, k=k, tr=tr: e.transpose(out=ps[:, b, jx * 128:(jx + 1) * 128],
                                                                             in_=tr[:, k * 128:(k + 1) * 128], identity=self.ident_f[:]),
                               r=[ktr, "ident_f"], w=[pk])
                    self.A(lambda e, b=b, half=half: e.copy(out=self.x1T[:, half * 4:half * 4 + 4, :],
                                                           in_=ps[:, b, :].rearrange("p (j n) -> p j n", j=4)), r=[pk], w=["x1T"])
                b = self.bank()
                pk = ("ps", b)
                for k in range(8):
                    self.T(lambda e, b=b, k=k: e.matmul(ps[:, b, 0:NE], lhsT=self.x1T[:, k, :], rhs=self.wr[:, k, :],
                                                        start=(k == 0), stop=(k == 7)), r=["x1T", "wr"], w=[pk])
                sm = self.small
                c0 = 8 + par * 4
                e0 = 16 + par * 16
                kq = [("rt", par, i) for i in range(5)]
                self.V(lambda e, b=b, c0=c0: e.reduce_max(out=sm[:, c0:c0 + 1], in_=ps[:, b, 0:NE], axis=mybir.AxisListType.X), r=[pk], w=[kq[0]])
                self.V(lambda e, c0=c0: e.tensor_scalar(out=sm[:, c0 + 1:c0 + 2], in0=sm[:, c0:c0 + 1], scalar1=-1.0, scalar2=None, op0=ALU.mult),
                       r=[kq[0]], w=[kq[1]])
                self.A(lambda e, b=b, c0=c0, e0=e0: e.activation(out=sm[:, e0:e0 + NE], in_=ps[:, b, 0:NE], func=AF.Exp, bias=sm[:, c0 + 1:c0 + 2],
                                                                scale=1.0, accum_out=sm[:, c0 + 2:c0 + 3]), r=[pk, kq[1]], w=[kq[2], kq[3]])
                self.V(lambda e, c0=c0: e.reciprocal(out=sm[:, c0 + 3:c0 + 4], in_=sm[:, c0 + 2:c0 + 3]), r=[kq[3]], w=[kq[4]])
                self.V(lambda e, sg=sg, c0=c0, e0=e0: e.tensor_scalar(out=self.aff[:, sg, :], in0=sm[:, e0:e0 + NE], scalar1=sm[:, c0 + 3:c0 + 4],
                                                                     scalar2=None, op0=ALU.mult), r=[kq[2], kq[4]], w=["aff"])

            def wo_load(st):
                r0 = t0 + st * 128
                tr = self.row[st % 2]
                self.dma(lambda e, r0=r0, tr=tr: e.dma_start(out=tr[:], in_=self.Xd[r0:r0 + 128, :]), r=["Xd"], w=[("row", st % 2)])

            wo_load(0)
            wo_load(1)
            for step in range(5):
                if step < 4:
                    wo_p1(step)
                if step >= 1:
                    wo_p2(step - 1)
                    if step + 1 < 4:
                        wo_load(step + 1)

    def topk(self, l):
        ps = self.ps
        work = self.Bf[0:NE, :, :].rearrange("p a b -> p (a b)")
        wk = [("Bf", c) for c in range(8)]
        for bi in range(8):
            b = self.bank()
            pk = ("ps", b)
            for j in range(4):
                sg = bi * 4 + j
                self.T(lambda e, b=b, j=j, sg=sg: e.transpose(out=ps[0:NE, b, j * 128:(j + 1) * 128], in_=self.aff[:, sg, :],
                                                              identity=self.ident_f[:]), r=["aff", "ident_f"], w=[pk])
            self.V(lambda e, b=b, bi=bi: e.tensor_copy(out=work[:, bi * 512:(bi + 1) * 512], in_=ps[0:NE, b, :]), r=[pk], w=[wk[bi]])
        for r in range(CAP // 8):
            self.V(lambda e, r=r: e.max(out=self.topv[:, r * 8:(r + 1) * 8], in_=work), r=wk, w=["topv"])
            self.V(lambda e, r=r: e.max_index(out=self.topi[:, r * 8:(r + 1) * 8], in_max=self.topv[:, r * 8:(r + 1) * 8], in_values=work),
                   r=wk + ["topv"], w=["topi"])
            self.V(lambda e, r=r: e.match_replace(out=work, in_to_replace=self.topv[:, r * 8:(r + 1) * 8], in_values=work, imm_value=-1.0),
                   r=wk + ["topv"], w=wk)
        self.V(lambda e: e.tensor_copy(out=self.topif[:], in_=self.topi[:]), r=["topi"], w=["topif"])
        for (src, srck, dst, dstk) in ((self.topv, "topv", self.slot_g, "slot_g"), (self.topif, "topif", self.slot_i, "slot_i")):
            b = self.bank()
            pk = ("ps", b)
            for col in range(4):
                self.T(lambda e, b=b, col=col, src=src: e.transpose(out=ps[:, b, col * NE:(col + 1) * NE], in_=src[:, col * 128:(col + 1) * 128],
                                                                    identity=self.ident_f[0:NE, 0:NE]), r=[srck, "ident_f"], w=[pk])
            self.V(lambda e, b=b, dst=dst: e.tensor_copy(out=dst[:].rearrange("p e c -> p c e"),
                                                         in_=ps[:, b, 0:4 * NE].rearrange("p (c e) -> p c e", c=4)), r=[pk], w=[dstk])

    def moe_specs(self, l):
        p = self.p
        specs = []
        for ex in range(NE):
            wg, wu, wd = p["w_gate"][l, ex], p["w_up"][l, ex], p["w_down"][l, ex]
            for i in range(4):
                specs.append([(0, 512, wg[:, i * 512:(i + 1) * 512])])
                specs.append([(0, 512, wu[:, i * 512:(i + 1) * 512])])
            for ch in range(2):
                for rh in range(2):
                    specs.append([(0, 512, wd[rh * 1024:(rh + 1) * 1024, ch * 512:(ch + 1) * 512])])
        return specs

    def moe(self, l):
        ps = self.ps
        self.ws_reset(self.moe_specs(l))
        xe = [self.row[0], self.row[1]]
        for ex in range(NE):
            xeb = self.Db
            xebv = xeb[:].rearrange("p (c a) n -> p c (a n)", c=4)
            xk = [("Db", h) for h in range(8)]
            for col in range(4):
                self.dma(lambda e, ex=ex, col=col: e.indirect_dma_start(
                    out=xebv[:, col, :], out_offset=None, in_=self.X1d[:, :],
                    in_offset=bass.IndirectOffsetOnAxis(ap=self.slot_i[:, ex, col:col + 1], axis=0)),
                    r=["X1d", "slot_i"], w=[("Db", 2 * col), ("Db", 2 * col + 1)], eng="gpsimd", semkey=("gather", col))
            for k in range(8):
                b = self.bank()
                pk = ("ps", b)
                pv = ps[:, b, :].bitcast(BF16)
                for col in range(4):
                    self.T(lambda e, pv=pv, col=col, k=k: e.transpose(out=pv[:, col * 128:(col + 1) * 128],
                                                                      in_=xebv[:, col, k * 128:(k + 1) * 128], identity=self.ident_b[:]),
                           r=[("Db", 2 * col), ("Db", 2 * col + 1), "ident_b"], w=[pk])
                self.A(lambda e, pv=pv, k=k: e.copy(out=self.Eb[:, k, :], in_=pv[:, 0:512]), r=[pk], w=[("Eb", k)])
            ek = [("Eb", k) for k in range(8)]
            hid = [self.uT[:, f, 0:512] for f in range(8)] + [self.Cb[:, f, :] for f in range(8)]
            hk = [("uT", f) for f in range(8)] + [("Cb", f) for f in range(8)]
            for i in range(4):
                wg, wgk = self.ws_next()
                wu, wuk = self.ws_next()
                for jj in range(4):
                    f = i * 4 + jj
                    bg, bu = self.bank(), self.bank()
                    for (w_, wk_, b_) in ((wg, wgk, bg), (wu, wuk, bu)):
                        for k in range(8):
                            self.T(lambda e, w_=w_, b_=b_, k=k, jj=jj: e.matmul(ps[:, b_, :], lhsT=w_[:, k, jj * 128:(jj + 1) * 128],
                                                                               rhs=self.Eb[:, k, :], start=(k == 0), stop=(k == 7)),
                                   r=[wk_, ("Eb", k)], w=[("ps", b_)])
                    ti = self.tmpk()
                    sg = self.tmp[ti]
                    kt = ("tmp", ti)
                    self.A(lambda e, bg=bg, sg=sg: e.activation(out=sg[:], in_=ps[:, bg, :], func=AF.Silu), r=[("ps", bg)], w=[kt])
                    self.V(lambda e, bu=bu, sg=sg, f=f: e.tensor_tensor(out=hid[f], in0=sg[:], in1=ps[:, bu, :], op=ALU.mult),
                           r=[kt, ("ps", bu)], w=[hk[f]])
            ye = self.Bf
            yev = ye[:].rearrange("p (c a) n -> p c (a n)", c=4)
            yk = [("Bf", c) for c in range(8)]
            for ch in range(2):
                wd0, wdk0 = self.ws_next()
                wd1, wdk1 = self.ws_next()
                for col in range(4):
                    b = self.bank()
                    pk = ("ps", b)
                    for f in range(16):
                        w_, wk_ = (wd0, wdk0) if f < 8 else (wd1, wdk1)
                        self.T(lambda e, b=b, f=f, col=col, w_=w_: e.matmul(ps[:, b, :], lhsT=hid[f][:, col * 128:(col + 1) * 128],
                                                                           rhs=w_[:, f % 8, :], start=(f == 0), stop=(f == 15)),
                               r=[wk_, hk[f]], w=[pk])
                    self.V(lambda e, b=b, col=col, ch=ch, ex=ex: e.tensor_scalar(
                        out=yev[:, col, ch * 512:(ch + 1) * 512], in0=ps[:, b, :], scalar1=self.slot_g[:, ex, col:col + 1], scalar2=None,
                        op0=ALU.mult), r=[pk, "slot_g"], w=[("Bf", col * 2), ("Bf", col * 2 + 1)])
            for col in range(4):
                self.dma(lambda e, ex=ex, col=col: e.indirect_dma_start(
                    out=self.ACCd[:, :], out_offset=bass.IndirectOffsetOnAxis(ap=self.slot_i[:, ex, col:col + 1], axis=0),
                    in_=yev[:, col, :], in_offset=None, compute_op=ALU.add),
                    r=yk + ["slot_i"], w=["ACCd"], eng="gpsimd", semkey=("scat", col))

    def final(self, l):
        p = self.p
        self.load_bc(0, p["ln2_g"][l:l + 1, :])
        self.load_bc(1, p["ln2_b"][l:l + 1, :])
        for sg in range(S_LEN // 128):
            r0 = sg * 128
            rw = self.row[sg % 2]
            rk = ("row", sg % 2)
            self.dma(lambda e, rw=rw, r0=r0: e.dma_start(out=rw[:], in_=self.ACCd[r0:r0 + 128, :]), r=["ACCd"], w=[rk])
            self.ln_rows(rw[:], rw[:], 0, 1, srck=rk, dstk=rk, par=sg % 2)
            self.dma(lambda e, rw=rw, r0=r0: e.dma_start(out=self.out[r0:r0 + 128, :], in_=rw[:]), r=[rk], w=["out"], semkey=("st", rk))

    def build(self, stop_after=None):
        with ExitStack() as st:
            self.st = st
            self.declare()
            for nm, shp, dt in self.dbg_decl:
                self.dbg_out(nm, shp, dt)
            self.alloc()
            self.S = Sched(self.nc, st)
            self.setup_consts()
            p = self.p
            for li, l in enumerate(self.layers):
                self.load_layer_params(li)
                if l == 0:
                    src, gb = self.x_in, (p["ln0_g"], p["ln0_b"])
                else:
                    src, gb = self.ACCd, (p["ln2_g"][li - 1:li, :], p["ln2_b"][li - 1:li, :])
                self.pass1(li, src, gb)
                if stop_after == "pass1":
                    break
                self.pass2(li)
                if stop_after == "pass2":
                    break
                self.topk(li)
                if stop_after == "topk":
                    break
                self.moe(li)
            if stop_after is None:
                self.final(len(self.layers) - 1)
                fin = ["out"]
            else:
                fin = []
            self.debug_dumps(stop_after)
            fin += ["dbg_" + n for n in self.dbg_written]
            self.S.finish("sync", fin)
            self.S.emit()
        return self.nc

    dbg_decl = ()
    dbg_written = ()

    def debug_dumps(self, stop_after):
        pass


def rope_tables():
    t = np.arange(S_LEN)
    row = (t // 64).astype(np.float32)
    col = (t % 64).astype(np.float32)
    axis_dim = DH // 2
    freqs = (1.0 / (np.float32(10000.0) ** (np.arange(0, axis_dim, 2, dtype=np.float32) / np.float32(axis_dim)))).astype(np.float32)
    ang = np.concatenate([row[:, None] * freqs[None], col[:, None] * freqs[None]], axis=-1).astype(np.float32)
    cos = np.cos(ang).astype(np.float32)
    sin = np.sin(ang).astype(np.float32)
    C = np.repeat(cos.T, 2, axis=0)
    Sg = np.repeat(sin.T, 2, axis=0)
    sign = np.where(np.arange(DH) % 2 == 0, -1.0, 1.0).astype(np.float32)[:, None]
    return np.ascontiguousarray(C), np.ascontiguousarray(Sg * sign)


def const_inputs():
    ident = np.eye(128, dtype=np.float32)
    swap = np.zeros((128, 128), np.float32)
    idx = np.arange(128)
    swap[idx, idx ^ 1] = 1.0
    C, Sg = rope_tables()
    return {"c_ident": ident, "c_swap": swap, "c_ropeC": C, "c_ropeS": Sg}


def core_inputs(inputs, b, layers=None):
    m = {"x": np.ascontiguousarray(inputs["x"][b])}
    if layers is not None:
        inputs = dict(inputs)
        for nm in inputs:
            if nm not in ("x", "ln0_g", "ln0_b"):
                inputs[nm] = np.ascontiguousarray(inputs[nm][list(layers)])
    L = NL if layers is None else len(layers)
    m["ln0_g"] = inputs["ln0_g"].reshape(1, D)
    m["ln0_b"] = inputs["ln0_b"].reshape(1, D)
    m["b_in"] = inputs["b_in"].reshape(L, N_IN // 128, 128)
    for nm in ("conv_dw_b", "conv_ln_g", "conv_ln_b", "conv_pw_b"):
        m[nm] = inputs[nm].reshape(L, 8, 128)
    m["q_norm_g"] = inputs["q_norm_g"].reshape(L, 1, DH)
    m["k_norm_g"] = inputs["k_norm_g"].reshape(L, 1, DH)
    for nm in ("w_in", "conv_dw", "conv_pw_w", "w_o", "w_out", "b_out", "ln1_g", "ln1_b", "w_router", "w_gate", "w_up",
               "w_down", "ln2_g", "ln2_b"):
        m[nm] = inputs[nm]
    m.update(const_inputs())
    return m


def kernel(**inputs):
    inputs = {k: np.asarray(v) for k, v in inputs.items()}
    mk = MK()
    nc = mk.build()
    n = 4
    in_maps = [core_inputs(inputs, c) for c in range(n)]
    res = run_bass_kernel_spmd(nc, in_maps, core_ids=list(range(n)))
    out = np.stack([np.asarray(res.results[c]["out"]) for c in range(n)], axis=0)
    return out.astype(np.float32)
```

```python
import math
import numpy as np
from contextlib import ExitStack
import concourse.bass as bass
import concourse.mybir as mybir
from concourse.bass_utils import run_bass_kernel_spmd

F32 = mybir.dt.float32
BF16 = mybir.dt.bfloat16
U32 = mybir.dt.uint32
AF = mybir.ActivationFunctionType
ALU = mybir.AluOpType

S_LEN = 4096
D = 1024
NL = 4
TT = 512
NT = S_LEN // TT
XW = 544
PADC = 16
NH = 8
NKV = 2
DH = 128
NE = 16
CAP = 512
DFF = 2048
KC = 31
N_IN = 5632
OFF_Q = 2048
OFF_K = 3072
OFF_V = 3328
OFF_GC = 3584
OFF_GA = 4608
LN_EPS = 1e-5
RMS_EPS = 1e-6
ALPHA = (2.0 * NL) ** 0.25
ATT_SCALE = 1.0 / math.sqrt(DH)
NRING = 4
NTMP = 10

ENGS = ("tensor", "vector", "scalar", "gpsimd", "sync")


class Sched:
    def __init__(self, nc, stack):
        self.nc = nc
        self.stack = stack
        self.prog = {e: [] for e in ENGS}
        self.cnt = {e: 0 for e in ENGS}
        self.esem = {e: stack.enter_context(nc.semaphore("es_" + e)) for e in ENGS}
        self.known = {e: {} for e in ENGS}
        self.last_w = {}
        self.readers = {}
        self.dsem = {}
        self.dcnt = {}
        self.nops = 0
        self.nwaits = 0

    def _sem_for_key(self, key):
        if key not in self.dsem:
            self.dsem[key] = self.stack.enter_context(self.nc.semaphore("ds%d" % len(self.dsem)))
            self.dcnt[key] = 0
        return self.dsem[key]

    def _waits(self, eng, reads, writes):
        waits = {}
        own = self.esem[eng].num

        def need(sv, raw):
            s, v = sv
            if s.num == own and not raw and eng == "tensor":
                return
            cur = waits.get(s.num)
            if cur is None or cur[1] < v:
                waits[s.num] = (s, v)

        for k in reads:
            for sv in self.last_w.get(k, {}).values():
                need(sv, True)
        for k in writes:
            for sv in self.last_w.get(k, {}).values():
                need(sv, False)
            for sv in self.readers.get(k, {}).values():
                need(sv, False)
        out = []
        kn = self.known[eng]
        for num, (s, v) in waits.items():
            if kn.get(num, 0) >= v:
                continue
            kn[num] = v
            out.append((s, v))
        return out

    def _commit(self, s, v, reads, writes):
        for k in writes:
            self.last_w.setdefault(k, {})[s.num] = (s, v)
            self.readers[k] = {}
        for k in reads:
            self.readers.setdefault(k, {})[s.num] = (s, v)

    def op(self, eng, fn, reads=(), writes=()):
        waits = self._waits(eng, reads, writes)
        self.cnt[eng] += 1
        s = self.esem[eng]
        self.prog[eng].append((waits, fn, s, 1))
        self._commit(s, self.cnt[eng], reads, writes)
        self.nops += 1
        self.nwaits += len(waits)

    def dma(self, eng, fn, reads=(), writes=(), semkey=None):
        waits = self._waits(eng, reads, writes)
        key = semkey if semkey is not None else writes[0]
        s = self._sem_for_key(key)
        self.dcnt[key] += 16
        self.prog[eng].append((waits, fn, s, 16))
        self._commit(s, self.dcnt[key], reads, writes)
        self.nops += 1
        self.nwaits += len(waits)

    def finish(self, eng, keys):
        waits = self._waits(eng, keys, ())
        self.prog[eng].append((waits, None, None, 0))

    def emit(self):
        with self.nc.Block() as block:
            for e in ENGS:
                prog = self.prog[e]
                if not prog:
                    continue

                def body(engine, prog=prog):
                    for waits, fn, s, inc in prog:
                        for ws, wv in waits:
                            engine.wait_ge(ws, wv)
                        if fn is not None:
                            fn(engine).then_inc(s, inc)

                getattr(block, e)(body)


class MK:
    def __init__(self, layers=(0, 1, 2, 3), first=True, last=True, dbg=()):
        self.layers = list(layers)
        self.first = first
        self.last = last
        self.dbg = set(dbg)
        self.nc = bass.Bass("TRN2", target_bir_lowering=False)
        self.bank_rr = 0
        self.pinned = set()
        self.tmp_rr = 0

    def din(self, name, shape, dt=F32):
        return self.nc.dram_tensor(name, list(shape), dt, kind="ExternalInput").ap()

    def sb(self, name, shape, dt):
        return self.st.enter_context(self.nc.sbuf_tensor(name, list(shape), dt))

    def declare(self):
        nc = self.nc
        L = len(self.layers)
        self.x_in = self.din("x", [S_LEN, D])
        self.p = {}
        for nm, shp in [("ln0_g", [1, D]), ("ln0_b", [1, D]), ("w_in", [L, D, N_IN]), ("b_in", [L, N_IN // 128, 128]),
                        ("conv_dw", [L, KC, D]), ("conv_dw_b", [L, 8, 128]), ("conv_ln_g", [L, 8, 128]),
                        ("conv_ln_b", [L, 8, 128]), ("conv_pw_w", [L, D, D]), ("conv_pw_b", [L, 8, 128]),
                        ("q_norm_g", [L, 1, DH]), ("k_norm_g", [L, 1, DH]), ("w_o", [L, D, D]), ("w_out", [L, D, D]),
                        ("b_out", [L, D]), ("ln1_g", [L, D]), ("ln1_b", [L, D]), ("w_router", [L, D, NE]),
                        ("w_gate", [L, NE, D, DFF]), ("w_up", [L, NE, D, DFF]), ("w_down", [L, NE, DFF, D]),
                        ("ln2_g", [L, D]), ("ln2_b", [L, D])]:
            self.p[nm] = self.din(nm, shp)
        self.c_ident = self.din("c_ident", [128, 128])
        self.c_swap = self.din("c_swap", [128, 128])
        self.c_ropeC = self.din("c_ropeC", [128, S_LEN])
        self.c_ropeS = self.din("c_ropeS", [128, S_LEN])
        self.out = nc.dram_tensor("out", [S_LEN, D], F32, kind="ExternalOutput").ap()
        self.Xd = nc.dram_tensor("Xd", [S_LEN, D], F32, kind="Internal").ap()
        self.XTd = nc.dram_tensor("XTd", [D, S_LEN + 2 * PADC], BF16, kind="Internal").ap()
        self.X1d = nc.dram_tensor("X1d", [S_LEN, D], BF16, kind="Internal").ap()
        self.ACCd = nc.dram_tensor("ACCd", [S_LEN, D], F32, kind="Internal").ap()
        self.dbg_t = {}

    def dbg_out(self, name, shape, dt=F32):
        t = self.nc.dram_tensor("dbg_" + name, list(shape), dt, kind="ExternalOutput").ap()
        self.dbg_t[name] = t
        return t

    def alloc(self):
        sb = self.sb
        self.KT = sb("KT", [128, NKV, S_LEN], BF16)
        self.Vt = sb("Vt", [128, S_LEN // 128, NKV * DH], BF16)
        self.aff = sb("aff", [128, S_LEN // 128, NE], F32)
        self.ident_f = sb("ident_f", [128, 128], F32)
        self.ident_b = sb("ident_b", [128, 128], BF16)
        self.swapm = sb("swapm", [128, 128], F32)
        self.ones_d = sb("ones_d", [128, 128], F32)
        self.ones_h = sb("ones_h", [128, 128], F32)
        self.ones_b = sb("ones_b", [128, 128], BF16)
        self.prm_rows = sb("prm_rows", [80, 128], F32)
        self.prm = sb("prm", [128, 80], F32)
        self.dw_rows = sb("dw_rows", [KC, D], F32)
        self.dw_pp = sb("dw_pp", [128, 8, KC], F32)
        self.bc = [sb("bc%d" % i, [128, D], F32) for i in range(3)]
        self.bv_bc = sb("bv_bc", [128, NKV * DH], F32)
        self.wr = sb("wr", [128, 8, NE], F32)
        self.ring = [sb("ring%d" % i, [128, 8, 512], BF16) for i in range(NRING)]
        self.xT = sb("xT", [128, 8, XW], BF16)
        self.uT = sb("uT", [128, 8, XW], BF16)
        self.Bf = sb("Bf", [128, 8, TT], F32)
        self.Cb = sb("Cb", [128, 8, TT], BF16)
        self.Db = sb("Db", [128, 8, TT], BF16)
        self.Eb = sb("Eb", [128, 8, TT], BF16)
        self.tmp = [sb("tmp%d" % i, [128, TT], F32) for i in range(NTMP)]
        self.PT = [sb("PT%d" % i, [128, 2 * TT], BF16) for i in range(3)]
        self.ropeC = [sb("ropeC%d" % i, [128, TT], F32) for i in range(2)]
        self.ropeS = [sb("ropeS%d" % i, [128, TT], F32) for i in range(2)]
        self.row = [sb("row%d" % i, [128, D], F32) for i in range(3)]
        self.rowb = sb("rowb", [128, D], BF16)
        self.x1T = sb("x1T", [128, 8, 128], F32)
        self.dg = [sb("dg%d" % i, [128, 128], BF16) for i in range(8)]
        self.small = sb("small", [128, 64], F32)
        self.bst = [sb("bst%d" % i, [128, 2, 6], F32) for i in range(2)]
        self.zpad = sb("zpad", [128, 8, PADC], BF16)
        self.topv = sb("topv", [NE, CAP], F32)
        self.topi = sb("topi", [NE, CAP], U32)
        self.topif = sb("topif", [NE, CAP], F32)
        self.slot_g = sb("slot_g", [128, NE, 4], F32)
        self.slot_i = sb("slot_i", [128, NE, 4], U32)
        self.ps = self.st.enter_context(self.nc.psum_tensor("ps", [128, 8, 512], F32))

    def bank(self, pin=False):
        while True:
            b = self.bank_rr
            self.bank_rr = (self.bank_rr + 1) % 8
            if b not in self.pinned:
                break
        if pin:
            self.pinned.add(b)
        return b

    def unpin(self, b):
        self.pinned.discard(b)

    def T(self, fn, r=(), w=()):
        self.S.op("tensor", fn, r, w)

    def V(self, fn, r=(), w=()):
        self.S.op("vector", fn, r, w)

    def A(self, fn, r=(), w=()):
        self.S.op("scalar", fn, r, w)

    def G(self, fn, r=(), w=()):
        self.S.op("gpsimd", fn, r, w)

    def dma(self, fn, r=(), w=(), eng="sync", semkey=None):
        self.S.dma(eng, fn, r, w, semkey)

    def tmpk(self):
        i = self.tmp_rr
        self.tmp_rr = (self.tmp_rr + 1) % NTMP
        return i

    def ws_reset(self, specs, hw_mask=None):
        self.ws_specs = specs
        self.ws_issued = 0
        self.ws_used = 0
        self.ws_hw = hw_mask
        if hw_mask is not None:
            self.ws_hwlist = [i for i in range(len(specs)) if hw_mask[i]]
            self.ws_hwdma = 0
            self.stg = [(self.KT[:].rearrange("p a n -> p (a n)").bitcast(F32).rearrange("p (k n) -> p k n", k=8), "KT"),
                        (self.Vt[:].rearrange("p a n -> p (a n)").bitcast(F32).rearrange("p (k n) -> p k n", k=8), "Vt")]
            self.ws_hw_dma_upto(2)

    def ws_hw_dma_upto(self, n):
        n = min(n, len(self.ws_hwlist))
        while self.ws_hwdma < n:
            hn = self.ws_hwdma
            i = self.ws_hwlist[hn]
            stg, sk = self.stg[hn % 2]
            for (c0, ncol, src) in self.ws_specs[i]:
                self.dma(lambda e, stg=stg, c0=c0, ncol=ncol, src=src:
                         e.dma_start(out=stg[:, :, c0:c0 + ncol], in_=src.rearrange("(k p) n -> p k n", p=128)),
                         r=(), w=[sk], eng=("sync" if hn % 2 == 0 else "scalar"))
            self.ws_hwdma += 1

    def ws_issue_upto(self, n):
        n = min(n, len(self.ws_specs))
        while self.ws_issued < n:
            i = self.ws_issued
            slot = i % NRING
            if self.ws_hw is not None and self.ws_hw[i]:
                hn = self.ws_hwlist.index(i)
                stg, sk = self.stg[hn % 2]
                self.A(lambda e, slot=slot, stg=stg: e.copy(out=self.ring[slot][:], in_=stg), r=[sk], w=[("ring", slot)])
                self.ws_hw_dma_upto(hn + 3)
            else:
                for (c0, ncol, src) in self.ws_specs[i]:
                    self.dma(lambda e, slot=slot, c0=c0, ncol=ncol, src=src:
                             e.dma_start(out=self.ring[slot][:, :, c0:c0 + ncol],
                                         in_=src.rearrange("(k p) n -> p k n", p=128)),
                             r=(), w=[("ring", slot)], eng="gpsimd")
            self.ws_issued += 1

    def ws_next(self):
        i = self.ws_used
        self.ws_issue_upto(i + NRING - 1)
        self.ws_used += 1
        slot = i % NRING
        return self.ring[slot], ("ring", slot)

    def setup_consts(self):
        self.dma(lambda e: e.dma_start(out=self.ident_f[:], in_=self.c_ident), w=["ident_f"])
        self.dma(lambda e: e.dma_start(out=self.swapm[:], in_=self.c_swap), w=["swapm"])
        self.V(lambda e: e.tensor_copy(out=self.ident_b[:], in_=self.ident_f[:]), r=["ident_f"], w=["ident_b"])
        self.V(lambda e: e.memset(self.ones_d[:], 1.0 / D), w=["ones_d"])
        self.V(lambda e: e.memset(self.ones_h[:], 1.0 / DH), w=["ones_h"])
        self.V(lambda e: e.memset(self.ones_b[:], 1.0), w=["ones_b"])
        self.V(lambda e: e.memset(self.zpad[:], 0.0), w=["zpad"])
        for c0 in (0, PADC + S_LEN):
            self.dma(lambda e, c0=c0: e.dma_start(out=self.XTd[:, c0:c0 + PADC].rearrange("(k p) n -> p k n", p=128),
                                                  in_=self.zpad[:]), r=["zpad"], w=["XTd"], semkey=("st", "zpad"))

    def load_bc(self, i, src_row):
        self.dma(lambda e: e.dma_start(out=self.bc[i][:], in_=src_row.to_broadcast([128, D])), w=[("bc", i)])

    P_BIN = 0
    P_DWB = 44
    P_CLG = 52
    P_CLB = 60
    P_PWB = 68
    P_QG = 76
    P_KG = 77

    def load_layer_params(self, l):
        p = self.p
        rows = self.prm_rows
        segs = [(self.P_BIN, 44, p["b_in"][l]), (self.P_DWB, 8, p["conv_dw_b"][l]), (self.P_CLG, 8, p["conv_ln_g"][l]),
                (self.P_CLB, 8, p["conv_ln_b"][l]), (self.P_PWB, 8, p["conv_pw_b"][l]), (self.P_QG, 1, p["q_norm_g"][l]),
                (self.P_KG, 1, p["k_norm_g"][l])]
        for (r0, n, src) in segs:
            self.dma(lambda e, r0=r0, n=n, src=src: e.dma_start(out=rows[r0:r0 + n, :], in_=src), w=["prm_rows"])
        b = self.bank()
        pk = ("ps", b)
        self.T(lambda e, b=b: e.transpose(out=self.ps[:, b, 0:78], in_=rows[0:78, :], identity=self.ident_f[0:78, 0:78]),
               r=["prm_rows", "ident_f"], w=[pk])
        self.V(lambda e, b=b: e.tensor_copy(out=self.prm[:, 0:78], in_=self.ps[:, b, 0:78]), r=[pk], w=["prm"])
        self.dma(lambda e: e.dma_start(out=self.dw_rows[:], in_=p["conv_dw"][l]), w=["dw_rows"])
        for c in range(8):
            b = self.bank()
            pk = ("ps", b)
            self.T(lambda e, b=b, c=c: e.transpose(out=self.ps[:, b, 0:KC], in_=self.dw_rows[:, c * 128:(c + 1) * 128],
                                                   identity=self.ident_f[0:KC, 0:KC]),
                   r=["dw_rows", "ident_f"], w=[pk])
            self.V(lambda e, b=b, c=c: e.tensor_copy(out=self.dw_pp[:, c, :], in_=self.ps[:, b, 0:KC]), r=[pk], w=["dw_pp"])
        self.dma(lambda e: e.dma_start(out=self.bv_bc[:], in_=p["b_in"][l].rearrange("c p -> (c p)")[OFF_V:OFF_V + 256]
                                       .rearrange("(o n) -> o n", o=1).to_broadcast([128, 256])), w=["bv_bc"])
        self.dma(lambda e: e.dma_start(out=self.wr[:], in_=p["w_router"][l].rearrange("(k p) n -> p k n", p=128)), w=["wr"])

    def ln_rows(self, src, dst, gi, bi, eps=LN_EPS, srck=None, dstk=None, par=0):
        sm = self.small
        c0 = par * 4
        bst = self.bst[par]
        kb0, kb1, kmv, ksd, krs = (("lnst", par, i) for i in range(5))
        self.V(lambda e: e.bn_stats(out=bst[:, 0, :], in_=src[:, 0:512]), r=[srck], w=[kb0])
        self.V(lambda e: e.bn_stats(out=bst[:, 1, :], in_=src[:, 512:1024]), r=[srck], w=[kb1])
        self.V(lambda e: e.bn_aggr(out=sm[:, c0:c0 + 2], in_=bst[:].rearrange("p a b -> p (a b)")), r=[kb0, kb1], w=[kmv])
        self.A(lambda e: e.activation(out=sm[:, c0 + 2:c0 + 3], in_=sm[:, c0 + 1:c0 + 2], func=AF.Sqrt, bias=eps, scale=1.0), r=[kmv], w=[ksd])
        self.V(lambda e: e.reciprocal(out=sm[:, c0 + 3:c0 + 4], in_=sm[:, c0 + 2:c0 + 3]), r=[ksd], w=[krs])
        self.V(lambda e: e.tensor_scalar(out=dst, in0=src, scalar1=sm[:, c0:c0 + 1], scalar2=sm[:, c0 + 3:c0 + 4],
                                         op0=ALU.subtract, op1=ALU.mult), r=[srck, kmv, krs], w=[dstk])
        self.V(lambda e: e.tensor_tensor(out=dst, in0=dst, in1=self.bc[gi][:], op=ALU.mult), r=[dstk, ("bc", gi)], w=[dstk])
        self.V(lambda e: e.tensor_tensor(out=dst, in0=dst, in1=self.bc[bi][:], op=ALU.add), r=[dstk, ("bc", bi)], w=[dstk])

    def nr_stageA(self, pk, psap, bias_ap):
        i_raw, i_sq, i_rs = self.tmpk(), self.tmpk(), self.tmpk()
        stt = {"raw": self.tmp[i_raw], "sq": self.tmp[i_sq], "rs": self.tmp[i_rs],
               "kr": ("tmp", i_raw), "ks": ("tmp", i_sq), "krs": ("tmp", i_rs)}
        raw, sq = stt["raw"], stt["sq"]
        self.A(lambda e: e.activation(out=raw[:], in_=psap, func=AF.Identity, bias=bias_ap, scale=1.0), r=[pk, "prm"], w=[stt["kr"]])
        self.A(lambda e: e.activation(out=sq[:], in_=psap, func=AF.Square, bias=bias_ap, scale=1.0), r=[pk, "prm"], w=[stt["ks"]])
        return stt

    def nr_stageB(self, stt, g_ap):
        raw, sq, rs = stt["raw"], stt["sq"], stt["rs"]
        kr, ks, krs = stt["kr"], stt["ks"], stt["krs"]
        b2 = self.bank()
        pk2 = ("ps", b2)
        self.T(lambda e: e.matmul(self.ps[:, b2, :], lhsT=self.ones_h[:], rhs=sq[:], start=True, stop=True),
               r=[ks, "ones_h"], w=[pk2])
        self.A(lambda e: e.activation(out=rs[:], in_=self.ps[:, b2, :], func=AF.Sqrt, bias=RMS_EPS, scale=1.0), r=[pk2], w=[krs])
        self.V(lambda e: e.reciprocal(out=rs[:], in_=rs[:]), r=[krs], w=[krs])
        self.V(lambda e: e.scalar_tensor_tensor(out=raw[:], in0=raw[:], scalar=g_ap, in1=rs[:], op0=ALU.mult, op1=ALU.mult),
               r=[kr, krs, "prm"], w=[kr])

    def nr_stageC(self, stt, rp, out_ap, outk):
        raw, sq = stt["raw"], stt["sq"]
        kr, ks = stt["kr"], stt["ks"]
        b3 = self.bank()
        pk3 = ("ps", b3)
        self.T(lambda e: e.matmul(self.ps[:, b3, :], lhsT=self.swapm[:], rhs=raw[:], start=True, stop=True),
               r=[kr, "swapm"], w=[pk3])
        self.V(lambda e: e.tensor_tensor(out=sq[:], in0=self.ps[:, b3, :], in1=self.ropeS[rp][:], op=ALU.mult),
               r=[pk3, ("ropeS", rp)], w=[ks])
        self.G(lambda e: e.tensor_tensor(out=raw[:], in0=raw[:], in1=self.ropeC[rp][:], op=ALU.mult), r=[kr, ("ropeC", rp)], w=[kr])
        self.V(lambda e: e.tensor_tensor(out=out_ap, in0=raw[:], in1=sq[:], op=ALU.add), r=[kr, ks], w=[outk])

    def norm_rope_pipe(self, n, projA, g_ap, rp, outs):
        stts = [None] * n
        for step in range(n + 2):
            if step < n:
                pk, psap, bias_ap = projA(step)
                stts[step] = self.nr_stageA(pk, psap, bias_ap)
            if 0 <= step - 1 < n:
                self.nr_stageB(stts[step - 1], g_ap)
            if 0 <= step - 2 < n:
                self.nr_stageC(stts[step - 2], rp, *outs[step - 2])

    def load_rope(self, t, rp):
        t0 = t * TT
        self.dma(lambda e: e.dma_start(out=self.ropeC[rp][:], in_=self.c_ropeC[:, t0:t0 + TT]), w=[("ropeC", rp)])
        self.dma(lambda e: e.dma_start(out=self.ropeS[rp][:], in_=self.c_ropeS[:, t0:t0 + TT]), w=[("ropeS", rp)])

    def pass1(self, l, src_d, gi_src):
        p = self.p
        g_row, b_row = gi_src
        self.load_bc(0, g_row)
        self.load_bc(1, b_row)
        self.ws_reset([[(0, 512, p["w_in"][l][:, OFF_K:OFF_K + 512])]])
        wkv, wk_key = self.ws_next()
        for t in range(NT):
            t0 = t * TT
            rp = t % 2
            self.load_rope(t, rp)
            for st in range(4):
                r0 = t0 + st * 128
                rw = self.row[st % 2]
                rk = ("row", st % 2)
                self.dma(lambda e, rw=rw, r0=r0: e.dma_start(out=rw[:], in_=src_d[r0:r0 + 128, :]), r=["ACCd"] + [("sc", NE - 1, c) for c in range(4)], w=[rk])
                self.ln_rows(rw[:], rw[:], 0, 1, srck=rk, dstk=rk, par=st % 2)
                self.dma(lambda e, rw=rw, r0=r0: e.dma_start(out=self.Xd[r0:r0 + 128, :], in_=rw[:]), r=[rk], w=["Xd"], semkey=("st", rk))
                for half in range(2):
                    b = self.bank()
                    pk = ("ps", b)
                    for j in range(4):
                        k = half * 4 + j
                        self.T(lambda e, b=b, j=j, k=k, rw=rw: e.transpose(out=self.ps[:, b, j * 128:(j + 1) * 128],
                                                                          in_=rw[:, k * 128:(k + 1) * 128], identity=self.ident_f[:]),
                               r=[rk, "ident_f"], w=[pk])
                    self.A(lambda e, b=b, half=half, st=st: e.copy(
                        out=self.xT[:, half * 4:half * 4 + 4, st * 128:(st + 1) * 128],
                        in_=self.ps[:, b, :].rearrange("p (j n) -> p j n", j=4)), r=[pk], w=["xT"])
            self.dma(lambda e, t0=t0: e.dma_start(out=self.XTd[:, PADC + t0:PADC + t0 + TT].rearrange("(k p) n -> p k n", p=128),
                                                  in_=self.xT[:, :, 0:TT]), r=["xT"], w=["XTd"], semkey=("st", "xT"))
            def projK(h, t0=t0):
                b = self.bank()
                pk = ("ps", b)
                for k in range(8):
                    self.T(lambda e, b=b, k=k, h=h: e.matmul(self.ps[:, b, :], lhsT=wkv[:, k, h * 128:(h + 1) * 128],
                                                            rhs=self.xT[:, k, 0:TT], start=(k == 0), stop=(k == 7)),
                           r=[wk_key, "xT"], w=[pk])
                cb = self.P_BIN + (OFF_K // 128) + h
                return pk, self.ps[:, b, :], self.prm[:, cb:cb + 1]

            self.norm_rope_pipe(NKV, projK, self.prm[:, self.P_KG:self.P_KG + 1], rp,
                                [(self.KT[:, h, t0:t0 + TT], "KT") for h in range(NKV)])
            for st in range(4):
                b = self.bank()
                pk = ("ps", b)
                for k in range(8):
                    self.T(lambda e, b=b, k=k, st=st: e.matmul(self.ps[:, b, 0:256], lhsT=self.xT[:, k, st * 128:(st + 1) * 128],
                                                              rhs=wkv[:, k, 256:512], start=(k == 0), stop=(k == 7)),
                           r=[wk_key, "xT"], w=[pk])
                sg = t * 4 + st
                self.V(lambda e, b=b, sg=sg: e.tensor_tensor(out=self.Vt[:, sg, :], in0=self.ps[:, b, 0:256], in1=self.bv_bc[:],
                                                            op=ALU.add), r=[pk, "bv_bc"], w=["Vt"])

    def pass2_specs(self, l):
        p = self.p
        w_in = p["w_in"][l]
        one = []
        for i in range(4):
            one.append([(0, 256, w_in[:, i * 256:(i + 1) * 256]), (256, 256, w_in[:, D + i * 256:D + (i + 1) * 256])])
        for i in range(2):
            one.append([(0, 512, p["conv_pw_w"][l][:, i * 512:(i + 1) * 512])])
            one.append([(0, 512, w_in[:, OFF_GC + i * 512:OFF_GC + (i + 1) * 512])])
        for i in range(2):
            one.append([(0, 512, w_in[:, OFF_Q + i * 512:OFF_Q + (i + 1) * 512])])
        for i in range(2):
            one.append([(0, 512, w_in[:, OFF_GA + i * 512:OFF_GA + (i + 1) * 512])])
            one.append([(0, 512, p["w_o"][l][:, i * 512:(i + 1) * 512])])
        for i in range(2):
            one.append([(0, 512, p["w_out"][l][:, i * 512:(i + 1) * 512])])
        return one

    def pass2(self, l):
        p = self.p
        prm = self.prm
        ps = self.ps
        self.load_bc(0, p["ln1_g"][l:l + 1, :])
        self.load_bc(1, p["ln1_b"][l:l + 1, :])
        self.load_bc(2, p["b_out"][l:l + 1, :])
        specs = []
        for t in range(NT):
            specs += self.pass2_specs(l)
        self.ws_reset(specs)
        for t in range(NT):
            t0 = t * TT
            rp = t % 2
            self.load_rope(t, rp)
            self.dma(lambda e, t0=t0: e.dma_start(out=self.xT[:], in_=self.XTd[:, t0:t0 + XW].rearrange("(k p) n -> p k n", p=128)),
                     r=["XTd"], w=["xT"])
            for i in range(4):
                wp, wk = self.ws_next()
                for cc in range(2):
                    c = i * 2 + cc
                    bA, bB, bC = self.bank(), self.bank(), self.bank()
                    kA, kB, kC = ("ps", bA), ("ps", bB), ("ps", bC)
                    for (col0, bm, bh, hc) in ((cc * 128, bA, bB, 0), (256 + cc * 128, bC, bB, 32)):
                        for k in range(8):
                            self.T(lambda e, k=k, col0=col0, bm=bm, wp=wp: e.matmul(ps[:, bm, :], lhsT=wp[:, k, col0:col0 + 128],
                                                                            rhs=self.xT[:, k, 0:512], start=(k == 0), stop=(k == 7)),
                                   r=[wk, "xT"], w=[("ps", bm)])
                        for k in range(8):
                            self.T(lambda e, k=k, col0=col0, bh=bh, hc=hc, wp=wp: e.matmul(ps[:, bh, hc:hc + 32], lhsT=wp[:, k, col0:col0 + 128],
                                                                                    rhs=self.xT[:, k, 512:544], start=(k == 0), stop=(k == 7)),
                                   r=[wk, "xT"], w=[("ps", bh)])
                    cv = self.P_BIN + c
                    cg = self.P_BIN + 8 + c
                    ti = self.tmpk()
                    sg = self.tmp[ti]
                    kt = ("tmp", ti)
                    ti2 = self.tmpk()
                    sg2 = self.tmp[ti2]
                    kt2 = ("tmp", ti2)
                    self.A(lambda e, bC=bC, cg=cg, sg=sg: e.activation(out=sg[:], in_=ps[:, bC, :], func=AF.Sigmoid,
                                                                      bias=prm[:, cg:cg + 1], scale=1.0), r=[kC, "prm"], w=[kt])
                    self.A(lambda e, bB=bB, cg=cg, sg2=sg2: e.activation(out=sg2[:, 0:32], in_=ps[:, bB, 32:64], func=AF.Sigmoid,
                                                                        bias=prm[:, cg:cg + 1], scale=1.0), r=[kB, "prm"], w=[kt2])
                    uk = ("uT", c)
                    self.V(lambda e, bA=bA, cv=cv, sg=sg, c=c: e.scalar_tensor_tensor(out=self.uT[:, c, 0:512], in0=ps[:, bA, :],
                                                                                     scalar=prm[:, cv:cv + 1], in1=sg[:],
                                                                                     op0=ALU.add, op1=ALU.mult), r=[kA, kt, "prm"], w=[uk])
                    self.V(lambda e, bB=bB, cv=cv, sg2=sg2, c=c: e.scalar_tensor_tensor(out=self.uT[:, c, 512:544], in0=ps[:, bB, 0:32],
                                                                                       scalar=prm[:, cv:cv + 1], in1=sg2[:, 0:32],
                                                                                       op0=ALU.add, op1=ALU.mult), r=[kB, kt2, "prm"], w=[uk])
                    if t == 0:
                        self.V(lambda e, c=c: e.memset(self.uT[:, c, 0:16], 0.0), w=[uk])
                    if t == NT - 1:
                        self.V(lambda e, c=c: e.memset(self.uT[:, c, 528:544], 0.0), w=[uk])
            bM = self.bank(pin=True)
            bQ = self.bank(pin=True)
            kM, kQ = ("ps", bM), ("ps", bQ)
            for c in range(8):
                b = self.bank()
                pk = ("ps", b)
                for k in range(KC):
                    di = (c * KC + k) % 8
                    dk = ("dg", di)
                    self.V(lambda e, di=di, c=c, k=k: e.tensor_scalar(out=self.dg[di][:], in0=self.ident_b[:],
                                                                     scalar1=self.dw_pp[:, c, k:k + 1], scalar2=None, op0=ALU.mult),
                           r=["ident_b", "dw_pp"], w=[dk])
                    self.T(lambda e, b=b, di=di, c=c, k=k: e.matmul(ps[:, b, :], lhsT=self.dg[di][:], rhs=self.uT[:, c, k + 1:k + 1 + 512],
                                                                   start=(k == 0), stop=(k == KC - 1)), r=[dk, ("uT", c)], w=[pk])
                cdb = self.P_DWB + c
                self.A(lambda e, b=b, c=c, cdb=cdb: e.activation(out=self.Bf[:, c, :], in_=ps[:, b, :], func=AF.Identity,
                                                                bias=prm[:, cdb:cdb + 1], scale=1.0), r=[pk, "prm"], w=[("Bf", c)])
                ti = self.tmpk()
                sq = self.tmp[ti]
                kt = ("tmp", ti)
                self.A(lambda e, b=b, cdb=cdb, sq=sq: e.activation(out=sq[:], in_=ps[:, b, :], func=AF.Square,
                                                                  bias=prm[:, cdb:cdb + 1], scale=1.0), r=[pk, "prm"], w=[kt])
                self.T(lambda e, c=c, bM=bM: e.matmul(ps[:, bM, :], lhsT=self.ones_d[:], rhs=self.Bf[:, c, :], start=(c == 0), stop=(c == 7)),
                       r=[("Bf", c), "ones_d"], w=[kM])
                self.T(lambda e, c=c, sq=sq, bQ=bQ: e.matmul(ps[:, bQ, :], lhsT=self.ones_d[:], rhs=sq[:], start=(c == 0), stop=(c == 7)),
                       r=[kt, "ones_d"], w=[kQ])
            im, iv = self.tmpk(), self.tmpk()
            mean, rstd = self.tmp[im], self.tmp[iv]
            kmn, krs = ("tmp", im), ("tmp", iv)
            self.A(lambda e, mean=mean, bM=bM: e.copy(out=mean[:], in_=ps[:, bM, :]), r=[kM], w=[kmn])
            self.V(lambda e, mean=mean, rstd=rstd, bM=bM: e.tensor_tensor(out=rstd[:], in0=mean[:], in1=ps[:, bM, :], op=ALU.mult), r=[kmn, kM], w=[krs])
            self.V(lambda e, rstd=rstd, bQ=bQ: e.tensor_tensor(out=rstd[:], in0=ps[:, bQ, :], in1=rstd[:], op=ALU.subtract), r=[kQ, krs], w=[krs])
            self.A(lambda e, rstd=rstd: e.activation(out=rstd[:], in_=rstd[:], func=AF.Sqrt, bias=LN_EPS, scale=1.0), r=[krs], w=[krs])
            self.V(lambda e, rstd=rstd: e.reciprocal(out=rstd[:], in_=rstd[:]), r=[krs], w=[krs])
            self.unpin(bM)
            self.unpin(bQ)
            for c in range(8):
                kb = ("Bf", c)
                self.V(lambda e, c=c, mean=mean: e.tensor_tensor(out=self.Bf[:, c, :], in0=self.Bf[:, c, :], in1=mean[:], op=ALU.subtract),
                       r=[kb, kmn], w=[kb])
                self.V(lambda e, c=c, rstd=rstd: e.tensor_tensor(out=self.Bf[:, c, :], in0=self.Bf[:, c, :], in1=rstd[:], op=ALU.mult),
                       r=[kb, krs], w=[kb])
                cg, cb2 = self.P_CLG + c, self.P_CLB + c
                self.A(lambda e, c=c, cg=cg, cb2=cb2: e.activation(out=self.Cb[:, c, :], in_=self.Bf[:, c, :], func=AF.Silu,
                                                                  bias=prm[:, cb2:cb2 + 1], scale=prm[:, cg:cg + 1]),
                       r=[kb, "prm"], w=[("Cb", c)])
            if getattr(self, 'dbg_stop', None) == 'conv':
                return
            for i in range(2):
                wpw, wpwk = self.ws_next()
                wgc, wgck = self.ws_next()
                for jj in range(4):
                    j = i * 4 + jj
                    bg = self.bank()
                    kg_ = ("ps", bg)
                    for k in range(8):
                        self.T(lambda e, bg=bg, k=k, jj=jj, wgc=wgc: e.matmul(ps[:, bg, :], lhsT=wgc[:, k, jj * 128:(jj + 1) * 128],
                                                                             rhs=self.xT[:, k, PADC:PADC + TT], start=(k == 0), stop=(k == 7)),
                               r=[wgck, "xT"], w=[kg_])
                    cgc = self.P_BIN + OFF_GC // 128 + j
                    ti = self.tmpk()
                    gs = self.tmp[ti]
                    kt = ("tmp", ti)
                    self.A(lambda e, bg=bg, cgc=cgc, gs=gs: e.activation(out=gs[:], in_=ps[:, bg, :], func=AF.Sigmoid,
                                                                        bias=prm[:, cgc:cgc + 1], scale=1.0), r=[kg_, "prm"], w=[kt])
                    by = self.bank()
                    ky = ("ps", by)
                    for c in range(8):
                        self.T(lambda e, by=by, c=c, jj=jj, wpw=wpw: e.matmul(ps[:, by, :], lhsT=wpw[:, c, jj * 128:(jj + 1) * 128],
                                                                             rhs=self.Cb[:, c, :], start=(c == 0), stop=(c == 7)),
                               r=[wpwk, ("Cb", c)], w=[ky])
                    cpb = self.P_PWB + j
                    self.V(lambda e, by=by, j=j, cpb=cpb, gs=gs: e.scalar_tensor_tensor(out=self.Bf[:, j, :], in0=ps[:, by, :],
                                                                                      scalar=prm[:, cpb:cpb + 1], in1=gs[:],
                                                                                      op0=ALU.add, op1=ALU.mult),
                           r=[ky, kt, "prm"], w=[("Bf", j)])
            if getattr(self, 'dbg_stop', None) == 'pw':
                return
            qw = {}

            def projQ(h):
                i, jj = divmod(h, 4)
                if jj == 0:
                    qw["wp"], qw["wk"] = self.ws_next()
                wp, wk = qw["wp"], qw["wk"]
                b = self.bank()
                pk = ("ps", b)
                for k in range(8):
                    self.T(lambda e, b=b, k=k, jj=jj, wp=wp: e.matmul(ps[:, b, :], lhsT=wp[:, k, jj * 128:(jj + 1) * 128],
                                                                     rhs=self.xT[:, k, PADC:PADC + TT], start=(k == 0), stop=(k == 7)),
                           r=[wk, "xT"], w=[pk])
                cq = self.P_BIN + OFF_Q // 128 + h
                return pk, ps[:, b, :], prm[:, cq:cq + 1]

            self.norm_rope_pipe(NH, projQ, prm[:, self.P_QG:self.P_QG + 1], rp,
                                [(self.Db[:, h, :], ("Db", h)) for h in range(NH)])
            NPAIR = S_LEN // 256
            units = [(g, qs) for g in range(NKV) for qs in range(4)]

            def emit_S(f):
                n, j = divmod(f, NPAIR)
                g, qs = units[n]
                sp = (0, 1) if f % 2 == 0 else (2, 3)
                qk = [("Db", g * 4 + hh) for hh in range(4)]
                for u in range(2):
                    kc = 2 * j + u
                    self.T(lambda e, b=sp[u], g=g, kc=kc, qs=qs: e.matmul(
                        ps[:, b, :].rearrange("p (h q) -> p h q", h=4), lhsT=self.KT[:, g, kc * 128:(kc + 1) * 128],
                        rhs=self.Db[:, g * 4:g * 4 + 4, qs * 128:(qs + 1) * 128], start=True, stop=True),
                        r=["KT"] + qk, w=[("ps", sp[u])])

            NF = len(units) * NPAIR
            emit_S(0)
            for f in range(NF):
                n, j = divmod(f, NPAIR)
                g, qs = units[n]
                if f + 1 < NF:
                    emit_S(f + 1)
                sp = (0, 1) if f % 2 == 0 else (2, 3)
                bO, bS = (4, 5) if n % 2 == 0 else (6, 7)
                kO, kS = ("ps", bO), ("ps", bS)
                pi = f % 3
                pt = self.PT[pi]
                kp = ("PT", pi)
                self.A(lambda e, s0=sp[0], pt=pt: e.activation(out=pt[:].rearrange("p (u n) -> p u n", u=2), in_=ps[:, s0:s0 + 2, :],
                                                               func=AF.Exp, scale=ATT_SCALE),
                       r=[("ps", sp[0]), ("ps", sp[1])], w=[kp])
                for u in range(2):
                    kc = 2 * j + u
                    first = (j == 0 and u == 0)
                    last = (j == NPAIR - 1 and u == 1)
                    self.T(lambda e, g=g, kc=kc, pt=pt, bO=bO, u=u, first=first, last=last: e.matmul(
                        ps[:, bO, :], lhsT=self.Vt[:, kc, g * 128:(g + 1) * 128], rhs=pt[:, u * 512:(u + 1) * 512],
                        start=first, stop=last), r=["Vt", kp], w=[kO])
                    self.T(lambda e, pt=pt, bS=bS, u=u, first=first, last=last: e.matmul(
                        ps[:, bS, :], lhsT=self.ones_b[:], rhs=pt[:, u * 512:(u + 1) * 512],
                        start=first, stop=last), r=["ones_b", kp], w=[kS])
                if j == NPAIR - 1:
                    ti = self.tmpk()
                    rec = self.tmp[ti]
                    kt = ("tmp", ti)
                    self.V(lambda e, rec=rec, bS=bS: e.reciprocal(out=rec[:], in_=ps[:, bS, :]), r=[kS], w=[kt])
                    ek = [("Eb", g * 4 + hh) for hh in range(4)]
                    self.V(lambda e, g=g, qs=qs, rec=rec, bO=bO: e.tensor_tensor(
                        out=self.Eb[:, g * 4:g * 4 + 4, qs * 128:(qs + 1) * 128],
                        in0=ps[:, bO, :].rearrange("p (h q) -> p h q", h=4), in1=rec[:].rearrange("p (h q) -> p h q", h=4),
                        op=ALU.mult), r=[kO, kt] + ek, w=ek)
            for i in range(2):
                wga, wgak = self.ws_next()
                wwo, wwok = self.ws_next()
                for jj in range(4):
                    j = i * 4 + jj
                    bg = self.bank()
                    kg_ = ("ps", bg)
                    for k in range(8):
                        self.T(lambda e, bg=bg, k=k, jj=jj, wga=wga: e.matmul(ps[:, bg, :], lhsT=wga[:, k, jj * 128:(jj + 1) * 128],
                                                                             rhs=self.xT[:, k, PADC:PADC + TT], start=(k == 0), stop=(k == 7)),
                               r=[wgak, "xT"], w=[kg_])
                    cga = self.P_BIN + OFF_GA // 128 + j
                    ti = self.tmpk()
                    gs = self.tmp[ti]
                    kt = ("tmp", ti)
                    self.A(lambda e, bg=bg, cga=cga, gs=gs: e.activation(out=gs[:], in_=ps[:, bg, :], func=AF.Sigmoid,
                                                                        bias=prm[:, cga:cga + 1], scale=1.0), r=[kg_, "prm"], w=[kt])
                    by = self.bank()
                    ky = ("ps", by)
                    for h in range(8):
                        self.T(lambda e, by=by, h=h, jj=jj, wwo=wwo: e.matmul(ps[:, by, :], lhsT=wwo[:, h, jj * 128:(jj + 1) * 128],
                                                                             rhs=self.Eb[:, h, :], start=(h == 0), stop=(h == 7)),
                               r=[wwok, ("Eb", h)], w=[ky])
                    self.V(lambda e, by=by, gs=gs: e.tensor_tensor(out=gs[:], in0=gs[:], in1=ps[:, by, :], op=ALU.mult),
                           r=[kt, ky], w=[kt])
                    self.V(lambda e, j=j, gs=gs: e.tensor_tensor(out=self.Cb[:, j, :], in0=self.Bf[:, j, :], in1=gs[:], op=ALU.add),
                           r=[("Bf", j), kt], w=[("Cb", j)])
            wpa, wka = self.ws_next()
            wpb, wkb = self.ws_next()
            def wo_p1(st):
                r0 = t0 + st * 128
                sg = t * 4 + st
                par = st % 2
                tr, ar = self.row[par], self.row[2]
                ktr = ("row", par)
                bb = []
                for (wp, wk) in ((wpa, wka), (wpb, wkb)):
                    b = self.bank()
                    bb.append(b)
                    for k in range(8):
                        self.T(lambda e, b=b, k=k, st=st, wp=wp: e.matmul(ps[:, b, :], lhsT=self.Cb[:, k, st * 128:(st + 1) * 128],
                                                                         rhs=wp[:, k, :], start=(k == 0), stop=(k == 7)),
                               r=[wk, ("Cb", k)], w=[("ps", b)])
                for hf in range(2):
                    b = bb[hf]
                    self.V(lambda e, b=b, hf=hf, tr=tr: e.scalar_tensor_tensor(out=tr[:, hf * 512:(hf + 1) * 512], in0=tr[:, hf * 512:(hf + 1) * 512],
                                                                              scalar=ALPHA, in1=ps[:, b, :], op0=ALU.mult, op1=ALU.add),
                           r=[ktr, ("ps", b)], w=[ktr])
                self.V(lambda e, tr=tr: e.tensor_tensor(out=tr[:], in0=tr[:], in1=self.bc[2][:], op=ALU.add), r=[ktr, ("bc", 2)], w=[ktr])
                self.ln_rows(tr[:], tr[:], 0, 1, srck=ktr, dstk=ktr, par=par)

            def wo_p2(st):
                r0 = t0 + st * 128
                sg = t * 4 + st
                par = st % 2
                tr, ar = self.row[par], self.row[2]
                ktr = ("row", par)
                self.A(lambda e, tr=tr: e.copy(out=self.rowb[:], in_=tr[:]), r=[ktr], w=["rowb"])
                self.dma(lambda e, r0=r0: e.dma_start(out=self.X1d[r0:r0 + 128, :], in_=self.rowb[:]), r=["rowb"], w=["X1d"], semkey=("st", "rowb"))
                self.A(lambda e, tr=tr: e.mul(ar[:], tr[:], ALPHA), r=[ktr], w=[("row", 2)])
                self.dma(lambda e, r0=r0: e.dma_start(out=self.ACCd[r0:r0 + 128, :], in_=ar[:]), r=[("row", 2)] + [("sc", NE - 1, c) for c in range(4)], w=["ACCd"], semkey=("st", "row2"))
                if "x1" in self.dbg and l == 0:
                    self.dma(lambda e, r0=r0, tr=tr: e.dma_start(out=self.dbg_t["x1"][r0:r0 + 128, :], in_=tr[:]), r=[ktr], w=["dbg_x1"],
                             semkey=("st", "dbgx1"))
                for half in range(2):
                    b = self.bank()
                    pk = ("ps", b)
                    for jx in range(4):
                        k = half * 4 + jx
                        self.T(lambda e, b=b, jx=jx, k=k, tr=tr: e.transpose(out=ps[:, b, jx * 128:(jx + 1) * 128],
                                                                             in_=tr[:, k * 128:(k + 1) * 128], identity=self.ident_f[:]),
                               r=[ktr, "ident_f"], w=[pk])
                    self.A(lambda e, b=b, half=half: e.copy(out=self.x1T[:, half * 4:half * 4 + 4, :],
                                                           in_=ps[:, b, :].rearrange("p (j n) -> p j n", j=4)), r=[pk], w=["x1T"])
                b = self.bank()
                pk = ("ps", b)
                for k in range(8):
                    self.T(lambda e, b=b, k=k: e.matmul(ps[:, b, 0:NE], lhsT=self.x1T[:, k, :], rhs=self.wr[:, k, :],
                                                        start=(k == 0), stop=(k == 7)), r=["x1T", "wr"], w=[pk])
                sm = self.small
                c0 = 8 + par * 4
                e0 = 16 + par * 16
                kq = [("rt", par, i) for i in range(5)]
                self.V(lambda e, b=b, c0=c0: e.reduce_max(out=sm[:, c0:c0 + 1], in_=ps[:, b, 0:NE], axis=mybir.AxisListType.X), r=[pk], w=[kq[0]])
                self.V(lambda e, c0=c0: e.tensor_scalar(out=sm[:, c0 + 1:c0 + 2], in0=sm[:, c0:c0 + 1], scalar1=-1.0, scalar2=None, op0=ALU.mult),
                       r=[kq[0]], w=[kq[1]])
                self.A(lambda e, b=b, c0=c0, e0=e0: e.activation(out=sm[:, e0:e0 + NE], in_=ps[:, b, 0:NE], func=AF.Exp, bias=sm[:, c0 + 1:c0 + 2],
                                                                scale=1.0, accum_out=sm[:, c0 + 2:c0 + 3]), r=[pk, kq[1]], w=[kq[2], kq[3]])
                self.V(lambda e, c0=c0: e.reciprocal(out=sm[:, c0 + 3:c0 + 4], in_=sm[:, c0 + 2:c0 + 3]), r=[kq[3]], w=[kq[4]])
                self.V(lambda e, sg=sg, c0=c0, e0=e0: e.tensor_scalar(out=self.aff[:, sg, :], in0=sm[:, e0:e0 + NE], scalar1=sm[:, c0 + 3:c0 + 4],
                                                                     scalar2=None, op0=ALU.mult), r=[kq[2], kq[4]], w=["aff"])

            def wo_load(st):
                r0 = t0 + st * 128
                tr = self.row[st % 2]
                self.dma(lambda e, r0=r0, tr=tr: e.dma_start(out=tr[:], in_=self.Xd[r0:r0 + 128, :]), r=["Xd"], w=[("row", st % 2)])

            wo_load(0)
            wo_load(1)
            for step in range(5):
                if step < 4:
                    wo_p1(step)
                if step >= 1:
                    wo_p2(step - 1)
                    if step + 1 < 4:
                        wo_load(step + 1)

    def topk(self, l):
        ps = self.ps
        work = self.Bf[0:NE, :, :].rearrange("p a b -> p (a b)")
        wk = [("Bf", c) for c in range(8)]
        for bi in range(8):
            b = self.bank()
            pk = ("ps", b)
            for j in range(4):
                sg = bi * 4 + j
                self.T(lambda e, b=b, j=j, sg=sg: e.transpose(out=ps[0:NE, b, j * 128:(j + 1) * 128], in_=self.aff[:, sg, :],
                                                              identity=self.ident_f[:]), r=["aff", "ident_f"], w=[pk])
            self.V(lambda e, b=b, bi=bi: e.tensor_copy(out=work[:, bi * 512:(bi + 1) * 512], in_=ps[0:NE, b, :]), r=[pk], w=[wk[bi]])
        for r in range(CAP // 8):
            self.V(lambda e, r=r: e.max(out=self.topv[:, r * 8:(r + 1) * 8], in_=work), r=wk, w=["topv"])
            self.V(lambda e, r=r: e.max_index(out=self.topi[:, r * 8:(r + 1) * 8], in_max=self.topv[:, r * 8:(r + 1) * 8], in_values=work),
                   r=wk + ["topv"], w=["topi"])
            self.V(lambda e, r=r: e.match_replace(out=work, in_to_replace=self.topv[:, r * 8:(r + 1) * 8], in_values=work, imm_value=-1.0),
                   r=wk + ["topv"], w=wk)
        self.V(lambda e: e.tensor_copy(out=self.topif[:], in_=self.topi[:]), r=["topi"], w=["topif"])
        for (src, srck, dst, dstk) in ((self.topv, "topv", self.slot_g, "slot_g"), (self.topif, "topif", self.slot_i, "slot_i")):
            b = self.bank()
            pk = ("ps", b)
            for col in range(4):
                self.T(lambda e, b=b, col=col, src=src: e.transpose(out=ps[:, b, col * NE:(col + 1) * NE], in_=src[:, col * 128:(col + 1) * 128],
                                                                    identity=self.ident_f[0:NE, 0:NE]), r=[srck, "ident_f"], w=[pk])
            self.V(lambda e, b=b, dst=dst: e.tensor_copy(out=dst[:].rearrange("p e c -> p c e"),
                                                         in_=ps[:, b, 0:4 * NE].rearrange("p (c e) -> p c e", c=4)), r=[pk], w=[dstk])

    def moe_specs(self, l):
        p = self.p
        specs = []
        for ex in range(NE):
            wg, wu, wd = p["w_gate"][l, ex], p["w_up"][l, ex], p["w_down"][l, ex]
            for i in range(4):
                specs.append([(0, 512, wg[:, i * 512:(i + 1) * 512])])
                specs.append([(0, 512, wu[:, i * 512:(i + 1) * 512])])
            for ch in range(2):
                for rh in range(2):
                    specs.append([(0, 512, wd[rh * 1024:(rh + 1) * 1024, ch * 512:(ch + 1) * 512])])
        return specs

    def moe(self, l):
        ps = self.ps
        specs = self.moe_specs(l)
        self.ws_reset(specs, hw_mask=[(i % 3) != 2 for i in range(len(specs))])
        xeb = self.Db
        xebv = xeb[:].rearrange("p (c a) n -> p c (a n)", c=4)

        def gather(ex):
            for col in range(4):
                self.dma(lambda e, ex=ex, col=col: e.indirect_dma_start(
                    out=xebv[:, col, :], out_offset=None, in_=self.X1d[:, :],
                    in_offset=bass.IndirectOffsetOnAxis(ap=self.slot_i[:, ex, col:col + 1], axis=0)),
                    r=["X1d", "slot_i"], w=[("Db", 2 * col), ("Db", 2 * col + 1)], eng="gpsimd", semkey=("gather", col))

        def transposes(ex):
            for k in range(8):
                b = self.bank()
                pk = ("ps", b)
                pv = ps[:, b, :].bitcast(BF16)
                for col in range(4):
                    self.T(lambda e, pv=pv, col=col, k=k: e.transpose(out=pv[:, col * 128:(col + 1) * 128],
                                                                      in_=xebv[:, col, k * 128:(k + 1) * 128], identity=self.ident_b[:]),
                           r=[("Db", 2 * col), ("Db", 2 * col + 1), "ident_b"], w=[pk])
                self.A(lambda e, pv=pv, k=k: e.copy(out=self.Eb[:, k, :], in_=pv[:, 0:512]), r=[pk], w=[("Eb", k)])

        gather(0)
        transposes(0)
        gather(1)
        for ex in range(NE):
            ek = [("Eb", k) for k in range(8)]
            hid = [self.uT[:, f, 0:512] for f in range(8)] + [self.Cb[:, f, :] for f in range(8)]
            hk = [("uT", f) for f in range(8)] + [("Cb", f) for f in range(8)]
            for i in range(4):
                wg, wgk = self.ws_next()
                wu, wuk = self.ws_next()
                for jj in range(4):
                    f = i * 4 + jj
                    bg, bu = self.bank(), self.bank()
                    for (w_, wk_, b_) in ((wg, wgk, bg), (wu, wuk, bu)):
                        for k in range(8):
                            self.T(lambda e, w_=w_, b_=b_, k=k, jj=jj: e.matmul(ps[:, b_, :], lhsT=w_[:, k, jj * 128:(jj + 1) * 128],
                                                                               rhs=self.Eb[:, k, :], start=(k == 0), stop=(k == 7)),
                                   r=[wk_, ("Eb", k)], w=[("ps", b_)])
                    ti = self.tmpk()
                    sg = self.tmp[ti]
                    kt = ("tmp", ti)
                    self.A(lambda e, bg=bg, sg=sg: e.activation(out=sg[:], in_=ps[:, bg, :], func=AF.Silu), r=[("ps", bg)], w=[kt])
                    self.V(lambda e, bu=bu, sg=sg, f=f: e.tensor_tensor(out=hid[f], in0=sg[:], in1=ps[:, bu, :], op=ALU.mult),
                           r=[kt, ("ps", bu)], w=[hk[f]])
            if ex + 1 < NE:
                transposes(ex + 1)
            if ex + 2 < NE:
                gather(ex + 2)
            ye = self.Bf
            yev = ye[:].rearrange("p (c a) n -> p c (a n)", c=4)
            yk = [("Bf", c) for c in range(8)]
            for ch in range(2):
                wd0, wdk0 = self.ws_next()
                wd1, wdk1 = self.ws_next()
                for col in range(4):
                    b = self.bank()
                    pk = ("ps", b)
                    for f in range(16):
                        w_, wk_ = (wd0, wdk0) if f < 8 else (wd1, wdk1)
                        self.T(lambda e, b=b, f=f, col=col, w_=w_: e.matmul(ps[:, b, :], lhsT=hid[f][:, col * 128:(col + 1) * 128],
                                                                           rhs=w_[:, f % 8, :], start=(f == 0), stop=(f == 15)),
                               r=[wk_, hk[f]], w=[pk])
                    self.V(lambda e, b=b, col=col, ch=ch, ex=ex: e.tensor_scalar(
                        out=yev[:, col, ch * 512:(ch + 1) * 512], in0=ps[:, b, :], scalar1=self.slot_g[:, ex, col:col + 1], scalar2=None,
                        op0=ALU.mult), r=[pk, "slot_g"], w=[("Bf", col * 2), ("Bf", col * 2 + 1)])
            for col in range(4):
                self.dma(lambda e, ex=ex, col=col: e.indirect_dma_start(
                    out=self.ACCd[:, :], out_offset=bass.IndirectOffsetOnAxis(ap=self.slot_i[:, ex, col:col + 1], axis=0),
                    in_=yev[:, col, :], in_offset=None, compute_op=ALU.add),
                    r=yk + ["slot_i"] + (["ACCd"] if ex == 0 else [("sc", ex - 1, c) for c in range(4)]),
                    w=[("sc", ex, col)], eng="gpsimd", semkey=("scat", col))

    def final(self, l):
        p = self.p
        self.load_bc(0, p["ln2_g"][l:l + 1, :])
        self.load_bc(1, p["ln2_b"][l:l + 1, :])
        for sg in range(S_LEN // 128):
            r0 = sg * 128
            rw = self.row[sg % 2]
            rk = ("row", sg % 2)
            self.dma(lambda e, rw=rw, r0=r0: e.dma_start(out=rw[:], in_=self.ACCd[r0:r0 + 128, :]), r=["ACCd"] + [("sc", NE - 1, c) for c in range(4)], w=[rk])
            self.ln_rows(rw[:], rw[:], 0, 1, srck=rk, dstk=rk, par=sg % 2)
            self.dma(lambda e, rw=rw, r0=r0: e.dma_start(out=self.out[r0:r0 + 128, :], in_=rw[:]), r=[rk], w=["out"], semkey=("st", rk))

    def build(self, stop_after=None):
        with ExitStack() as st:
            self.st = st
            self.declare()
            for nm, shp, dt in self.dbg_decl:
                self.dbg_out(nm, shp, dt)
            self.alloc()
            self.S = Sched(self.nc, st)
            self.setup_consts()
            p = self.p
            for li, l in enumerate(self.layers):
                self.load_layer_params(li)
                if l == 0:
                    src, gb = self.x_in, (p["ln0_g"], p["ln0_b"])
                else:
                    src, gb = self.ACCd, (p["ln2_g"][li - 1:li, :], p["ln2_b"][li - 1:li, :])
                self.pass1(li, src, gb)
                if stop_after == "pass1":
                    break
                self.pass2(li)
                if stop_after == "pass2":
                    break
                self.topk(li)
                if stop_after == "topk":
                    break
                self.moe(li)
            if stop_after is None:
                self.final(len(self.layers) - 1)
                fin = ["out"]
            else:
                fin = []
            self.debug_dumps(stop_after)
            fin += ["dbg_" + n for n in self.dbg_written]
            self.S.finish("sync", fin)
            self.S.emit()
        return self.nc

    dbg_decl = ()
    dbg_written = ()

    def debug_dumps(self, stop_after):
        pass


def rope_tables():
    t = np.arange(S_LEN)
    row = (t // 64).astype(np.float32)
    col = (t % 64).astype(np.float32)
    axis_dim = DH // 2
    freqs = (1.0 / (np.float32(10000.0) ** (np.arange(0, axis_dim, 2, dtype=np.float32) / np.float32(axis_dim)))).astype(np.float32)
    ang = np.concatenate([row[:, None] * freqs[None], col[:, None] * freqs[None]], axis=-1).astype(np.float32)
    cos = np.cos(ang).astype(np.float32)
    sin = np.sin(ang).astype(np.float32)
    C = np.repeat(cos.T, 2, axis=0)
    Sg = np.repeat(sin.T, 2, axis=0)
    sign = np.where(np.arange(DH) % 2 == 0, -1.0, 1.0).astype(np.float32)[:, None]
    return np.ascontiguousarray(C), np.ascontiguousarray(Sg * sign)


def const_inputs():
    ident = np.eye(128, dtype=np.float32)
    swap = np.zeros((128, 128), np.float32)
    idx = np.arange(128)
    swap[idx, idx ^ 1] = 1.0
    C, Sg = rope_tables()
    return {"c_ident": ident, "c_swap": swap, "c_ropeC": C, "c_ropeS": Sg}


def core_inputs(inputs, b, layers=None):
    m = {"x": np.ascontiguousarray(inputs["x"][b])}
    if layers is not None:
        inputs = dict(inputs)
        for nm in inputs:
            if nm not in ("x", "ln0_g", "ln0_b"):
                inputs[nm] = np.ascontiguousarray(inputs[nm][list(layers)])
    L = NL if layers is None else len(layers)
    m["ln0_g"] = inputs["ln0_g"].reshape(1, D)
    m["ln0_b"] = inputs["ln0_b"].reshape(1, D)
    m["b_in"] = inputs["b_in"].reshape(L, N_IN // 128, 128)
    for nm in ("conv_dw_b", "conv_ln_g", "conv_ln_b", "conv_pw_b"):
        m[nm] = inputs[nm].reshape(L, 8, 128)
    m["q_norm_g"] = inputs["q_norm_g"].reshape(L, 1, DH)
    m["k_norm_g"] = inputs["k_norm_g"].reshape(L, 1, DH)
    for nm in ("w_in", "conv_dw", "conv_pw_w", "w_o", "w_out", "b_out", "ln1_g", "ln1_b", "w_router", "w_gate", "w_up",
               "w_down", "ln2_g", "ln2_b"):
        m[nm] = inputs[nm]
    m.update(const_inputs())
    return m


def kernel(**inputs):
    inputs = {k: np.asarray(v) for k, v in inputs.items()}
    mk = MK()
    nc = mk.build()
    n = 4
    in_maps = [core_inputs(inputs, c) for c in range(n)]
    res = run_bass_kernel_spmd(nc, in_maps, core_ids=list(range(n)))
    out = np.stack([np.asarray(res.results[c]["out"]) for c in range(n)], axis=0)
    return out.astype(np.float32)
```

```python
import math
import numpy as np
from contextlib import ExitStack
import concourse.bass as bass
import concourse.mybir as mybir
from concourse.bass_utils import run_bass_kernel_spmd

F32 = mybir.dt.float32
BF16 = mybir.dt.bfloat16
U32 = mybir.dt.uint32
AF = mybir.ActivationFunctionType
ALU = mybir.AluOpType

S_LEN = 4096
D = 1024
NL = 4
TT = 512
NT = S_LEN // TT
XW = 544
PADC = 16
NH = 8
NKV = 2
DH = 128
NE = 16
CAP = 512
DFF = 2048
KC = 31
N_IN = 5632
OFF_Q = 2048
OFF_K = 3072
OFF_V = 3328
OFF_GC = 3584
OFF_GA = 4608
LN_EPS = 1e-5
RMS_EPS = 1e-6
ALPHA = (2.0 * NL) ** 0.25
ATT_SCALE = 1.0 / math.sqrt(DH)
NRING = 4
NTMP = 10

ENGS = ("tensor", "vector", "scalar", "gpsimd", "sync")


class Sched:
    def __init__(self, nc, stack):
        self.nc = nc
        self.stack = stack
        self.prog = {e: [] for e in ENGS}
        self.cnt = {e: 0 for e in ENGS}
        self.esem = {e: stack.enter_context(nc.semaphore("es_" + e)) for e in ENGS}
        self.known = {e: {} for e in ENGS}
        self.last_w = {}
        self.readers = {}
        self.dsem = {}
        self.dcnt = {}
        self.nops = 0
        self.nwaits = 0

    def _sem_for_key(self, key):
        if key not in self.dsem:
            self.dsem[key] = self.stack.enter_context(self.nc.semaphore("ds%d" % len(self.dsem)))
            self.dcnt[key] = 0
        return self.dsem[key]

    def _waits(self, eng, reads, writes):
        waits = {}
        own = self.esem[eng].num

        def need(sv, raw):
            s, v = sv
            if s.num == own and not raw and eng == "tensor":
                return
            cur = waits.get(s.num)
            if cur is None or cur[1] < v:
                waits[s.num] = (s, v)

        for k in reads:
            for sv in self.last_w.get(k, {}).values():
                need(sv, True)
        for k in writes:
            for sv in self.last_w.get(k, {}).values():
                need(sv, False)
            for sv in self.readers.get(k, {}).values():
                need(sv, False)
        out = []
        kn = self.known[eng]
        for num, (s, v) in waits.items():
            if kn.get(num, 0) >= v:
                continue
            kn[num] = v
            out.append((s, v))
        return out

    def _commit(self, s, v, reads, writes):
        for k in writes:
            self.last_w.setdefault(k, {})[s.num] = (s, v)
            self.readers[k] = {}
        for k in reads:
            self.readers.setdefault(k, {})[s.num] = (s, v)

    def op(self, eng, fn, reads=(), writes=()):
        waits = self._waits(eng, reads, writes)
        self.cnt[eng] += 1
        s = self.esem[eng]
        self.prog[eng].append((waits, fn, s, 1))
        self._commit(s, self.cnt[eng], reads, writes)
        self.nops += 1
        self.nwaits += len(waits)

    def dma(self, eng, fn, reads=(), writes=(), semkey=None):
        waits = self._waits(eng, reads, writes)
        key = semkey if semkey is not None else writes[0]
        s = self._sem_for_key(key)
        self.dcnt[key] += 16
        self.prog[eng].append((waits, fn, s, 16))
        self._commit(s, self.dcnt[key], reads, writes)
        self.nops += 1
        self.nwaits += len(waits)

    def finish(self, eng, keys):
        waits = self._waits(eng, keys, ())
        self.prog[eng].append((waits, None, None, 0))

    def emit(self):
        with self.nc.Block() as block:
            for e in ENGS:
                prog = self.prog[e]
                if not prog:
                    continue

                def body(engine, prog=prog):
                    for waits, fn, s, inc in prog:
                        for ws, wv in waits:
                            engine.wait_ge(ws, wv)
                        if fn is not None:
                            fn(engine).then_inc(s, inc)

                getattr(block, e)(body)


class MK:
    def __init__(self, layers=(0, 1, 2, 3), first=True, last=True, dbg=()):
        self.layers = list(layers)
        self.first = first
        self.last = last
        self.dbg = set(dbg)
        self.nc = bass.Bass("TRN2", target_bir_lowering=False)
        self.bank_rr = 0
        self.pinned = set()
        self.tmp_rr = 0

    def din(self, name, shape, dt=F32):
        return self.nc.dram_tensor(name, list(shape), dt, kind="ExternalInput").ap()

    def sb(self, name, shape, dt):
        return self.st.enter_context(self.nc.sbuf_tensor(name, list(shape), dt))

    def declare(self):
        nc = self.nc
        L = len(self.layers)
        self.x_in = self.din("x", [S_LEN, D])
        self.p = {}
        for nm, shp in [("ln0_g", [1, D]), ("ln0_b", [1, D]), ("w_in", [L, D, N_IN]), ("b_in", [L, N_IN // 128, 128]),
                        ("conv_dw", [L, KC, D]), ("conv_dw_b", [L, 8, 128]), ("conv_ln_g", [L, 8, 128]),
                        ("conv_ln_b", [L, 8, 128]), ("conv_pw_w", [L, D, D]), ("conv_pw_b", [L, 8, 128]),
                        ("q_norm_g", [L, 1, DH]), ("k_norm_g", [L, 1, DH]), ("w_o", [L, D, D]), ("w_out", [L, D, D]),
                        ("b_out", [L, D]), ("ln1_g", [L, D]), ("ln1_b", [L, D]), ("w_router", [L, D, NE]),
                        ("w_gate", [L, NE, D, DFF]), ("w_up", [L, NE, D, DFF]), ("w_down", [L, NE, DFF, D]),
                        ("ln2_g", [L, D]), ("ln2_b", [L, D])]:
            self.p[nm] = self.din(nm, shp)
        self.c_ident = self.din("c_ident", [128, 128])
        self.c_swap = self.din("c_swap", [128, 128])
        self.c_ropeC = self.din("c_ropeC", [128, S_LEN])
        self.c_ropeS = self.din("c_ropeS", [128, S_LEN])
        self.out = nc.dram_tensor("out", [S_LEN, D], F32, kind="ExternalOutput").ap()
        self.Xd = nc.dram_tensor("Xd", [S_LEN, D], F32, kind="Internal").ap()
        self.XTd = nc.dram_tensor("XTd", [D, S_LEN + 2 * PADC], BF16, kind="Internal").ap()
        self.X1d = nc.dram_tensor("X1d", [S_LEN, D], BF16, kind="Internal").ap()
        self.ACCd = nc.dram_tensor("ACCd", [S_LEN, D], F32, kind="Internal").ap()
        self.dbg_t = {}

    def dbg_out(self, name, shape, dt=F32):
        t = self.nc.dram_tensor("dbg_" + name, list(shape), dt, kind="ExternalOutput").ap()
        self.dbg_t[name] = t
        return t

    def alloc(self):
        sb = self.sb
        self.KT = sb("KT", [128, NKV, S_LEN], BF16)
        self.Vt = sb("Vt", [128, S_LEN // 128, NKV * DH], BF16)
        self.aff = sb("aff", [128, S_LEN // 128, NE], F32)
        self.ident_f = sb("ident_f", [128, 128], F32)
        self.ident_b = sb("ident_b", [128, 128], BF16)
        self.swapm = sb("swapm", [128, 128], F32)
        self.ones_d = sb("ones_d", [128, 128], F32)
        self.ones_h = sb("ones_h", [128, 128], F32)
        self.ones_b = sb("ones_b", [128, 128], BF16)
        self.prm_rows = sb("prm_rows", [80, 128], F32)
        self.prm = sb("prm", [128, 80], F32)
        self.dw_rows = sb("dw_rows", [KC, D], F32)
        self.dw_pp = sb("dw_pp", [128, 8, KC], F32)
        self.bc = [sb("bc%d" % i, [128, D], F32) for i in range(3)]
        self.bv_bc = sb("bv_bc", [128, NKV * DH], F32)
        self.wr = sb("wr", [128, 8, NE], F32)
        self.ring = [sb("ring%d" % i, [128, 8, 512], BF16) for i in range(NRING)]
        self.xT = sb("xT", [128, 8, XW], BF16)
        self.uT = sb("uT", [128, 8, XW], BF16)
        self.Bf = sb("Bf", [128, 8, TT], F32)
        self.Cb = sb("Cb", [128, 8, TT], BF16)
        self.Db = sb("Db", [128, 8, TT], BF16)
        self.Eb = sb("Eb", [128, 8, TT], BF16)
        self.tmp = [sb("tmp%d" % i, [128, TT], F32) for i in range(NTMP)]
        self.PT = [sb("PT%d" % i, [128, 2 * TT], BF16) for i in range(3)]
        self.ropeC = [sb("ropeC%d" % i, [128, TT], F32) for i in range(2)]
        self.ropeS = [sb("ropeS%d" % i, [128, TT], F32) for i in range(2)]
        self.row = [sb("row%d" % i, [128, D], F32) for i in range(3)]
        self.rowb = sb("rowb", [128, D], BF16)
        self.x1T = sb("x1T", [128, 8, 128], F32)
        self.dg = [sb("dg%d" % i, [128, 128], BF16) for i in range(8)]
        self.small = sb("small", [128, 64], F32)
        self.bst = [sb("bst%d" % i, [128, 2, 6], F32) for i in range(2)]
        self.zpad = sb("zpad", [128, 8, PADC], BF16)
        self.topv = sb("topv", [NE, CAP], F32)
        self.topi = sb("topi", [NE, CAP], U32)
        self.topif = sb("topif", [NE, CAP], F32)
        self.slot_g = sb("slot_g", [128, NE, 4], F32)
        self.slot_i = sb("slot_i", [128, NE, 4], U32)
        self.ps = self.st.enter_context(self.nc.psum_tensor("ps", [128, 8, 512], F32))

    def bank(self, pin=False):
        while True:
            b = self.bank_rr
            self.bank_rr = (self.bank_rr + 1) % 8
            if b not in self.pinned:
                break
        if pin:
            self.pinned.add(b)
        return b

    def unpin(self, b):
        self.pinned.discard(b)

    def T(self, fn, r=(), w=()):
        self.S.op("tensor", fn, r, w)

    def V(self, fn, r=(), w=()):
        self.S.op("vector", fn, r, w)

    def A(self, fn, r=(), w=()):
        self.S.op("scalar", fn, r, w)

    def G(self, fn, r=(), w=()):
        self.S.op("gpsimd", fn, r, w)

    def dma(self, fn, r=(), w=(), eng="sync", semkey=None):
        self.S.dma(eng, fn, r, w, semkey)

    def tmpk(self):
        i = self.tmp_rr
        self.tmp_rr = (self.tmp_rr + 1) % NTMP
        return i

    def ws_reset(self, specs, hw_mask=None, extra_slots=()):
        self.ws_specs = specs
        self.ws_slots = [(self.ring[i][:], ("ring", i)) for i in range(NRING)] + list(extra_slots)
        self.ws_n = len(self.ws_slots)
        self.ws_issued = 0
        self.ws_used = 0
        self.ws_hw = hw_mask
        if hw_mask is not None:
            self.ws_hwlist = [i for i in range(len(specs)) if hw_mask[i]]
            self.ws_hwdma = 0
            self.stg = [(self.KT[:].rearrange("p a n -> p (a n)").bitcast(F32).rearrange("p (k n) -> p k n", k=8), "KT"),
                        (self.Vt[:].rearrange("p a n -> p (a n)").bitcast(F32).rearrange("p (k n) -> p k n", k=8), "Vt")]
            self.ws_hw_dma_upto(2)

    def ws_hw_dma_upto(self, n):
        n = min(n, len(self.ws_hwlist))
        while self.ws_hwdma < n:
            hn = self.ws_hwdma
            i = self.ws_hwlist[hn]
            stg, sk = self.stg[hn % 2]
            for (c0, ncol, src) in self.ws_specs[i]:
                self.dma(lambda e, stg=stg, c0=c0, ncol=ncol, src=src:
                         e.dma_start(out=stg[:, :, c0:c0 + ncol], in_=src.rearrange("(k p) n -> p k n", p=128)),
                         r=(), w=[sk], eng=("sync" if hn % 2 == 0 else "scalar"))
            self.ws_hwdma += 1

    def ws_issue_upto(self, n):
        n = min(n, len(self.ws_specs))
        while self.ws_issued < n:
            i = self.ws_issued
            sap, skey = self.ws_slots[i % self.ws_n]
            if self.ws_hw is not None and self.ws_hw[i]:
                hn = self.ws_hwlist.index(i)
                stg, sk = self.stg[hn % 2]
                self.A(lambda e, sap=sap, stg=stg: e.copy(out=sap, in_=stg), r=[sk], w=[skey])
                self.ws_hw_dma_upto(hn + 3)
            else:
                for (c0, ncol, src) in self.ws_specs[i]:
                    self.dma(lambda e, sap=sap, c0=c0, ncol=ncol, src=src:
                             e.dma_start(out=sap[:, :, c0:c0 + ncol],
                                         in_=src.rearrange("(k p) n -> p k n", p=128)),
                             r=(), w=[skey], eng="gpsimd", semkey=("sw", skey))
            self.ws_issued += 1

    def ws_next(self):
        i = self.ws_used
        self.ws_issue_upto(i + self.ws_n - 1)
        self.ws_used += 1
        return self.ws_slots[i % self.ws_n]

    def setup_consts(self):
        self.dma(lambda e: e.dma_start(out=self.ident_f[:], in_=self.c_ident), w=["ident_f"])
        self.dma(lambda e: e.dma_start(out=self.swapm[:], in_=self.c_swap), w=["swapm"])
        self.V(lambda e: e.tensor_copy(out=self.ident_b[:], in_=self.ident_f[:]), r=["ident_f"], w=["ident_b"])
        self.V(lambda e: e.memset(self.ones_d[:], 1.0 / D), w=["ones_d"])
        self.V(lambda e: e.memset(self.ones_h[:], 1.0 / DH), w=["ones_h"])
        self.V(lambda e: e.memset(self.ones_b[:], 1.0), w=["ones_b"])
        self.V(lambda e: e.memset(self.zpad[:], 0.0), w=["zpad"])
        for c0 in (0, PADC + S_LEN):
            self.dma(lambda e, c0=c0: e.dma_start(out=self.XTd[:, c0:c0 + PADC].rearrange("(k p) n -> p k n", p=128),
                                                  in_=self.zpad[:]), r=["zpad"], w=["XTd"], semkey=("st", "zpad"))

    def load_bc(self, i, src_row):
        self.dma(lambda e: e.dma_start(out=self.bc[i][:], in_=src_row.to_broadcast([128, D])), w=[("bc", i)])

    P_BIN = 0
    P_DWB = 44
    P_CLG = 52
    P_CLB = 60
    P_PWB = 68
    P_QG = 76
    P_KG = 77

    def load_layer_params(self, l):
        p = self.p
        rows = self.prm_rows
        segs = [(self.P_BIN, 44, p["b_in"][l]), (self.P_DWB, 8, p["conv_dw_b"][l]), (self.P_CLG, 8, p["conv_ln_g"][l]),
                (self.P_CLB, 8, p["conv_ln_b"][l]), (self.P_PWB, 8, p["conv_pw_b"][l]), (self.P_QG, 1, p["q_norm_g"][l]),
                (self.P_KG, 1, p["k_norm_g"][l])]
        for (r0, n, src) in segs:
            self.dma(lambda e, r0=r0, n=n, src=src: e.dma_start(out=rows[r0:r0 + n, :], in_=src), w=["prm_rows"])
        b = self.bank()
        pk = ("ps", b)
        self.T(lambda e, b=b: e.transpose(out=self.ps[:, b, 0:78], in_=rows[0:78, :], identity=self.ident_f[0:78, 0:78]),
               r=["prm_rows", "ident_f"], w=[pk])
        self.V(lambda e, b=b: e.tensor_copy(out=self.prm[:, 0:78], in_=self.ps[:, b, 0:78]), r=[pk], w=["prm"])
        self.dma(lambda e: e.dma_start(out=self.dw_rows[:], in_=p["conv_dw"][l]), w=["dw_rows"])
        for c in range(8):
            b = self.bank()
            pk = ("ps", b)
            self.T(lambda e, b=b, c=c: e.transpose(out=self.ps[:, b, 0:KC], in_=self.dw_rows[:, c * 128:(c + 1) * 128],
                                                   identity=self.ident_f[0:KC, 0:KC]),
                   r=["dw_rows", "ident_f"], w=[pk])
            self.V(lambda e, b=b, c=c: e.tensor_copy(out=self.dw_pp[:, c, :], in_=self.ps[:, b, 0:KC]), r=[pk], w=["dw_pp"])
        self.dma(lambda e: e.dma_start(out=self.bv_bc[:], in_=p["b_in"][l].rearrange("c p -> (c p)")[OFF_V:OFF_V + 256]
                                       .rearrange("(o n) -> o n", o=1).to_broadcast([128, 256])), w=["bv_bc"])
        self.dma(lambda e: e.dma_start(out=self.wr[:], in_=p["w_router"][l].rearrange("(k p) n -> p k n", p=128)), w=["wr"])

    def ln_rows(self, src, dst, gi, bi, eps=LN_EPS, srck=None, dstk=None, par=0):
        sm = self.small
        c0 = par * 4
        bst = self.bst[par]
        kb0, kb1, kmv, ksd, krs = (("lnst", par, i) for i in range(5))
        self.V(lambda e: e.bn_stats(out=bst[:, 0, :], in_=src[:, 0:512]), r=[srck], w=[kb0])
        self.V(lambda e: e.bn_stats(out=bst[:, 1, :], in_=src[:, 512:1024]), r=[srck], w=[kb1])
        self.V(lambda e: e.bn_aggr(out=sm[:, c0:c0 + 2], in_=bst[:].rearrange("p a b -> p (a b)")), r=[kb0, kb1], w=[kmv])
        self.A(lambda e: e.activation(out=sm[:, c0 + 2:c0 + 3], in_=sm[:, c0 + 1:c0 + 2], func=AF.Sqrt, bias=eps, scale=1.0), r=[kmv], w=[ksd])
        self.V(lambda e: e.reciprocal(out=sm[:, c0 + 3:c0 + 4], in_=sm[:, c0 + 2:c0 + 3]), r=[ksd], w=[krs])
        self.V(lambda e: e.tensor_scalar(out=dst, in0=src, scalar1=sm[:, c0:c0 + 1], scalar2=sm[:, c0 + 3:c0 + 4],
                                         op0=ALU.subtract, op1=ALU.mult), r=[srck, kmv, krs], w=[dstk])
        self.V(lambda e: e.tensor_tensor(out=dst, in0=dst, in1=self.bc[gi][:], op=ALU.mult), r=[dstk, ("bc", gi)], w=[dstk])
        self.V(lambda e: e.tensor_tensor(out=dst, in0=dst, in1=self.bc[bi][:], op=ALU.add), r=[dstk, ("bc", bi)], w=[dstk])

    def nr_stageA(self, pk, psap, bias_ap):
        i_raw, i_sq, i_rs = self.tmpk(), self.tmpk(), self.tmpk()
        stt = {"raw": self.tmp[i_raw], "sq": self.tmp[i_sq], "rs": self.tmp[i_rs],
               "kr": ("tmp", i_raw), "ks": ("tmp", i_sq), "krs": ("tmp", i_rs)}
        raw, sq = stt["raw"], stt["sq"]
        self.A(lambda e: e.activation(out=raw[:], in_=psap, func=AF.Identity, bias=bias_ap, scale=1.0), r=[pk, "prm"], w=[stt["kr"]])
        self.A(lambda e: e.activation(out=sq[:], in_=psap, func=AF.Square, bias=bias_ap, scale=1.0), r=[pk, "prm"], w=[stt["ks"]])
        return stt

    def nr_stageB(self, stt, g_ap):
        raw, sq, rs = stt["raw"], stt["sq"], stt["rs"]
        kr, ks, krs = stt["kr"], stt["ks"], stt["krs"]
        b2 = self.bank()
        pk2 = ("ps", b2)
        self.T(lambda e: e.matmul(self.ps[:, b2, :], lhsT=self.ones_h[:], rhs=sq[:], start=True, stop=True),
               r=[ks, "ones_h"], w=[pk2])
        self.A(lambda e: e.activation(out=rs[:], in_=self.ps[:, b2, :], func=AF.Sqrt, bias=RMS_EPS, scale=1.0), r=[pk2], w=[krs])
        self.V(lambda e: e.reciprocal(out=rs[:], in_=rs[:]), r=[krs], w=[krs])
        self.V(lambda e: e.scalar_tensor_tensor(out=raw[:], in0=raw[:], scalar=g_ap, in1=rs[:], op0=ALU.mult, op1=ALU.mult),
               r=[kr, krs, "prm"], w=[kr])

    def nr_stageC(self, stt, rp, out_ap, outk):
        raw, sq = stt["raw"], stt["sq"]
        kr, ks = stt["kr"], stt["ks"]
        b3 = self.bank()
        pk3 = ("ps", b3)
        self.T(lambda e: e.matmul(self.ps[:, b3, :], lhsT=self.swapm[:], rhs=raw[:], start=True, stop=True),
               r=[kr, "swapm"], w=[pk3])
        self.V(lambda e: e.tensor_tensor(out=sq[:], in0=self.ps[:, b3, :], in1=self.ropeS[rp][:], op=ALU.mult),
               r=[pk3, ("ropeS", rp)], w=[ks])
        self.G(lambda e: e.tensor_tensor(out=raw[:], in0=raw[:], in1=self.ropeC[rp][:], op=ALU.mult), r=[kr, ("ropeC", rp)], w=[kr])
        self.V(lambda e: e.tensor_tensor(out=out_ap, in0=raw[:], in1=sq[:], op=ALU.add), r=[kr, ks], w=[outk])

    def norm_rope_pipe(self, n, projA, g_ap, rp, outs):
        stts = [None] * n
        for step in range(n + 2):
            if step < n:
                pk, psap, bias_ap = projA(step)
                stts[step] = self.nr_stageA(pk, psap, bias_ap)
            if 0 <= step - 1 < n:
                self.nr_stageB(stts[step - 1], g_ap)
            if 0 <= step - 2 < n:
                self.nr_stageC(stts[step - 2], rp, *outs[step - 2])

    def load_rope(self, t, rp):
        t0 = t * TT
        self.dma(lambda e: e.dma_start(out=self.ropeC[rp][:], in_=self.c_ropeC[:, t0:t0 + TT]), w=[("ropeC", rp)])
        self.dma(lambda e: e.dma_start(out=self.ropeS[rp][:], in_=self.c_ropeS[:, t0:t0 + TT]), w=[("ropeS", rp)])

    def pass1(self, l, src_d, gi_src):
        p = self.p
        g_row, b_row = gi_src
        self.load_bc(0, g_row)
        self.load_bc(1, b_row)
        self.ws_reset([[(0, 512, p["w_in"][l][:, OFF_K:OFF_K + 512])]])
        wkv, wk_key = self.ws_next()

        def p1_load(sgi):
            rw = self.row[sgi % 3]
            r0 = sgi * 128
            self.dma(lambda e, rw=rw, r0=r0: e.dma_start(out=rw[:], in_=src_d[r0:r0 + 128, :]),
                     r=["ACCd"] + [("sc", NE - 1, c) for c in range(4)], w=[("row", sgi % 3)])

        for t in range(NT):
            t0 = t * TT
            rp = t % 2
            self.load_rope(t, rp)
            for st in range(4):
                r0 = t0 + st * 128
                sgi = t * 4 + st
                if sgi == 0:
                    p1_load(0)
                    p1_load(1)
                rw = self.row[sgi % 3]
                rk = ("row", sgi % 3)
                self.ln_rows(rw[:], rw[:], 0, 1, srck=rk, dstk=rk, par=st % 2)
                self.dma(lambda e, rw=rw, r0=r0: e.dma_start(out=self.Xd[r0:r0 + 128, :], in_=rw[:]), r=[rk], w=["Xd"], semkey=("st", rk))
                for half in range(2):
                    b = self.bank()
                    pk = ("ps", b)
                    for j in range(4):
                        k = half * 4 + j
                        self.T(lambda e, b=b, j=j, k=k, rw=rw: e.transpose(out=self.ps[:, b, j * 128:(j + 1) * 128],
                                                                          in_=rw[:, k * 128:(k + 1) * 128], identity=self.ident_f[:]),
                               r=[rk, "ident_f"], w=[pk])
                    self.A(lambda e, b=b, half=half, st=st: e.copy(
                        out=self.xT[:, half * 4:half * 4 + 4, st * 128:(st + 1) * 128],
                        in_=self.ps[:, b, :].rearrange("p (j n) -> p j n", j=4)), r=[pk], w=["xT"])
                if sgi + 2 < S_LEN // 128:
                    p1_load(sgi + 2)
            self.dma(lambda e, t0=t0: e.dma_start(out=self.XTd[:, PADC + t0:PADC + t0 + TT].rearrange("(k p) n -> p k n", p=128),
                                                  in_=self.xT[:, :, 0:TT]), r=["xT"], w=["XTd"], semkey=("st", "xT"))
            def projK(h, t0=t0):
                b = self.bank()
                pk = ("ps", b)
                for k in range(8):
                    self.T(lambda e, b=b, k=k, h=h: e.matmul(self.ps[:, b, :], lhsT=wkv[:, k, h * 128:(h + 1) * 128],
                                                            rhs=self.xT[:, k, 0:TT], start=(k == 0), stop=(k == 7)),
                           r=[wk_key, "xT"], w=[pk])
                cb = self.P_BIN + (OFF_K // 128) + h
                return pk, self.ps[:, b, :], self.prm[:, cb:cb + 1]

            self.norm_rope_pipe(NKV, projK, self.prm[:, self.P_KG:self.P_KG + 1], rp,
                                [(self.KT[:, h, t0:t0 + TT], "KT") for h in range(NKV)])
            for st in range(4):
                b = self.bank()
                pk = ("ps", b)
                for k in range(8):
                    self.T(lambda e, b=b, k=k, st=st: e.matmul(self.ps[:, b, 0:256], lhsT=self.xT[:, k, st * 128:(st + 1) * 128],
                                                              rhs=wkv[:, k, 256:512], start=(k == 0), stop=(k == 7)),
                           r=[wk_key, "xT"], w=[pk])
                sg = t * 4 + st
                self.V(lambda e, b=b, sg=sg: e.tensor_tensor(out=self.Vt[:, sg, :], in0=self.ps[:, b, 0:256], in1=self.bv_bc[:],
                                                            op=ALU.add), r=[pk, "bv_bc"], w=["Vt"])

    def pass2_specs(self, l):
        p = self.p
        w_in = p["w_in"][l]
        one = []
        for i in range(4):
            one.append([(0, 256, w_in[:, i * 256:(i + 1) * 256]), (256, 256, w_in[:, D + i * 256:D + (i + 1) * 256])])
        for i in range(2):
            one.append([(0, 512, p["conv_pw_w"][l][:, i * 512:(i + 1) * 512])])
            one.append([(0, 512, w_in[:, OFF_GC + i * 512:OFF_GC + (i + 1) * 512])])
        for i in range(2):
            one.append([(0, 512, w_in[:, OFF_Q + i * 512:OFF_Q + (i + 1) * 512])])
        for i in range(2):
            one.append([(0, 512, w_in[:, OFF_GA + i * 512:OFF_GA + (i + 1) * 512])])
            one.append([(0, 512, p["w_o"][l][:, i * 512:(i + 1) * 512])])
        for i in range(2):
            one.append([(0, 512, p["w_out"][l][:, i * 512:(i + 1) * 512])])
        return one

    def pass2(self, l):
        p = self.p
        prm = self.prm
        ps = self.ps
        self.load_bc(0, p["ln1_g"][l:l + 1, :])
        self.load_bc(1, p["ln1_b"][l:l + 1, :])
        self.load_bc(2, p["b_out"][l:l + 1, :])
        specs = []
        for t in range(NT):
            specs += self.pass2_specs(l)
        self.ws_reset(specs)
        for t in range(NT):
            t0 = t * TT
            rp = t % 2
            self.load_rope(t, rp)
            self.dma(lambda e, t0=t0: e.dma_start(out=self.xT[:], in_=self.XTd[:, t0:t0 + XW].rearrange("(k p) n -> p k n", p=128)),
                     r=["XTd"], w=["xT"])
            for i in range(4):
                wp, wk = self.ws_next()
                for cc in range(2):
                    c = i * 2 + cc
                    bA, bB, bC = self.bank(), self.bank(), self.bank()
                    kA, kB, kC = ("ps", bA), ("ps", bB), ("ps", bC)
                    for (col0, bm, bh, hc) in ((cc * 128, bA, bB, 0), (256 + cc * 128, bC, bB, 32)):
                        for k in range(8):
                            self.T(lambda e, k=k, col0=col0, bm=bm, wp=wp: e.matmul(ps[:, bm, :], lhsT=wp[:, k, col0:col0 + 128],
                                                                            rhs=self.xT[:, k, 0:512], start=(k == 0), stop=(k == 7)),
                                   r=[wk, "xT"], w=[("ps", bm)])
                        for k in range(8):
                            self.T(lambda e, k=k, col0=col0, bh=bh, hc=hc, wp=wp: e.matmul(ps[:, bh, hc:hc + 32], lhsT=wp[:, k, col0:col0 + 128],
                                                                                    rhs=self.xT[:, k, 512:544], start=(k == 0), stop=(k == 7)),
                                   r=[wk, "xT"], w=[("ps", bh)])
                    cv = self.P_BIN + c
                    cg = self.P_BIN + 8 + c
                    ti = self.tmpk()
                    sg = self.tmp[ti]
                    kt = ("tmp", ti)
                    ti2 = self.tmpk()
                    sg2 = self.tmp[ti2]
                    kt2 = ("tmp", ti2)
                    self.A(lambda e, bC=bC, cg=cg, sg=sg: e.activation(out=sg[:], in_=ps[:, bC, :], func=AF.Sigmoid,
                                                                      bias=prm[:, cg:cg + 1], scale=1.0), r=[kC, "prm"], w=[kt])
                    self.A(lambda e, bB=bB, cg=cg, sg2=sg2: e.activation(out=sg2[:, 0:32], in_=ps[:, bB, 32:64], func=AF.Sigmoid,
                                                                        bias=prm[:, cg:cg + 1], scale=1.0), r=[kB, "prm"], w=[kt2])
                    uk = ("uT", c)
                    self.V(lambda e, bA=bA, cv=cv, sg=sg, c=c: e.scalar_tensor_tensor(out=self.uT[:, c, 0:512], in0=ps[:, bA, :],
                                                                                     scalar=prm[:, cv:cv + 1], in1=sg[:],
                                                                                     op0=ALU.add, op1=ALU.mult), r=[kA, kt, "prm"], w=[uk])
                    self.V(lambda e, bB=bB, cv=cv, sg2=sg2, c=c: e.scalar_tensor_tensor(out=self.uT[:, c, 512:544], in0=ps[:, bB, 0:32],
                                                                                       scalar=prm[:, cv:cv + 1], in1=sg2[:, 0:32],
                                                                                       op0=ALU.add, op1=ALU.mult), r=[kB, kt2, "prm"], w=[uk])
                    if t == 0:
                        self.V(lambda e, c=c: e.memset(self.uT[:, c, 0:16], 0.0), w=[uk])
                    if t == NT - 1:
                        self.V(lambda e, c=c: e.memset(self.uT[:, c, 528:544], 0.0), w=[uk])
            bM = self.bank(pin=True)
            bQ = self.bank(pin=True)
            kM, kQ = ("ps", bM), ("ps", bQ)
            for c in range(8):
                b = self.bank()
                pk = ("ps", b)
                for k in range(KC):
                    di = (c * KC + k) % 8
                    dk = ("dg", di)
                    self.V(lambda e, di=di, c=c, k=k: e.tensor_scalar(out=self.dg[di][:], in0=self.ident_b[:],
                                                                     scalar1=self.dw_pp[:, c, k:k + 1], scalar2=None, op0=ALU.mult),
                           r=["ident_b", "dw_pp"], w=[dk])
                    self.T(lambda e, b=b, di=di, c=c, k=k: e.matmul(ps[:, b, :], lhsT=self.dg[di][:], rhs=self.uT[:, c, k + 1:k + 1 + 512],
                                                                   start=(k == 0), stop=(k == KC - 1)), r=[dk, ("uT", c)], w=[pk])
                cdb = self.P_DWB + c
                self.A(lambda e, b=b, c=c, cdb=cdb: e.activation(out=self.Bf[:, c, :], in_=ps[:, b, :], func=AF.Identity,
                                                                bias=prm[:, cdb:cdb + 1], scale=1.0), r=[pk, "prm"], w=[("Bf", c)])
                ti = self.tmpk()
                sq = self.tmp[ti]
                kt = ("tmp", ti)
                self.A(lambda e, b=b, cdb=cdb, sq=sq: e.activation(out=sq[:], in_=ps[:, b, :], func=AF.Square,
                                                                  bias=prm[:, cdb:cdb + 1], scale=1.0), r=[pk, "prm"], w=[kt])
                self.T(lambda e, c=c, bM=bM: e.matmul(ps[:, bM, :], lhsT=self.ones_d[:], rhs=self.Bf[:, c, :], start=(c == 0), stop=(c == 7)),
                       r=[("Bf", c), "ones_d"], w=[kM])
                self.T(lambda e, c=c, sq=sq, bQ=bQ: e.matmul(ps[:, bQ, :], lhsT=self.ones_d[:], rhs=sq[:], start=(c == 0), stop=(c == 7)),
                       r=[kt, "ones_d"], w=[kQ])
            im, iv = self.tmpk(), self.tmpk()
            mean, rstd = self.tmp[im], self.tmp[iv]
            kmn, krs = ("tmp", im), ("tmp", iv)
            self.A(lambda e, mean=mean, bM=bM: e.copy(out=mean[:], in_=ps[:, bM, :]), r=[kM], w=[kmn])
            self.V(lambda e, mean=mean, rstd=rstd, bM=bM: e.tensor_tensor(out=rstd[:], in0=mean[:], in1=ps[:, bM, :], op=ALU.mult), r=[kmn, kM], w=[krs])
            self.V(lambda e, rstd=rstd, bQ=bQ: e.tensor_tensor(out=rstd[:], in0=ps[:, bQ, :], in1=rstd[:], op=ALU.subtract), r=[kQ, krs], w=[krs])
            self.A(lambda e, rstd=rstd: e.activation(out=rstd[:], in_=rstd[:], func=AF.Sqrt, bias=LN_EPS, scale=1.0), r=[krs], w=[krs])
            self.V(lambda e, rstd=rstd: e.reciprocal(out=rstd[:], in_=rstd[:]), r=[krs], w=[krs])
            self.unpin(bM)
            self.unpin(bQ)
            for c in range(8):
                kb = ("Bf", c)
                self.V(lambda e, c=c, mean=mean: e.tensor_tensor(out=self.Bf[:, c, :], in0=self.Bf[:, c, :], in1=mean[:], op=ALU.subtract),
                       r=[kb, kmn], w=[kb])
                self.V(lambda e, c=c, rstd=rstd: e.tensor_tensor(out=self.Bf[:, c, :], in0=self.Bf[:, c, :], in1=rstd[:], op=ALU.mult),
                       r=[kb, krs], w=[kb])
                cg, cb2 = self.P_CLG + c, self.P_CLB + c
                self.A(lambda e, c=c, cg=cg, cb2=cb2: e.activation(out=self.Cb[:, c, :], in_=self.Bf[:, c, :], func=AF.Silu,
                                                                  bias=prm[:, cb2:cb2 + 1], scale=prm[:, cg:cg + 1]),
                       r=[kb, "prm"], w=[("Cb", c)])
            if getattr(self, 'dbg_stop', None) == 'conv':
                return
            for i in range(2):
                wpw, wpwk = self.ws_next()
                wgc, wgck = self.ws_next()
                for jj in range(4):
                    j = i * 4 + jj
                    bg = self.bank()
                    kg_ = ("ps", bg)
                    for k in range(8):
                        self.T(lambda e, bg=bg, k=k, jj=jj, wgc=wgc: e.matmul(ps[:, bg, :], lhsT=wgc[:, k, jj * 128:(jj + 1) * 128],
                                                                             rhs=self.xT[:, k, PADC:PADC + TT], start=(k == 0), stop=(k == 7)),
                               r=[wgck, "xT"], w=[kg_])
                    cgc = self.P_BIN + OFF_GC // 128 + j
                    ti = self.tmpk()
                    gs = self.tmp[ti]
                    kt = ("tmp", ti)
                    self.A(lambda e, bg=bg, cgc=cgc, gs=gs: e.activation(out=gs[:], in_=ps[:, bg, :], func=AF.Sigmoid,
                                                                        bias=prm[:, cgc:cgc + 1], scale=1.0), r=[kg_, "prm"], w=[kt])
                    by = self.bank()
                    ky = ("ps", by)
                    for c in range(8):
                        self.T(lambda e, by=by, c=c, jj=jj, wpw=wpw: e.matmul(ps[:, by, :], lhsT=wpw[:, c, jj * 128:(jj + 1) * 128],
                                                                             rhs=self.Cb[:, c, :], start=(c == 0), stop=(c == 7)),
                               r=[wpwk, ("Cb", c)], w=[ky])
                    cpb = self.P_PWB + j
                    self.V(lambda e, by=by, j=j, cpb=cpb, gs=gs: e.scalar_tensor_tensor(out=self.Bf[:, j, :], in0=ps[:, by, :],
                                                                                      scalar=prm[:, cpb:cpb + 1], in1=gs[:],
                                                                                      op0=ALU.add, op1=ALU.mult),
                           r=[ky, kt, "prm"], w=[("Bf", j)])
            if getattr(self, 'dbg_stop', None) == 'pw':
                return
            qw = {}

            def projQ(h):
                i, jj = divmod(h, 4)
                if jj == 0:
                    qw["wp"], qw["wk"] = self.ws_next()
                wp, wk = qw["wp"], qw["wk"]
                b = self.bank()
                pk = ("ps", b)
                for k in range(8):
                    self.T(lambda e, b=b, k=k, jj=jj, wp=wp: e.matmul(ps[:, b, :], lhsT=wp[:, k, jj * 128:(jj + 1) * 128],
                                                                     rhs=self.xT[:, k, PADC:PADC + TT], start=(k == 0), stop=(k == 7)),
                           r=[wk, "xT"], w=[pk])
                cq = self.P_BIN + OFF_Q // 128 + h
                return pk, ps[:, b, :], prm[:, cq:cq + 1]

            self.norm_rope_pipe(NH, projQ, prm[:, self.P_QG:self.P_QG + 1], rp,
                                [(self.Db[:, h, :], ("Db", h)) for h in range(NH)])
            NPAIR = S_LEN // 256
            units = [(g, qs) for g in range(NKV) for qs in range(4)]

            def emit_S(f):
                n, j = divmod(f, NPAIR)
                g, qs = units[n]
                sp = (0, 1) if f % 2 == 0 else (2, 3)
                qk = [("Db", g * 4 + hh) for hh in range(4)]
                for u in range(2):
                    kc = 2 * j + u
                    self.T(lambda e, b=sp[u], g=g, kc=kc, qs=qs: e.matmul(
                        ps[:, b, :].rearrange("p (h q) -> p h q", h=4), lhsT=self.KT[:, g, kc * 128:(kc + 1) * 128],
                        rhs=self.Db[:, g * 4:g * 4 + 4, qs * 128:(qs + 1) * 128], start=True, stop=True),
                        r=["KT"] + qk, w=[("ps", sp[u])])

            NF = len(units) * NPAIR
            emit_S(0)
            for f in range(NF):
                n, j = divmod(f, NPAIR)
                g, qs = units[n]
                if f + 1 < NF:
                    emit_S(f + 1)
                sp = (0, 1) if f % 2 == 0 else (2, 3)
                bO, bS = (4, 5) if n % 2 == 0 else (6, 7)
                kO, kS = ("ps", bO), ("ps", bS)
                pi = f % 3
                pt = self.PT[pi]
                kp = ("PT", pi)
                self.A(lambda e, s0=sp[0], pt=pt: e.activation(out=pt[:].rearrange("p (u n) -> p u n", u=2), in_=ps[:, s0:s0 + 2, :],
                                                               func=AF.Exp, scale=ATT_SCALE),
                       r=[("ps", sp[0]), ("ps", sp[1])], w=[kp])
                for u in range(2):
                    kc = 2 * j + u
                    first = (j == 0 and u == 0)
                    last = (j == NPAIR - 1 and u == 1)
                    self.T(lambda e, g=g, kc=kc, pt=pt, bO=bO, u=u, first=first, last=last: e.matmul(
                        ps[:, bO, :], lhsT=self.Vt[:, kc, g * 128:(g + 1) * 128], rhs=pt[:, u * 512:(u + 1) * 512],
                        start=first, stop=last), r=["Vt", kp], w=[kO])
                    self.T(lambda e, pt=pt, bS=bS, u=u, first=first, last=last: e.matmul(
                        ps[:, bS, :], lhsT=self.ones_b[:], rhs=pt[:, u * 512:(u + 1) * 512],
                        start=first, stop=last), r=["ones_b", kp], w=[kS])
                if j == NPAIR - 1:
                    ti = self.tmpk()
                    rec = self.tmp[ti]
                    kt = ("tmp", ti)
                    self.V(lambda e, rec=rec, bS=bS: e.reciprocal(out=rec[:], in_=ps[:, bS, :]), r=[kS], w=[kt])
                    ek = [("Eb", g * 4 + hh) for hh in range(4)]
                    self.V(lambda e, g=g, qs=qs, rec=rec, bO=bO: e.tensor_tensor(
                        out=self.Eb[:, g * 4:g * 4 + 4, qs * 128:(qs + 1) * 128],
                        in0=ps[:, bO, :].rearrange("p (h q) -> p h q", h=4), in1=rec[:].rearrange("p (h q) -> p h q", h=4),
                        op=ALU.mult), r=[kO, kt] + ek, w=ek)
            for i in range(2):
                wga, wgak = self.ws_next()
                wwo, wwok = self.ws_next()
                for jj in range(4):
                    j = i * 4 + jj
                    bg = self.bank()
                    kg_ = ("ps", bg)
                    for k in range(8):
                        self.T(lambda e, bg=bg, k=k, jj=jj, wga=wga: e.matmul(ps[:, bg, :], lhsT=wga[:, k, jj * 128:(jj + 1) * 128],
                                                                             rhs=self.xT[:, k, PADC:PADC + TT], start=(k == 0), stop=(k == 7)),
                               r=[wgak, "xT"], w=[kg_])
                    cga = self.P_BIN + OFF_GA // 128 + j
                    ti = self.tmpk()
                    gs = self.tmp[ti]
                    kt = ("tmp", ti)
                    self.A(lambda e, bg=bg, cga=cga, gs=gs: e.activation(out=gs[:], in_=ps[:, bg, :], func=AF.Sigmoid,
                                                                        bias=prm[:, cga:cga + 1], scale=1.0), r=[kg_, "prm"], w=[kt])
                    by = self.bank()
                    ky = ("ps", by)
                    for h in range(8):
                        self.T(lambda e, by=by, h=h, jj=jj, wwo=wwo: e.matmul(ps[:, by, :], lhsT=wwo[:, h, jj * 128:(jj + 1) * 128],
                                                                             rhs=self.Eb[:, h, :], start=(h == 0), stop=(h == 7)),
                               r=[wwok, ("Eb", h)], w=[ky])
                    self.V(lambda e, by=by, gs=gs: e.tensor_tensor(out=gs[:], in0=gs[:], in1=ps[:, by, :], op=ALU.mult),
                           r=[kt, ky], w=[kt])
                    self.V(lambda e, j=j, gs=gs: e.tensor_tensor(out=self.Cb[:, j, :], in0=self.Bf[:, j, :], in1=gs[:], op=ALU.add),
                           r=[("Bf", j), kt], w=[("Cb", j)])
            wpa, wka = self.ws_next()
            wpb, wkb = self.ws_next()
            def wo_p1(st):
                r0 = t0 + st * 128
                sg = t * 4 + st
                par = st % 2
                tr, ar = self.row[par], self.row[2]
                ktr = ("row", par)
                bb = []
                for (wp, wk) in ((wpa, wka), (wpb, wkb)):
                    b = self.bank()
                    bb.append(b)
                    for k in range(8):
                        self.T(lambda e, b=b, k=k, st=st, wp=wp: e.matmul(ps[:, b, :], lhsT=self.Cb[:, k, st * 128:(st + 1) * 128],
                                                                         rhs=wp[:, k, :], start=(k == 0), stop=(k == 7)),
                               r=[wk, ("Cb", k)], w=[("ps", b)])
                for hf in range(2):
                    b = bb[hf]
                    self.V(lambda e, b=b, hf=hf, tr=tr: e.scalar_tensor_tensor(out=tr[:, hf * 512:(hf + 1) * 512], in0=tr[:, hf * 512:(hf + 1) * 512],
                                                                              scalar=ALPHA, in1=ps[:, b, :], op0=ALU.mult, op1=ALU.add),
                           r=[ktr, ("ps", b)], w=[ktr])
                self.V(lambda e, tr=tr: e.tensor_tensor(out=tr[:], in0=tr[:], in1=self.bc[2][:], op=ALU.add), r=[ktr, ("bc", 2)], w=[ktr])
                self.ln_rows(tr[:], tr[:], 0, 1, srck=ktr, dstk=ktr, par=par)

            def wo_p2(st):
                r0 = t0 + st * 128
                sg = t * 4 + st
                par = st % 2
                tr, ar = self.row[par], self.row[2]
                ktr = ("row", par)
                self.A(lambda e, tr=tr: e.copy(out=self.rowb[:], in_=tr[:]), r=[ktr], w=["rowb"])
                self.dma(lambda e, r0=r0: e.dma_start(out=self.X1d[r0:r0 + 128, :], in_=self.rowb[:]), r=["rowb"], w=["X1d"], semkey=("st", "rowb"))
                self.A(lambda e, tr=tr: e.mul(ar[:], tr[:], ALPHA), r=[ktr], w=[("row", 2)])
                self.dma(lambda e, r0=r0: e.dma_start(out=self.ACCd[r0:r0 + 128, :], in_=ar[:]), r=[("row", 2)] + [("sc", NE - 1, c) for c in range(4)], w=["ACCd"], semkey=("st", "row2"))
                if "x1" in self.dbg and l == 0:
                    self.dma(lambda e, r0=r0, tr=tr: e.dma_start(out=self.dbg_t["x1"][r0:r0 + 128, :], in_=tr[:]), r=[ktr], w=["dbg_x1"],
                             semkey=("st", "dbgx1"))
                for half in range(2):
                    b = self.bank()
                    pk = ("ps", b)
                    for jx in range(4):
                        k = half * 4 + jx
                        self.T(lambda e, b=b, jx=jx, k=k, tr=tr: e.transpose(out=ps[:, b, jx * 128:(jx + 1) * 128],
                                                                             in_=tr[:, k * 128:(k + 1) * 128], identity=self.ident_f[:]),
                               r=[ktr, "ident_f"], w=[pk])
                    self.A(lambda e, b=b, half=half: e.copy(out=self.x1T[:, half * 4:half * 4 + 4, :],
                                                           in_=ps[:, b, :].rearrange("p (j n) -> p j n", j=4)), r=[pk], w=["x1T"])
                b = self.bank()
                pk = ("ps", b)
                for k in range(8):
                    self.T(lambda e, b=b, k=k: e.matmul(ps[:, b, 0:NE], lhsT=self.x1T[:, k, :], rhs=self.wr[:, k, :],
                                                        start=(k == 0), stop=(k == 7)), r=["x1T", "wr"], w=[pk])
                sm = self.small
                c0 = 8 + par * 4
                e0 = 16 + par * 16
                kq = [("rt", par, i) for i in range(5)]
                self.V(lambda e, b=b, c0=c0: e.reduce_max(out=sm[:, c0:c0 + 1], in_=ps[:, b, 0:NE], axis=mybir.AxisListType.X), r=[pk], w=[kq[0]])
                self.V(lambda e, c0=c0: e.tensor_scalar(out=sm[:, c0 + 1:c0 + 2], in0=sm[:, c0:c0 + 1], scalar1=-1.0, scalar2=None, op0=ALU.mult),
                       r=[kq[0]], w=[kq[1]])
                self.A(lambda e, b=b, c0=c0, e0=e0: e.activation(out=sm[:, e0:e0 + NE], in_=ps[:, b, 0:NE], func=AF.Exp, bias=sm[:, c0 + 1:c0 + 2],
                                                                scale=1.0, accum_out=sm[:, c0 + 2:c0 + 3]), r=[pk, kq[1]], w=[kq[2], kq[3]])
                self.V(lambda e, c0=c0: e.reciprocal(out=sm[:, c0 + 3:c0 + 4], in_=sm[:, c0 + 2:c0 + 3]), r=[kq[3]], w=[kq[4]])
                self.V(lambda e, sg=sg, c0=c0, e0=e0: e.tensor_scalar(out=self.aff[:, sg, :], in0=sm[:, e0:e0 + NE], scalar1=sm[:, c0 + 3:c0 + 4],
                                                                     scalar2=None, op0=ALU.mult), r=[kq[2], kq[4]], w=["aff"])

            def wo_load(st):
                r0 = t0 + st * 128
                tr = self.row[st % 2]
                self.dma(lambda e, r0=r0, tr=tr: e.dma_start(out=tr[:], in_=self.Xd[r0:r0 + 128, :]), r=["Xd"], w=[("row", st % 2)])

            wo_load(0)
            wo_load(1)
            for step in range(5):
                if step < 4:
                    wo_p1(step)
                if step >= 1:
                    wo_p2(step - 1)
                    if step + 1 < 4:
                        wo_load(step + 1)

    def topk(self, l):
        ps = self.ps
        work = self.Bf[0:NE, :, :].rearrange("p a b -> p (a b)")
        wk = [("Bf", c) for c in range(8)]
        for bi in range(8):
            b = self.bank()
            pk = ("ps", b)
            for j in range(4):
                sg = bi * 4 + j
                self.T(lambda e, b=b, j=j, sg=sg: e.transpose(out=ps[0:NE, b, j * 128:(j + 1) * 128], in_=self.aff[:, sg, :],
                                                              identity=self.ident_f[:]), r=["aff", "ident_f"], w=[pk])
            self.V(lambda e, b=b, bi=bi: e.tensor_copy(out=work[:, bi * 512:(bi + 1) * 512], in_=ps[0:NE, b, :]), r=[pk], w=[wk[bi]])
        for r in range(CAP // 8):
            self.V(lambda e, r=r: e.max(out=self.topv[:, r * 8:(r + 1) * 8], in_=work), r=wk, w=["topv"])
            self.V(lambda e, r=r: e.max_index(out=self.topi[:, r * 8:(r + 1) * 8], in_max=self.topv[:, r * 8:(r + 1) * 8], in_values=work),
                   r=wk + ["topv"], w=["topi"])
            self.V(lambda e, r=r: e.match_replace(out=work, in_to_replace=self.topv[:, r * 8:(r + 1) * 8], in_values=work, imm_value=-1.0),
                   r=wk + ["topv"], w=wk)
        self.V(lambda e: e.tensor_copy(out=self.topif[:], in_=self.topi[:]), r=["topi"], w=["topif"])
        for (src, srck, dst, dstk) in ((self.topv, "topv", self.slot_g, "slot_g"), (self.topif, "topif", self.slot_i, "slot_i")):
            b = self.bank()
            pk = ("ps", b)
            for col in range(4):
                self.T(lambda e, b=b, col=col, src=src: e.transpose(out=ps[:, b, col * NE:(col + 1) * NE], in_=src[:, col * 128:(col + 1) * 128],
                                                                    identity=self.ident_f[0:NE, 0:NE]), r=[srck, "ident_f"], w=[pk])
            self.V(lambda e, b=b, dst=dst: e.tensor_copy(out=dst[:].rearrange("p e c -> p c e"),
                                                         in_=ps[:, b, 0:4 * NE].rearrange("p (c e) -> p c e", c=4)), r=[pk], w=[dstk])

    def moe_specs(self, l):
        p = self.p
        specs = []
        for ex in range(NE):
            wg, wu, wd = p["w_gate"][l, ex], p["w_up"][l, ex], p["w_down"][l, ex]
            for i in range(4):
                specs.append([(0, 512, wg[:, i * 512:(i + 1) * 512])])
                specs.append([(0, 512, wu[:, i * 512:(i + 1) * 512])])
            for ch in range(2):
                for rh in range(2):
                    specs.append([(0, 512, wd[rh * 1024:(rh + 1) * 1024, ch * 512:(ch + 1) * 512])])
        return specs

    def moe(self, l):
        ps = self.ps
        specs = self.moe_specs(l)
        self.ws_reset(specs, hw_mask=[(i % 3) != 2 for i in range(len(specs))],
                      extra_slots=[(self.xT[:, :, 0:512], "xT")])
        xeb = self.Db
        xebv = xeb[:].rearrange("p (c a) n -> p c (a n)", c=4)

        def gather(ex):
            for col in range(4):
                self.dma(lambda e, ex=ex, col=col: e.indirect_dma_start(
                    out=xebv[:, col, :], out_offset=None, in_=self.X1d[:, :],
                    in_offset=bass.IndirectOffsetOnAxis(ap=self.slot_i[:, ex, col:col + 1], axis=0)),
                    r=["X1d", "slot_i"], w=[("Db", 2 * col), ("Db", 2 * col + 1)], eng="gpsimd", semkey=("gather", col))

        def transposes(ex):
            for k in range(8):
                b = self.bank()
                pk = ("ps", b)
                pv = ps[:, b, :].bitcast(BF16)
                for col in range(4):
                    self.T(lambda e, pv=pv, col=col, k=k: e.transpose(out=pv[:, col * 128:(col + 1) * 128],
                                                                      in_=xebv[:, col, k * 128:(k + 1) * 128], identity=self.ident_b[:]),
                           r=[("Db", 2 * col), ("Db", 2 * col + 1), "ident_b"], w=[pk])
                self.A(lambda e, pv=pv, k=k: e.copy(out=self.Eb[:, k, :], in_=pv[:, 0:512]), r=[pk], w=[("Eb", k)])

        gather(0)
        transposes(0)
        gather(1)
        for ex in range(NE):
            ek = [("Eb", k) for k in range(8)]
            hid = [self.uT[:, f, 0:512] for f in range(8)] + [self.Cb[:, f, :] for f in range(8)]
            hk = [("uT", f) for f in range(8)] + [("Cb", f) for f in range(8)]
            for i in range(4):
                wg, wgk = self.ws_next()
                wu, wuk = self.ws_next()
                for jj in range(4):
                    f = i * 4 + jj
                    bg, bu = self.bank(), self.bank()
                    for (w_, wk_, b_) in ((wg, wgk, bg), (wu, wuk, bu)):
                        for k in range(8):
                            self.T(lambda e, w_=w_, b_=b_, k=k, jj=jj: e.matmul(ps[:, b_, :], lhsT=w_[:, k, jj * 128:(jj + 1) * 128],
                                                                               rhs=self.Eb[:, k, :], start=(k == 0), stop=(k == 7)),
                                   r=[wk_, ("Eb", k)], w=[("ps", b_)])
                    ti = self.tmpk()
                    sg = self.tmp[ti]
                    kt = ("tmp", ti)
                    self.A(lambda e, bg=bg, sg=sg: e.activation(out=sg[:], in_=ps[:, bg, :], func=AF.Silu), r=[("ps", bg)], w=[kt])
                    self.V(lambda e, bu=bu, sg=sg, f=f: e.tensor_tensor(out=hid[f], in0=sg[:], in1=ps[:, bu, :], op=ALU.mult),
                           r=[kt, ("ps", bu)], w=[hk[f]])
            if ex + 1 < NE:
                transposes(ex + 1)
            ye = self.Bf
            yev = ye[:].rearrange("p (c a) n -> p c (a n)", c=4)
            yk = [("Bf", c) for c in range(8)]
            for ch in range(2):
                wd0, wdk0 = self.ws_next()
                wd1, wdk1 = self.ws_next()
                for col in range(4):
                    b = self.bank()
                    pk = ("ps", b)
                    for f in range(16):
                        w_, wk_ = (wd0, wdk0) if f < 8 else (wd1, wdk1)
                        self.T(lambda e, b=b, f=f, col=col, w_=w_: e.matmul(ps[:, b, :], lhsT=hid[f][:, col * 128:(col + 1) * 128],
                                                                           rhs=w_[:, f % 8, :], start=(f == 0), stop=(f == 15)),
                               r=[wk_, hk[f]], w=[pk])
                    self.V(lambda e, b=b, col=col, ch=ch, ex=ex: e.tensor_scalar(
                        out=yev[:, col, ch * 512:(ch + 1) * 512], in0=ps[:, b, :], scalar1=self.slot_g[:, ex, col:col + 1], scalar2=None,
                        op0=ALU.mult), r=[pk, "slot_g"], w=[("Bf", col * 2), ("Bf", col * 2 + 1)])
            if ex + 2 < NE:
                gather(ex + 2)
            for col in range(4):
                self.dma(lambda e, ex=ex, col=col: e.indirect_dma_start(
                    out=self.ACCd[:, :], out_offset=bass.IndirectOffsetOnAxis(ap=self.slot_i[:, ex, col:col + 1], axis=0),
                    in_=yev[:, col, :], in_offset=None, compute_op=ALU.add),
                    r=yk + ["slot_i"] + (["ACCd"] if ex == 0 else [("sc", ex - 1, c) for c in range(4)]),
                    w=[("sc", ex, col)], eng="gpsimd", semkey=("scat", col))

    def final(self, l):
        p = self.p
        self.load_bc(0, p["ln2_g"][l:l + 1, :])
        self.load_bc(1, p["ln2_b"][l:l + 1, :])
        def f_load(sg):
            rw = self.row[sg % 3]
            self.dma(lambda e, rw=rw, r0=sg * 128: e.dma_start(out=rw[:], in_=self.ACCd[r0:r0 + 128, :]),
                     r=["ACCd"] + [("sc", NE - 1, c) for c in range(4)], w=[("row", sg % 3)])

        f_load(0)
        f_load(1)
        for sg in range(S_LEN // 128):
            r0 = sg * 128
            rw = self.row[sg % 3]
            rk = ("row", sg % 3)
            self.ln_rows(rw[:], rw[:], 0, 1, srck=rk, dstk=rk, par=sg % 2)
            self.dma(lambda e, rw=rw, r0=r0: e.dma_start(out=self.out[r0:r0 + 128, :], in_=rw[:]), r=[rk], w=["out"], semkey=("st", rk))
            if sg + 2 < S_LEN // 128:
                f_load(sg + 2)

    def build(self, stop_after=None):
        with ExitStack() as st:
            self.st = st
            self.declare()
            for nm, shp, dt in self.dbg_decl:
                self.dbg_out(nm, shp, dt)
            self.alloc()
            self.S = Sched(self.nc, st)
            self.setup_consts()
            p = self.p
            for li, l in enumerate(self.layers):
                self.load_layer_params(li)
                if l == 0:
                    src, gb = self.x_in, (p["ln0_g"], p["ln0_b"])
                else:
                    src, gb = self.ACCd, (p["ln2_g"][li - 1:li, :], p["ln2_b"][li - 1:li, :])
                self.pass1(li, src, gb)
                if stop_after == "pass1":
                    break
                self.pass2(li)
                if stop_after == "pass2":
                    break
                self.topk(li)
                if stop_after == "topk":
                    break
                self.moe(li)
            if stop_after is None:
                self.final(len(self.layers) - 1)
                fin = ["out"]
            else:
                fin = []
            self.debug_dumps(stop_after)
            fin += ["dbg_" + n for n in self.dbg_written]
            self.S.finish("sync", fin)
            self.S.emit()
        return self.nc

    dbg_decl = ()
    dbg_written = ()

    def debug_dumps(self, stop_after):
        pass


def rope_tables():
    t = np.arange(S_LEN)
    row = (t // 64).astype(np.float32)
    col = (t % 64).astype(np.float32)
    axis_dim = DH // 2
    freqs = (1.0 / (np.float32(10000.0) ** (np.arange(0, axis_dim, 2, dtype=np.float32) / np.float32(axis_dim)))).astype(np.float32)
    ang = np.concatenate([row[:, None] * freqs[None], col[:, None] * freqs[None]], axis=-1).astype(np.float32)
    cos = np.cos(ang).astype(np.float32)
    sin = np.sin(ang).astype(np.float32)
    C = np.repeat(cos.T, 2, axis=0)
    Sg = np.repeat(sin.T, 2, axis=0)
    sign = np.where(np.arange(DH) % 2 == 0, -1.0, 1.0).astype(np.float32)[:, None]
    return np.ascontiguousarray(C), np.ascontiguousarray(Sg * sign)


def const_inputs():
    ident = np.eye(128, dtype=np.float32)
    swap = np.zeros((128, 128), np.float32)
    idx = np.arange(128)
    swap[idx, idx ^ 1] = 1.0
    C, Sg = rope_tables()
    return {"c_ident": ident, "c_swap": swap, "c_ropeC": C, "c_ropeS": Sg}


def core_inputs(inputs, b, layers=None):
    m = {"x": np.ascontiguousarray(inputs["x"][b])}
    if layers is not None:
        inputs = dict(inputs)
        for nm in inputs:
            if nm not in ("x", "ln0_g", "ln0_b"):
                inputs[nm] = np.ascontiguousarray(inputs[nm][list(layers)])
    L = NL if layers is None else len(layers)
    m["ln0_g"] = inputs["ln0_g"].reshape(1, D)
    m["ln0_b"] = inputs["ln0_b"].reshape(1, D)
    m["b_in"] = inputs["b_in"].reshape(L, N_IN // 128, 128)
    for nm in ("conv_dw_b", "conv_ln_g", "conv_ln_b", "conv_pw_b"):
        m[nm] = inputs[nm].reshape(L, 8, 128)
    m["q_norm_g"] = inputs["q_norm_g"].reshape(L, 1, DH)
    m["k_norm_g"] = inputs["k_norm_g"].reshape(L, 1, DH)
    for nm in ("w_in", "conv_dw", "conv_pw_w", "w_o", "w_out", "b_out", "ln1_g", "ln1_b", "w_router", "w_gate", "w_up",
               "w_down", "ln2_g", "ln2_b"):
        m[nm] = inputs[nm]
    m.update(const_inputs())
    return m


def kernel(**inputs):
    inputs = {k: np.asarray(v) for k, v in inputs.items()}
    mk = MK()
    nc = mk.build()
    n = 4
    in_maps = [core_inputs(inputs, c) for c in range(n)]
    res = run_bass_kernel_spmd(nc, in_maps, core_ids=list(range(n)))
    out = np.stack([np.asarray(res.results[c]["out"]) for c in range(n)], axis=0)
    return out.astype(np.float32)
```

```python
import math
import numpy as np
from contextlib import ExitStack
import concourse.bass as bass
import concourse.mybir as mybir
from concourse.bass_utils import run_bass_kernel_spmd

F32 = mybir.dt.float32
BF16 = mybir.dt.bfloat16
U32 = mybir.dt.uint32
AF = mybir.ActivationFunctionType
ALU = mybir.AluOpType

S_LEN = 4096
D = 1024
NL = 4
TT = 512
NT = S_LEN // TT
XW = 544
PADC = 16
NH = 8
NKV = 2
DH = 128
NE = 16
CAP = 512
DFF = 2048
KC = 31
N_IN = 5632
OFF_Q = 2048
OFF_K = 3072
OFF_V = 3328
OFF_GC = 3584
OFF_GA = 4608
LN_EPS = 1e-5
RMS_EPS = 1e-6
ALPHA = (2.0 * NL) ** 0.25
ATT_SCALE = 1.0 / math.sqrt(DH)
NRING = 4
NTMP = 10

ENGS = ("tensor", "vector", "scalar", "gpsimd", "sync")


class Sched:
    def __init__(self, nc, stack):
        self.nc = nc
        self.stack = stack
        self.prog = {e: [] for e in ENGS}
        self.cnt = {e: 0 for e in ENGS}
        self.esem = {e: stack.enter_context(nc.semaphore("es_" + e)) for e in ENGS}
        self.known = {e: {} for e in ENGS}
        self.last_w = {}
        self.readers = {}
        self.dsem = {}
        self.dcnt = {}
        self.nops = 0
        self.nwaits = 0

    def _sem_for_key(self, key):
        if key not in self.dsem:
            self.dsem[key] = self.stack.enter_context(self.nc.semaphore("ds%d" % len(self.dsem)))
            self.dcnt[key] = 0
        return self.dsem[key]

    def _waits(self, eng, reads, writes):
        waits = {}
        own = self.esem[eng].num

        def need(sv, raw):
            s, v = sv
            if s.num == own and not raw and eng == "tensor":
                return
            cur = waits.get(s.num)
            if cur is None or cur[1] < v:
                waits[s.num] = (s, v)

        for k in reads:
            for sv in self.last_w.get(k, {}).values():
                need(sv, True)
        for k in writes:
            for sv in self.last_w.get(k, {}).values():
                need(sv, False)
            for sv in self.readers.get(k, {}).values():
                need(sv, False)
        out = []
        kn = self.known[eng]
        for num, (s, v) in waits.items():
            if kn.get(num, 0) >= v:
                continue
            kn[num] = v
            out.append((s, v))
        return out

    def _commit(self, s, v, reads, writes):
        for k in writes:
            self.last_w.setdefault(k, {})[s.num] = (s, v)
            self.readers[k] = {}
        for k in reads:
            self.readers.setdefault(k, {})[s.num] = (s, v)

    def op(self, eng, fn, reads=(), writes=()):
        waits = self._waits(eng, reads, writes)
        self.cnt[eng] += 1
        s = self.esem[eng]
        self.prog[eng].append((waits, fn, s, 1))
        self._commit(s, self.cnt[eng], reads, writes)
        self.nops += 1
        self.nwaits += len(waits)

    def dma(self, eng, fn, reads=(), writes=(), semkey=None):
        waits = self._waits(eng, reads, writes)
        key = semkey if semkey is not None else writes[0]
        s = self._sem_for_key(key)
        self.dcnt[key] += 16
        self.prog[eng].append((waits, fn, s, 16))
        self._commit(s, self.dcnt[key], reads, writes)
        self.nops += 1
        self.nwaits += len(waits)

    def finish(self, eng, keys):
        waits = self._waits(eng, keys, ())
        self.prog[eng].append((waits, None, None, 0))

    def emit(self):
        with self.nc.Block() as block:
            for e in ENGS:
                prog = self.prog[e]
                if not prog:
                    continue

                def body(engine, prog=prog):
                    for waits, fn, s, inc in prog:
                        for ws, wv in waits:
                            engine.wait_ge(ws, wv)
                        if fn is not None:
                            fn(engine).then_inc(s, inc)

                getattr(block, e)(body)


class MK:
    def __init__(self, layers=(0, 1, 2, 3), first=True, last=True, dbg=()):
        self.layers = list(layers)
        self.first = first
        self.last = last
        self.dbg = set(dbg)
        self.nc = bass.Bass("TRN2", target_bir_lowering=False)
        self.bank_rr = 0
        self.pinned = set()
        self.tmp_rr = 0

    def din(self, name, shape, dt=F32):
        return self.nc.dram_tensor(name, list(shape), dt, kind="ExternalInput").ap()

    def sb(self, name, shape, dt):
        return self.st.enter_context(self.nc.sbuf_tensor(name, list(shape), dt))

    def declare(self):
        nc = self.nc
        L = len(self.layers)
        self.x_in = self.din("x", [S_LEN, D])
        self.p = {}
        for nm, shp in [("ln0_g", [1, D]), ("ln0_b", [1, D]), ("w_in", [L, D, N_IN]), ("b_in", [L, N_IN // 128, 128]),
                        ("conv_dw", [L, KC, D]), ("conv_dw_b", [L, 8, 128]), ("conv_ln_g", [L, 8, 128]),
                        ("conv_ln_b", [L, 8, 128]), ("conv_pw_w", [L, D, D]), ("conv_pw_b", [L, 8, 128]),
                        ("q_norm_g", [L, 1, DH]), ("k_norm_g", [L, 1, DH]), ("w_o", [L, D, D]), ("w_out", [L, D, D]),
                        ("b_out", [L, D]), ("ln1_g", [L, D]), ("ln1_b", [L, D]), ("w_router", [L, D, NE]),
                        ("w_gate", [L, NE, D, DFF]), ("w_up", [L, NE, D, DFF]), ("w_down", [L, NE, DFF, D]),
                        ("ln2_g", [L, D]), ("ln2_b", [L, D])]:
            self.p[nm] = self.din(nm, shp)
        self.c_ident = self.din("c_ident", [128, 128])
        self.c_swap = self.din("c_swap", [128, 128])
        self.c_ropeC = self.din("c_ropeC", [128, S_LEN])
        self.c_ropeS = self.din("c_ropeS", [128, S_LEN])
        self.out = nc.dram_tensor("out", [S_LEN, D], F32, kind="ExternalOutput").ap()
        self.Xd = nc.dram_tensor("Xd", [S_LEN, D], F32, kind="Internal").ap()
        self.XTd = nc.dram_tensor("XTd", [D, S_LEN + 2 * PADC], BF16, kind="Internal").ap()
        self.X1d = nc.dram_tensor("X1d", [S_LEN, D], BF16, kind="Internal").ap()
        self.ACCd = nc.dram_tensor("ACCd", [S_LEN, D], F32, kind="Internal").ap()
        self.dbg_t = {}

    def dbg_out(self, name, shape, dt=F32):
        t = self.nc.dram_tensor("dbg_" + name, list(shape), dt, kind="ExternalOutput").ap()
        self.dbg_t[name] = t
        return t

    def alloc(self):
        sb = self.sb
        self.KT = sb("KT", [128, NKV, S_LEN], BF16)
        self.Vt = sb("Vt", [128, S_LEN // 128, NKV * DH], BF16)
        self.aff = sb("aff", [128, S_LEN // 128, NE], F32)
        self.ident_f = sb("ident_f", [128, 128], F32)
        self.ident_b = sb("ident_b", [128, 128], BF16)
        self.swapm = sb("swapm", [128, 128], F32)
        self.ones_d = sb("ones_d", [128, 128], F32)
        self.ones_h = sb("ones_h", [128, 128], F32)
        self.ones_b = sb("ones_b", [128, 128], BF16)
        self.prm_rows = sb("prm_rows", [80, 128], F32)
        self.prm = sb("prm", [128, 80], F32)
        self.P2 = [sb("P2_%d" % i, [128, TT], BF16) for i in range(4)]
        self.dw_pp = sb("dw_pp", [128, 8, KC], F32)
        self.bc = [sb("bc%d" % i, [128, D], F32) for i in range(3)]
        self.bv_bc = sb("bv_bc", [128, NKV * DH], F32)
        self.wr = sb("wr", [128, 8, NE], F32)
        self.ring = [sb("ring%d" % i, [128, 8, 512], BF16) for i in range(NRING)]
        self.xT = sb("xT", [128, 8, XW], BF16)
        self.uT = sb("uT", [128, 8, XW], BF16)
        self.Bf = sb("Bf", [128, 8, TT], F32)
        self.Cb = sb("Cb", [128, 8, TT], BF16)
        self.Db = sb("Db", [128, 8, TT], BF16)
        self.Eb = sb("Eb", [128, 8, TT], BF16)
        self.tmp = [sb("tmp%d" % i, [128, TT], F32) for i in range(NTMP)]
        self.PT = [sb("PT%d" % i, [128, 2 * TT], BF16) for i in range(3)]
        self.ropeC = [sb("ropeC%d" % i, [128, TT], F32) for i in range(2)]
        self.ropeS = [sb("ropeS%d" % i, [128, TT], F32) for i in range(2)]
        self.row = [sb("row%d" % i, [128, D], F32) for i in range(3)]
        self.rowb = sb("rowb", [128, D], BF16)
        self.x1T = sb("x1T", [128, 8, 128], F32)
        self.dg = [sb("dg%d" % i, [128, 128], BF16) for i in range(8)]
        self.small = sb("small", [128, 64], F32)
        self.bst = [sb("bst%d" % i, [128, 2, 6], F32) for i in range(2)]
        self.zpad = sb("zpad", [128, 8, PADC], BF16)
        self.topv = sb("topv", [NE, CAP], F32)
        self.topi = sb("topi", [NE, CAP], U32)
        self.topif = sb("topif", [NE, CAP], F32)
        self.slot_g = sb("slot_g", [128, NE, 4], F32)
        self.slot_i = sb("slot_i", [128, NE, 4], U32)
        self.ps = self.st.enter_context(self.nc.psum_tensor("ps", [128, 8, 512], F32))

    def bank(self, pin=False):
        while True:
            b = self.bank_rr
            self.bank_rr = (self.bank_rr + 1) % 8
            if b not in self.pinned:
                break
        if pin:
            self.pinned.add(b)
        return b

    def unpin(self, b):
        self.pinned.discard(b)

    def T(self, fn, r=(), w=()):
        self.S.op("tensor", fn, r, w)

    def V(self, fn, r=(), w=()):
        self.S.op("vector", fn, r, w)

    def A(self, fn, r=(), w=()):
        self.S.op("scalar", fn, r, w)

    def G(self, fn, r=(), w=()):
        self.S.op("gpsimd", fn, r, w)

    def dma(self, fn, r=(), w=(), eng="sync", semkey=None):
        self.S.dma(eng, fn, r, w, semkey)

    def tmpk(self):
        i = self.tmp_rr
        self.tmp_rr = (self.tmp_rr + 1) % NTMP
        return i

    def ws_reset(self, specs, hw_mask=None, extra_slots=()):
        self.ws_specs = specs
        self.ws_slots = [(self.ring[i][:], ("ring", i)) for i in range(NRING)] + list(extra_slots)
        self.ws_n = len(self.ws_slots)
        self.ws_issued = 0
        self.ws_used = 0
        self.ws_hw = hw_mask
        if hw_mask is not None:
            self.ws_hwlist = [i for i in range(len(specs)) if hw_mask[i]]
            self.ws_hwdma = 0
            self.stg = [(self.KT[:].rearrange("p a n -> p (a n)").bitcast(F32).rearrange("p (k n) -> p k n", k=8), "KT"),
                        (self.Vt[:].rearrange("p a n -> p (a n)").bitcast(F32).rearrange("p (k n) -> p k n", k=8), "Vt")]
            self.ws_hw_dma_upto(2)

    def ws_hw_dma_upto(self, n):
        n = min(n, len(self.ws_hwlist))
        while self.ws_hwdma < n:
            hn = self.ws_hwdma
            i = self.ws_hwlist[hn]
            stg, sk = self.stg[hn % 2]
            for (c0, ncol, src) in self.ws_specs[i]:
                self.dma(lambda e, stg=stg, c0=c0, ncol=ncol, src=src:
                         e.dma_start(out=stg[:, :, c0:c0 + ncol], in_=src.rearrange("(k p) n -> p k n", p=128)),
                         r=(), w=[sk], eng=("sync" if hn % 2 == 0 else "scalar"))
            self.ws_hwdma += 1

    def ws_issue_upto(self, n):
        n = min(n, len(self.ws_specs))
        while self.ws_issued < n:
            i = self.ws_issued
            sap, skey = self.ws_slots[i % self.ws_n]
            if self.ws_hw is not None and self.ws_hw[i]:
                hn = self.ws_hwlist.index(i)
                stg, sk = self.stg[hn % 2]
                self.A(lambda e, sap=sap, stg=stg: e.copy(out=sap, in_=stg), r=[sk], w=[skey])
                self.ws_hw_dma_upto(hn + 3)
            else:
                for (c0, ncol, src) in self.ws_specs[i]:
                    self.dma(lambda e, sap=sap, c0=c0, ncol=ncol, src=src:
                             e.dma_start(out=sap[:, :, c0:c0 + ncol],
                                         in_=src.rearrange("(k p) n -> p k n", p=128)),
                             r=(), w=[skey], eng="gpsimd", semkey=("sw", skey))
            self.ws_issued += 1

    def ws_next(self):
        i = self.ws_used
        self.ws_issue_upto(i + self.ws_n - 1)
        self.ws_used += 1
        return self.ws_slots[i % self.ws_n]

    def setup_consts(self):
        self.dma(lambda e: e.dma_start(out=self.ident_f[:], in_=self.c_ident), w=["ident_f"])
        self.dma(lambda e: e.dma_start(out=self.swapm[:], in_=self.c_swap), w=["swapm"])
        self.V(lambda e: e.tensor_copy(out=self.ident_b[:], in_=self.ident_f[:]), r=["ident_f"], w=["ident_b"])
        self.V(lambda e: e.memset(self.ones_d[:], 1.0 / D), w=["ones_d"])
        self.V(lambda e: e.memset(self.ones_h[:], 1.0 / DH), w=["ones_h"])
        self.V(lambda e: e.memset(self.ones_b[:], 1.0), w=["ones_b"])
        self.V(lambda e: e.memset(self.zpad[:], 0.0), w=["zpad"])
        for c0 in (0, PADC + S_LEN):
            self.dma(lambda e, c0=c0: e.dma_start(out=self.XTd[:, c0:c0 + PADC].rearrange("(k p) n -> p k n", p=128),
                                                  in_=self.zpad[:]), r=["zpad"], w=["XTd"], semkey=("st", "zpad"))

    def load_bc(self, i, src_row):
        self.dma(lambda e: e.dma_start(out=self.bc[i][:], in_=src_row.to_broadcast([128, D])), w=[("bc", i)])

    P_BIN = 0
    P_DWB = 44
    P_CLG = 52
    P_CLB = 60
    P_PWB = 68
    P_QG = 76
    P_KG = 77

    def load_layer_params(self, l):
        p = self.p
        rows = self.prm_rows
        segs = [(self.P_BIN, 44, p["b_in"][l]), (self.P_DWB, 8, p["conv_dw_b"][l]), (self.P_CLG, 8, p["conv_ln_g"][l]),
                (self.P_CLB, 8, p["conv_ln_b"][l]), (self.P_PWB, 8, p["conv_pw_b"][l]), (self.P_QG, 1, p["q_norm_g"][l]),
                (self.P_KG, 1, p["k_norm_g"][l])]
        for (r0, n, src) in segs:
            self.dma(lambda e, r0=r0, n=n, src=src: e.dma_start(out=rows[r0:r0 + n, :], in_=src), w=["prm_rows"])
        b = self.bank()
        pk = ("ps", b)
        self.T(lambda e, b=b: e.transpose(out=self.ps[:, b, 0:78], in_=rows[0:78, :], identity=self.ident_f[0:78, 0:78]),
               r=["prm_rows", "ident_f"], w=[pk])
        self.V(lambda e, b=b: e.tensor_copy(out=self.prm[:, 0:78], in_=self.ps[:, b, 0:78]), r=[pk], w=["prm"])
        dw_rows = self.row[2][0:KC, :]
        self.dma(lambda e: e.dma_start(out=dw_rows, in_=p["conv_dw"][l]), w=[("row", 2)])
        for c in range(8):
            b = self.bank()
            pk = ("ps", b)
            self.T(lambda e, b=b, c=c: e.transpose(out=self.ps[:, b, 0:KC], in_=dw_rows[:, c * 128:(c + 1) * 128],
                                                   identity=self.ident_f[0:KC, 0:KC]),
                   r=[("row", 2), "ident_f"], w=[pk])
            self.V(lambda e, b=b, c=c: e.tensor_copy(out=self.dw_pp[:, c, :], in_=self.ps[:, b, 0:KC]), r=[pk], w=["dw_pp"])
        self.dma(lambda e: e.dma_start(out=self.bv_bc[:], in_=p["b_in"][l].rearrange("c p -> (c p)")[OFF_V:OFF_V + 256]
                                       .rearrange("(o n) -> o n", o=1).to_broadcast([128, 256])), w=["bv_bc"])
        self.dma(lambda e: e.dma_start(out=self.wr[:], in_=p["w_router"][l].rearrange("(k p) n -> p k n", p=128)), w=["wr"])

    def ln_rows(self, src, dst, gi, bi, eps=LN_EPS, srck=None, dstk=None, par=0):
        sm = self.small
        c0 = par * 4
        bst = self.bst[par]
        kb0, kb1, kmv, ksd, krs = (("lnst", par, i) for i in range(5))
        self.V(lambda e: e.bn_stats(out=bst[:, 0, :], in_=src[:, 0:512]), r=[srck], w=[kb0])
        self.V(lambda e: e.bn_stats(out=bst[:, 1, :], in_=src[:, 512:1024]), r=[srck], w=[kb1])
        self.V(lambda e: e.bn_aggr(out=sm[:, c0:c0 + 2], in_=bst[:].rearrange("p a b -> p (a b)")), r=[kb0, kb1], w=[kmv])
        self.A(lambda e: e.activation(out=sm[:, c0 + 2:c0 + 3], in_=sm[:, c0 + 1:c0 + 2], func=AF.Sqrt, bias=eps, scale=1.0), r=[kmv], w=[ksd])
        self.V(lambda e: e.reciprocal(out=sm[:, c0 + 3:c0 + 4], in_=sm[:, c0 + 2:c0 + 3]), r=[ksd], w=[krs])
        self.V(lambda e: e.tensor_scalar(out=dst, in0=src, scalar1=sm[:, c0:c0 + 1], scalar2=sm[:, c0 + 3:c0 + 4],
                                         op0=ALU.subtract, op1=ALU.mult), r=[srck, kmv, krs], w=[dstk])
        self.V(lambda e: e.tensor_tensor(out=dst, in0=dst, in1=self.bc[gi][:], op=ALU.mult), r=[dstk, ("bc", gi)], w=[dstk])
        self.V(lambda e: e.tensor_tensor(out=dst, in0=dst, in1=self.bc[bi][:], op=ALU.add), r=[dstk, ("bc", bi)], w=[dstk])

    def nr_stageA(self, pk, psap, bias_ap):
        i_raw, i_sq, i_rs = self.tmpk(), self.tmpk(), self.tmpk()
        stt = {"raw": self.tmp[i_raw], "sq": self.tmp[i_sq], "rs": self.tmp[i_rs],
               "kr": ("tmp", i_raw), "ks": ("tmp", i_sq), "krs": ("tmp", i_rs)}
        raw, sq = stt["raw"], stt["sq"]
        self.A(lambda e: e.activation(out=raw[:], in_=psap, func=AF.Identity, bias=bias_ap, scale=1.0), r=[pk, "prm"], w=[stt["kr"]])
        self.A(lambda e: e.activation(out=sq[:], in_=psap, func=AF.Square, bias=bias_ap, scale=1.0), r=[pk, "prm"], w=[stt["ks"]])
        return stt

    def nr_stageB(self, stt, g_ap):
        raw, sq, rs = stt["raw"], stt["sq"], stt["rs"]
        kr, ks, krs = stt["kr"], stt["ks"], stt["krs"]
        b2 = self.bank()
        pk2 = ("ps", b2)
        self.T(lambda e: e.matmul(self.ps[:, b2, :], lhsT=self.ones_h[:], rhs=sq[:], start=True, stop=True),
               r=[ks, "ones_h"], w=[pk2])
        self.A(lambda e: e.activation(out=rs[:], in_=self.ps[:, b2, :], func=AF.Ln, bias=RMS_EPS, scale=1.0), r=[pk2], w=[krs])
        self.A(lambda e: e.activation(out=rs[:], in_=rs[:], func=AF.Exp, scale=-0.5), r=[krs], w=[krs])
        self.V(lambda e: e.scalar_tensor_tensor(out=raw[:], in0=raw[:], scalar=g_ap, in1=rs[:], op0=ALU.mult, op1=ALU.mult),
               r=[kr, krs, "prm"], w=[kr])

    def nr_stageC(self, stt, rp, out_ap, outk):
        raw, sq = stt["raw"], stt["sq"]
        kr, ks = stt["kr"], stt["ks"]
        b3 = self.bank()
        pk3 = ("ps", b3)
        self.T(lambda e: e.matmul(self.ps[:, b3, :], lhsT=self.swapm[:], rhs=raw[:], start=True, stop=True),
               r=[kr, "swapm"], w=[pk3])
        self.V(lambda e: e.tensor_tensor(out=sq[:], in0=self.ps[:, b3, :], in1=self.ropeS[rp][:], op=ALU.mult),
               r=[pk3, ("ropeS", rp)], w=[ks])
        self.G(lambda e: e.tensor_tensor(out=raw[:], in0=raw[:], in1=self.ropeC[rp][:], op=ALU.mult), r=[kr, ("ropeC", rp)], w=[kr])
        self.V(lambda e: e.tensor_tensor(out=out_ap, in0=raw[:], in1=sq[:], op=ALU.add), r=[kr, ks], w=[outk])

    def norm_rope_pipe(self, n, projA, g_ap, rp, outs):
        stts = [None] * n
        for step in range(n + 2):
            if step < n:
                pk, psap, bias_ap = projA(step)
                stts[step] = self.nr_stageA(pk, psap, bias_ap)
            if 0 <= step - 1 < n:
                self.nr_stageB(stts[step - 1], g_ap)
            if 0 <= step - 2 < n:
                self.nr_stageC(stts[step - 2], rp, *outs[step - 2])

    def load_rope(self, t, rp):
        t0 = t * TT
        self.dma(lambda e: e.dma_start(out=self.ropeC[rp][:], in_=self.c_ropeC[:, t0:t0 + TT]), w=[("ropeC", rp)])
        self.dma(lambda e: e.dma_start(out=self.ropeS[rp][:], in_=self.c_ropeS[:, t0:t0 + TT]), w=[("ropeS", rp)])

    def pass1(self, l, src_d, gi_src):
        p = self.p
        g_row, b_row = gi_src
        self.load_bc(0, g_row)
        self.load_bc(1, b_row)
        self.ws_reset([[(0, 512, p["w_in"][l][:, OFF_K:OFF_K + 512])]])
        wkv, wk_key = self.ws_next()

        def p1_load(sgi):
            rw = self.row[sgi % 3]
            r0 = sgi * 128
            self.dma(lambda e, rw=rw, r0=r0: e.dma_start(out=rw[:], in_=src_d[r0:r0 + 128, :]),
                     r=["ACCd"] + [("sc", NE - 1, c) for c in range(4)], w=[("row", sgi % 3)])

        for t in range(NT):
            t0 = t * TT
            rp = t % 2
            self.load_rope(t, rp)
            for st in range(4):
                r0 = t0 + st * 128
                sgi = t * 4 + st
                if sgi == 0:
                    p1_load(0)
                    p1_load(1)
                rw = self.row[sgi % 3]
                rk = ("row", sgi % 3)
                self.ln_rows(rw[:], rw[:], 0, 1, srck=rk, dstk=rk, par=st % 2)
                self.dma(lambda e, rw=rw, r0=r0: e.dma_start(out=self.Xd[r0:r0 + 128, :], in_=rw[:]), r=[rk], w=["Xd"], semkey=("st", rk))
                for half in range(2):
                    b = self.bank()
                    pk = ("ps", b)
                    for j in range(4):
                        k = half * 4 + j
                        self.T(lambda e, b=b, j=j, k=k, rw=rw: e.transpose(out=self.ps[:, b, j * 128:(j + 1) * 128],
                                                                          in_=rw[:, k * 128:(k + 1) * 128], identity=self.ident_f[:]),
                               r=[rk, "ident_f"], w=[pk])
                    self.A(lambda e, b=b, half=half, st=st: e.copy(
                        out=self.xT[:, half * 4:half * 4 + 4, st * 128:(st + 1) * 128],
                        in_=self.ps[:, b, :].rearrange("p (j n) -> p j n", j=4)), r=[pk], w=["xT"])
                if sgi + 2 < S_LEN // 128:
                    p1_load(sgi + 2)
            self.dma(lambda e, t0=t0: e.dma_start(out=self.XTd[:, PADC + t0:PADC + t0 + TT].rearrange("(k p) n -> p k n", p=128),
                                                  in_=self.xT[:, :, 0:TT]), r=["xT"], w=["XTd"], semkey=("st", "xT"))
            def projK(h, t0=t0):
                b = self.bank()
                pk = ("ps", b)
                for k in range(8):
                    self.T(lambda e, b=b, k=k, h=h: e.matmul(self.ps[:, b, :], lhsT=wkv[:, k, h * 128:(h + 1) * 128],
                                                            rhs=self.xT[:, k, 0:TT], start=(k == 0), stop=(k == 7)),
                           r=[wk_key, "xT"], w=[pk])
                cb = self.P_BIN + (OFF_K // 128) + h
                return pk, self.ps[:, b, :], self.prm[:, cb:cb + 1]

            self.norm_rope_pipe(NKV, projK, self.prm[:, self.P_KG:self.P_KG + 1], rp,
                                [(self.KT[:, h, t0:t0 + TT], "KT") for h in range(NKV)])
            for st in range(4):
                b = self.bank()
                pk = ("ps", b)
                for k in range(8):
                    self.T(lambda e, b=b, k=k, st=st: e.matmul(self.ps[:, b, 0:256], lhsT=self.xT[:, k, st * 128:(st + 1) * 128],
                                                              rhs=wkv[:, k, 256:512], start=(k == 0), stop=(k == 7)),
                           r=[wk_key, "xT"], w=[pk])
                sg = t * 4 + st
                self.V(lambda e, b=b, sg=sg: e.tensor_tensor(out=self.Vt[:, sg, :], in0=self.ps[:, b, 0:256], in1=self.bv_bc[:],
                                                            op=ALU.add), r=[pk, "bv_bc"], w=["Vt"])

    def pass2_specs(self, l):
        p = self.p
        w_in = p["w_in"][l]
        one = []
        for i in range(4):
            one.append([(0, 256, w_in[:, i * 256:(i + 1) * 256]), (256, 256, w_in[:, D + i * 256:D + (i + 1) * 256])])
        for i in range(2):
            one.append([(0, 512, w_in[:, OFF_GC + i * 512:OFF_GC + (i + 1) * 512])])
        for i in range(2):
            one.append([(0, 512, p["conv_pw_w"][l][:, i * 512:(i + 1) * 512])])
        for i in range(2):
            one.append([(0, 512, w_in[:, OFF_Q + i * 512:OFF_Q + (i + 1) * 512])])
        for i in range(2):
            one.append([(0, 512, w_in[:, OFF_GA + i * 512:OFF_GA + (i + 1) * 512])])
            one.append([(0, 512, p["w_o"][l][:, i * 512:(i + 1) * 512])])
        for i in range(2):
            one.append([(0, 512, p["w_out"][l][:, i * 512:(i + 1) * 512])])
        return one

    def pass2(self, l):
        p = self.p
        prm = self.prm
        ps = self.ps
        self.load_bc(0, p["ln1_g"][l:l + 1, :])
        self.load_bc(1, p["ln1_b"][l:l + 1, :])
        self.load_bc(2, p["b_out"][l:l + 1, :])
        specs = []
        for t in range(NT):
            specs += self.pass2_specs(l)
        self.ws_reset(specs)
        for t in range(NT):
            t0 = t * TT
            rp = t % 2
            self.load_rope(t, rp)
            self.dma(lambda e, t0=t0: e.dma_start(out=self.xT[:], in_=self.XTd[:, t0:t0 + XW].rearrange("(k p) n -> p k n", p=128)),
                     r=["XTd"], w=["xT"])
            for i in range(4):
                wp, wk = self.ws_next()
                for cc in range(2):
                    c = i * 2 + cc
                    bA, bB, bC = self.bank(), self.bank(), self.bank()
                    kA, kB, kC = ("ps", bA), ("ps", bB), ("ps", bC)
                    for (col0, bm, bh, hc) in ((cc * 128, bA, bB, 0), (256 + cc * 128, bC, bB, 32)):
                        for k in range(8):
                            self.T(lambda e, k=k, col0=col0, bm=bm, wp=wp: e.matmul(ps[:, bm, :], lhsT=wp[:, k, col0:col0 + 128],
                                                                            rhs=self.xT[:, k, 0:512], start=(k == 0), stop=(k == 7)),
                                   r=[wk, "xT"], w=[("ps", bm)])
                        for k in range(8):
                            self.T(lambda e, k=k, col0=col0, bh=bh, hc=hc, wp=wp: e.matmul(ps[:, bh, hc:hc + 32], lhsT=wp[:, k, col0:col0 + 128],
                                                                                    rhs=self.xT[:, k, 512:544], start=(k == 0), stop=(k == 7)),
                                   r=[wk, "xT"], w=[("ps", bh)])
                    cv = self.P_BIN + c
                    cg = self.P_BIN + 8 + c
                    ti = self.tmpk()
                    sg = self.tmp[ti]
                    kt = ("tmp", ti)
                    ti2 = self.tmpk()
                    sg2 = self.tmp[ti2]
                    kt2 = ("tmp", ti2)
                    self.A(lambda e, bC=bC, cg=cg, sg=sg: e.activation(out=sg[:], in_=ps[:, bC, :], func=AF.Sigmoid,
                                                                      bias=prm[:, cg:cg + 1], scale=1.0), r=[kC, "prm"], w=[kt])
                    self.A(lambda e, bB=bB, cg=cg, sg2=sg2: e.activation(out=sg2[:, 0:32], in_=ps[:, bB, 32:64], func=AF.Sigmoid,
                                                                        bias=prm[:, cg:cg + 1], scale=1.0), r=[kB, "prm"], w=[kt2])
                    uk = ("uT", c)
                    self.V(lambda e, bA=bA, cv=cv, sg=sg, c=c: e.scalar_tensor_tensor(out=self.uT[:, c, 0:512], in0=ps[:, bA, :],
                                                                                     scalar=prm[:, cv:cv + 1], in1=sg[:],
                                                                                     op0=ALU.add, op1=ALU.mult), r=[kA, kt, "prm"], w=[uk])
                    self.V(lambda e, bB=bB, cv=cv, sg2=sg2, c=c: e.scalar_tensor_tensor(out=self.uT[:, c, 512:544], in0=ps[:, bB, 0:32],
                                                                                       scalar=prm[:, cv:cv + 1], in1=sg2[:, 0:32],
                                                                                       op0=ALU.add, op1=ALU.mult), r=[kB, kt2, "prm"], w=[uk])
                    if t == 0:
                        self.V(lambda e, c=c: e.memset(self.uT[:, c, 0:16], 0.0), w=[uk])
                    if t == NT - 1:
                        self.V(lambda e, c=c: e.memset(self.uT[:, c, 528:544], 0.0), w=[uk])
            bM = self.bank(pin=True)
            bQ = self.bank(pin=True)
            kM, kQ = ("ps", bM), ("ps", bQ)
            for c in range(8):
                b = self.bank()
                pk = ("ps", b)
                for k in range(KC):
                    di = (c * KC + k) % 8
                    dk = ("dg", di)
                    self.V(lambda e, di=di, c=c, k=k: e.tensor_scalar(out=self.dg[di][:], in0=self.ident_b[:],
                                                                     scalar1=self.dw_pp[:, c, k:k + 1], scalar2=None, op0=ALU.mult),
                           r=["ident_b", "dw_pp"], w=[dk])
                    self.T(lambda e, b=b, di=di, c=c, k=k: e.matmul(ps[:, b, :], lhsT=self.dg[di][:], rhs=self.uT[:, c, k + 1:k + 1 + 512],
                                                                   start=(k == 0), stop=(k == KC - 1)), r=[dk, ("uT", c)], w=[pk])
                cdb = self.P_DWB + c
                self.A(lambda e, b=b, c=c, cdb=cdb: e.activation(out=self.Bf[:, c, :], in_=ps[:, b, :], func=AF.Identity,
                                                                bias=prm[:, cdb:cdb + 1], scale=1.0), r=[pk, "prm"], w=[("Bf", c)])
                ti = self.tmpk()
                sq = self.tmp[ti]
                kt = ("tmp", ti)
                self.A(lambda e, b=b, cdb=cdb, sq=sq: e.activation(out=sq[:], in_=ps[:, b, :], func=AF.Square,
                                                                  bias=prm[:, cdb:cdb + 1], scale=1.0), r=[pk, "prm"], w=[kt])
                self.T(lambda e, c=c, bM=bM: e.matmul(ps[:, bM, :], lhsT=self.ones_d[:], rhs=self.Bf[:, c, :], start=(c == 0), stop=(c == 7)),
                       r=[("Bf", c), "ones_d"], w=[kM])
                self.T(lambda e, c=c, sq=sq, bQ=bQ: e.matmul(ps[:, bQ, :], lhsT=self.ones_d[:], rhs=sq[:], start=(c == 0), stop=(c == 7)),
                       r=[kt, "ones_d"], w=[kQ])
            im, iv = self.tmpk(), self.tmpk()
            mean, rstd = self.tmp[im], self.tmp[iv]
            kmn, krs = ("tmp", im), ("tmp", iv)
            self.A(lambda e, mean=mean, bM=bM: e.copy(out=mean[:], in_=ps[:, bM, :]), r=[kM], w=[kmn])
            self.V(lambda e, mean=mean, rstd=rstd, bM=bM: e.tensor_tensor(out=rstd[:], in0=mean[:], in1=ps[:, bM, :], op=ALU.mult), r=[kmn, kM], w=[krs])
            self.V(lambda e, rstd=rstd, bQ=bQ: e.tensor_tensor(out=rstd[:], in0=ps[:, bQ, :], in1=rstd[:], op=ALU.subtract), r=[kQ, krs], w=[krs])
            self.A(lambda e, rstd=rstd: e.activation(out=rstd[:], in_=rstd[:], func=AF.Ln, bias=LN_EPS, scale=1.0), r=[krs], w=[krs])
            self.A(lambda e, rstd=rstd: e.activation(out=rstd[:], in_=rstd[:], func=AF.Exp, scale=-0.5), r=[krs], w=[krs])
            self.unpin(bM)
            self.unpin(bQ)
            gcs = []
            for i in range(2):
                wgc, wgck = self.ws_next()
                for jj in range(4):
                    j = i * 4 + jj
                    bg = self.bank()
                    kg_ = ("ps", bg)
                    for k in range(8):
                        self.T(lambda e, bg=bg, k=k, jj=jj, wgc=wgc: e.matmul(ps[:, bg, :], lhsT=wgc[:, k, jj * 128:(jj + 1) * 128],
                                                                             rhs=self.xT[:, k, PADC:PADC + TT], start=(k == 0), stop=(k == 7)),
                               r=[wgck, "xT"], w=[kg_])
                    cgc = self.P_BIN + OFF_GC // 128 + j
                    ti = self.tmpk()
                    gs = self.tmp[ti]
                    kt = ("tmp", ti)
                    self.A(lambda e, bg=bg, cgc=cgc, gs=gs: e.activation(out=gs[:], in_=ps[:, bg, :], func=AF.Sigmoid,
                                                                        bias=prm[:, cgc:cgc + 1], scale=1.0), r=[kg_, "prm"], w=[kt])
                    gcs.append((gs, kt))
            for c in range(8):
                kb = ("Bf", c)
                self.V(lambda e, c=c, mean=mean: e.tensor_tensor(out=self.Bf[:, c, :], in0=self.Bf[:, c, :], in1=mean[:], op=ALU.subtract),
                       r=[kb, kmn], w=[kb])
                self.V(lambda e, c=c, rstd=rstd: e.tensor_tensor(out=self.Bf[:, c, :], in0=self.Bf[:, c, :], in1=rstd[:], op=ALU.mult),
                       r=[kb, krs], w=[kb])
                cg, cb2 = self.P_CLG + c, self.P_CLB + c
                self.A(lambda e, c=c, cg=cg, cb2=cb2: e.activation(out=self.Cb[:, c, :], in_=self.Bf[:, c, :], func=AF.Silu,
                                                                  bias=prm[:, cb2:cb2 + 1], scale=prm[:, cg:cg + 1]),
                       r=[kb, "prm"], w=[("Cb", c)])
            if getattr(self, 'dbg_stop', None) == 'conv':
                return
            for i in range(2):
                wpw, wpwk = self.ws_next()
                for jj in range(4):
                    j = i * 4 + jj
                    gs, kt = gcs[j]
                    by = self.bank()
                    ky = ("ps", by)
                    for c in range(8):
                        self.T(lambda e, by=by, c=c, jj=jj, wpw=wpw: e.matmul(ps[:, by, :], lhsT=wpw[:, c, jj * 128:(jj + 1) * 128],
                                                                             rhs=self.Cb[:, c, :], start=(c == 0), stop=(c == 7)),
                               r=[wpwk, ("Cb", c)], w=[ky])
                    cpb = self.P_PWB + j
                    self.V(lambda e, by=by, j=j, cpb=cpb, gs=gs: e.scalar_tensor_tensor(out=self.Bf[:, j, :], in0=ps[:, by, :],
                                                                                      scalar=prm[:, cpb:cpb + 1], in1=gs[:],
                                                                                      op0=ALU.add, op1=ALU.mult),
                           r=[ky, kt, "prm"], w=[("Bf", j)])
            if getattr(self, 'dbg_stop', None) == 'pw':
                return
            qw = {}

            def projQ(h):
                i, jj = divmod(h, 4)
                if jj == 0:
                    qw["wp"], qw["wk"] = self.ws_next()
                wp, wk = qw["wp"], qw["wk"]
                b = self.bank()
                pk = ("ps", b)
                for k in range(8):
                    self.T(lambda e, b=b, k=k, jj=jj, wp=wp: e.matmul(ps[:, b, :], lhsT=wp[:, k, jj * 128:(jj + 1) * 128],
                                                                     rhs=self.xT[:, k, PADC:PADC + TT], start=(k == 0), stop=(k == 7)),
                           r=[wk, "xT"], w=[pk])
                cq = self.P_BIN + OFF_Q // 128 + h
                return pk, ps[:, b, :], prm[:, cq:cq + 1]

            self.norm_rope_pipe(NH, projQ, prm[:, self.P_QG:self.P_QG + 1], rp,
                                [(self.Db[:, h, :], ("Db", h)) for h in range(NH)])
            NPAIR = S_LEN // 256
            units = [(g, qs) for g in range(NKV) for qs in range(4)]

            SP = ((0, 1), (2, 3), (4, 5))
            bO, bS = 6, 7
            kO, kS = ("ps", bO), ("ps", bS)

            def emit_S(f):
                n, j = divmod(f, NPAIR)
                g, qs = units[n]
                sp = SP[f % 3]
                qk = [("Db", g * 4 + hh) for hh in range(4)]
                for u in range(2):
                    kc = 2 * j + u
                    self.T(lambda e, b=sp[u], g=g, kc=kc, qs=qs: e.matmul(
                        ps[:, b, :].rearrange("p (h q) -> p h q", h=4), lhsT=self.KT[:, g, kc * 128:(kc + 1) * 128],
                        rhs=self.Db[:, g * 4:g * 4 + 4, qs * 128:(qs + 1) * 128], start=True, stop=True),
                        r=["KT"] + qk, w=[("ps", sp[u])])

            NF = len(units) * NPAIR
            pending = []
            emit_S(0)
            emit_S(1)
            for f in range(NF):
                n, j = divmod(f, NPAIR)
                g, qs = units[n]
                if f + 2 < NF:
                    emit_S(f + 2)
                sp = SP[f % 3]
                pi = f % 3
                pt = self.PT[pi]
                kp = ("PT", pi)
                self.A(lambda e, s0=sp[0], pt=pt: e.activation(out=pt[:].rearrange("p (u n) -> p u n", u=2), in_=ps[:, s0:s0 + 2, :],
                                                               func=AF.Exp, scale=ATT_SCALE),
                       r=[("ps", sp[0]), ("ps", sp[1])], w=[kp])
                for u in range(2):
                    kc = 2 * j + u
                    first = (j == 0 and u == 0)
                    last = (j == NPAIR - 1 and u == 1)
                    self.T(lambda e, g=g, kc=kc, pt=pt, u=u, first=first, last=last: e.matmul(
                        ps[:, bO, :], lhsT=self.Vt[:, kc, g * 128:(g + 1) * 128], rhs=pt[:, u * 512:(u + 1) * 512],
                        start=first, stop=last), r=["Vt", kp], w=[kO])
                while pending:
                    self.T(*pending.pop(0))
                p2i = f % 4
                p2 = self.P2[p2i]
                kp2 = ("P2", p2i)
                self.V(lambda e, pt=pt, p2=p2: e.tensor_tensor(out=p2[:], in0=pt[:, 0:512], in1=pt[:, 512:1024], op=ALU.add),
                       r=[kp], w=[kp2])
                if j % 2 == 1:
                    pp = self.P2[(f - 1) % 4]
                    self.V(lambda e, p2=p2, pp=pp: e.tensor_tensor(out=p2[:], in0=p2[:], in1=pp[:], op=ALU.add),
                           r=[kp2, ("P2", (f - 1) % 4)], w=[kp2])
                    sums_mm = (lambda e, p2=p2, j=j: e.matmul(ps[:, bS, :], lhsT=self.ones_b[:], rhs=p2[:],
                                                             start=(j == 1), stop=(j == NPAIR - 1)), ["ones_b", kp2], [kS])
                    if j == NPAIR - 1:
                        self.T(*sums_mm)
                    else:
                        pending.append(sums_mm)
                if j == NPAIR - 1:
                    ia, ib = self.tmpk(), self.tmpk()
                    ta, tb = self.tmp[ia], self.tmp[ib]
                    ka, kb = ("tmp", ia), ("tmp", ib)
                    self.V(lambda e, ta=ta: e.tensor_copy(out=ta[:], in_=ps[:, bO, :]), r=[kO], w=[ka])
                    self.A(lambda e, tb=tb: e.copy(out=tb[:], in_=ps[:, bS, :]), r=[kS], w=[kb])
                    self.V(lambda e, tb=tb: e.reciprocal(out=tb[:], in_=tb[:]), r=[kb], w=[kb])
                    ek = [("Eb", g * 4 + hh) for hh in range(4)]
                    self.V(lambda e, g=g, qs=qs, ta=ta, tb=tb: e.tensor_tensor(
                        out=self.Eb[:, g * 4:g * 4 + 4, qs * 128:(qs + 1) * 128],
                        in0=ta[:].rearrange("p (h q) -> p h q", h=4), in1=tb[:].rearrange("p (h q) -> p h q", h=4),
                        op=ALU.mult), r=[ka, kb] + ek, w=ek)
            for i in range(2):
                wga, wgak = self.ws_next()
                wwo, wwok = self.ws_next()
                for jj in range(4):
                    j = i * 4 + jj
                    bg = self.bank()
                    kg_ = ("ps", bg)
                    for k in range(8):
                        self.T(lambda e, bg=bg, k=k, jj=jj, wga=wga: e.matmul(ps[:, bg, :], lhsT=wga[:, k, jj * 128:(jj + 1) * 128],
                                                                             rhs=self.xT[:, k, PADC:PADC + TT], start=(k == 0), stop=(k == 7)),
                               r=[wgak, "xT"], w=[kg_])
                    cga = self.P_BIN + OFF_GA // 128 + j
                    ti = self.tmpk()
                    gs = self.tmp[ti]
                    kt = ("tmp", ti)
                    self.A(lambda e, bg=bg, cga=cga, gs=gs: e.activation(out=gs[:], in_=ps[:, bg, :], func=AF.Sigmoid,
                                                                        bias=prm[:, cga:cga + 1], scale=1.0), r=[kg_, "prm"], w=[kt])
                    by = self.bank()
                    ky = ("ps", by)
                    for h in range(8):
                        self.T(lambda e, by=by, h=h, jj=jj, wwo=wwo: e.matmul(ps[:, by, :], lhsT=wwo[:, h, jj * 128:(jj + 1) * 128],
                                                                             rhs=self.Eb[:, h, :], start=(h == 0), stop=(h == 7)),
                               r=[wwok, ("Eb", h)], w=[ky])
                    self.V(lambda e, by=by, gs=gs: e.tensor_tensor(out=gs[:], in0=gs[:], in1=ps[:, by, :], op=ALU.mult),
                           r=[kt, ky], w=[kt])
                    self.V(lambda e, j=j, gs=gs: e.tensor_tensor(out=self.Cb[:, j, :], in0=self.Bf[:, j, :], in1=gs[:], op=ALU.add),
                           r=[("Bf", j), kt], w=[("Cb", j)])
            wpa, wka = self.ws_next()
            wpb, wkb = self.ws_next()
            def wo_p1(st):
                r0 = t0 + st * 128
                sg = t * 4 + st
                par = st % 2
                tr, ar = self.row[par], self.row[2]
                ktr = ("row", par)
                bb = []
                for (wp, wk) in ((wpa, wka), (wpb, wkb)):
                    b = self.bank()
                    bb.append(b)
                    for k in range(8):
                        self.T(lambda e, b=b, k=k, st=st, wp=wp: e.matmul(ps[:, b, :], lhsT=self.Cb[:, k, st * 128:(st + 1) * 128],
                                                                         rhs=wp[:, k, :], start=(k == 0), stop=(k == 7)),
                               r=[wk, ("Cb", k)], w=[("ps", b)])
                for hf in range(2):
                    b = bb[hf]
                    self.V(lambda e, b=b, hf=hf, tr=tr: e.scalar_tensor_tensor(out=tr[:, hf * 512:(hf + 1) * 512], in0=tr[:, hf * 512:(hf + 1) * 512],
                                                                              scalar=ALPHA, in1=ps[:, b, :], op0=ALU.mult, op1=ALU.add),
                           r=[ktr, ("ps", b)], w=[ktr])
                self.V(lambda e, tr=tr: e.tensor_tensor(out=tr[:], in0=tr[:], in1=self.bc[2][:], op=ALU.add), r=[ktr, ("bc", 2)], w=[ktr])
                self.ln_rows(tr[:], tr[:], 0, 1, srck=ktr, dstk=ktr, par=par)

            def wo_p2(st):
                r0 = t0 + st * 128
                sg = t * 4 + st
                par = st % 2
                tr, ar = self.row[par], self.row[2]
                ktr = ("row", par)
                self.A(lambda e, tr=tr: e.copy(out=self.rowb[:], in_=tr[:]), r=[ktr], w=["rowb"])
                self.dma(lambda e, r0=r0: e.dma_start(out=self.X1d[r0:r0 + 128, :], in_=self.rowb[:]), r=["rowb"], w=["X1d"], semkey=("st", "rowb"))
                self.A(lambda e, tr=tr: e.mul(ar[:], tr[:], ALPHA), r=[ktr], w=[("row", 2)])
                self.dma(lambda e, r0=r0: e.dma_start(out=self.ACCd[r0:r0 + 128, :], in_=ar[:]), r=[("row", 2)] + [("sc", NE - 1, c) for c in range(4)], w=["ACCd"], semkey=("st", "row2"))
                if "x1" in self.dbg and l == 0:
                    self.dma(lambda e, r0=r0, tr=tr: e.dma_start(out=self.dbg_t["x1"][r0:r0 + 128, :], in_=tr[:]), r=[ktr], w=["dbg_x1"],
                             semkey=("st", "dbgx1"))
                for half in range(2):
                    b = self.bank()
                    pk = ("ps", b)
                    for jx in range(4):
                        k = half * 4 + jx
                        self.T(lambda e, b=b, jx=jx, k=k, tr=tr: e.transpose(out=ps[:, b, jx * 128:(jx + 1) * 128],
                                                                             in_=tr[:, k * 128:(k + 1) * 128], identity=self.ident_f[:]),
                               r=[ktr, "ident_f"], w=[pk])
                    self.A(lambda e, b=b, half=half: e.copy(out=self.x1T[:, half * 4:half * 4 + 4, :],
                                                           in_=ps[:, b, :].rearrange("p (j n) -> p j n", j=4)), r=[pk], w=["x1T"])
                b = self.bank()
                pk = ("ps", b)
                for k in range(8):
                    self.T(lambda e, b=b, k=k: e.matmul(ps[:, b, 0:NE], lhsT=self.x1T[:, k, :], rhs=self.wr[:, k, :],
                                                        start=(k == 0), stop=(k == 7)), r=["x1T", "wr"], w=[pk])
                sm = self.small
                c0 = 8 + par * 4
                e0 = 16 + par * 16
                kq = [("rt", par, i) for i in range(5)]
                self.V(lambda e, b=b, c0=c0: e.reduce_max(out=sm[:, c0:c0 + 1], in_=ps[:, b, 0:NE], axis=mybir.AxisListType.X), r=[pk], w=[kq[0]])
                self.V(lambda e, c0=c0: e.tensor_scalar(out=sm[:, c0 + 1:c0 + 2], in0=sm[:, c0:c0 + 1], scalar1=-1.0, scalar2=None, op0=ALU.mult),
                       r=[kq[0]], w=[kq[1]])
                self.A(lambda e, b=b, c0=c0, e0=e0: e.activation(out=sm[:, e0:e0 + NE], in_=ps[:, b, 0:NE], func=AF.Exp, bias=sm[:, c0 + 1:c0 + 2],
                                                                scale=1.0, accum_out=sm[:, c0 + 2:c0 + 3]), r=[pk, kq[1]], w=[kq[2], kq[3]])
                self.V(lambda e, c0=c0: e.reciprocal(out=sm[:, c0 + 3:c0 + 4], in_=sm[:, c0 + 2:c0 + 3]), r=[kq[3]], w=[kq[4]])
                self.V(lambda e, sg=sg, c0=c0, e0=e0: e.tensor_scalar(out=self.aff[:, sg, :], in0=sm[:, e0:e0 + NE], scalar1=sm[:, c0 + 3:c0 + 4],
                                                                     scalar2=None, op0=ALU.mult), r=[kq[2], kq[4]], w=["aff"])

            def wo_load(st):
                r0 = t0 + st * 128
                tr = self.row[st % 2]
                self.dma(lambda e, r0=r0, tr=tr: e.dma_start(out=tr[:], in_=self.Xd[r0:r0 + 128, :]), r=["Xd"], w=[("row", st % 2)])

            wo_load(0)
            wo_load(1)
            for step in range(5):
                if step < 4:
                    wo_p1(step)
                if step >= 1:
                    wo_p2(step - 1)
                    if step + 1 < 4:
                        wo_load(step + 1)

    def topk(self, l):
        ps = self.ps
        work = self.Bf[0:NE, :, :].rearrange("p a b -> p (a b)")
        wk = [("Bf", c) for c in range(8)]
        for bi in range(8):
            b = self.bank()
            pk = ("ps", b)
            for j in range(4):
                sg = bi * 4 + j
                self.T(lambda e, b=b, j=j, sg=sg: e.transpose(out=ps[0:NE, b, j * 128:(j + 1) * 128], in_=self.aff[:, sg, :],
                                                              identity=self.ident_f[:]), r=["aff", "ident_f"], w=[pk])
            self.V(lambda e, b=b, bi=bi: e.tensor_copy(out=work[:, bi * 512:(bi + 1) * 512], in_=ps[0:NE, b, :]), r=[pk], w=[wk[bi]])
        for r in range(CAP // 8):
            self.V(lambda e, r=r: e.max(out=self.topv[:, r * 8:(r + 1) * 8], in_=work), r=wk, w=["topv"])
            self.V(lambda e, r=r: e.max_index(out=self.topi[:, r * 8:(r + 1) * 8], in_max=self.topv[:, r * 8:(r + 1) * 8], in_values=work),
                   r=wk + ["topv"], w=["topi"])
            self.V(lambda e, r=r: e.match_replace(out=work, in_to_replace=self.topv[:, r * 8:(r + 1) * 8], in_values=work, imm_value=-1.0),
                   r=wk + ["topv"], w=wk)
        self.V(lambda e: e.tensor_copy(out=self.topif[:], in_=self.topi[:]), r=["topi"], w=["topif"])
        for (src, srck, dst, dstk) in ((self.topv, "topv", self.slot_g, "slot_g"), (self.topif, "topif", self.slot_i, "slot_i")):
            b = self.bank()
            pk = ("ps", b)
            for col in range(4):
                self.T(lambda e, b=b, col=col, src=src: e.transpose(out=ps[:, b, col * NE:(col + 1) * NE], in_=src[:, col * 128:(col + 1) * 128],
                                                                    identity=self.ident_f[0:NE, 0:NE]), r=[srck, "ident_f"], w=[pk])
            self.V(lambda e, b=b, dst=dst: e.tensor_copy(out=dst[:].rearrange("p e c -> p c e"),
                                                         in_=ps[:, b, 0:4 * NE].rearrange("p (c e) -> p c e", c=4)), r=[pk], w=[dstk])

    def moe_specs(self, l):
        p = self.p
        specs = []
        for ex in range(NE):
            wg, wu, wd = p["w_gate"][l, ex], p["w_up"][l, ex], p["w_down"][l, ex]
            for i in range(4):
                specs.append([(0, 512, wg[:, i * 512:(i + 1) * 512])])
                specs.append([(0, 512, wu[:, i * 512:(i + 1) * 512])])
            for ch in range(2):
                for rh in range(2):
                    specs.append([(0, 512, wd[rh * 1024:(rh + 1) * 1024, ch * 512:(ch + 1) * 512])])
        return specs

    def moe(self, l):
        ps = self.ps
        specs = self.moe_specs(l)
        self.ws_reset(specs, hw_mask=[(i % 3) != 2 for i in range(len(specs))],
                      extra_slots=[(self.xT[:, :, 0:512], "xT")])
        xeb = self.Db
        xebv = xeb[:].rearrange("p (c a) n -> p c (a n)", c=4)

        def gather(ex):
            for col in range(4):
                self.dma(lambda e, ex=ex, col=col: e.indirect_dma_start(
                    out=xebv[:, col, :], out_offset=None, in_=self.X1d[:, :],
                    in_offset=bass.IndirectOffsetOnAxis(ap=self.slot_i[:, ex, col:col + 1], axis=0)),
                    r=["X1d", "slot_i"], w=[("Db", 2 * col), ("Db", 2 * col + 1)], eng="gpsimd", semkey=("gather", col))

        def transposes(ex):
            for k in range(8):
                b = self.bank()
                pk = ("ps", b)
                pv = ps[:, b, :].bitcast(BF16)
                for col in range(4):
                    self.T(lambda e, pv=pv, col=col, k=k: e.transpose(out=pv[:, col * 128:(col + 1) * 128],
                                                                      in_=xebv[:, col, k * 128:(k + 1) * 128], identity=self.ident_b[:]),
                           r=[("Db", 2 * col), ("Db", 2 * col + 1), "ident_b"], w=[pk])
                self.A(lambda e, pv=pv, k=k: e.copy(out=self.Eb[:, k, :], in_=pv[:, 0:512]), r=[pk], w=[("Eb", k)])

        gather(0)
        transposes(0)
        gather(1)
        for ex in range(NE):
            ek = [("Eb", k) for k in range(8)]
            hid = [self.uT[:, f, 0:512] for f in range(8)] + [self.Cb[:, f, :] for f in range(8)]
            hk = [("uT", f) for f in range(8)] + [("Cb", f) for f in range(8)]
            for i in range(4):
                wg, wgk = self.ws_next()
                wu, wuk = self.ws_next()
                for jj in range(4):
                    f = i * 4 + jj
                    bg, bu = self.bank(), self.bank()
                    for (w_, wk_, b_) in ((wg, wgk, bg), (wu, wuk, bu)):
                        for k in range(8):
                            self.T(lambda e, w_=w_, b_=b_, k=k, jj=jj: e.matmul(ps[:, b_, :], lhsT=w_[:, k, jj * 128:(jj + 1) * 128],
                                                                               rhs=self.Eb[:, k, :], start=(k == 0), stop=(k == 7)),
                                   r=[wk_, ("Eb", k)], w=[("ps", b_)])
                    ti = self.tmpk()
                    sg = self.tmp[ti]
                    kt = ("tmp", ti)
                    self.A(lambda e, bg=bg, sg=sg: e.activation(out=sg[:], in_=ps[:, bg, :], func=AF.Silu), r=[("ps", bg)], w=[kt])
                    self.V(lambda e, bu=bu, sg=sg, f=f: e.tensor_tensor(out=hid[f], in0=sg[:], in1=ps[:, bu, :], op=ALU.mult),
                           r=[kt, ("ps", bu)], w=[hk[f]])
            if ex + 1 < NE:
                transposes(ex + 1)
            ye = self.Bf
            yev = ye[:].rearrange("p (c a) n -> p c (a n)", c=4)
            yk = [("Bf", c) for c in range(8)]
            for ch in range(2):
                wd0, wdk0 = self.ws_next()
                wd1, wdk1 = self.ws_next()
                for col in range(4):
                    b = self.bank()
                    pk = ("ps", b)
                    for f in range(16):
                        w_, wk_ = (wd0, wdk0) if f < 8 else (wd1, wdk1)
                        self.T(lambda e, b=b, f=f, col=col, w_=w_: e.matmul(ps[:, b, :], lhsT=hid[f][:, col * 128:(col + 1) * 128],
                                                                           rhs=w_[:, f % 8, :], start=(f == 0), stop=(f == 15)),
                               r=[wk_, hk[f]], w=[pk])
                    self.V(lambda e, b=b, col=col, ch=ch, ex=ex: e.tensor_scalar(
                        out=yev[:, col, ch * 512:(ch + 1) * 512], in0=ps[:, b, :], scalar1=self.slot_g[:, ex, col:col + 1], scalar2=None,
                        op0=ALU.mult), r=[pk, "slot_g"], w=[("Bf", col * 2), ("Bf", col * 2 + 1)])
            if ex + 2 < NE:
                gather(ex + 2)
            for col in range(4):
                self.dma(lambda e, ex=ex, col=col: e.indirect_dma_start(
                    out=self.ACCd[:, :], out_offset=bass.IndirectOffsetOnAxis(ap=self.slot_i[:, ex, col:col + 1], axis=0),
                    in_=yev[:, col, :], in_offset=None, compute_op=ALU.add),
                    r=yk + ["slot_i"] + (["ACCd"] if ex == 0 else [("sc", ex - 1, c) for c in range(4)]),
                    w=[("sc", ex, col)], eng="gpsimd", semkey=("scat", col))

    def final(self, l):
        p = self.p
        self.load_bc(0, p["ln2_g"][l:l + 1, :])
        self.load_bc(1, p["ln2_b"][l:l + 1, :])
        def f_load(sg):
            rw = self.row[sg % 3]
            self.dma(lambda e, rw=rw, r0=sg * 128: e.dma_start(out=rw[:], in_=self.ACCd[r0:r0 + 128, :]),
                     r=["ACCd"] + [("sc", NE - 1, c) for c in range(4)], w=[("row", sg % 3)])

        f_load(0)
        f_load(1)
        for sg in range(S_LEN // 128):
            r0 = sg * 128
            rw = self.row[sg % 3]
            rk = ("row", sg % 3)
            self.ln_rows(rw[:], rw[:], 0, 1, srck=rk, dstk=rk, par=sg % 2)
            self.dma(lambda e, rw=rw, r0=r0: e.dma_start(out=self.out[r0:r0 + 128, :], in_=rw[:]), r=[rk], w=["out"], semkey=("st", rk))
            if sg + 2 < S_LEN // 128:
                f_load(sg + 2)

    def build(self, stop_after=None):
        with ExitStack() as st:
            self.st = st
            self.declare()
            for nm, shp, dt in self.dbg_decl:
                self.dbg_out(nm, shp, dt)
            self.alloc()
            self.S = Sched(self.nc, st)
            self.setup_consts()
            p = self.p
            for li, l in enumerate(self.layers):
                self.load_layer_params(li)
                if l == 0:
                    src, gb = self.x_in, (p["ln0_g"], p["ln0_b"])
                else:
                    src, gb = self.ACCd, (p["ln2_g"][li - 1:li, :], p["ln2_b"][li - 1:li, :])
                self.pass1(li, src, gb)
                if stop_after == "pass1":
                    break
                self.pass2(li)
                if stop_after == "pass2":
                    break
                self.topk(li)
                if stop_after == "topk":
                    break
                self.moe(li)
            if stop_after is None:
                self.final(len(self.layers) - 1)
                fin = ["out"]
            else:
                fin = []
            self.debug_dumps(stop_after)
            fin += ["dbg_" + n for n in self.dbg_written]
            self.S.finish("sync", fin)
            self.S.emit()
        return self.nc

    dbg_decl = ()
    dbg_written = ()

    def debug_dumps(self, stop_after):
        pass


def rope_tables():
    t = np.arange(S_LEN)
    row = (t // 64).astype(np.float32)
    col = (t % 64).astype(np.float32)
    axis_dim = DH // 2
    freqs = (1.0 / (np.float32(10000.0) ** (np.arange(0, axis_dim, 2, dtype=np.float32) / np.float32(axis_dim)))).astype(np.float32)
    ang = np.concatenate([row[:, None] * freqs[None], col[:, None] * freqs[None]], axis=-1).astype(np.float32)
    cos = np.cos(ang).astype(np.float32)
    sin = np.sin(ang).astype(np.float32)
    C = np.repeat(cos.T, 2, axis=0)
    Sg = np.repeat(sin.T, 2, axis=0)
    sign = np.where(np.arange(DH) % 2 == 0, -1.0, 1.0).astype(np.float32)[:, None]
    return np.ascontiguousarray(C), np.ascontiguousarray(Sg * sign)


def const_inputs():
    ident = np.eye(128, dtype=np.float32)
    swap = np.zeros((128, 128), np.float32)
    idx = np.arange(128)
    swap[idx, idx ^ 1] = 1.0
    C, Sg = rope_tables()
    return {"c_ident": ident, "c_swap": swap, "c_ropeC": C, "c_ropeS": Sg}


def core_inputs(inputs, b, layers=None):
    m = {"x": np.ascontiguousarray(inputs["x"][b])}
    if layers is not None:
        inputs = dict(inputs)
        for nm in inputs:
            if nm not in ("x", "ln0_g", "ln0_b"):
                inputs[nm] = np.ascontiguousarray(inputs[nm][list(layers)])
    L = NL if layers is None else len(layers)
    m["ln0_g"] = inputs["ln0_g"].reshape(1, D)
    m["ln0_b"] = inputs["ln0_b"].reshape(1, D)
    m["b_in"] = inputs["b_in"].reshape(L, N_IN // 128, 128)
    for nm in ("conv_dw_b", "conv_ln_g", "conv_ln_b", "conv_pw_b"):
        m[nm] = inputs[nm].reshape(L, 8, 128)
    m["q_norm_g"] = inputs["q_norm_g"].reshape(L, 1, DH)
    m["k_norm_g"] = inputs["k_norm_g"].reshape(L, 1, DH)
    for nm in ("w_in", "conv_dw", "conv_pw_w", "w_o", "w_out", "b_out", "ln1_g", "ln1_b", "w_router", "w_gate", "w_up",
               "w_down", "ln2_g", "ln2_b"):
        m[nm] = inputs[nm]
    m.update(const_inputs())
    return m


def kernel(**inputs):
    inputs = {k: np.asarray(v) for k, v in inputs.items()}
    mk = MK()
    nc = mk.build()
    n = 4
    in_maps = [core_inputs(inputs, c) for c in range(n)]
    res = run_bass_kernel_spmd(nc, in_maps, core_ids=list(range(n)))
    out = np.stack([np.asarray(res.results[c]["out"]) for c in range(n)], axis=0)
    return out.astype(np.float32)
```

```python
import math
import numpy as np
from contextlib import ExitStack
import concourse.bass as bass
import concourse.mybir as mybir
from concourse.bass_utils import run_bass_kernel_spmd

F32 = mybir.dt.float32
BF16 = mybir.dt.bfloat16
U32 = mybir.dt.uint32
I32 = mybir.dt.int32
IDX_BITS = 12
AF = mybir.ActivationFunctionType
ALU = mybir.AluOpType

S_LEN = 4096
D = 1024
NL = 4
TT = 512
NT = S_LEN // TT
XW = 544
PADC = 16
NH = 8
NKV = 2
DH = 128
NE = 16
CAP = 512
DFF = 2048
KC = 31
N_IN = 5632
OFF_Q = 2048
OFF_K = 3072
OFF_V = 3328
OFF_GC = 3584
OFF_GA = 4608
LN_EPS = 1e-5
RMS_EPS = 1e-6
ALPHA = (2.0 * NL) ** 0.25
ATT_SCALE = 1.0 / math.sqrt(DH)
NRING = 4
NTMP = 10

ENGS = ("tensor", "vector", "scalar", "gpsimd", "sync")


class Sched:
    def __init__(self, nc, stack):
        self.nc = nc
        self.stack = stack
        self.prog = {e: [] for e in ENGS}
        self.cnt = {e: 0 for e in ENGS}
        self.esem = {e: stack.enter_context(nc.semaphore("es_" + e)) for e in ENGS}
        self.known = {e: {} for e in ENGS}
        self.last_w = {}
        self.readers = {}
        self.dsem = {}
        self.dcnt = {}
        self.nops = 0
        self.nwaits = 0

    def _sem_for_key(self, key):
        if key not in self.dsem:
            self.dsem[key] = self.stack.enter_context(self.nc.semaphore("ds%d" % len(self.dsem)))
            self.dcnt[key] = 0
        return self.dsem[key]

    def _waits(self, eng, reads, writes):
        waits = {}
        own = self.esem[eng].num

        def need(sv, raw):
            s, v = sv
            if s.num == own and not raw and eng == "tensor":
                return
            cur = waits.get(s.num)
            if cur is None or cur[1] < v:
                waits[s.num] = (s, v)

        for k in reads:
            for sv in self.last_w.get(k, {}).values():
                need(sv, True)
        for k in writes:
            for sv in self.last_w.get(k, {}).values():
                need(sv, False)
            for sv in self.readers.get(k, {}).values():
                need(sv, False)
        out = []
        kn = self.known[eng]
        for num, (s, v) in waits.items():
            if kn.get(num, 0) >= v:
                continue
            kn[num] = v
            out.append((s, v))
        return out

    def _commit(self, s, v, reads, writes):
        for k in writes:
            self.last_w.setdefault(k, {})[s.num] = (s, v)
            self.readers[k] = {}
        for k in reads:
            self.readers.setdefault(k, {})[s.num] = (s, v)

    def op(self, eng, fn, reads=(), writes=()):
        waits = self._waits(eng, reads, writes)
        self.cnt[eng] += 1
        s = self.esem[eng]
        self.prog[eng].append((waits, fn, s, 1))
        self._commit(s, self.cnt[eng], reads, writes)
        self.nops += 1
        self.nwaits += len(waits)

    def dma(self, eng, fn, reads=(), writes=(), semkey=None):
        waits = self._waits(eng, reads, writes)
        key = semkey if semkey is not None else writes[0]
        s = self._sem_for_key(key)
        self.dcnt[key] += 16
        self.prog[eng].append((waits, fn, s, 16))
        self._commit(s, self.dcnt[key], reads, writes)
        self.nops += 1
        self.nwaits += len(waits)

    def finish(self, eng, keys):
        waits = self._waits(eng, keys, ())
        self.prog[eng].append((waits, None, None, 0))

    def emit(self):
        with self.nc.Block() as block:
            for e in ENGS:
                prog = self.prog[e]
                if not prog:
                    continue

                def body(engine, prog=prog):
                    for waits, fn, s, inc in prog:
                        for ws, wv in waits:
                            engine.wait_ge(ws, wv)
                        if fn is not None:
                            fn(engine).then_inc(s, inc)

                getattr(block, e)(body)


class MK:
    def __init__(self, layers=(0, 1, 2, 3), first=True, last=True, dbg=()):
        self.layers = list(layers)
        self.first = first
        self.last = last
        self.dbg = set(dbg)
        self.nc = bass.Bass("TRN2", target_bir_lowering=False)
        self.bank_rr = 0
        self.pinned = set()
        self.tmp_rr = 0

    def din(self, name, shape, dt=F32):
        return self.nc.dram_tensor(name, list(shape), dt, kind="ExternalInput").ap()

    def sb(self, name, shape, dt):
        return self.st.enter_context(self.nc.sbuf_tensor(name, list(shape), dt))

    def declare(self):
        nc = self.nc
        L = len(self.layers)
        self.x_in = self.din("x", [S_LEN, D])
        self.p = {}
        for nm, shp in [("ln0_g", [1, D]), ("ln0_b", [1, D]), ("w_in", [L, D, N_IN]), ("b_in", [L, N_IN // 128, 128]),
                        ("conv_dw", [L, KC, D]), ("conv_dw_b", [L, 8, 128]), ("conv_ln_g", [L, 8, 128]),
                        ("conv_ln_b", [L, 8, 128]), ("conv_pw_w", [L, D, D]), ("conv_pw_b", [L, 8, 128]),
                        ("q_norm_g", [L, 1, DH]), ("k_norm_g", [L, 1, DH]), ("w_o", [L, D, D]), ("w_out", [L, D, D]),
                        ("b_out", [L, D]), ("ln1_g", [L, D]), ("ln1_b", [L, D]), ("w_router", [L, D, NE]),
                        ("w_gate", [L, NE, D, DFF]), ("w_up", [L, NE, D, DFF]), ("w_down", [L, NE, DFF, D]),
                        ("ln2_g", [L, D]), ("ln2_b", [L, D])]:
            self.p[nm] = self.din(nm, shp)
        self.c_ident = self.din("c_ident", [128, 128])
        self.c_swap = self.din("c_swap", [128, 128])
        self.c_ropeC = self.din("c_ropeC", [128, S_LEN])
        self.c_ropeS = self.din("c_ropeS", [128, S_LEN])
        self.out = nc.dram_tensor("out", [S_LEN, D], F32, kind="ExternalOutput").ap()
        self.Xd = nc.dram_tensor("Xd", [S_LEN, D], F32, kind="Internal").ap()
        self.XTd = nc.dram_tensor("XTd", [D, S_LEN + 2 * PADC], BF16, kind="Internal").ap()
        self.X1d = nc.dram_tensor("X1d", [S_LEN, D], BF16, kind="Internal").ap()
        self.ACCd = nc.dram_tensor("ACCd", [S_LEN, D], F32, kind="Internal").ap()
        self.AFFd = nc.dram_tensor("AFFd", [S_LEN, NE], F32, kind="Internal").ap()
        self.dbg_t = {}

    def dbg_out(self, name, shape, dt=F32):
        t = self.nc.dram_tensor("dbg_" + name, list(shape), dt, kind="ExternalOutput").ap()
        self.dbg_t[name] = t
        return t

    def alloc(self):
        sb = self.sb
        self.KT = sb("KT", [128, NKV, S_LEN], BF16)
        self.Vt = sb("Vt", [128, S_LEN // 128, NKV * DH], BF16)
        self.aff = sb("aff", [128, S_LEN // 128, NE], F32)
        self.ident_f = sb("ident_f", [128, 128], F32)
        self.ident_b = sb("ident_b", [128, 128], BF16)
        self.swapm = sb("swapm", [128, 128], F32)
        self.ones_d = sb("ones_d", [128, 128], F32)
        self.ones_h = sb("ones_h", [128, 128], F32)
        self.ones_b = sb("ones_b", [128, 128], BF16)
        self.prm_rows = sb("prm_rows", [80, 128], F32)
        self.prm = sb("prm", [128, 80], F32)
        self.P2 = [sb("P2_%d" % i, [128, TT], BF16) for i in range(4)]
        self.dw_pp = sb("dw_pp", [128, 8, KC], F32)
        self.bc = [sb("bc%d" % i, [128, D], F32) for i in range(3)]
        self.bv_bc = sb("bv_bc", [128, NKV * DH], F32)
        self.wr = sb("wr", [128, 8, NE], F32)
        self.ring = [sb("ring%d" % i, [128, 8, 512], BF16) for i in range(NRING)]
        self.xT = sb("xT", [128, 8, XW], BF16)
        self.uT = sb("uT", [128, 8, XW], BF16)
        self.Bf = sb("Bf", [128, 8, TT], F32)
        self.Cb = sb("Cb", [128, 8, TT], BF16)
        self.Db = sb("Db", [128, 8, TT], BF16)
        self.Eb = sb("Eb", [128, 8, TT], BF16)
        self.tmp = [sb("tmp%d" % i, [128, TT], F32) for i in range(NTMP)]
        self.PT = [sb("PT%d" % i, [128, 2 * TT], BF16) for i in range(3)]
        self.ropeC = [sb("ropeC%d" % i, [128, TT], F32) for i in range(2)]
        self.ropeS = [sb("ropeS%d" % i, [128, TT], F32) for i in range(2)]
        self.row = [sb("row%d" % i, [128, D], F32) for i in range(3)]
        self.rowb = sb("rowb", [128, D], BF16)
        self.x1T = sb("x1T", [128, 8, 128], F32)
        self.dg = [sb("dg%d" % i, [128, 128], BF16) for i in range(8)]
        self.small = sb("small", [128, 64], F32)
        self.bst = [sb("bst%d" % i, [128, 2, 6], F32) for i in range(2)]
        self.zpad = sb("zpad", [128, 8, PADC], BF16)
        self.topv = sb("topv", [NE, CAP], F32)
        self.gg = [sb("gg%d" % i, [128, 4, NE], F32) for i in range(2)]
        self.bmask = sb("bmask", [128, 2], I32)
        self.slot_g = sb("slot_g", [128, NE, 4], F32)
        self.slot_i = sb("slot_i", [128, NE, 4], U32)
        self.ps = self.st.enter_context(self.nc.psum_tensor("ps", [128, 8, 512], F32))

    def bank(self, pin=False):
        while True:
            b = self.bank_rr
            self.bank_rr = (self.bank_rr + 1) % 8
            if b not in self.pinned:
                break
        if pin:
            self.pinned.add(b)
        return b

    def unpin(self, b):
        self.pinned.discard(b)

    def T(self, fn, r=(), w=()):
        self.S.op("tensor", fn, r, w)

    def V(self, fn, r=(), w=()):
        self.S.op("vector", fn, r, w)

    def A(self, fn, r=(), w=()):
        self.S.op("scalar", fn, r, w)

    def G(self, fn, r=(), w=()):
        self.S.op("gpsimd", fn, r, w)

    def dma(self, fn, r=(), w=(), eng="sync", semkey=None):
        self.S.dma(eng, fn, r, w, semkey)

    def uodd(self, c):
        buf, nm = (self.Db, "Db") if c < 4 else (self.Eb, "Eb")
        cc = c % 4
        ap = buf[:].rearrange("p a n -> p (a n)")[:, cc * 1024:cc * 1024 + XW]
        return ap, [(nm, 2 * cc), (nm, 2 * cc + 1)]

    def tmpk(self):
        i = self.tmp_rr
        self.tmp_rr = (self.tmp_rr + 1) % NTMP
        return i

    def ws_reset(self, specs, hw_mask=None, extra_slots=()):
        self.ws_specs = specs
        self.ws_slots = [(self.ring[i][:], ("ring", i)) for i in range(NRING)] + list(extra_slots)
        self.ws_n = len(self.ws_slots)
        self.ws_issued = 0
        self.ws_used = 0
        self.ws_hw = hw_mask
        if hw_mask is not None:
            self.ws_hwlist = [i for i in range(len(specs)) if hw_mask[i]]
            self.ws_hwdma = 0
            self.stg = [(self.KT[:].rearrange("p a n -> p (a n)").bitcast(F32).rearrange("p (k n) -> p k n", k=8), "KT"),
                        (self.Vt[:].rearrange("p a n -> p (a n)").bitcast(F32).rearrange("p (k n) -> p k n", k=8), "Vt")]
            self.ws_hw_dma_upto(2)

    def ws_hw_dma_upto(self, n):
        n = min(n, len(self.ws_hwlist))
        while self.ws_hwdma < n:
            hn = self.ws_hwdma
            i = self.ws_hwlist[hn]
            stg, sk = self.stg[hn % 2]
            for (c0, ncol, src) in self.ws_specs[i]:
                self.dma(lambda e, stg=stg, c0=c0, ncol=ncol, src=src:
                         e.dma_start(out=stg[:, :, c0:c0 + ncol], in_=src.rearrange("(k p) n -> p k n", p=128)),
                         r=(), w=[sk], eng=("sync" if hn % 2 == 0 else "scalar"))
            self.ws_hwdma += 1

    def ws_issue_upto(self, n):
        n = min(n, len(self.ws_specs))
        while self.ws_issued < n:
            i = self.ws_issued
            sap, skey = self.ws_slots[i % self.ws_n]
            if self.ws_hw is not None and self.ws_hw[i]:
                hn = self.ws_hwlist.index(i)
                stg, sk = self.stg[hn % 2]
                self.A(lambda e, sap=sap, stg=stg: e.copy(out=sap, in_=stg), r=[sk], w=[skey])
                self.ws_hw_dma_upto(hn + 3)
            else:
                for (c0, ncol, src) in self.ws_specs[i]:
                    self.dma(lambda e, sap=sap, c0=c0, ncol=ncol, src=src:
                             e.dma_start(out=sap[:, :, c0:c0 + ncol],
                                         in_=src.rearrange("(k p) n -> p k n", p=128)),
                             r=(), w=[skey], eng="gpsimd", semkey=("sw", skey))
            self.ws_issued += 1

    def ws_next(self):
        i = self.ws_used
        self.ws_issue_upto(i + self.ws_n - 1)
        self.ws_used += 1
        return self.ws_slots[i % self.ws_n]

    def setup_consts(self):
        self.dma(lambda e: e.dma_start(out=self.ident_f[:], in_=self.c_ident), w=["ident_f"])
        self.dma(lambda e: e.dma_start(out=self.swapm[:], in_=self.c_swap), w=["swapm"])
        self.V(lambda e: e.tensor_copy(out=self.ident_b[:], in_=self.ident_f[:]), r=["ident_f"], w=["ident_b"])
        self.V(lambda e: e.memset(self.ones_d[:], 1.0 / D), w=["ones_d"])
        self.V(lambda e: e.memset(self.ones_h[:], 1.0 / DH), w=["ones_h"])
        self.V(lambda e: e.memset(self.ones_b[:], 1.0), w=["ones_b"])
        self.V(lambda e: e.memset(self.bmask[:, 0:1], -(1 << IDX_BITS)), w=["bmask"])
        self.V(lambda e: e.memset(self.bmask[:, 1:2], (1 << IDX_BITS) - 1), w=["bmask"])
        self.V(lambda e: e.memset(self.zpad[:], 0.0), w=["zpad"])
        for c0 in (0, PADC + S_LEN):
            self.dma(lambda e, c0=c0: e.dma_start(out=self.XTd[:, c0:c0 + PADC].rearrange("(k p) n -> p k n", p=128),
                                                  in_=self.zpad[:]), r=["zpad"], w=["XTd"], semkey=("st", "zpad"))

    def load_bc(self, i, src_row):
        self.dma(lambda e: e.dma_start(out=self.bc[i][:], in_=src_row.to_broadcast([128, D])), w=[("bc", i)])

    P_BIN = 0
    P_DWB = 44
    P_CLG = 52
    P_CLB = 60
    P_PWB = 68
    P_QG = 76
    P_KG = 77

    def load_layer_params(self, l):
        p = self.p
        rows = self.prm_rows
        segs = [(self.P_BIN, 44, p["b_in"][l]), (self.P_DWB, 8, p["conv_dw_b"][l]), (self.P_CLG, 8, p["conv_ln_g"][l]),
                (self.P_CLB, 8, p["conv_ln_b"][l]), (self.P_PWB, 8, p["conv_pw_b"][l]), (self.P_QG, 1, p["q_norm_g"][l]),
                (self.P_KG, 1, p["k_norm_g"][l])]
        for (r0, n, src) in segs:
            self.dma(lambda e, r0=r0, n=n, src=src: e.dma_start(out=rows[r0:r0 + n, :], in_=src), w=["prm_rows"])
        b = self.bank()
        pk = ("ps", b)
        self.T(lambda e, b=b: e.transpose(out=self.ps[:, b, 0:78], in_=rows[0:78, :], identity=self.ident_f[0:78, 0:78]),
               r=["prm_rows", "ident_f"], w=[pk])
        self.V(lambda e, b=b: e.tensor_copy(out=self.prm[:, 0:78], in_=self.ps[:, b, 0:78]), r=[pk], w=["prm"])
        dw_rows = self.row[2][0:KC, :]
        self.dma(lambda e: e.dma_start(out=dw_rows, in_=p["conv_dw"][l]), w=[("row", 2)])
        for c in range(8):
            b = self.bank()
            pk = ("ps", b)
            self.T(lambda e, b=b, c=c: e.transpose(out=self.ps[:, b, 0:KC], in_=dw_rows[:, c * 128:(c + 1) * 128],
                                                   identity=self.ident_f[0:KC, 0:KC]),
                   r=[("row", 2), "ident_f"], w=[pk])
            self.V(lambda e, b=b, c=c: e.tensor_copy(out=self.dw_pp[:, c, :], in_=self.ps[:, b, 0:KC]), r=[pk], w=["dw_pp"])
        self.dma(lambda e: e.dma_start(out=self.bv_bc[:], in_=p["b_in"][l].rearrange("c p -> (c p)")[OFF_V:OFF_V + 256]
                                       .rearrange("(o n) -> o n", o=1).to_broadcast([128, 256])), w=["bv_bc"])
        self.dma(lambda e: e.dma_start(out=self.wr[:], in_=p["w_router"][l].rearrange("(k p) n -> p k n", p=128)), w=["wr"])

    def ln_rows(self, src, dst, gi, bi, eps=LN_EPS, srck=None, dstk=None, par=0):
        sm = self.small
        c0 = par * 4
        bst = self.bst[par]
        kb0, kb1, kmv, ksd, krs = (("lnst", par, i) for i in range(5))
        self.V(lambda e: e.bn_stats(out=bst[:, 0, :], in_=src[:, 0:512]), r=[srck], w=[kb0])
        self.V(lambda e: e.bn_stats(out=bst[:, 1, :], in_=src[:, 512:1024]), r=[srck], w=[kb1])
        self.V(lambda e: e.bn_aggr(out=sm[:, c0:c0 + 2], in_=bst[:].rearrange("p a b -> p (a b)")), r=[kb0, kb1], w=[kmv])
        self.A(lambda e: e.activation(out=sm[:, c0 + 2:c0 + 3], in_=sm[:, c0 + 1:c0 + 2], func=AF.Sqrt, bias=eps, scale=1.0), r=[kmv], w=[ksd])
        self.V(lambda e: e.reciprocal(out=sm[:, c0 + 3:c0 + 4], in_=sm[:, c0 + 2:c0 + 3]), r=[ksd], w=[krs])
        self.V(lambda e: e.tensor_scalar(out=dst, in0=src, scalar1=sm[:, c0:c0 + 1], scalar2=sm[:, c0 + 3:c0 + 4],
                                         op0=ALU.subtract, op1=ALU.mult), r=[srck, kmv, krs], w=[dstk])
        self.V(lambda e: e.tensor_tensor(out=dst, in0=dst, in1=self.bc[gi][:], op=ALU.mult), r=[dstk, ("bc", gi)], w=[dstk])
        self.V(lambda e: e.tensor_tensor(out=dst, in0=dst, in1=self.bc[bi][:], op=ALU.add), r=[dstk, ("bc", bi)], w=[dstk])

    def nr_stageA(self, pk, psap, bias_ap):
        i_raw, i_sq, i_rs = self.tmpk(), self.tmpk(), self.tmpk()
        stt = {"raw": self.tmp[i_raw], "sq": self.tmp[i_sq], "rs": self.tmp[i_rs],
               "kr": ("tmp", i_raw), "ks": ("tmp", i_sq), "krs": ("tmp", i_rs)}
        raw, sq = stt["raw"], stt["sq"]
        self.A(lambda e: e.activation(out=raw[:], in_=psap, func=AF.Identity, bias=bias_ap, scale=1.0), r=[pk, "prm"], w=[stt["kr"]])
        self.A(lambda e: e.activation(out=sq[:], in_=psap, func=AF.Square, bias=bias_ap, scale=1.0), r=[pk, "prm"], w=[stt["ks"]])
        return stt

    def nr_stageB(self, stt, g_ap):
        raw, sq, rs = stt["raw"], stt["sq"], stt["rs"]
        kr, ks, krs = stt["kr"], stt["ks"], stt["krs"]
        b2 = self.bank()
        pk2 = ("ps", b2)
        self.T(lambda e: e.matmul(self.ps[:, b2, :], lhsT=self.ones_h[:], rhs=sq[:], start=True, stop=True),
               r=[ks, "ones_h"], w=[pk2])
        self.A(lambda e: e.activation(out=rs[:], in_=self.ps[:, b2, :], func=AF.Ln, bias=RMS_EPS, scale=1.0), r=[pk2], w=[krs])
        self.A(lambda e: e.activation(out=rs[:], in_=rs[:], func=AF.Exp, scale=-0.5), r=[krs], w=[krs])
        self.V(lambda e: e.scalar_tensor_tensor(out=raw[:], in0=raw[:], scalar=g_ap, in1=rs[:], op0=ALU.mult, op1=ALU.mult),
               r=[kr, krs, "prm"], w=[kr])

    def nr_stageC(self, stt, rp, out_ap, outk):
        raw, sq = stt["raw"], stt["sq"]
        kr, ks = stt["kr"], stt["ks"]
        b3 = self.bank()
        pk3 = ("ps", b3)
        self.T(lambda e: e.matmul(self.ps[:, b3, :], lhsT=self.swapm[:], rhs=raw[:], start=True, stop=True),
               r=[kr, "swapm"], w=[pk3])
        self.V(lambda e: e.tensor_tensor(out=sq[:], in0=self.ps[:, b3, :], in1=self.ropeS[rp][:], op=ALU.mult),
               r=[pk3, ("ropeS", rp)], w=[ks])
        self.G(lambda e: e.tensor_tensor(out=raw[:], in0=raw[:], in1=self.ropeC[rp][:], op=ALU.mult), r=[kr, ("ropeC", rp)], w=[kr])
        self.V(lambda e: e.tensor_tensor(out=out_ap, in0=raw[:], in1=sq[:], op=ALU.add), r=[kr, ks], w=[outk])

    def norm_rope_pipe(self, n, projA, g_ap, rp, outs):
        stts = [None] * n
        for step in range(n + 2):
            if step < n:
                pk, psap, bias_ap = projA(step)
                stts[step] = self.nr_stageA(pk, psap, bias_ap)
            if 0 <= step - 1 < n:
                self.nr_stageB(stts[step - 1], g_ap)
            if 0 <= step - 2 < n:
                self.nr_stageC(stts[step - 2], rp, *outs[step - 2])

    def load_rope(self, t, rp):
        t0 = t * TT
        self.dma(lambda e: e.dma_start(out=self.ropeC[rp][:], in_=self.c_ropeC[:, t0:t0 + TT]), w=[("ropeC", rp)])
        self.dma(lambda e: e.dma_start(out=self.ropeS[rp][:], in_=self.c_ropeS[:, t0:t0 + TT]), w=[("ropeS", rp)])

    def pass1(self, l, src_d, gi_src):
        p = self.p
        g_row, b_row = gi_src
        self.load_bc(0, g_row)
        self.load_bc(1, b_row)
        self.ws_reset([[(0, 512, p["w_in"][l][:, OFF_K:OFF_K + 512])]])
        wkv, wk_key = self.ws_next()

        def p1_load(sgi):
            rw = self.row[sgi % 3]
            r0 = sgi * 128
            self.dma(lambda e, rw=rw, r0=r0: e.dma_start(out=rw[:], in_=src_d[r0:r0 + 128, :]),
                     r=["ACCd"] + [("sc", NE - 1, c) for c in range(4)], w=[("row", sgi % 3)])

        for t in range(NT):
            t0 = t * TT
            rp = t % 2
            self.load_rope(t, rp)
            for st in range(4):
                r0 = t0 + st * 128
                sgi = t * 4 + st
                if sgi == 0:
                    p1_load(0)
                    p1_load(1)
                rw = self.row[sgi % 3]
                rk = ("row", sgi % 3)
                self.ln_rows(rw[:], rw[:], 0, 1, srck=rk, dstk=rk, par=st % 2)
                self.dma(lambda e, rw=rw, r0=r0: e.dma_start(out=self.Xd[r0:r0 + 128, :], in_=rw[:]), r=[rk], w=["Xd"], semkey=("st", rk))
                for half in range(2):
                    b = self.bank()
                    pk = ("ps", b)
                    for j in range(4):
                        k = half * 4 + j
                        self.T(lambda e, b=b, j=j, k=k, rw=rw: e.transpose(out=self.ps[:, b, j * 128:(j + 1) * 128],
                                                                          in_=rw[:, k * 128:(k + 1) * 128], identity=self.ident_f[:]),
                               r=[rk, "ident_f"], w=[pk])
                    self.A(lambda e, b=b, half=half, st=st: e.copy(
                        out=self.xT[:, half * 4:half * 4 + 4, st * 128:(st + 1) * 128],
                        in_=self.ps[:, b, :].rearrange("p (j n) -> p j n", j=4)), r=[pk], w=["xT"])
                if sgi + 2 < S_LEN // 128:
                    p1_load(sgi + 2)
            self.dma(lambda e, t0=t0: e.dma_start(out=self.XTd[:, PADC + t0:PADC + t0 + TT].rearrange("(k p) n -> p k n", p=128),
                                                  in_=self.xT[:, :, 0:TT]), r=["xT"], w=["XTd"], semkey=("st", "xT"))
            def projK(h, t0=t0):
                b = self.bank()
                pk = ("ps", b)
                for k in range(8):
                    self.T(lambda e, b=b, k=k, h=h: e.matmul(self.ps[:, b, :], lhsT=wkv[:, k, h * 128:(h + 1) * 128],
                                                            rhs=self.xT[:, k, 0:TT], start=(k == 0), stop=(k == 7)),
                           r=[wk_key, "xT"], w=[pk])
                cb = self.P_BIN + (OFF_K // 128) + h
                return pk, self.ps[:, b, :], self.prm[:, cb:cb + 1]

            self.norm_rope_pipe(NKV, projK, self.prm[:, self.P_KG:self.P_KG + 1], rp,
                                [(self.KT[:, h, t0:t0 + TT], "KT") for h in range(NKV)])
            for st in range(4):
                b = self.bank()
                pk = ("ps", b)
                for k in range(8):
                    self.T(lambda e, b=b, k=k, st=st: e.matmul(self.ps[:, b, 0:256], lhsT=self.xT[:, k, st * 128:(st + 1) * 128],
                                                              rhs=wkv[:, k, 256:512], start=(k == 0), stop=(k == 7)),
                           r=[wk_key, "xT"], w=[pk])
                sg = t * 4 + st
                self.V(lambda e, b=b, sg=sg: e.tensor_tensor(out=self.Vt[:, sg, :], in0=self.ps[:, b, 0:256], in1=self.bv_bc[:],
                                                            op=ALU.add), r=[pk, "bv_bc"], w=["Vt"])

    def pass2_specs(self, l):
        p = self.p
        w_in = p["w_in"][l]
        one = []
        for i in range(4):
            one.append([(0, 256, w_in[:, i * 256:(i + 1) * 256]), (256, 256, w_in[:, D + i * 256:D + (i + 1) * 256])])
        for i in range(2):
            one.append([(0, 512, w_in[:, OFF_GC + i * 512:OFF_GC + (i + 1) * 512])])
        for i in range(2):
            one.append([(0, 512, p["conv_pw_w"][l][:, i * 512:(i + 1) * 512])])
        for i in range(2):
            one.append([(0, 512, w_in[:, OFF_Q + i * 512:OFF_Q + (i + 1) * 512])])
        for i in range(2):
            one.append([(0, 512, w_in[:, OFF_GA + i * 512:OFF_GA + (i + 1) * 512])])
            one.append([(0, 512, p["w_o"][l][:, i * 512:(i + 1) * 512])])
        for i in range(2):
            one.append([(0, 512, p["w_out"][l][:, i * 512:(i + 1) * 512])])
        return one

    def pass2(self, l):
        p = self.p
        prm = self.prm
        ps = self.ps
        self.load_bc(0, p["ln1_g"][l:l + 1, :])
        self.load_bc(1, p["ln1_b"][l:l + 1, :])
        self.load_bc(2, p["b_out"][l:l + 1, :])
        specs = []
        for t in range(NT):
            specs += self.pass2_specs(l)
        self.ws_reset(specs)
        for t in range(NT):
            t0 = t * TT
            rp = t % 2
            self.load_rope(t, rp)
            self.dma(lambda e, t0=t0: e.dma_start(out=self.xT[:], in_=self.XTd[:, t0:t0 + XW].rearrange("(k p) n -> p k n", p=128)),
                     r=["XTd"], w=["xT"])
            for i in range(4):
                wp, wk = self.ws_next()
                for cc in range(2):
                    c = i * 2 + cc
                    bA, bB, bC = self.bank(), self.bank(), self.bank()
                    kA, kB, kC = ("ps", bA), ("ps", bB), ("ps", bC)
                    for (col0, bm, bh, hc) in ((cc * 128, bA, bB, 0), (256 + cc * 128, bC, bB, 32)):
                        for k in range(8):
                            self.T(lambda e, k=k, col0=col0, bm=bm, wp=wp: e.matmul(ps[:, bm, :], lhsT=wp[:, k, col0:col0 + 128],
                                                                            rhs=self.xT[:, k, 0:512], start=(k == 0), stop=(k == 7)),
                                   r=[wk, "xT"], w=[("ps", bm)])
                        for k in range(8):
                            self.T(lambda e, k=k, col0=col0, bh=bh, hc=hc, wp=wp: e.matmul(ps[:, bh, hc:hc + 32], lhsT=wp[:, k, col0:col0 + 128],
                                                                                    rhs=self.xT[:, k, 512:544], start=(k == 0), stop=(k == 7)),
                                   r=[wk, "xT"], w=[("ps", bh)])
                    cv = self.P_BIN + c
                    cg = self.P_BIN + 8 + c
                    ti = self.tmpk()
                    sg = self.tmp[ti]
                    kt = ("tmp", ti)
                    ti2 = self.tmpk()
                    sg2 = self.tmp[ti2]
                    kt2 = ("tmp", ti2)
                    self.A(lambda e, bC=bC, cg=cg, sg=sg: e.activation(out=sg[:], in_=ps[:, bC, :], func=AF.Sigmoid,
                                                                      bias=prm[:, cg:cg + 1], scale=1.0), r=[kC, "prm"], w=[kt])
                    self.A(lambda e, bB=bB, cg=cg, sg2=sg2: e.activation(out=sg2[:, 0:32], in_=ps[:, bB, 32:64], func=AF.Sigmoid,
                                                                        bias=prm[:, cg:cg + 1], scale=1.0), r=[kB, "prm"], w=[kt2])
                    uk = ("uT", c)
                    self.V(lambda e, bA=bA, cv=cv, sg=sg, c=c: e.scalar_tensor_tensor(out=self.uT[:, c, 0:512], in0=ps[:, bA, :],
                                                                                     scalar=prm[:, cv:cv + 1], in1=sg[:],
                                                                                     op0=ALU.add, op1=ALU.mult), r=[kA, kt, "prm"], w=[uk])
                    self.V(lambda e, bB=bB, cv=cv, sg2=sg2, c=c: e.scalar_tensor_tensor(out=self.uT[:, c, 512:544], in0=ps[:, bB, 0:32],
                                                                                       scalar=prm[:, cv:cv + 1], in1=sg2[:, 0:32],
                                                                                       op0=ALU.add, op1=ALU.mult), r=[kB, kt2, "prm"], w=[uk])
                    if t == 0:
                        self.V(lambda e, c=c: e.memset(self.uT[:, c, 0:16], 0.0), w=[uk])
                    if t == NT - 1:
                        self.V(lambda e, c=c: e.memset(self.uT[:, c, 528:544], 0.0), w=[uk])
            bM = self.bank(pin=True)
            bQ = self.bank(pin=True)
            kM, kQ = ("ps", bM), ("ps", bQ)
            stat_q = []
            for c in range(8):
                b = self.bank()
                pk = ("ps", b)
                if len(stat_q) > 1:
                    for mm in stat_q.pop(0):
                        self.T(*mm)
                dbuf, dnm = (self.Db, "Db") if c % 2 == 0 else (self.Eb, "Eb")
                dkeys = [(dnm, i) for i in range(8)]
                dflat = dbuf[:].rearrange("p a n -> p (a n)")[:, 0:KC * 128].rearrange("p (k m) -> p k m", k=KC)
                self.V(lambda e, c=c, dflat=dflat: e.tensor_tensor(
                    out=dflat, in0=self.ident_b[:].unsqueeze(1).to_broadcast([128, KC, 128]),
                    in1=self.dw_pp[:, c, :].unsqueeze(2).to_broadcast([128, KC, 128]), op=ALU.mult),
                    r=["ident_b", "dw_pp"], w=dkeys)
                for k in range(KC):
                    self.T(lambda e, b=b, c=c, k=k, dflat=dflat: e.matmul(ps[:, b, :], lhsT=dflat[:, k, :], rhs=self.uT[:, c, k + 1:k + 1 + 512],
                                                                         start=(k == 0), stop=(k == KC - 1)), r=dkeys + [("uT", c)], w=[pk])
                cdb = self.P_DWB + c
                self.A(lambda e, b=b, c=c, cdb=cdb: e.activation(out=self.Bf[:, c, :], in_=ps[:, b, :], func=AF.Identity,
                                                                bias=prm[:, cdb:cdb + 1], scale=1.0), r=[pk, "prm"], w=[("Bf", c)])
                ti = self.tmpk()
                sq = self.tmp[ti]
                kt = ("tmp", ti)
                self.A(lambda e, b=b, cdb=cdb, sq=sq: e.activation(out=sq[:], in_=ps[:, b, :], func=AF.Square,
                                                                  bias=prm[:, cdb:cdb + 1], scale=1.0), r=[pk, "prm"], w=[kt])
                stat_q.append(((lambda e, c=c, bM=bM: e.matmul(ps[:, bM, :], lhsT=self.ones_d[:], rhs=self.Bf[:, c, :],
                                                                start=(c == 0), stop=(c == 7)), [("Bf", c), "ones_d"], [kM]),
                               (lambda e, c=c, sq=sq, bQ=bQ: e.matmul(ps[:, bQ, :], lhsT=self.ones_d[:], rhs=sq[:],
                                                                       start=(c == 0), stop=(c == 7)), [kt, "ones_d"], [kQ])))
            while stat_q:
                for mm in stat_q.pop(0):
                    self.T(*mm)
            im, iv = self.tmpk(), self.tmpk()
            mean, rstd = self.tmp[im], self.tmp[iv]
            kmn, krs = ("tmp", im), ("tmp", iv)
            self.A(lambda e, mean=mean, bM=bM: e.copy(out=mean[:], in_=ps[:, bM, :]), r=[kM], w=[kmn])
            self.V(lambda e, mean=mean, rstd=rstd, bM=bM: e.tensor_tensor(out=rstd[:], in0=mean[:], in1=ps[:, bM, :], op=ALU.mult), r=[kmn, kM], w=[krs])
            self.V(lambda e, rstd=rstd, bQ=bQ: e.tensor_tensor(out=rstd[:], in0=ps[:, bQ, :], in1=rstd[:], op=ALU.subtract), r=[kQ, krs], w=[krs])
            self.A(lambda e, rstd=rstd: e.activation(out=rstd[:], in_=rstd[:], func=AF.Ln, bias=LN_EPS, scale=1.0), r=[krs], w=[krs])
            self.A(lambda e, rstd=rstd: e.activation(out=rstd[:], in_=rstd[:], func=AF.Exp, scale=-0.5), r=[krs], w=[krs])
            self.unpin(bM)
            self.unpin(bQ)
            gcs = []
            for i in range(2):
                wgc, wgck = self.ws_next()
                for jj in range(4):
                    j = i * 4 + jj
                    bg = self.bank()
                    kg_ = ("ps", bg)
                    for k in range(8):
                        self.T(lambda e, bg=bg, k=k, jj=jj, wgc=wgc: e.matmul(ps[:, bg, :], lhsT=wgc[:, k, jj * 128:(jj + 1) * 128],
                                                                             rhs=self.xT[:, k, PADC:PADC + TT], start=(k == 0), stop=(k == 7)),
                               r=[wgck, "xT"], w=[kg_])
                    cgc = self.P_BIN + OFF_GC // 128 + j
                    ti = self.tmpk()
                    gs = self.tmp[ti]
                    kt = ("tmp", ti)
                    self.A(lambda e, bg=bg, cgc=cgc, gs=gs: e.activation(out=gs[:], in_=ps[:, bg, :], func=AF.Sigmoid,
                                                                        bias=prm[:, cgc:cgc + 1], scale=1.0), r=[kg_, "prm"], w=[kt])
                    gcs.append((gs, kt))
            for c in range(8):
                kb = ("Bf", c)
                self.V(lambda e, c=c, mean=mean: e.tensor_tensor(out=self.Bf[:, c, :], in0=self.Bf[:, c, :], in1=mean[:], op=ALU.subtract),
                       r=[kb, kmn], w=[kb])
                self.V(lambda e, c=c, rstd=rstd: e.tensor_tensor(out=self.Bf[:, c, :], in0=self.Bf[:, c, :], in1=rstd[:], op=ALU.mult),
                       r=[kb, krs], w=[kb])
                cg, cb2 = self.P_CLG + c, self.P_CLB + c
                self.A(lambda e, c=c, cg=cg, cb2=cb2: e.activation(out=self.Cb[:, c, :], in_=self.Bf[:, c, :], func=AF.Silu,
                                                                  bias=prm[:, cb2:cb2 + 1], scale=prm[:, cg:cg + 1]),
                       r=[kb, "prm"], w=[("Cb", c)])
            if getattr(self, 'dbg_stop', None) == 'conv':
                return
            for i in range(2):
                wpw, wpwk = self.ws_next()
                for jj in range(4):
                    j = i * 4 + jj
                    gs, kt = gcs[j]
                    by = self.bank()
                    ky = ("ps", by)
                    for c in range(8):
                        self.T(lambda e, by=by, c=c, jj=jj, wpw=wpw: e.matmul(ps[:, by, :], lhsT=wpw[:, c, jj * 128:(jj + 1) * 128],
                                                                             rhs=self.Cb[:, c, :], start=(c == 0), stop=(c == 7)),
                               r=[wpwk, ("Cb", c)], w=[ky])
                    cpb = self.P_PWB + j
                    self.V(lambda e, by=by, j=j, cpb=cpb, gs=gs: e.scalar_tensor_tensor(out=self.Bf[:, j, :], in0=ps[:, by, :],
                                                                                      scalar=prm[:, cpb:cpb + 1], in1=gs[:],
                                                                                      op0=ALU.add, op1=ALU.mult),
                           r=[ky, kt, "prm"], w=[("Bf", j)])
            if getattr(self, 'dbg_stop', None) == 'pw':
                return
            qw = {}

            def projQ(h):
                i, jj = divmod(h, 4)
                if jj == 0:
                    qw["wp"], qw["wk"] = self.ws_next()
                wp, wk = qw["wp"], qw["wk"]
                b = self.bank()
                pk = ("ps", b)
                for k in range(8):
                    self.T(lambda e, b=b, k=k, jj=jj, wp=wp: e.matmul(ps[:, b, :], lhsT=wp[:, k, jj * 128:(jj + 1) * 128],
                                                                     rhs=self.xT[:, k, PADC:PADC + TT], start=(k == 0), stop=(k == 7)),
                           r=[wk, "xT"], w=[pk])
                cq = self.P_BIN + OFF_Q // 128 + h
                return pk, ps[:, b, :], prm[:, cq:cq + 1]

            self.norm_rope_pipe(NH, projQ, prm[:, self.P_QG:self.P_QG + 1], rp,
                                [(self.Db[:, h, :], ("Db", h)) for h in range(NH)])
            NPAIR = S_LEN // 256
            units = [(g, qs) for g in range(NKV) for qs in range(4)]

            SP = ((0, 1), (2, 3), (4, 5))
            bO, bS = 6, 7
            kO, kS = ("ps", bO), ("ps", bS)

            def emit_S(f):
                n, j = divmod(f, NPAIR)
                g, qs = units[n]
                sp = SP[f % 3]
                qk = [("Db", g * 4 + hh) for hh in range(4)]
                for u in range(2):
                    kc = 2 * j + u
                    self.T(lambda e, b=sp[u], g=g, kc=kc, qs=qs: e.matmul(
                        ps[:, b, :].rearrange("p (h q) -> p h q", h=4), lhsT=self.KT[:, g, kc * 128:(kc + 1) * 128],
                        rhs=self.Db[:, g * 4:g * 4 + 4, qs * 128:(qs + 1) * 128], start=True, stop=True),
                        r=["KT"] + qk, w=[("ps", sp[u])])

            NF = len(units) * NPAIR
            pending = []
            emit_S(0)
            emit_S(1)
            for f in range(NF):
                n, j = divmod(f, NPAIR)
                g, qs = units[n]
                if f + 2 < NF:
                    emit_S(f + 2)
                sp = SP[f % 3]
                pi = f % 3
                pt = self.PT[pi]
                kp = ("PT", pi)
                self.A(lambda e, s0=sp[0], pt=pt: e.activation(out=pt[:].rearrange("p (u n) -> p u n", u=2), in_=ps[:, s0:s0 + 2, :],
                                                               func=AF.Exp, scale=ATT_SCALE),
                       r=[("ps", sp[0]), ("ps", sp[1])], w=[kp])
                for u in range(2):
                    kc = 2 * j + u
                    first = (j == 0 and u == 0)
                    last = (j == NPAIR - 1 and u == 1)
                    self.T(lambda e, g=g, kc=kc, pt=pt, u=u, first=first, last=last: e.matmul(
                        ps[:, bO, :], lhsT=self.Vt[:, kc, g * 128:(g + 1) * 128], rhs=pt[:, u * 512:(u + 1) * 512],
                        start=first, stop=last), r=["Vt", kp], w=[kO])
                while pending:
                    self.T(*pending.pop(0))
                p2i = f % 4
                p2 = self.P2[p2i]
                kp2 = ("P2", p2i)
                self.V(lambda e, pt=pt, p2=p2: e.tensor_tensor(out=p2[:], in0=pt[:, 0:512], in1=pt[:, 512:1024], op=ALU.add),
                       r=[kp], w=[kp2])
                if j % 2 == 1:
                    pp = self.P2[(f - 1) % 4]
                    self.V(lambda e, p2=p2, pp=pp: e.tensor_tensor(out=p2[:], in0=p2[:], in1=pp[:], op=ALU.add),
                           r=[kp2, ("P2", (f - 1) % 4)], w=[kp2])
                    sums_mm = (lambda e, p2=p2, j=j: e.matmul(ps[:, bS, :], lhsT=self.ones_b[:], rhs=p2[:],
                                                             start=(j == 1), stop=(j == NPAIR - 1)), ["ones_b", kp2], [kS])
                    if j == NPAIR - 1:
                        self.T(*sums_mm)
                    else:
                        pending.append(sums_mm)
                if j == NPAIR - 1:
                    ia, ib = self.tmpk(), self.tmpk()
                    ta, tb = self.tmp[ia], self.tmp[ib]
                    ka, kb = ("tmp", ia), ("tmp", ib)
                    self.V(lambda e, ta=ta: e.tensor_copy(out=ta[:], in_=ps[:, bO, :]), r=[kO], w=[ka])
                    self.A(lambda e, tb=tb: e.copy(out=tb[:], in_=ps[:, bS, :]), r=[kS], w=[kb])
                    self.V(lambda e, tb=tb: e.reciprocal(out=tb[:], in_=tb[:]), r=[kb], w=[kb])
                    ek = [("Eb", g * 4 + hh) for hh in range(4)]
                    self.V(lambda e, g=g, qs=qs, ta=ta, tb=tb: e.tensor_tensor(
                        out=self.Eb[:, g * 4:g * 4 + 4, qs * 128:(qs + 1) * 128],
                        in0=ta[:].rearrange("p (h q) -> p h q", h=4), in1=tb[:].rearrange("p (h q) -> p h q", h=4),
                        op=ALU.mult), r=[ka, kb] + ek, w=ek)
            for i in range(2):
                wga, wgak = self.ws_next()
                wwo, wwok = self.ws_next()
                for jj in range(4):
                    j = i * 4 + jj
                    bg = self.bank()
                    kg_ = ("ps", bg)
                    for k in range(8):
                        self.T(lambda e, bg=bg, k=k, jj=jj, wga=wga: e.matmul(ps[:, bg, :], lhsT=wga[:, k, jj * 128:(jj + 1) * 128],
                                                                             rhs=self.xT[:, k, PADC:PADC + TT], start=(k == 0), stop=(k == 7)),
                               r=[wgak, "xT"], w=[kg_])
                    cga = self.P_BIN + OFF_GA // 128 + j
                    ti = self.tmpk()
                    gs = self.tmp[ti]
                    kt = ("tmp", ti)
                    self.A(lambda e, bg=bg, cga=cga, gs=gs: e.activation(out=gs[:], in_=ps[:, bg, :], func=AF.Sigmoid,
                                                                        bias=prm[:, cga:cga + 1], scale=1.0), r=[kg_, "prm"], w=[kt])
                    by = self.bank()
                    ky = ("ps", by)
                    for h in range(8):
                        self.T(lambda e, by=by, h=h, jj=jj, wwo=wwo: e.matmul(ps[:, by, :], lhsT=wwo[:, h, jj * 128:(jj + 1) * 128],
                                                                             rhs=self.Eb[:, h, :], start=(h == 0), stop=(h == 7)),
                               r=[wwok, ("Eb", h)], w=[ky])
                    self.V(lambda e, by=by, gs=gs: e.tensor_tensor(out=gs[:], in0=gs[:], in1=ps[:, by, :], op=ALU.mult),
                           r=[kt, ky], w=[kt])
                    self.V(lambda e, j=j, gs=gs: e.tensor_tensor(out=self.Cb[:, j, :], in0=self.Bf[:, j, :], in1=gs[:], op=ALU.add),
                           r=[("Bf", j), kt], w=[("Cb", j)])
            wpa, wka = self.ws_next()
            wpb, wkb = self.ws_next()
            def wo_p1(st):
                r0 = t0 + st * 128
                sg = t * 4 + st
                par = st % 2
                tr, ar = self.row[par], self.row[2]
                ktr = ("row", par)
                bb = []
                for (wp, wk) in ((wpa, wka), (wpb, wkb)):
                    b = self.bank()
                    bb.append(b)
                    for k in range(8):
                        self.T(lambda e, b=b, k=k, st=st, wp=wp: e.matmul(ps[:, b, :], lhsT=self.Cb[:, k, st * 128:(st + 1) * 128],
                                                                         rhs=wp[:, k, :], start=(k == 0), stop=(k == 7)),
                               r=[wk, ("Cb", k)], w=[("ps", b)])
                for hf in range(2):
                    b = bb[hf]
                    self.V(lambda e, b=b, hf=hf, tr=tr: e.scalar_tensor_tensor(out=tr[:, hf * 512:(hf + 1) * 512], in0=tr[:, hf * 512:(hf + 1) * 512],
                                                                              scalar=ALPHA, in1=ps[:, b, :], op0=ALU.mult, op1=ALU.add),
                           r=[ktr, ("ps", b)], w=[ktr])
                self.V(lambda e, tr=tr: e.tensor_tensor(out=tr[:], in0=tr[:], in1=self.bc[2][:], op=ALU.add), r=[ktr, ("bc", 2)], w=[ktr])
                self.ln_rows(tr[:], tr[:], 0, 1, srck=ktr, dstk=ktr, par=par)

            def wo_p2(st):
                r0 = t0 + st * 128
                sg = t * 4 + st
                par = st % 2
                tr, ar = self.row[par], self.row[2]
                ktr = ("row", par)
                self.A(lambda e, tr=tr: e.copy(out=self.rowb[:], in_=tr[:]), r=[ktr], w=["rowb"])
                self.dma(lambda e, r0=r0: e.dma_start(out=self.X1d[r0:r0 + 128, :], in_=self.rowb[:]), r=["rowb"], w=["X1d"], semkey=("st", "rowb"))
                self.A(lambda e, tr=tr: e.mul(ar[:], tr[:], ALPHA), r=[ktr], w=[("row", 2)])
                self.dma(lambda e, r0=r0: e.dma_start(out=self.ACCd[r0:r0 + 128, :], in_=ar[:]), r=[("row", 2)] + [("sc", NE - 1, c) for c in range(4)], w=["ACCd"], semkey=("st", "row2"))
                if "x1" in self.dbg and l == 0:
                    self.dma(lambda e, r0=r0, tr=tr: e.dma_start(out=self.dbg_t["x1"][r0:r0 + 128, :], in_=tr[:]), r=[ktr], w=["dbg_x1"],
                             semkey=("st", "dbgx1"))
                for half in range(2):
                    b = self.bank()
                    pk = ("ps", b)
                    for jx in range(4):
                        k = half * 4 + jx
                        self.T(lambda e, b=b, jx=jx, k=k, tr=tr: e.transpose(out=ps[:, b, jx * 128:(jx + 1) * 128],
                                                                             in_=tr[:, k * 128:(k + 1) * 128], identity=self.ident_f[:]),
                               r=[ktr, "ident_f"], w=[pk])
                    self.A(lambda e, b=b, half=half: e.copy(out=self.x1T[:, half * 4:half * 4 + 4, :],
                                                           in_=ps[:, b, :].rearrange("p (j n) -> p j n", j=4)), r=[pk], w=["x1T"])
                b = self.bank()
                pk = ("ps", b)
                for k in range(8):
                    self.T(lambda e, b=b, k=k: e.matmul(ps[:, b, 0:NE], lhsT=self.x1T[:, k, :], rhs=self.wr[:, k, :],
                                                        start=(k == 0), stop=(k == 7)), r=["x1T", "wr"], w=[pk])
                sm = self.small
                c0 = 8 + par * 4
                e0 = 16 + par * 16
                kq = [("rt", par, i) for i in range(5)]
                self.V(lambda e, b=b, c0=c0: e.reduce_max(out=sm[:, c0:c0 + 1], in_=ps[:, b, 0:NE], axis=mybir.AxisListType.X), r=[pk], w=[kq[0]])
                self.V(lambda e, c0=c0: e.tensor_scalar(out=sm[:, c0 + 1:c0 + 2], in0=sm[:, c0:c0 + 1], scalar1=-1.0, scalar2=None, op0=ALU.mult),
                       r=[kq[0]], w=[kq[1]])
                self.A(lambda e, b=b, c0=c0, e0=e0: e.activation(out=sm[:, e0:e0 + NE], in_=ps[:, b, 0:NE], func=AF.Exp, bias=sm[:, c0 + 1:c0 + 2],
                                                                scale=1.0, accum_out=sm[:, c0 + 2:c0 + 3]), r=[pk, kq[1]], w=[kq[2], kq[3]])
                self.V(lambda e, c0=c0: e.reciprocal(out=sm[:, c0 + 3:c0 + 4], in_=sm[:, c0 + 2:c0 + 3]), r=[kq[3]], w=[kq[4]])
                self.V(lambda e, sg=sg, c0=c0, e0=e0: e.tensor_scalar(out=self.aff[:, sg, :], in0=sm[:, e0:e0 + NE], scalar1=sm[:, c0 + 3:c0 + 4],
                                                                     scalar2=None, op0=ALU.mult), r=[kq[2], kq[4]], w=["aff"])
                self.dma(lambda e, sg=sg, r0=r0: e.dma_start(out=self.AFFd[r0:r0 + 128, :], in_=self.aff[:, sg, :]), r=["aff"], w=["AFFd"],
                         semkey=("st", "aff"))

            def wo_load(st):
                r0 = t0 + st * 128
                tr = self.row[st % 2]
                self.dma(lambda e, r0=r0, tr=tr: e.dma_start(out=tr[:], in_=self.Xd[r0:r0 + 128, :]), r=["Xd"], w=[("row", st % 2)])

            wo_load(0)
            wo_load(1)
            for step in range(5):
                if step < 4:
                    wo_p1(step)
                if step >= 1:
                    wo_p2(step - 1)
                    if step + 1 < 4:
                        wo_load(step + 1)

    def topk(self, l):
        ps = self.ps
        work = self.Bf[0:NE, :, :].rearrange("p a b -> p (a b)")
        wk = [("Bf", c) for c in range(8)]
        for bi in range(8):
            b = self.bank()
            pk = ("ps", b)
            for j in range(4):
                sg = bi * 4 + j
                self.T(lambda e, b=b, j=j, sg=sg: e.transpose(out=ps[0:NE, b, j * 128:(j + 1) * 128], in_=self.aff[:, sg, :],
                                                              identity=self.ident_f[:]), r=["aff", "ident_f"], w=[pk])
            self.V(lambda e, b=b, bi=bi: e.tensor_copy(out=work[:, bi * 512:(bi + 1) * 512], in_=ps[0:NE, b, :]), r=[pk], w=[wk[bi]])
        iota_t = self.KT[:].rearrange("p a n -> p (a n)").bitcast(I32)[0:NE, :]
        self.G(lambda e: e.iota(out=iota_t, pattern=[[1, S_LEN]], base=0, channel_multiplier=0), w=["KT"])
        wi = work.bitcast(I32)
        self.V(lambda e: e.tensor_scalar(out=wi, in0=wi, scalar1=self.bmask[0:NE, 0:1], scalar2=None, op0=ALU.bitwise_and),
               r=wk + ["bmask"], w=wk)
        self.V(lambda e: e.tensor_tensor(out=wi, in0=wi, in1=iota_t, op=ALU.bitwise_or), r=wk + ["KT"], w=wk)
        for r in range(CAP // 8):
            self.V(lambda e, r=r: e.max(out=self.topv[:, r * 8:(r + 1) * 8], in_=work), r=wk, w=["topv"])
            self.V(lambda e, r=r: e.match_replace(out=work, in_to_replace=self.topv[:, r * 8:(r + 1) * 8], in_values=work, imm_value=-1.0),
                   r=wk + ["topv"], w=wk)
        b = self.bank()
        pk = ("ps", b)
        for col in range(4):
            self.T(lambda e, b=b, col=col: e.transpose(out=ps[:, b, col * NE:(col + 1) * NE], in_=self.topv[:, col * 128:(col + 1) * 128],
                                                       identity=self.ident_f[0:NE, 0:NE]), r=["topv", "ident_f"], w=[pk])
        self.V(lambda e, b=b: e.tensor_copy(out=self.slot_g[:].rearrange("p e c -> p c e"),
                                            in_=ps[:, b, 0:4 * NE].rearrange("p (c e) -> p c e", c=4)), r=[pk], w=["slot_g"])
        self.V(lambda e: e.tensor_scalar(out=self.slot_i[:].bitcast(I32), in0=self.slot_g[:].bitcast(I32), scalar1=self.bmask[:, 1:2],
                                         scalar2=None, op0=ALU.bitwise_and), r=["slot_g", "bmask"], w=["slot_i"])

    def moe_specs(self, l):
        p = self.p
        specs = []
        for ex in range(NE):
            wg, wu, wd = p["w_gate"][l, ex], p["w_up"][l, ex], p["w_down"][l, ex]
            for i in range(4):
                specs.append([(0, 512, wg[:, i * 512:(i + 1) * 512])])
                specs.append([(0, 512, wu[:, i * 512:(i + 1) * 512])])
            for ch in range(2):
                for rh in range(2):
                    specs.append([(0, 512, wd[rh * 1024:(rh + 1) * 1024, ch * 512:(ch + 1) * 512])])
        return specs

    def moe(self, l):
        ps = self.ps
        specs = self.moe_specs(l)
        self.ws_reset(specs, hw_mask=[(i % 3) != 2 for i in range(len(specs))],
                      extra_slots=[(self.xT[:, :, 0:512], "xT")])
        xeb = self.Db
        xebv = xeb[:].rearrange("p (c a) n -> p c (a n)", c=4)

        def gather(ex):
            for col in range(4):
                self.dma(lambda e, ex=ex, col=col: e.indirect_dma_start(
                    out=self.gg[ex % 2][:, col, :], out_offset=None, in_=self.AFFd[:, :],
                    in_offset=bass.IndirectOffsetOnAxis(ap=self.slot_i[:, ex, col:col + 1], axis=0)),
                    r=["AFFd", "slot_i"], w=[("gg", ex % 2, col)], eng="gpsimd", semkey=("ggather", ex % 2, col))
                self.dma(lambda e, ex=ex, col=col: e.indirect_dma_start(
                    out=xebv[:, col, :], out_offset=None, in_=self.X1d[:, :],
                    in_offset=bass.IndirectOffsetOnAxis(ap=self.slot_i[:, ex, col:col + 1], axis=0)),
                    r=["X1d", "slot_i"], w=[("Db", 2 * col), ("Db", 2 * col + 1)], eng="gpsimd", semkey=("gather", col))

        def transposes(ex):
            for k in range(8):
                b = self.bank()
                pk = ("ps", b)
                pv = ps[:, b, :].bitcast(BF16)
                for col in range(4):
                    self.T(lambda e, pv=pv, col=col, k=k: e.transpose(out=pv[:, col * 128:(col + 1) * 128],
                                                                      in_=xebv[:, col, k * 128:(k + 1) * 128], identity=self.ident_b[:]),
                           r=[("Db", 2 * col), ("Db", 2 * col + 1), "ident_b"], w=[pk])
                self.A(lambda e, pv=pv, k=k: e.copy(out=self.Eb[:, k, :], in_=pv[:, 0:512]), r=[pk], w=[("Eb", k)])

        gather(0)
        transposes(0)
        gather(1)
        for ex in range(NE):
            ek = [("Eb", k) for k in range(8)]
            hid = [self.uT[:, f, 0:512] for f in range(8)] + [self.Cb[:, f, :] for f in range(8)]
            hk = [("uT", f) for f in range(8)] + [("Cb", f) for f in range(8)]
            for i in range(4):
                wg, wgk = self.ws_next()
                wu, wuk = self.ws_next()
                for jj in range(4):
                    f = i * 4 + jj
                    bg, bu = self.bank(), self.bank()
                    for (w_, wk_, b_) in ((wg, wgk, bg), (wu, wuk, bu)):
                        for k in range(8):
                            self.T(lambda e, w_=w_, b_=b_, k=k, jj=jj: e.matmul(ps[:, b_, :], lhsT=w_[:, k, jj * 128:(jj + 1) * 128],
                                                                               rhs=self.Eb[:, k, :], start=(k == 0), stop=(k == 7)),
                                   r=[wk_, ("Eb", k)], w=[("ps", b_)])
                    ti = self.tmpk()
                    sg = self.tmp[ti]
                    kt = ("tmp", ti)
                    self.A(lambda e, bg=bg, sg=sg: e.activation(out=sg[:], in_=ps[:, bg, :], func=AF.Silu), r=[("ps", bg)], w=[kt])
                    self.V(lambda e, bu=bu, sg=sg, f=f: e.tensor_tensor(out=hid[f], in0=sg[:], in1=ps[:, bu, :], op=ALU.mult),
                           r=[kt, ("ps", bu)], w=[hk[f]])
            if ex + 1 < NE:
                transposes(ex + 1)
            ye = self.Bf
            yev = ye[:].rearrange("p (c a) n -> p c (a n)", c=4)
            yk = [("Bf", c) for c in range(8)]
            for ch in range(2):
                wd0, wdk0 = self.ws_next()
                wd1, wdk1 = self.ws_next()
                for col in range(4):
                    b = self.bank()
                    pk = ("ps", b)
                    for f in range(16):
                        w_, wk_ = (wd0, wdk0) if f < 8 else (wd1, wdk1)
                        self.T(lambda e, b=b, f=f, col=col, w_=w_: e.matmul(ps[:, b, :], lhsT=hid[f][:, col * 128:(col + 1) * 128],
                                                                           rhs=w_[:, f % 8, :], start=(f == 0), stop=(f == 15)),
                               r=[wk_, hk[f]], w=[pk])
                    self.V(lambda e, b=b, col=col, ch=ch, ex=ex: e.tensor_scalar(
                        out=yev[:, col, ch * 512:(ch + 1) * 512], in0=ps[:, b, :], scalar1=self.gg[ex % 2][:, col, ex:ex + 1], scalar2=None,
                        op0=ALU.mult), r=[pk, ("gg", ex % 2, col)], w=[("Bf", col * 2), ("Bf", col * 2 + 1)])
            if ex + 2 < NE:
                gather(ex + 2)
            for col in range(4):
                self.dma(lambda e, ex=ex, col=col: e.indirect_dma_start(
                    out=self.ACCd[:, :], out_offset=bass.IndirectOffsetOnAxis(ap=self.slot_i[:, ex, col:col + 1], axis=0),
                    in_=yev[:, col, :], in_offset=None, compute_op=ALU.add),
                    r=yk + ["slot_i"] + (["ACCd"] if ex == 0 else [("sc", ex - 1, c) for c in range(4)]),
                    w=[("sc", ex, col)], eng="gpsimd", semkey=("scat", col))

    def final(self, l):
        p = self.p
        self.load_bc(0, p["ln2_g"][l:l + 1, :])
        self.load_bc(1, p["ln2_b"][l:l + 1, :])
        def f_load(sg):
            rw = self.row[sg % 3]
            self.dma(lambda e, rw=rw, r0=sg * 128: e.dma_start(out=rw[:], in_=self.ACCd[r0:r0 + 128, :]),
                     r=["ACCd"] + [("sc", NE - 1, c) for c in range(4)], w=[("row", sg % 3)])

        f_load(0)
        f_load(1)
        for sg in range(S_LEN // 128):
            r0 = sg * 128
            rw = self.row[sg % 3]
            rk = ("row", sg % 3)
            self.ln_rows(rw[:], rw[:], 0, 1, srck=rk, dstk=rk, par=sg % 2)
            self.dma(lambda e, rw=rw, r0=r0: e.dma_start(out=self.out[r0:r0 + 128, :], in_=rw[:]), r=[rk], w=["out"], semkey=("st", rk))
            if sg + 2 < S_LEN // 128:
                f_load(sg + 2)

    def build(self, stop_after=None):
        with ExitStack() as st:
            self.st = st
            self.declare()
            for nm, shp, dt in self.dbg_decl:
                self.dbg_out(nm, shp, dt)
            self.alloc()
            self.S = Sched(self.nc, st)
            self.setup_consts()
            p = self.p
            for li, l in enumerate(self.layers):
                self.load_layer_params(li)
                if l == 0:
                    src, gb = self.x_in, (p["ln0_g"], p["ln0_b"])
                else:
                    src, gb = self.ACCd, (p["ln2_g"][li - 1:li, :], p["ln2_b"][li - 1:li, :])
                self.pass1(li, src, gb)
                if stop_after == "pass1":
                    break
                self.pass2(li)
                if stop_after == "pass2":
                    break
                self.topk(li)
                if stop_after == "topk":
                    break
                self.moe(li)
            if stop_after is None:
                self.final(len(self.layers) - 1)
                fin = ["out"]
            else:
                fin = []
            self.debug_dumps(stop_after)
            fin += ["dbg_" + n for n in self.dbg_written]
            self.S.finish("sync", fin)
            self.S.emit()
        return self.nc

    dbg_decl = ()
    dbg_written = ()

    def debug_dumps(self, stop_after):
        pass


def rope_tables():
    t = np.arange(S_LEN)
    row = (t // 64).astype(np.float32)
    col = (t % 64).astype(np.float32)
    axis_dim = DH // 2
    freqs = (1.0 / (np.float32(10000.0) ** (np.arange(0, axis_dim, 2, dtype=np.float32) / np.float32(axis_dim)))).astype(np.float32)
    ang = np.concatenate([row[:, None] * freqs[None], col[:, None] * freqs[None]], axis=-1).astype(np.float32)
    cos = np.cos(ang).astype(np.float32)
    sin = np.sin(ang).astype(np.float32)
    C = np.repeat(cos.T, 2, axis=0)
    Sg = np.repeat(sin.T, 2, axis=0)
    sign = np.where(np.arange(DH) % 2 == 0, -1.0, 1.0).astype(np.float32)[:, None]
    return np.ascontiguousarray(C), np.ascontiguousarray(Sg * sign)


def const_inputs():
    ident = np.eye(128, dtype=np.float32)
    swap = np.zeros((128, 128), np.float32)
    idx = np.arange(128)
    swap[idx, idx ^ 1] = 1.0
    C, Sg = rope_tables()
    return {"c_ident": ident, "c_swap": swap, "c_ropeC": C, "c_ropeS": Sg}


def core_inputs(inputs, b, layers=None):
    m = {"x": np.ascontiguousarray(inputs["x"][b])}
    if layers is not None:
        inputs = dict(inputs)
        for nm in inputs:
            if nm not in ("x", "ln0_g", "ln0_b"):
                inputs[nm] = np.ascontiguousarray(inputs[nm][list(layers)])
    L = NL if layers is None else len(layers)
    m["ln0_g"] = inputs["ln0_g"].reshape(1, D)
    m["ln0_b"] = inputs["ln0_b"].reshape(1, D)
    m["b_in"] = inputs["b_in"].reshape(L, N_IN // 128, 128)
    for nm in ("conv_dw_b", "conv_ln_g", "conv_ln_b", "conv_pw_b"):
        m[nm] = inputs[nm].reshape(L, 8, 128)
    m["q_norm_g"] = inputs["q_norm_g"].reshape(L, 1, DH)
    m["k_norm_g"] = inputs["k_norm_g"].reshape(L, 1, DH)
    for nm in ("w_in", "conv_dw", "conv_pw_w", "w_o", "w_out", "b_out", "ln1_g", "ln1_b", "w_router", "w_gate", "w_up",
               "w_down", "ln2_g", "ln2_b"):
        m[nm] = inputs[nm]
    m.update(const_inputs())
    return m


def kernel(**inputs):
    inputs = {k: np.asarray(v) for k, v in inputs.items()}
    mk = MK()
    nc = mk.build()
    n = 4
    in_maps = [core_inputs(inputs, c) for c in range(n)]
    res = run_bass_kernel_spmd(nc, in_maps, core_ids=list(range(n)))
    out = np.stack([np.asarray(res.results[c]["out"]) for c in range(n)], axis=0)
    return out.astype(np.float32)
```

```python
import math
import numpy as np
from contextlib import ExitStack
import concourse.bass as bass
import concourse.mybir as mybir
from concourse.bass_utils import run_bass_kernel_spmd

F32 = mybir.dt.float32
BF16 = mybir.dt.bfloat16
U32 = mybir.dt.uint32
I32 = mybir.dt.int32
IDX_BITS = 12
AF = mybir.ActivationFunctionType
ALU = mybir.AluOpType

S_LEN = 4096
D = 1024
NL = 4
TT = 512
NT = S_LEN // TT
XW = 544
PADC = 16
NH = 8
NKV = 2
DH = 128
NE = 16
CAP = 512
DFF = 2048
KC = 31
N_IN = 5632
OFF_Q = 2048
OFF_K = 3072
OFF_V = 3328
OFF_GC = 3584
OFF_GA = 4608
LN_EPS = 1e-5
RMS_EPS = 1e-6
ALPHA = (2.0 * NL) ** 0.25
ATT_SCALE = 1.0 / math.sqrt(DH)
NRING = 4
NTMP = 10

ENGS = ("tensor", "vector", "scalar", "gpsimd", "sync")


class Sched:
    def __init__(self, nc, stack):
        self.nc = nc
        self.stack = stack
        self.prog = {e: [] for e in ENGS}
        self.cnt = {e: 0 for e in ENGS}
        self.esem = {e: stack.enter_context(nc.semaphore("es_" + e)) for e in ENGS}
        self.known = {e: {} for e in ENGS}
        self.last_w = {}
        self.readers = {}
        self.dsem = {}
        self.dcnt = {}
        self.nops = 0
        self.nwaits = 0

    def _sem_for_key(self, key):
        if key not in self.dsem:
            self.dsem[key] = self.stack.enter_context(self.nc.semaphore("ds%d" % len(self.dsem)))
            self.dcnt[key] = 0
        return self.dsem[key]

    def _waits(self, eng, reads, writes):
        waits = {}
        own = self.esem[eng].num

        def need(sv, raw):
            s, v = sv
            if s.num == own and not raw and eng == "tensor":
                return
            cur = waits.get(s.num)
            if cur is None or cur[1] < v:
                waits[s.num] = (s, v)

        for k in reads:
            for sv in self.last_w.get(k, {}).values():
                need(sv, True)
        for k in writes:
            for sv in self.last_w.get(k, {}).values():
                need(sv, False)
            for sv in self.readers.get(k, {}).values():
                need(sv, False)
        out = []
        kn = self.known[eng]
        for num, (s, v) in waits.items():
            if kn.get(num, 0) >= v:
                continue
            kn[num] = v
            out.append((s, v))
        return out

    def _commit(self, s, v, reads, writes):
        for k in writes:
            self.last_w.setdefault(k, {})[s.num] = (s, v)
            self.readers[k] = {}
        for k in reads:
            self.readers.setdefault(k, {})[s.num] = (s, v)

    def op(self, eng, fn, reads=(), writes=()):
        waits = self._waits(eng, reads, writes)
        self.cnt[eng] += 1
        s = self.esem[eng]
        self.prog[eng].append((waits, fn, s, 1))
        self._commit(s, self.cnt[eng], reads, writes)
        self.nops += 1
        self.nwaits += len(waits)

    def dma(self, eng, fn, reads=(), writes=(), semkey=None):
        waits = self._waits(eng, reads, writes)
        key = semkey if semkey is not None else writes[0]
        s = self._sem_for_key(key)
        self.dcnt[key] += 16
        self.prog[eng].append((waits, fn, s, 16))
        self._commit(s, self.dcnt[key], reads, writes)
        self.nops += 1
        self.nwaits += len(waits)

    def finish(self, eng, keys):
        waits = self._waits(eng, keys, ())
        self.prog[eng].append((waits, None, None, 0))

    def emit(self):
        with self.nc.Block() as block:
            for e in ENGS:
                prog = self.prog[e]
                if not prog:
                    continue

                def body(engine, prog=prog):
                    for waits, fn, s, inc in prog:
                        for ws, wv in waits:
                            engine.wait_ge(ws, wv)
                        if fn is not None:
                            fn(engine).then_inc(s, inc)

                getattr(block, e)(body)


class MK:
    def __init__(self, layers=(0, 1, 2, 3), first=True, last=True, dbg=()):
        self.layers = list(layers)
        self.first = first
        self.last = last
        self.dbg = set(dbg)
        self.nc = bass.Bass("TRN2", target_bir_lowering=False)
        self.bank_rr = 0
        self.pinned = set()
        self.tmp_rr = 0

    def din(self, name, shape, dt=F32):
        return self.nc.dram_tensor(name, list(shape), dt, kind="ExternalInput").ap()

    def sb(self, name, shape, dt):
        return self.st.enter_context(self.nc.sbuf_tensor(name, list(shape), dt))

    def declare(self):
        nc = self.nc
        L = len(self.layers)
        self.x_in = self.din("x", [S_LEN, D])
        self.p = {}
        for nm, shp in [("ln0_g", [1, D]), ("ln0_b", [1, D]), ("w_in", [L, D, N_IN]), ("b_in", [L, N_IN // 128, 128]),
                        ("conv_dw", [L, KC, D]), ("conv_dw_b", [L, 8, 128]), ("conv_ln_g", [L, 8, 128]),
                        ("conv_ln_b", [L, 8, 128]), ("conv_pw_w", [L, D, D]), ("conv_pw_b", [L, 8, 128]),
                        ("q_norm_g", [L, 1, DH]), ("k_norm_g", [L, 1, DH]), ("w_o", [L, D, D]), ("w_out", [L, D, D]),
                        ("b_out", [L, D]), ("ln1_g", [L, D]), ("ln1_b", [L, D]), ("w_router", [L, D, NE]),
                        ("w_gate", [L, NE, D, DFF]), ("w_up", [L, NE, D, DFF]), ("w_down", [L, NE, DFF, D]),
                        ("ln2_g", [L, D]), ("ln2_b", [L, D])]:
            self.p[nm] = self.din(nm, shp)
        self.c_ident = self.din("c_ident", [128, 128])
        self.c_swap = self.din("c_swap", [128, 128])
        self.c_ropeC = self.din("c_ropeC", [128, S_LEN])
        self.c_ropeS = self.din("c_ropeS", [128, S_LEN])
        self.out = nc.dram_tensor("out", [S_LEN, D], F32, kind="ExternalOutput").ap()
        self.Xd = nc.dram_tensor("Xd", [S_LEN, D], F32, kind="Internal").ap()
        self.XTd = nc.dram_tensor("XTd", [D, S_LEN + 2 * PADC], BF16, kind="Internal").ap()
        self.X1d = nc.dram_tensor("X1d", [S_LEN, D], BF16, kind="Internal").ap()
        self.ACCd = nc.dram_tensor("ACCd", [S_LEN, D], F32, kind="Internal").ap()
        self.AFFd = nc.dram_tensor("AFFd", [S_LEN, NE], F32, kind="Internal").ap()
        self.dbg_t = {}

    def dbg_out(self, name, shape, dt=F32):
        t = self.nc.dram_tensor("dbg_" + name, list(shape), dt, kind="ExternalOutput").ap()
        self.dbg_t[name] = t
        return t

    def alloc(self):
        sb = self.sb
        self.KT = sb("KT", [128, NKV, S_LEN], BF16)
        self.Vt = sb("Vt", [128, S_LEN // 128, NKV * DH], BF16)
        self.aff = sb("aff", [128, S_LEN // 128, NE], F32)
        self.ident_f = sb("ident_f", [128, 128], F32)
        self.ident_b = sb("ident_b", [128, 128], BF16)
        self.swapm = sb("swapm", [128, 128], F32)
        self.ones_d = sb("ones_d", [128, 128], F32)
        self.ones_h = sb("ones_h", [128, 128], F32)
        self.ones_b = sb("ones_b", [128, 128], BF16)
        self.prm_rows = sb("prm_rows", [80, 128], F32)
        self.prm = sb("prm", [128, 80], F32)
        self.P2 = [sb("P2_%d" % i, [128, TT], BF16) for i in range(4)]
        self.dw_pp = sb("dw_pp", [128, 8, KC], F32)
        self.bc = [sb("bc%d" % i, [128, D], F32) for i in range(3)]
        self.bv_bc = sb("bv_bc", [128, NKV * DH], F32)
        self.wr = sb("wr", [128, 8, NE], F32)
        self.ring = [sb("ring%d" % i, [128, 8, 512], BF16) for i in range(NRING)]
        self.xT = sb("xT", [128, 8, XW], BF16)
        self.uT = sb("uT", [128, 8, XW], BF16)
        self.Bf = sb("Bf", [128, 8, TT], F32)
        self.Cb = sb("Cb", [128, 8, TT], BF16)
        self.Db = sb("Db", [128, 8, TT], BF16)
        self.Eb = sb("Eb", [128, 8, TT], BF16)
        self.tmp = [sb("tmp%d" % i, [128, TT], F32) for i in range(NTMP)]
        self.PT = [sb("PT%d" % i, [128, 2 * TT], BF16) for i in range(3)]
        self.ropeC = [sb("ropeC%d" % i, [128, TT], F32) for i in range(2)]
        self.ropeS = [sb("ropeS%d" % i, [128, TT], F32) for i in range(2)]
        self.row = [sb("row%d" % i, [128, D], F32) for i in range(3)]
        self.rowb = sb("rowb", [128, D], BF16)
        self.x1T = sb("x1T", [128, 8, 128], F32)
        self.dg = [sb("dg%d" % i, [128, 128], BF16) for i in range(8)]
        self.small = sb("small", [128, 64], F32)
        self.bst = [sb("bst%d" % i, [128, 2, 6], F32) for i in range(2)]
        self.zpad = sb("zpad", [128, 8, PADC], BF16)
        self.topv = sb("topv", [NE, CAP], F32)
        self.gg = [sb("gg%d" % i, [128, 4, NE], F32) for i in range(2)]
        self.bmask = sb("bmask", [128, 2], I32)
        self.slot_g = sb("slot_g", [128, NE, 4], F32)
        self.slot_i = sb("slot_i", [128, NE, 4], U32)
        self.ps = self.st.enter_context(self.nc.psum_tensor("ps", [128, 8, 512], F32))

    def bank(self, pin=False):
        while True:
            b = self.bank_rr
            self.bank_rr = (self.bank_rr + 1) % 8
            if b not in self.pinned:
                break
        if pin:
            self.pinned.add(b)
        return b

    def unpin(self, b):
        self.pinned.discard(b)

    def T(self, fn, r=(), w=()):
        self.S.op("tensor", fn, r, w)

    def V(self, fn, r=(), w=()):
        self.S.op("vector", fn, r, w)

    def A(self, fn, r=(), w=()):
        self.S.op("scalar", fn, r, w)

    def G(self, fn, r=(), w=()):
        self.S.op("gpsimd", fn, r, w)

    def dma(self, fn, r=(), w=(), eng="sync", semkey=None):
        self.S.dma(eng, fn, r, w, semkey)

    def uodd(self, c):
        buf, nm = (self.Db, "Db") if c < 4 else (self.Eb, "Eb")
        cc = c % 4
        ap = buf[:].rearrange("p a n -> p (a n)")[:, cc * 1024:cc * 1024 + XW]
        return ap, [(nm, 2 * cc), (nm, 2 * cc + 1)]

    def tmpk(self):
        i = self.tmp_rr
        self.tmp_rr = (self.tmp_rr + 1) % NTMP
        return i

    def ws_reset(self, specs, hw_mask=None, extra_slots=()):
        self.ws_specs = specs
        self.ws_slots = [(self.ring[i][:], ("ring", i)) for i in range(NRING)] + list(extra_slots)
        self.ws_n = len(self.ws_slots)
        self.ws_issued = 0
        self.ws_used = 0
        self.ws_hw = hw_mask
        if hw_mask is not None:
            self.ws_hwlist = [i for i in range(len(specs)) if hw_mask[i]]
            self.ws_hwdma = 0
            self.stg = [(self.KT[:].rearrange("p a n -> p (a n)").bitcast(F32).rearrange("p (k n) -> p k n", k=8), "KT"),
                        (self.Vt[:].rearrange("p a n -> p (a n)").bitcast(F32).rearrange("p (k n) -> p k n", k=8), "Vt")]
            self.ws_hw_dma_upto(2)

    def ws_hw_dma_upto(self, n):
        n = min(n, len(self.ws_hwlist))
        while self.ws_hwdma < n:
            hn = self.ws_hwdma
            i = self.ws_hwlist[hn]
            stg, sk = self.stg[hn % 2]
            for (c0, ncol, src) in self.ws_specs[i]:
                self.dma(lambda e, stg=stg, c0=c0, ncol=ncol, src=src:
                         e.dma_start(out=stg[:, :, c0:c0 + ncol], in_=src.rearrange("(k p) n -> p k n", p=128)),
                         r=(), w=[sk], eng=("sync" if hn % 2 == 0 else "scalar"))
            self.ws_hwdma += 1

    def ws_issue_upto(self, n):
        n = min(n, len(self.ws_specs))
        while self.ws_issued < n:
            i = self.ws_issued
            sap, skey = self.ws_slots[i % self.ws_n]
            if self.ws_hw is not None and self.ws_hw[i]:
                hn = self.ws_hwlist.index(i)
                stg, sk = self.stg[hn % 2]
                self.A(lambda e, sap=sap, stg=stg: e.copy(out=sap, in_=stg), r=[sk], w=[skey])
                self.ws_hw_dma_upto(hn + 3)
            else:
                for (c0, ncol, src) in self.ws_specs[i]:
                    self.dma(lambda e, sap=sap, c0=c0, ncol=ncol, src=src:
                             e.dma_start(out=sap[:, :, c0:c0 + ncol],
                                         in_=src.rearrange("(k p) n -> p k n", p=128)),
                             r=(), w=[skey], eng="gpsimd", semkey=("sw", skey))
            self.ws_issued += 1

    def ws_next(self):
        i = self.ws_used
        self.ws_issue_upto(i + self.ws_n - 1)
        self.ws_used += 1
        return self.ws_slots[i % self.ws_n]

    def setup_consts(self):
        self.dma(lambda e: e.dma_start(out=self.ident_f[:], in_=self.c_ident), w=["ident_f"])
        self.dma(lambda e: e.dma_start(out=self.swapm[:], in_=self.c_swap), w=["swapm"])
        self.V(lambda e: e.tensor_copy(out=self.ident_b[:], in_=self.ident_f[:]), r=["ident_f"], w=["ident_b"])
        self.V(lambda e: e.memset(self.ones_d[:], 1.0 / D), w=["ones_d"])
        self.V(lambda e: e.memset(self.ones_h[:], 1.0 / DH), w=["ones_h"])
        self.V(lambda e: e.memset(self.ones_b[:], 1.0), w=["ones_b"])
        self.V(lambda e: e.memset(self.bmask[:, 0:1], -(1 << IDX_BITS)), w=["bmask"])
        self.V(lambda e: e.memset(self.bmask[:, 1:2], (1 << IDX_BITS) - 1), w=["bmask"])
        self.V(lambda e: e.memset(self.zpad[:], 0.0), w=["zpad"])
        for c0 in (0, PADC + S_LEN):
            self.dma(lambda e, c0=c0: e.dma_start(out=self.XTd[:, c0:c0 + PADC].rearrange("(k p) n -> p k n", p=128),
                                                  in_=self.zpad[:]), r=["zpad"], w=["XTd"], semkey=("st", "zpad"))

    def load_bc(self, i, src_row):
        self.dma(lambda e: e.dma_start(out=self.bc[i][:], in_=src_row.to_broadcast([128, D])), w=[("bc", i)])

    P_BIN = 0
    P_DWB = 44
    P_CLG = 52
    P_CLB = 60
    P_PWB = 68
    P_QG = 76
    P_KG = 77

    def load_layer_params(self, l):
        p = self.p
        rows = self.prm_rows
        segs = [(self.P_BIN, 44, p["b_in"][l]), (self.P_DWB, 8, p["conv_dw_b"][l]), (self.P_CLG, 8, p["conv_ln_g"][l]),
                (self.P_CLB, 8, p["conv_ln_b"][l]), (self.P_PWB, 8, p["conv_pw_b"][l]), (self.P_QG, 1, p["q_norm_g"][l]),
                (self.P_KG, 1, p["k_norm_g"][l])]
        for (r0, n, src) in segs:
            self.dma(lambda e, r0=r0, n=n, src=src: e.dma_start(out=rows[r0:r0 + n, :], in_=src), w=["prm_rows"])
        b = self.bank()
        pk = ("ps", b)
        self.T(lambda e, b=b: e.transpose(out=self.ps[:, b, 0:78], in_=rows[0:78, :], identity=self.ident_f[0:78, 0:78]),
               r=["prm_rows", "ident_f"], w=[pk])
        self.V(lambda e, b=b: e.tensor_copy(out=self.prm[:, 0:78], in_=self.ps[:, b, 0:78]), r=[pk], w=["prm"])
        dw_rows = self.row[2][0:KC, :]
        self.dma(lambda e: e.dma_start(out=dw_rows, in_=p["conv_dw"][l]), w=[("row", 2)])
        for c in range(8):
            b = self.bank()
            pk = ("ps", b)
            self.T(lambda e, b=b, c=c: e.transpose(out=self.ps[:, b, 0:KC], in_=dw_rows[:, c * 128:(c + 1) * 128],
                                                   identity=self.ident_f[0:KC, 0:KC]),
                   r=[("row", 2), "ident_f"], w=[pk])
            self.V(lambda e, b=b, c=c: e.tensor_copy(out=self.dw_pp[:, c, :], in_=self.ps[:, b, 0:KC]), r=[pk], w=["dw_pp"])
        self.dma(lambda e: e.dma_start(out=self.bv_bc[:], in_=p["b_in"][l].rearrange("c p -> (c p)")[OFF_V:OFF_V + 256]
                                       .rearrange("(o n) -> o n", o=1).to_broadcast([128, 256])), w=["bv_bc"])
        self.dma(lambda e: e.dma_start(out=self.wr[:], in_=p["w_router"][l].rearrange("(k p) n -> p k n", p=128)), w=["wr"])

    def ln_rows(self, src, dst, gi, bi, eps=LN_EPS, srck=None, dstk=None, par=0):
        sm = self.small
        c0 = par * 4
        bst = self.bst[par]
        kb0, kb1, kmv, ksd, krs = (("lnst", par, i) for i in range(5))
        self.V(lambda e: e.bn_stats(out=bst[:, 0, :], in_=src[:, 0:512]), r=[srck], w=[kb0])
        self.V(lambda e: e.bn_stats(out=bst[:, 1, :], in_=src[:, 512:1024]), r=[srck], w=[kb1])
        self.V(lambda e: e.bn_aggr(out=sm[:, c0:c0 + 2], in_=bst[:].rearrange("p a b -> p (a b)")), r=[kb0, kb1], w=[kmv])
        self.A(lambda e: e.activation(out=sm[:, c0 + 2:c0 + 3], in_=sm[:, c0 + 1:c0 + 2], func=AF.Sqrt, bias=eps, scale=1.0), r=[kmv], w=[ksd])
        self.V(lambda e: e.reciprocal(out=sm[:, c0 + 3:c0 + 4], in_=sm[:, c0 + 2:c0 + 3]), r=[ksd], w=[krs])
        self.V(lambda e: e.scalar_tensor_tensor(out=dst, in0=src, scalar=sm[:, c0:c0 + 1], in1=self.bc[gi][:],
                                                op0=ALU.subtract, op1=ALU.mult), r=[srck, kmv, ("bc", gi)], w=[dstk])
        self.V(lambda e: e.scalar_tensor_tensor(out=dst, in0=dst, scalar=sm[:, c0 + 3:c0 + 4], in1=self.bc[bi][:],
                                                op0=ALU.mult, op1=ALU.add), r=[dstk, krs, ("bc", bi)], w=[dstk])

    def nr_stageA(self, pk, psap, bias_ap):
        i_raw, i_sq, i_rs = self.tmpk(), self.tmpk(), self.tmpk()
        stt = {"raw": self.tmp[i_raw], "sq": self.tmp[i_sq], "rs": self.tmp[i_rs],
               "kr": ("tmp", i_raw), "ks": ("tmp", i_sq), "krs": ("tmp", i_rs)}
        raw, sq = stt["raw"], stt["sq"]
        self.A(lambda e: e.activation(out=raw[:], in_=psap, func=AF.Identity, bias=bias_ap, scale=1.0), r=[pk, "prm"], w=[stt["kr"]])
        self.A(lambda e: e.activation(out=sq[:], in_=psap, func=AF.Square, bias=bias_ap, scale=1.0), r=[pk, "prm"], w=[stt["ks"]])
        return stt

    def nr_stageB(self, stt, g_ap):
        raw, sq, rs = stt["raw"], stt["sq"], stt["rs"]
        kr, ks, krs = stt["kr"], stt["ks"], stt["krs"]
        b2 = self.bank()
        pk2 = ("ps", b2)
        self.T(lambda e: e.matmul(self.ps[:, b2, :], lhsT=self.ones_h[:], rhs=sq[:], start=True, stop=True),
               r=[ks, "ones_h"], w=[pk2])
        self.A(lambda e: e.activation(out=rs[:], in_=self.ps[:, b2, :], func=AF.Ln, bias=RMS_EPS, scale=1.0), r=[pk2], w=[krs])
        self.A(lambda e: e.activation(out=rs[:], in_=rs[:], func=AF.Exp, scale=-0.5), r=[krs], w=[krs])
        self.V(lambda e: e.scalar_tensor_tensor(out=raw[:], in0=raw[:], scalar=g_ap, in1=rs[:], op0=ALU.mult, op1=ALU.mult),
               r=[kr, krs, "prm"], w=[kr])

    def nr_stageC(self, stt, rp, out_ap, outk):
        raw, sq = stt["raw"], stt["sq"]
        kr, ks = stt["kr"], stt["ks"]
        b3 = self.bank()
        pk3 = ("ps", b3)
        self.T(lambda e: e.matmul(self.ps[:, b3, :], lhsT=self.swapm[:], rhs=raw[:], start=True, stop=True),
               r=[kr, "swapm"], w=[pk3])
        self.V(lambda e: e.tensor_tensor(out=sq[:], in0=self.ps[:, b3, :], in1=self.ropeS[rp][:], op=ALU.mult),
               r=[pk3, ("ropeS", rp)], w=[ks])
        self.G(lambda e: e.tensor_tensor(out=raw[:], in0=raw[:], in1=self.ropeC[rp][:], op=ALU.mult), r=[kr, ("ropeC", rp)], w=[kr])
        self.V(lambda e: e.tensor_tensor(out=out_ap, in0=raw[:], in1=sq[:], op=ALU.add), r=[kr, ks], w=[outk])

    def norm_rope_pipe(self, n, projA, g_ap, rp, outs):
        stts = [None] * n
        for step in range(n + 2):
            if step < n:
                pk, psap, bias_ap = projA(step)
                stts[step] = self.nr_stageA(pk, psap, bias_ap)
            if 0 <= step - 1 < n:
                self.nr_stageB(stts[step - 1], g_ap)
            if 0 <= step - 2 < n:
                self.nr_stageC(stts[step - 2], rp, *outs[step - 2])

    def load_rope(self, t, rp):
        t0 = t * TT
        self.dma(lambda e: e.dma_start(out=self.ropeC[rp][:], in_=self.c_ropeC[:, t0:t0 + TT]), w=[("ropeC", rp)])
        self.dma(lambda e: e.dma_start(out=self.ropeS[rp][:], in_=self.c_ropeS[:, t0:t0 + TT]), w=[("ropeS", rp)])

    def pass1(self, l, src_d, gi_src):
        p = self.p
        g_row, b_row = gi_src
        self.load_bc(0, g_row)
        self.load_bc(1, b_row)
        self.ws_reset([[(0, 512, p["w_in"][l][:, OFF_K:OFF_K + 512])]])
        wkv, wk_key = self.ws_next()

        def p1_load(sgi):
            rw = self.row[sgi % 3]
            r0 = sgi * 128
            self.dma(lambda e, rw=rw, r0=r0: e.dma_start(out=rw[:], in_=src_d[r0:r0 + 128, :]),
                     r=["ACCd"] + [("sc", NE - 1, c) for c in range(4)], w=[("row", sgi % 3)])

        for t in range(NT):
            t0 = t * TT
            rp = t % 2
            self.load_rope(t, rp)
            for st in range(4):
                r0 = t0 + st * 128
                sgi = t * 4 + st
                if sgi == 0:
                    p1_load(0)
                    p1_load(1)
                rw = self.row[sgi % 3]
                rk = ("row", sgi % 3)
                self.ln_rows(rw[:], rw[:], 0, 1, srck=rk, dstk=rk, par=st % 2)
                self.dma(lambda e, rw=rw, r0=r0: e.dma_start(out=self.Xd[r0:r0 + 128, :], in_=rw[:]), r=[rk], w=["Xd"], semkey=("st", rk))
                for half in range(2):
                    b = self.bank()
                    pk = ("ps", b)
                    for j in range(4):
                        k = half * 4 + j
                        self.T(lambda e, b=b, j=j, k=k, rw=rw: e.transpose(out=self.ps[:, b, j * 128:(j + 1) * 128],
                                                                          in_=rw[:, k * 128:(k + 1) * 128], identity=self.ident_f[:]),
                               r=[rk, "ident_f"], w=[pk])
                    self.A(lambda e, b=b, half=half, st=st: e.copy(
                        out=self.xT[:, half * 4:half * 4 + 4, st * 128:(st + 1) * 128],
                        in_=self.ps[:, b, :].rearrange("p (j n) -> p j n", j=4)), r=[pk], w=["xT"])
                if sgi + 2 < S_LEN // 128:
                    p1_load(sgi + 2)
            self.dma(lambda e, t0=t0: e.dma_start(out=self.XTd[:, PADC + t0:PADC + t0 + TT].rearrange("(k p) n -> p k n", p=128),
                                                  in_=self.xT[:, :, 0:TT]), r=["xT"], w=["XTd"], semkey=("st", "xT"))
            def projK(h, t0=t0):
                b = self.bank()
                pk = ("ps", b)
                for k in range(8):
                    self.T(lambda e, b=b, k=k, h=h: e.matmul(self.ps[:, b, :], lhsT=wkv[:, k, h * 128:(h + 1) * 128],
                                                            rhs=self.xT[:, k, 0:TT], start=(k == 0), stop=(k == 7)),
                           r=[wk_key, "xT"], w=[pk])
                cb = self.P_BIN + (OFF_K // 128) + h
                return pk, self.ps[:, b, :], self.prm[:, cb:cb + 1]

            self.norm_rope_pipe(NKV, projK, self.prm[:, self.P_KG:self.P_KG + 1], rp,
                                [(self.KT[:, h, t0:t0 + TT], "KT") for h in range(NKV)])
            for st in range(4):
                b = self.bank()
                pk = ("ps", b)
                for k in range(8):
                    self.T(lambda e, b=b, k=k, st=st: e.matmul(self.ps[:, b, 0:256], lhsT=self.xT[:, k, st * 128:(st + 1) * 128],
                                                              rhs=wkv[:, k, 256:512], start=(k == 0), stop=(k == 7)),
                           r=[wk_key, "xT"], w=[pk])
                sg = t * 4 + st
                self.V(lambda e, b=b, sg=sg: e.tensor_tensor(out=self.Vt[:, sg, :], in0=self.ps[:, b, 0:256], in1=self.bv_bc[:],
                                                            op=ALU.add), r=[pk, "bv_bc"], w=["Vt"])

    def pass2_specs(self, l):
        p = self.p
        w_in = p["w_in"][l]
        one = []
        for i in range(4):
            one.append([(0, 256, w_in[:, i * 256:(i + 1) * 256]), (256, 256, w_in[:, D + i * 256:D + (i + 1) * 256])])
        for i in range(2):
            one.append([(0, 512, w_in[:, OFF_GC + i * 512:OFF_GC + (i + 1) * 512])])
        for i in range(2):
            one.append([(0, 512, p["conv_pw_w"][l][:, i * 512:(i + 1) * 512])])
        for i in range(2):
            one.append([(0, 512, w_in[:, OFF_Q + i * 512:OFF_Q + (i + 1) * 512])])
        for i in range(2):
            one.append([(0, 512, w_in[:, OFF_GA + i * 512:OFF_GA + (i + 1) * 512])])
            one.append([(0, 512, p["w_o"][l][:, i * 512:(i + 1) * 512])])
        for i in range(2):
            one.append([(0, 512, p["w_out"][l][:, i * 512:(i + 1) * 512])])
        return one

    def pass2(self, l):
        p = self.p
        prm = self.prm
        ps = self.ps
        self.load_bc(0, p["ln1_g"][l:l + 1, :])
        self.load_bc(1, p["ln1_b"][l:l + 1, :])
        self.load_bc(2, p["b_out"][l:l + 1, :])
        specs = []
        for t in range(NT):
            specs += self.pass2_specs(l)
        self.ws_reset(specs)
        for t in range(NT):
            t0 = t * TT
            rp = t % 2
            self.load_rope(t, rp)
            self.dma(lambda e, t0=t0: e.dma_start(out=self.xT[:], in_=self.XTd[:, t0:t0 + XW].rearrange("(k p) n -> p k n", p=128)),
                     r=["XTd"], w=["xT"])
            for i in range(4):
                wp, wk = self.ws_next()
                for cc in range(2):
                    c = i * 2 + cc
                    bA, bB, bC = self.bank(), self.bank(), self.bank()
                    kA, kB, kC = ("ps", bA), ("ps", bB), ("ps", bC)
                    for (col0, bm, bh, hc) in ((cc * 128, bA, bB, 0), (256 + cc * 128, bC, bB, 32)):
                        for k in range(8):
                            self.T(lambda e, k=k, col0=col0, bm=bm, wp=wp: e.matmul(ps[:, bm, :], lhsT=wp[:, k, col0:col0 + 128],
                                                                            rhs=self.xT[:, k, 0:512], start=(k == 0), stop=(k == 7)),
                                   r=[wk, "xT"], w=[("ps", bm)])
                        for k in range(8):
                            self.T(lambda e, k=k, col0=col0, bh=bh, hc=hc, wp=wp: e.matmul(ps[:, bh, hc:hc + 32], lhsT=wp[:, k, col0:col0 + 128],
                                                                                    rhs=self.xT[:, k, 512:544], start=(k == 0), stop=(k == 7)),
                                   r=[wk, "xT"], w=[("ps", bh)])
                    cv = self.P_BIN + c
                    cg = self.P_BIN + 8 + c
                    ti = self.tmpk()
                    sg = self.tmp[ti]
                    kt = ("tmp", ti)
                    ti2 = self.tmpk()
                    sg2 = self.tmp[ti2]
                    kt2 = ("tmp", ti2)
                    self.A(lambda e, bC=bC, cg=cg, sg=sg: e.activation(out=sg[:], in_=ps[:, bC, :], func=AF.Sigmoid,
                                                                      bias=prm[:, cg:cg + 1], scale=1.0), r=[kC, "prm"], w=[kt])
                    self.A(lambda e, bB=bB, cg=cg, sg2=sg2: e.activation(out=sg2[:, 0:32], in_=ps[:, bB, 32:64], func=AF.Sigmoid,
                                                                        bias=prm[:, cg:cg + 1], scale=1.0), r=[kB, "prm"], w=[kt2])
                    uk = ("uT", c)
                    self.V(lambda e, bA=bA, cv=cv, sg=sg, c=c: e.scalar_tensor_tensor(out=self.uT[:, c, 0:512], in0=ps[:, bA, :],
                                                                                     scalar=prm[:, cv:cv + 1], in1=sg[:],
                                                                                     op0=ALU.add, op1=ALU.mult), r=[kA, kt, "prm"], w=[uk])
                    self.V(lambda e, bB=bB, cv=cv, sg2=sg2, c=c: e.scalar_tensor_tensor(out=self.uT[:, c, 512:544], in0=ps[:, bB, 0:32],
                                                                                       scalar=prm[:, cv:cv + 1], in1=sg2[:, 0:32],
                                                                                       op0=ALU.add, op1=ALU.mult), r=[kB, kt2, "prm"], w=[uk])
                    if t == 0:
                        self.V(lambda e, c=c: e.memset(self.uT[:, c, 0:16], 0.0), w=[uk])
                    if t == NT - 1:
                        self.V(lambda e, c=c: e.memset(self.uT[:, c, 528:544], 0.0), w=[uk])
            bM = self.bank(pin=True)
            bQ = self.bank(pin=True)
            kM, kQ = ("ps", bM), ("ps", bQ)
            stat_q = []
            for c in range(8):
                b = self.bank()
                pk = ("ps", b)
                if len(stat_q) > 1:
                    for mm in stat_q.pop(0):
                        self.T(*mm)
                dbuf, dnm = (self.Db, "Db") if c % 2 == 0 else (self.Eb, "Eb")
                dkeys = [(dnm, i) for i in range(8)]
                dflat = dbuf[:].rearrange("p a n -> p (a n)")[:, 0:KC * 128].rearrange("p (k m) -> p k m", k=KC)
                self.V(lambda e, c=c, dflat=dflat: e.tensor_tensor(
                    out=dflat, in0=self.ident_b[:].unsqueeze(1).to_broadcast([128, KC, 128]),
                    in1=self.dw_pp[:, c, :].unsqueeze(2).to_broadcast([128, KC, 128]), op=ALU.mult),
                    r=["ident_b", "dw_pp"], w=dkeys)
                for k in range(KC):
                    self.T(lambda e, b=b, c=c, k=k, dflat=dflat: e.matmul(ps[:, b, :], lhsT=dflat[:, k, :], rhs=self.uT[:, c, k + 1:k + 1 + 512],
                                                                         start=(k == 0), stop=(k == KC - 1)), r=dkeys + [("uT", c)], w=[pk])
                cdb = self.P_DWB + c
                self.A(lambda e, b=b, c=c, cdb=cdb: e.activation(out=self.Bf[:, c, :], in_=ps[:, b, :], func=AF.Identity,
                                                                bias=prm[:, cdb:cdb + 1], scale=1.0), r=[pk, "prm"], w=[("Bf", c)])
                ti = self.tmpk()
                sq = self.tmp[ti]
                kt = ("tmp", ti)
                self.A(lambda e, b=b, cdb=cdb, sq=sq: e.activation(out=sq[:], in_=ps[:, b, :], func=AF.Square,
                                                                  bias=prm[:, cdb:cdb + 1], scale=1.0), r=[pk, "prm"], w=[kt])
                stat_q.append(((lambda e, c=c, bM=bM: e.matmul(ps[:, bM, :], lhsT=self.ones_d[:], rhs=self.Bf[:, c, :],
                                                                start=(c == 0), stop=(c == 7)), [("Bf", c), "ones_d"], [kM]),
                               (lambda e, c=c, sq=sq, bQ=bQ: e.matmul(ps[:, bQ, :], lhsT=self.ones_d[:], rhs=sq[:],
                                                                       start=(c == 0), stop=(c == 7)), [kt, "ones_d"], [kQ])))
            while stat_q:
                for mm in stat_q.pop(0):
                    self.T(*mm)
            im, iv = self.tmpk(), self.tmpk()
            mean, rstd = self.tmp[im], self.tmp[iv]
            kmn, krs = ("tmp", im), ("tmp", iv)
            self.A(lambda e, mean=mean, bM=bM: e.copy(out=mean[:], in_=ps[:, bM, :]), r=[kM], w=[kmn])
            self.V(lambda e, mean=mean, rstd=rstd, bM=bM: e.tensor_tensor(out=rstd[:], in0=mean[:], in1=ps[:, bM, :], op=ALU.mult), r=[kmn, kM], w=[krs])
            self.V(lambda e, rstd=rstd, bQ=bQ: e.tensor_tensor(out=rstd[:], in0=ps[:, bQ, :], in1=rstd[:], op=ALU.subtract), r=[kQ, krs], w=[krs])
            self.A(lambda e, rstd=rstd: e.activation(out=rstd[:], in_=rstd[:], func=AF.Ln, bias=LN_EPS, scale=1.0), r=[krs], w=[krs])
            self.A(lambda e, rstd=rstd: e.activation(out=rstd[:], in_=rstd[:], func=AF.Exp, scale=-0.5), r=[krs], w=[krs])
            self.unpin(bM)
            self.unpin(bQ)
            gcs = []
            for i in range(2):
                wgc, wgck = self.ws_next()
                for jj in range(4):
                    j = i * 4 + jj
                    bg = self.bank()
                    kg_ = ("ps", bg)
                    for k in range(8):
                        self.T(lambda e, bg=bg, k=k, jj=jj, wgc=wgc: e.matmul(ps[:, bg, :], lhsT=wgc[:, k, jj * 128:(jj + 1) * 128],
                                                                             rhs=self.xT[:, k, PADC:PADC + TT], start=(k == 0), stop=(k == 7)),
                               r=[wgck, "xT"], w=[kg_])
                    cgc = self.P_BIN + OFF_GC // 128 + j
                    ti = self.tmpk()
                    gs = self.tmp[ti]
                    kt = ("tmp", ti)
                    self.A(lambda e, bg=bg, cgc=cgc, gs=gs: e.activation(out=gs[:], in_=ps[:, bg, :], func=AF.Sigmoid,
                                                                        bias=prm[:, cgc:cgc + 1], scale=1.0), r=[kg_, "prm"], w=[kt])
                    gcs.append((gs, kt))
            for c in range(8):
                kb = ("Bf", c)
                self.V(lambda e, c=c, mean=mean: e.tensor_tensor(out=self.Bf[:, c, :], in0=self.Bf[:, c, :], in1=mean[:], op=ALU.subtract),
                       r=[kb, kmn], w=[kb])
                self.V(lambda e, c=c, rstd=rstd: e.tensor_tensor(out=self.Bf[:, c, :], in0=self.Bf[:, c, :], in1=rstd[:], op=ALU.mult),
                       r=[kb, krs], w=[kb])
                cg, cb2 = self.P_CLG + c, self.P_CLB + c
                self.A(lambda e, c=c, cg=cg, cb2=cb2: e.activation(out=self.Cb[:, c, :], in_=self.Bf[:, c, :], func=AF.Silu,
                                                                  bias=prm[:, cb2:cb2 + 1], scale=prm[:, cg:cg + 1]),
                       r=[kb, "prm"], w=[("Cb", c)])
            if getattr(self, 'dbg_stop', None) == 'conv':
                return
            for i in range(2):
                wpw, wpwk = self.ws_next()
                for jj in range(4):
                    j = i * 4 + jj
                    gs, kt = gcs[j]
                    by = self.bank()
                    ky = ("ps", by)
                    for c in range(8):
                        self.T(lambda e, by=by, c=c, jj=jj, wpw=wpw: e.matmul(ps[:, by, :], lhsT=wpw[:, c, jj * 128:(jj + 1) * 128],
                                                                             rhs=self.Cb[:, c, :], start=(c == 0), stop=(c == 7)),
                               r=[wpwk, ("Cb", c)], w=[ky])
                    cpb = self.P_PWB + j
                    self.V(lambda e, by=by, j=j, cpb=cpb, gs=gs: e.scalar_tensor_tensor(out=self.Bf[:, j, :], in0=ps[:, by, :],
                                                                                      scalar=prm[:, cpb:cpb + 1], in1=gs[:],
                                                                                      op0=ALU.add, op1=ALU.mult),
                           r=[ky, kt, "prm"], w=[("Bf", j)])
            if getattr(self, 'dbg_stop', None) == 'pw':
                return
            qw = {}

            def projQ(h):
                i, jj = divmod(h, 4)
                if jj == 0:
                    qw["wp"], qw["wk"] = self.ws_next()
                wp, wk = qw["wp"], qw["wk"]
                b = self.bank()
                pk = ("ps", b)
                for k in range(8):
                    self.T(lambda e, b=b, k=k, jj=jj, wp=wp: e.matmul(ps[:, b, :], lhsT=wp[:, k, jj * 128:(jj + 1) * 128],
                                                                     rhs=self.xT[:, k, PADC:PADC + TT], start=(k == 0), stop=(k == 7)),
                           r=[wk, "xT"], w=[pk])
                cq = self.P_BIN + OFF_Q // 128 + h
                return pk, ps[:, b, :], prm[:, cq:cq + 1]

            self.norm_rope_pipe(NH, projQ, prm[:, self.P_QG:self.P_QG + 1], rp,
                                [(self.Db[:, h, :], ("Db", h)) for h in range(NH)])
            NPAIR = S_LEN // 256
            units = [(g, qs) for g in range(NKV) for qs in range(4)]

            SP = ((0, 1), (2, 3), (4, 5))
            bO, bS = 6, 7
            kO, kS = ("ps", bO), ("ps", bS)

            def emit_S(f):
                n, j = divmod(f, NPAIR)
                g, qs = units[n]
                sp = SP[f % 3]
                qk = [("Db", g * 4 + hh) for hh in range(4)]
                for u in range(2):
                    kc = 2 * j + u
                    self.T(lambda e, b=sp[u], g=g, kc=kc, qs=qs: e.matmul(
                        ps[:, b, :].rearrange("p (h q) -> p h q", h=4), lhsT=self.KT[:, g, kc * 128:(kc + 1) * 128],
                        rhs=self.Db[:, g * 4:g * 4 + 4, qs * 128:(qs + 1) * 128], start=True, stop=True),
                        r=["KT"] + qk, w=[("ps", sp[u])])

            NF = len(units) * NPAIR
            pending = []
            emit_S(0)
            emit_S(1)
            for f in range(NF):
                n, j = divmod(f, NPAIR)
                g, qs = units[n]
                if f + 2 < NF:
                    emit_S(f + 2)
                sp = SP[f % 3]
                pi = f % 3
                pt = self.PT[pi]
                kp = ("PT", pi)
                self.A(lambda e, s0=sp[0], pt=pt: e.activation(out=pt[:].rearrange("p (u n) -> p u n", u=2), in_=ps[:, s0:s0 + 2, :],
                                                               func=AF.Exp, scale=ATT_SCALE),
                       r=[("ps", sp[0]), ("ps", sp[1])], w=[kp])
                for u in range(2):
                    kc = 2 * j + u
                    first = (j == 0 and u == 0)
                    last = (j == NPAIR - 1 and u == 1)
                    self.T(lambda e, g=g, kc=kc, pt=pt, u=u, first=first, last=last: e.matmul(
                        ps[:, bO, :], lhsT=self.Vt[:, kc, g * 128:(g + 1) * 128], rhs=pt[:, u * 512:(u + 1) * 512],
                        start=first, stop=last), r=["Vt", kp], w=[kO])
                while pending:
                    self.T(*pending.pop(0))
                p2i = f % 4
                p2 = self.P2[p2i]
                kp2 = ("P2", p2i)
                self.V(lambda e, pt=pt, p2=p2: e.tensor_tensor(out=p2[:], in0=pt[:, 0:512], in1=pt[:, 512:1024], op=ALU.add),
                       r=[kp], w=[kp2])
                if j % 2 == 1:
                    pp = self.P2[(f - 1) % 4]
                    self.V(lambda e, p2=p2, pp=pp: e.tensor_tensor(out=p2[:], in0=p2[:], in1=pp[:], op=ALU.add),
                           r=[kp2, ("P2", (f - 1) % 4)], w=[kp2])
                    sums_mm = (lambda e, p2=p2, j=j: e.matmul(ps[:, bS, :], lhsT=self.ones_b[:], rhs=p2[:],
                                                             start=(j == 1), stop=(j == NPAIR - 1)), ["ones_b", kp2], [kS])
                    if j == NPAIR - 1:
                        self.T(*sums_mm)
                    else:
                        pending.append(sums_mm)
                if j == NPAIR - 1:
                    ia, ib = self.tmpk(), self.tmpk()
                    ta, tb = self.tmp[ia], self.tmp[ib]
                    ka, kb = ("tmp", ia), ("tmp", ib)
                    self.V(lambda e, ta=ta: e.tensor_copy(out=ta[:], in_=ps[:, bO, :]), r=[kO], w=[ka])
                    self.A(lambda e, tb=tb: e.copy(out=tb[:], in_=ps[:, bS, :]), r=[kS], w=[kb])
                    self.V(lambda e, tb=tb: e.reciprocal(out=tb[:], in_=tb[:]), r=[kb], w=[kb])
                    ek = [("Eb", g * 4 + hh) for hh in range(4)]
                    self.V(lambda e, g=g, qs=qs, ta=ta, tb=tb: e.tensor_tensor(
                        out=self.Eb[:, g * 4:g * 4 + 4, qs * 128:(qs + 1) * 128],
                        in0=ta[:].rearrange("p (h q) -> p h q", h=4), in1=tb[:].rearrange("p (h q) -> p h q", h=4),
                        op=ALU.mult), r=[ka, kb] + ek, w=ek)
            for i in range(2):
                wga, wgak = self.ws_next()
                wwo, wwok = self.ws_next()
                for jj in range(4):
                    j = i * 4 + jj
                    bg = self.bank()
                    kg_ = ("ps", bg)
                    for k in range(8):
                        self.T(lambda e, bg=bg, k=k, jj=jj, wga=wga: e.matmul(ps[:, bg, :], lhsT=wga[:, k, jj * 128:(jj + 1) * 128],
                                                                             rhs=self.xT[:, k, PADC:PADC + TT], start=(k == 0), stop=(k == 7)),
                               r=[wgak, "xT"], w=[kg_])
                    cga = self.P_BIN + OFF_GA // 128 + j
                    ti = self.tmpk()
                    gs = self.tmp[ti]
                    kt = ("tmp", ti)
                    self.A(lambda e, bg=bg, cga=cga, gs=gs: e.activation(out=gs[:], in_=ps[:, bg, :], func=AF.Sigmoid,
                                                                        bias=prm[:, cga:cga + 1], scale=1.0), r=[kg_, "prm"], w=[kt])
                    by = self.bank()
                    ky = ("ps", by)
                    for h in range(8):
                        self.T(lambda e, by=by, h=h, jj=jj, wwo=wwo: e.matmul(ps[:, by, :], lhsT=wwo[:, h, jj * 128:(jj + 1) * 128],
                                                                             rhs=self.Eb[:, h, :], start=(h == 0), stop=(h == 7)),
                               r=[wwok, ("Eb", h)], w=[ky])
                    self.V(lambda e, by=by, gs=gs: e.tensor_tensor(out=gs[:], in0=gs[:], in1=ps[:, by, :], op=ALU.mult),
                           r=[kt, ky], w=[kt])
                    self.V(lambda e, j=j, gs=gs: e.tensor_tensor(out=self.Cb[:, j, :], in0=self.Bf[:, j, :], in1=gs[:], op=ALU.add),
                           r=[("Bf", j), kt], w=[("Cb", j)])
            wpa, wka = self.ws_next()
            wpb, wkb = self.ws_next()
            def wo_p1(st):
                r0 = t0 + st * 128
                sg = t * 4 + st
                par = st % 2
                tr, ar = self.row[par], self.row[2]
                ktr = ("row", par)
                bb = []
                for (wp, wk) in ((wpa, wka), (wpb, wkb)):
                    b = self.bank()
                    bb.append(b)
                    for k in range(8):
                        self.T(lambda e, b=b, k=k, st=st, wp=wp: e.matmul(ps[:, b, :], lhsT=self.Cb[:, k, st * 128:(st + 1) * 128],
                                                                         rhs=wp[:, k, :], start=(k == 0), stop=(k == 7)),
                               r=[wk, ("Cb", k)], w=[("ps", b)])
                for hf in range(2):
                    b = bb[hf]
                    self.V(lambda e, b=b, hf=hf, tr=tr: e.scalar_tensor_tensor(out=tr[:, hf * 512:(hf + 1) * 512], in0=tr[:, hf * 512:(hf + 1) * 512],
                                                                              scalar=ALPHA, in1=ps[:, b, :], op0=ALU.mult, op1=ALU.add),
                           r=[ktr, ("ps", b)], w=[ktr])
                self.V(lambda e, tr=tr: e.tensor_tensor(out=tr[:], in0=tr[:], in1=self.bc[2][:], op=ALU.add), r=[ktr, ("bc", 2)], w=[ktr])
                self.ln_rows(tr[:], tr[:], 0, 1, srck=ktr, dstk=ktr, par=par)

            def wo_p2(st):
                r0 = t0 + st * 128
                sg = t * 4 + st
                par = st % 2
                tr, ar = self.row[par], self.row[2]
                ktr = ("row", par)
                self.A(lambda e, tr=tr: e.copy(out=self.rowb[:], in_=tr[:]), r=[ktr], w=["rowb"])
                self.dma(lambda e, r0=r0: e.dma_start(out=self.X1d[r0:r0 + 128, :], in_=self.rowb[:]), r=["rowb"], w=["X1d"], semkey=("st", "rowb"))
                self.A(lambda e, tr=tr: e.mul(ar[:], tr[:], ALPHA), r=[ktr], w=[("row", 2)])
                self.dma(lambda e, r0=r0: e.dma_start(out=self.ACCd[r0:r0 + 128, :], in_=ar[:]), r=[("row", 2)] + [("sc", NE - 1, c) for c in range(4)], w=["ACCd"], semkey=("st", "row2"))
                if "x1" in self.dbg and l == 0:
                    self.dma(lambda e, r0=r0, tr=tr: e.dma_start(out=self.dbg_t["x1"][r0:r0 + 128, :], in_=tr[:]), r=[ktr], w=["dbg_x1"],
                             semkey=("st", "dbgx1"))
                for half in range(2):
                    b = self.bank()
                    pk = ("ps", b)
                    for jx in range(4):
                        k = half * 4 + jx
                        self.T(lambda e, b=b, jx=jx, k=k, tr=tr: e.transpose(out=ps[:, b, jx * 128:(jx + 1) * 128],
                                                                             in_=tr[:, k * 128:(k + 1) * 128], identity=self.ident_f[:]),
                               r=[ktr, "ident_f"], w=[pk])
                    self.A(lambda e, b=b, half=half: e.copy(out=self.x1T[:, half * 4:half * 4 + 4, :],
                                                           in_=ps[:, b, :].rearrange("p (j n) -> p j n", j=4)), r=[pk], w=["x1T"])
                b = self.bank()
                pk = ("ps", b)
                for k in range(8):
                    self.T(lambda e, b=b, k=k: e.matmul(ps[:, b, 0:NE], lhsT=self.x1T[:, k, :], rhs=self.wr[:, k, :],
                                                        start=(k == 0), stop=(k == 7)), r=["x1T", "wr"], w=[pk])
                sm = self.small
                c0 = 8 + par * 4
                e0 = 16 + par * 16
                kq = [("rt", par, i) for i in range(5)]
                self.V(lambda e, b=b, c0=c0: e.reduce_max(out=sm[:, c0:c0 + 1], in_=ps[:, b, 0:NE], axis=mybir.AxisListType.X), r=[pk], w=[kq[0]])
                self.V(lambda e, c0=c0: e.tensor_scalar(out=sm[:, c0 + 1:c0 + 2], in0=sm[:, c0:c0 + 1], scalar1=-1.0, scalar2=None, op0=ALU.mult),
                       r=[kq[0]], w=[kq[1]])
                self.A(lambda e, b=b, c0=c0, e0=e0: e.activation(out=sm[:, e0:e0 + NE], in_=ps[:, b, 0:NE], func=AF.Exp, bias=sm[:, c0 + 1:c0 + 2],
                                                                scale=1.0, accum_out=sm[:, c0 + 2:c0 + 3]), r=[pk, kq[1]], w=[kq[2], kq[3]])
                self.V(lambda e, c0=c0: e.reciprocal(out=sm[:, c0 + 3:c0 + 4], in_=sm[:, c0 + 2:c0 + 3]), r=[kq[3]], w=[kq[4]])
                self.V(lambda e, sg=sg, c0=c0, e0=e0: e.tensor_scalar(out=self.aff[:, sg, :], in0=sm[:, e0:e0 + NE], scalar1=sm[:, c0 + 3:c0 + 4],
                                                                     scalar2=None, op0=ALU.mult), r=[kq[2], kq[4]], w=["aff"])
                self.dma(lambda e, sg=sg, r0=r0: e.dma_start(out=self.AFFd[r0:r0 + 128, :], in_=self.aff[:, sg, :]), r=["aff"], w=["AFFd"],
                         semkey=("st", "aff"))

            def wo_load(st):
                r0 = t0 + st * 128
                tr = self.row[st % 2]
                self.dma(lambda e, r0=r0, tr=tr: e.dma_start(out=tr[:], in_=self.Xd[r0:r0 + 128, :]), r=["Xd"], w=[("row", st % 2)])

            wo_load(0)
            wo_load(1)
            for step in range(5):
                if step < 4:
                    wo_p1(step)
                if step >= 1:
                    wo_p2(step - 1)
                    if step + 1 < 4:
                        wo_load(step + 1)

    def topk(self, l):
        ps = self.ps
        work = self.Bf[0:NE, :, :].rearrange("p a b -> p (a b)")
        wk = [("Bf", c) for c in range(8)]
        for bi in range(8):
            b = self.bank()
            pk = ("ps", b)
            for j in range(4):
                sg = bi * 4 + j
                self.T(lambda e, b=b, j=j, sg=sg: e.transpose(out=ps[0:NE, b, j * 128:(j + 1) * 128], in_=self.aff[:, sg, :],
                                                              identity=self.ident_f[:]), r=["aff", "ident_f"], w=[pk])
            self.V(lambda e, b=b, bi=bi: e.tensor_copy(out=work[:, bi * 512:(bi + 1) * 512], in_=ps[0:NE, b, :]), r=[pk], w=[wk[bi]])
        iota_t = self.KT[:].rearrange("p a n -> p (a n)").bitcast(I32)[0:NE, :]
        self.G(lambda e: e.iota(out=iota_t, pattern=[[1, S_LEN]], base=0, channel_multiplier=0), w=["KT"])
        wi = work.bitcast(I32)
        self.V(lambda e: e.tensor_scalar(out=wi, in0=wi, scalar1=self.bmask[0:NE, 0:1], scalar2=None, op0=ALU.bitwise_and),
               r=wk + ["bmask"], w=wk)
        self.V(lambda e: e.tensor_tensor(out=wi, in0=wi, in1=iota_t, op=ALU.bitwise_or), r=wk + ["KT"], w=wk)
        for r in range(CAP // 8):
            self.V(lambda e, r=r: e.max(out=self.topv[:, r * 8:(r + 1) * 8], in_=work), r=wk, w=["topv"])
            self.V(lambda e, r=r: e.match_replace(out=work, in_to_replace=self.topv[:, r * 8:(r + 1) * 8], in_values=work, imm_value=-1.0),
                   r=wk + ["topv"], w=wk)
        b = self.bank()
        pk = ("ps", b)
        for col in range(4):
            self.T(lambda e, b=b, col=col: e.transpose(out=ps[:, b, col * NE:(col + 1) * NE], in_=self.topv[:, col * 128:(col + 1) * 128],
                                                       identity=self.ident_f[0:NE, 0:NE]), r=["topv", "ident_f"], w=[pk])
        self.V(lambda e, b=b: e.tensor_copy(out=self.slot_g[:].rearrange("p e c -> p c e"),
                                            in_=ps[:, b, 0:4 * NE].rearrange("p (c e) -> p c e", c=4)), r=[pk], w=["slot_g"])
        self.V(lambda e: e.tensor_scalar(out=self.slot_i[:].bitcast(I32), in0=self.slot_g[:].bitcast(I32), scalar1=self.bmask[:, 1:2],
                                         scalar2=None, op0=ALU.bitwise_and), r=["slot_g", "bmask"], w=["slot_i"])

    def moe_specs(self, l):
        p = self.p
        specs = []
        for ex in range(NE):
            wg, wu, wd = p["w_gate"][l, ex], p["w_up"][l, ex], p["w_down"][l, ex]
            for i in range(4):
                specs.append([(0, 512, wg[:, i * 512:(i + 1) * 512])])
                specs.append([(0, 512, wu[:, i * 512:(i + 1) * 512])])
            for ch in range(2):
                for rh in range(2):
                    specs.append([(0, 512, wd[rh * 1024:(rh + 1) * 1024, ch * 512:(ch + 1) * 512])])
        return specs

    def moe(self, l):
        ps = self.ps
        specs = self.moe_specs(l)
        self.ws_reset(specs, hw_mask=[(i % 3) != 2 for i in range(len(specs))],
                      extra_slots=[(self.xT[:, :, 0:512], "xT")])
        xeb = self.Db
        xebv = xeb[:].rearrange("p (c a) n -> p c (a n)", c=4)

        def gather(ex):
            for col in range(4):
                self.dma(lambda e, ex=ex, col=col: e.indirect_dma_start(
                    out=self.gg[ex % 2][:, col, :], out_offset=None, in_=self.AFFd[:, :],
                    in_offset=bass.IndirectOffsetOnAxis(ap=self.slot_i[:, ex, col:col + 1], axis=0)),
                    r=["AFFd", "slot_i"], w=[("gg", ex % 2, col)], eng="gpsimd", semkey=("ggather", ex % 2, col))
                self.dma(lambda e, ex=ex, col=col: e.indirect_dma_start(
                    out=xebv[:, col, :], out_offset=None, in_=self.X1d[:, :],
                    in_offset=bass.IndirectOffsetOnAxis(ap=self.slot_i[:, ex, col:col + 1], axis=0)),
                    r=["X1d", "slot_i"], w=[("Db", 2 * col), ("Db", 2 * col + 1)], eng="gpsimd", semkey=("gather", col))

        def transposes(ex):
            for k in range(8):
                b = self.bank()
                pk = ("ps", b)
                pv = ps[:, b, :].bitcast(BF16)
                for col in range(4):
                    self.T(lambda e, pv=pv, col=col, k=k: e.transpose(out=pv[:, col * 128:(col + 1) * 128],
                                                                      in_=xebv[:, col, k * 128:(k + 1) * 128], identity=self.ident_b[:]),
                           r=[("Db", 2 * col), ("Db", 2 * col + 1), "ident_b"], w=[pk])
                self.A(lambda e, pv=pv, k=k: e.copy(out=self.Eb[:, k, :], in_=pv[:, 0:512]), r=[pk], w=[("Eb", k)])

        gather(0)
        transposes(0)
        gather(1)
        for ex in range(NE):
            ek = [("Eb", k) for k in range(8)]
            hid = [self.uT[:, f, 0:512] for f in range(8)] + [self.Cb[:, f, :] for f in range(8)]
            hk = [("uT", f) for f in range(8)] + [("Cb", f) for f in range(8)]
            for i in range(4):
                wg, wgk = self.ws_next()
                wu, wuk = self.ws_next()
                for jj in range(4):
                    f = i * 4 + jj
                    bg, bu = self.bank(), self.bank()
                    for (w_, wk_, b_) in ((wg, wgk, bg), (wu, wuk, bu)):
                        for k in range(8):
                            self.T(lambda e, w_=w_, b_=b_, k=k, jj=jj: e.matmul(ps[:, b_, :], lhsT=w_[:, k, jj * 128:(jj + 1) * 128],
                                                                               rhs=self.Eb[:, k, :], start=(k == 0), stop=(k == 7)),
                                   r=[wk_, ("Eb", k)], w=[("ps", b_)])
                    ti = self.tmpk()
                    sg = self.tmp[ti]
                    kt = ("tmp", ti)
                    self.A(lambda e, bg=bg, sg=sg: e.activation(out=sg[:], in_=ps[:, bg, :], func=AF.Silu), r=[("ps", bg)], w=[kt])
                    self.V(lambda e, bu=bu, sg=sg, f=f: e.tensor_tensor(out=hid[f], in0=sg[:], in1=ps[:, bu, :], op=ALU.mult),
                           r=[kt, ("ps", bu)], w=[hk[f]])
            if ex + 1 < NE:
                transposes(ex + 1)
            ye = self.Bf
            yev = ye[:].rearrange("p (c a) n -> p c (a n)", c=4)
            yk = [("Bf", c) for c in range(8)]
            for ch in range(2):
                wd0, wdk0 = self.ws_next()
                wd1, wdk1 = self.ws_next()
                for col in range(4):
                    b = self.bank()
                    pk = ("ps", b)
                    for f in range(16):
                        w_, wk_ = (wd0, wdk0) if f < 8 else (wd1, wdk1)
                        self.T(lambda e, b=b, f=f, col=col, w_=w_: e.matmul(ps[:, b, :], lhsT=hid[f][:, col * 128:(col + 1) * 128],
                                                                           rhs=w_[:, f % 8, :], start=(f == 0), stop=(f == 15)),
                               r=[wk_, hk[f]], w=[pk])
                    self.V(lambda e, b=b, col=col, ch=ch, ex=ex: e.tensor_scalar(
                        out=yev[:, col, ch * 512:(ch + 1) * 512], in0=ps[:, b, :], scalar1=self.gg[ex % 2][:, col, ex:ex + 1], scalar2=None,
                        op0=ALU.mult), r=[pk, ("gg", ex % 2, col)], w=[("Bf", col * 2), ("Bf", col * 2 + 1)])
            if ex + 2 < NE:
                gather(ex + 2)
            for col in range(4):
                self.dma(lambda e, ex=ex, col=col: e.indirect_dma_start(
                    out=self.ACCd[:, :], out_offset=bass.IndirectOffsetOnAxis(ap=self.slot_i[:, ex, col:col + 1], axis=0),
                    in_=yev[:, col, :], in_offset=None, compute_op=ALU.add),
                    r=yk + ["slot_i"] + (["ACCd"] if ex == 0 else [("sc", ex - 1, c) for c in range(4)]),
                    w=[("sc", ex, col)], eng="gpsimd", semkey=("scat", col))

    def final(self, l):
        p = self.p
        self.load_bc(0, p["ln2_g"][l:l + 1, :])
        self.load_bc(1, p["ln2_b"][l:l + 1, :])
        def f_load(sg):
            rw = self.row[sg % 3]
            self.dma(lambda e, rw=rw, r0=sg * 128: e.dma_start(out=rw[:], in_=self.ACCd[r0:r0 + 128, :]),
                     r=["ACCd"] + [("sc", NE - 1, c) for c in range(4)], w=[("row", sg % 3)])

        f_load(0)
        f_load(1)
        for sg in range(S_LEN // 128):
            r0 = sg * 128
            rw = self.row[sg % 3]
            rk = ("row", sg % 3)
            self.ln_rows(rw[:], rw[:], 0, 1, srck=rk, dstk=rk, par=sg % 2)
            self.dma(lambda e, rw=rw, r0=r0: e.dma_start(out=self.out[r0:r0 + 128, :], in_=rw[:]), r=[rk], w=["out"], semkey=("st", rk))
            if sg + 2 < S_LEN // 128:
                f_load(sg + 2)

    def build(self, stop_after=None):
        with ExitStack() as st:
            self.st = st
            self.declare()
            for nm, shp, dt in self.dbg_decl:
                self.dbg_out(nm, shp, dt)
            self.alloc()
            self.S = Sched(self.nc, st)
            self.setup_consts()
            p = self.p
            for li, l in enumerate(self.layers):
                self.load_layer_params(li)
                if l == 0:
                    src, gb = self.x_in, (p["ln0_g"], p["ln0_b"])
                else:
                    src, gb = self.ACCd, (p["ln2_g"][li - 1:li, :], p["ln2_b"][li - 1:li, :])
                self.pass1(li, src, gb)
                if stop_after == "pass1":
                    break
                self.pass2(li)
                if stop_after == "pass2":
                    break
                self.topk(li)
                if stop_after == "topk":
                    break
                self.moe(li)
            if stop_after is None:
                self.final(len(self.layers) - 1)
                fin = ["out"]
            else:
                fin = []
            self.debug_dumps(stop_after)
            fin += ["dbg_" + n for n in self.dbg_written]
            self.S.finish("sync", fin)
            self.S.emit()
        return self.nc

    dbg_decl = ()
    dbg_written = ()

    def debug_dumps(self, stop_after):
        pass


def rope_tables():
    t = np.arange(S_LEN)
    row = (t // 64).astype(np.float32)
    col = (t % 64).astype(np.float32)
    axis_dim = DH // 2
    freqs = (1.0 / (np.float32(10000.0) ** (np.arange(0, axis_dim, 2, dtype=np.float32) / np.float32(axis_dim)))).astype(np.float32)
    ang = np.concatenate([row[:, None] * freqs[None], col[:, None] * freqs[None]], axis=-1).astype(np.float32)
    cos = np.cos(ang).astype(np.float32)
    sin = np.sin(ang).astype(np.float32)
    C = np.repeat(cos.T, 2, axis=0)
    Sg = np.repeat(sin.T, 2, axis=0)
    sign = np.where(np.arange(DH) % 2 == 0, -1.0, 1.0).astype(np.float32)[:, None]
    return np.ascontiguousarray(C), np.ascontiguousarray(Sg * sign)


def const_inputs():
    ident = np.eye(128, dtype=np.float32)
    swap = np.zeros((128, 128), np.float32)
    idx = np.arange(128)
    swap[idx, idx ^ 1] = 1.0
    C, Sg = rope_tables()
    return {"c_ident": ident, "c_swap": swap, "c_ropeC": C, "c_ropeS": Sg}


def core_inputs(inputs, b, layers=None):
    m = {"x": np.ascontiguousarray(inputs["x"][b])}
    if layers is not None:
        inputs = dict(inputs)
        for nm in inputs:
            if nm not in ("x", "ln0_g", "ln0_b"):
                inputs[nm] = np.ascontiguousarray(inputs[nm][list(layers)])
    L = NL if layers is None else len(layers)
    m["ln0_g"] = inputs["ln0_g"].reshape(1, D)
    m["ln0_b"] = inputs["ln0_b"].reshape(1, D)
    m["b_in"] = inputs["b_in"].reshape(L, N_IN // 128, 128)
    for nm in ("conv_dw_b", "conv_ln_g", "conv_ln_b", "conv_pw_b"):
        m[nm] = inputs[nm].reshape(L, 8, 128)
    m["q_norm_g"] = inputs["q_norm_g"].reshape(L, 1, DH)
    m["k_norm_g"] = inputs["k_norm_g"].reshape(L, 1, DH)
    for nm in ("w_in", "conv_dw", "conv_pw_w", "w_o", "w_out", "b_out", "ln1_g", "ln1_b", "w_router", "w_gate", "w_up",
               "w_down", "ln2_g", "ln2_b"):
        m[nm] = inputs[nm]
    m.update(const_inputs())
    return m


def kernel(**inputs):
    inputs = {k: np.asarray(v) for k, v in inputs.items()}
    mk = MK()
    nc = mk.build()
    n = 4
    in_maps = [core_inputs(inputs, c) for c in range(n)]
    res = run_bass_kernel_spmd(nc, in_maps, core_ids=list(range(n)))
    out = np.stack([np.asarray(res.results[c]["out"]) for c in range(n)], axis=0)
    return out.astype(np.float32)
```
